# Optimizing a Trainium2 kernel written in Bass

```python
import math
import jax, jax.numpy as jnp
from jax import lax
import numpy as np

D_MODEL = 1024
BATCH = 4
SEQ = 4096
DEPTH = 2
DEC_BATCH = 128
DEC_SEQ = 4
PAST_LEN = 8192
PAGE_SIZE = 128

D_MIX = D_MODEL
HEAD_DIM = 64
D_SSM = D_MIX // 2
N_SSM_HEADS = D_SSM // HEAD_DIM
N_SSM_GROUPS = 2
D_STATE = 128
CONV_W = 4
CONV_DIM = D_SSM + 2 * N_SSM_GROUPS * D_STATE
SSD_CHUNK = 128
D_ATTN = D_MIX - D_SSM
N_HEADS = D_ATTN // HEAD_DIM
N_KV_HEADS = 2
Q_PER_KV = N_HEADS // N_KV_HEADS
WINDOW = 128
ATTN_BLOCK = 128
D_FF = 2752
D_PLE = 256
D_PROJ = D_SSM + CONV_DIM + N_SSM_HEADS + D_ATTN + 2 * N_KV_HEADS * HEAD_DIM
RMS_EPS = 1e-6

kernel_name = 'hymba_ssd_swa_sink_macaron_step'


def _rmsnorm(x, w):
    xf = x.astype(jnp.float32)
    y = xf * lax.rsqrt(jnp.mean(xf * xf, axis=-1, keepdims=True) + RMS_EPS)
    return (y * w.astype(jnp.float32)).astype(x.dtype)


def _swiglu(x, w1, w3, w2):
    return (jax.nn.silu(x @ w1) * (x @ w3)) @ w2


def _alibi_slopes():
    hh = np.arange(1, N_HEADS + 1, dtype=np.float32)
    return jnp.asarray(np.power(np.float32(2.0), -8.0 * hh / N_HEADS), dtype=jnp.float32)


def _causal_conv(xbc, buf, w, b):
    full = jnp.concatenate([buf.astype(xbc.dtype), xbc], axis=1)
    out = lax.conv_general_dilated(full, w.astype(xbc.dtype)[:, None, :], window_strides=(1,),
                                   padding='VALID', dimension_numbers=('NWC', 'WIO', 'NWC'),
                                   feature_group_count=CONV_DIM)
    return jax.nn.silu(out + b.astype(out.dtype)), full[:, -(CONV_W - 1):]


def _ssd(x, dt, a, bm, cm, h0):
    f32 = jnp.float32
    bsz, L = x.shape[0], x.shape[1]
    T = SSD_CHUNK if L % SSD_CHUNK == 0 else L
    nc = L // T
    rep = N_SSM_HEADS // N_SSM_GROUPS
    xc = x.astype(f32).reshape(bsz, nc, T, N_SSM_HEADS, HEAD_DIM)
    dtc = dt.reshape(bsz, nc, T, N_SSM_HEADS)
    bc = jnp.repeat(bm.astype(f32), rep, axis=2).reshape(bsz, nc, T, N_SSM_HEADS, D_STATE)
    cc = jnp.repeat(cm.astype(f32), rep, axis=2).reshape(bsz, nc, T, N_SSM_HEADS, D_STATE)
    cum = jnp.cumsum(dtc * a, axis=2)
    tt = np.arange(T)
    causal = tt[:, None] >= tt[None, :]
    seg = jnp.transpose(cum, (0, 1, 3, 2))
    lmat = jnp.exp(jnp.where(causal, seg[..., :, None] - seg[..., None, :], -jnp.inf))
    xdt = xc * dtc[..., None]
    scores = jnp.einsum('bcthn,bcshn->bchts', cc, bc) * lmat
    y = jnp.einsum('bchts,bcshp->bcthp', scores, xdt)
    decay_end = jnp.exp(cum[:, :, -1:, :] - cum)
    states = jnp.einsum('bcthn,bcthp->bchpn', bc * decay_end[..., None], xdt)
    chunk_decay = jnp.exp(cum[:, :, -1, :])

    def step(hs, inp):
        st, dec = inp
        return hs * dec[:, :, None, None] + st, hs

    h_last, h_in = lax.scan(step, h0.astype(f32),
                            (jnp.transpose(states, (1, 0, 2, 3, 4)), jnp.transpose(chunk_decay, (1, 0, 2))))
    h_in = jnp.transpose(h_in, (1, 0, 2, 3, 4))
    y = y + jnp.einsum('bcthn,bchpn->bcthp', cc * jnp.exp(cum)[..., None], h_in)
    return y.reshape(bsz, L, N_SSM_HEADS, HEAD_DIM), h_last


def _window_attention(q, k, v, k_past, v_past, pos0, sinks):
    f32 = jnp.float32
    bsz, L = q.shape[0], q.shape[1]
    qb_len = ATTN_BLOCK if L % ATTN_BLOCK == 0 else L
    nb = L // qb_len
    span = qb_len + WINDOW
    k_full = jnp.concatenate([k_past.astype(k.dtype), k], axis=1)
    v_full = jnp.concatenate([v_past.astype(v.dtype), v], axis=1)
    kidx = np.arange(nb)[:, None] * qb_len + np.arange(span)[None, :]
    kb = k_full[:, kidx].astype(f32)
    vb = v_full[:, kidx].astype(f32)
    qb = q.astype(f32).reshape(bsz, nb, qb_len, N_KV_HEADS, Q_PER_KV, HEAD_DIM)
    qpos = pos0 + np.arange(L).reshape(nb, qb_len)
    kpos = pos0 - WINDOW + kidx
    rel = qpos[:, :, None] - kpos[:, None, :]
    valid = (rel >= 0) & (rel < WINDOW) & (kpos[:, None, :] >= 0)
    slopes = _alibi_slopes().reshape(N_KV_HEADS, Q_PER_KV)
    bias = -slopes[None, :, :, None, None] * jnp.asarray(rel, f32)[:, None, None]
    s = jnp.einsum('bnqkgd,bnskd->bnkgqs', qb, kb) * (HEAD_DIM ** -0.5)
    s = jnp.where(valid[:, None, None], s + bias, -jnp.inf)
    sink = sinks.astype(f32).reshape(N_KV_HEADS, Q_PER_KV)[:, :, None, None]
    m = jnp.maximum(jnp.max(s, axis=-1, keepdims=True), sink)
    pr = jnp.exp(s - m)
    denom = jnp.sum(pr, axis=-1, keepdims=True) + jnp.exp(sink - m)
    o = jnp.einsum('bnkgqs,bnskd->bnqkgd', pr / denom, vb)
    return o.reshape(bsz, L, N_HEADS * HEAD_DIM), k_full[:, -WINDOW:], v_full[:, -WINDOW:]


def _layer(h, pe, ssm0, conv0, k_past, v_past, pos0, lp):
    f32 = jnp.float32
    bsz, L, _ = h.shape
    hd = h.dtype
    h = h + (0.5 * _swiglu(_rmsnorm(h, lp['g_ffn1']), lp['w1_a'], lp['w3_a'], lp['w2_a'])).astype(hd)
    u = _rmsnorm(h, lp['g_mix'])
    proj = u @ lp['w_in']
    cuts = [D_SSM, D_SSM + CONV_DIM, D_SSM + CONV_DIM + N_SSM_HEADS,
            D_SSM + CONV_DIM + N_SSM_HEADS + D_ATTN,
            D_SSM + CONV_DIM + N_SSM_HEADS + D_ATTN + N_KV_HEADS * HEAD_DIM]
    z, xbc, dt_raw, q, k, v = jnp.split(proj, cuts, axis=-1)
    xbc, conv_new = _causal_conv(xbc, conv0, lp['conv_w'], lp['conv_b'])
    xs, bm, cm = jnp.split(xbc, [D_SSM, D_SSM + N_SSM_GROUPS * D_STATE], axis=-1)
    xs = xs.reshape(bsz, L, N_SSM_HEADS, HEAD_DIM)
    dt = jax.nn.softplus(dt_raw.astype(f32) + lp['dt_bias'].astype(f32))
    a = -jnp.exp(lp['a_log'].astype(f32))
    y, ssm_new = _ssd(xs, dt, a, bm.reshape(bsz, L, N_SSM_GROUPS, D_STATE),
                      cm.reshape(bsz, L, N_SSM_GROUPS, D_STATE), ssm0)
    y = y + lp['d_skip'].astype(f32)[:, None] * xs.astype(f32)
    y = y.reshape(bsz, L, D_SSM) * jax.nn.silu(z.astype(f32))
    y = _rmsnorm(y.reshape(bsz, L, N_SSM_GROUPS, D_SSM // N_SSM_GROUPS),
                 lp['ssm_norm'].reshape(N_SSM_GROUPS, D_SSM // N_SSM_GROUPS)).reshape(bsz, L, D_SSM)
    qh = _rmsnorm(q.reshape(bsz, L, N_HEADS, HEAD_DIM), lp['q_norm'])
    kh = _rmsnorm(k.reshape(bsz, L, N_KV_HEADS, HEAD_DIM), lp['k_norm'])
    vh = v.reshape(bsz, L, N_KV_HEADS, HEAD_DIM)
    ya, k_new, v_new = _window_attention(qh, kh, vh, k_past, v_past, pos0, lp['sinks'])
    mix = jnp.concatenate([y.astype(hd), ya.astype(hd)], axis=-1) @ lp['w_out']
    h = h + mix.astype(hd)
    h = h + (0.5 * _swiglu(_rmsnorm(h, lp['g_ffn2']), lp['w1_b'], lp['w3_b'], lp['w2_b'])).astype(hd)
    gate = jax.nn.sigmoid(_rmsnorm(h, lp['g_ple']) @ lp['w_ple_gate'])
    h = h + (gate * (pe.astype(hd) @ lp['w_ple_proj'])).astype(hd)
    return h, ssm_new, conv_new, k_new, v_new


def setup_inputs(seed: int = 0) -> dict:
    key = jax.random.key(seed)
    ks = jax.random.split(key, 40)
    f32 = jnp.float32

    def nrm(k, shape, scale):
        return jax.random.normal(k, shape, f32) * scale

    def gain(k, shape):
        return 1.0 + 0.05 * jax.random.normal(k, shape, f32)

    dtv = jnp.exp(jax.random.uniform(ks[20], (DEPTH, N_SSM_HEADS), f32) * (math.log(0.1) - math.log(0.001)) + math.log(0.001))
    return {
        'x_prompt': nrm(ks[0], (BATCH, SEQ, D_MODEL), 1.0),
        'x_sample': nrm(ks[1], (DEC_BATCH, DEC_SEQ, D_MODEL), 1.0),
        'state_ssm': nrm(ks[2], (DEPTH, DEC_BATCH, N_SSM_HEADS, HEAD_DIM, D_STATE), 0.5),
        'state_conv': nrm(ks[3], (DEPTH, DEC_BATCH, CONV_W - 1, CONV_DIM), 1.0),
        'cache_k_win': nrm(ks[4], (DEPTH, DEC_BATCH, WINDOW, N_KV_HEADS, HEAD_DIM), 1.0),
        'cache_v_win': nrm(ks[5], (DEPTH, DEC_BATCH, WINDOW, N_KV_HEADS, HEAD_DIM), 1.0),
        'p_prompt': nrm(ks[6], (DEPTH, BATCH, SEQ, D_PLE), 1.0),
        'p_sample': nrm(ks[7], (DEPTH, DEC_BATCH, DEC_SEQ, D_PLE), 1.0),
        'g_ffn1': gain(ks[8], (DEPTH, D_MODEL)),
        'w1_a': nrm(ks[9], (DEPTH, D_MODEL, D_FF), D_MODEL ** -0.5),
        'w3_a': nrm(ks[10], (DEPTH, D_MODEL, D_FF), D_MODEL ** -0.5),
        'w2_a': nrm(ks[11], (DEPTH, D_FF, D_MODEL), D_FF ** -0.5),
        'g_mix': gain(ks[12], (DEPTH, D_MODEL)),
        'w_in': nrm(ks[13], (DEPTH, D_MODEL, D_PROJ), D_MODEL ** -0.5),
        'conv_w': nrm(ks[14], (DEPTH, CONV_W, CONV_DIM), CONV_W ** -0.5),
        'conv_b': nrm(ks[15], (DEPTH, CONV_DIM), 0.02),
        'dt_bias': dtv + jnp.log(-jnp.expm1(-dtv)),
        'a_log': jnp.log(jax.random.uniform(ks[16], (DEPTH, N_SSM_HEADS), f32, 1.0, 16.0)),
        'd_skip': gain(ks[17], (DEPTH, N_SSM_HEADS)),
        'ssm_norm': gain(ks[18], (DEPTH, D_SSM)),
        'q_norm': gain(ks[19], (DEPTH, HEAD_DIM)),
        'k_norm': gain(ks[21], (DEPTH, HEAD_DIM)),
        'sinks': nrm(ks[22], (DEPTH, N_HEADS), 0.5),
        'w_out': nrm(ks[23], (DEPTH, D_MIX, D_MODEL), D_MIX ** -0.5),
        'g_ffn2': gain(ks[24], (DEPTH, D_MODEL)),
        'w1_b': nrm(ks[25], (DEPTH, D_MODEL, D_FF), D_MODEL ** -0.5),
        'w3_b': nrm(ks[26], (DEPTH, D_MODEL, D_FF), D_MODEL ** -0.5),
        'w2_b': nrm(ks[27], (DEPTH, D_FF, D_MODEL), D_FF ** -0.5),
        'g_ple': gain(ks[28], (DEPTH, D_MODEL)),
        'w_ple_gate': nrm(ks[29], (DEPTH, D_MODEL, D_MODEL), D_MODEL ** -0.5),
        'w_ple_proj': nrm(ks[30], (DEPTH, D_PLE, D_MODEL), D_PLE ** -0.5),
    }


def reference(x_prompt, x_sample, state_ssm, state_conv, cache_k_win, cache_v_win, p_prompt, p_sample,
              g_ffn1, w1_a, w3_a, w2_a, g_mix, w_in, conv_w, conv_b, dt_bias, a_log, d_skip, ssm_norm,
              q_norm, k_norm, sinks, w_out, g_ffn2, w1_b, w3_b, w2_b, g_ple, w_ple_gate, w_ple_proj):
    hp = x_prompt
    hs = x_sample
    bp = x_prompt.shape[0]
    dtp = x_prompt.dtype
    ssm_p, conv_p, k_p, v_p = [], [], [], []
    ssm_s, conv_s, k_s, v_s = [], [], [], []
    for l in range(DEPTH):
        lp = {'g_ffn1': g_ffn1[l], 'w1_a': w1_a[l], 'w3_a': w3_a[l], 'w2_a': w2_a[l],
              'g_mix': g_mix[l], 'w_in': w_in[l], 'conv_w': conv_w[l], 'conv_b': conv_b[l],
              'dt_bias': dt_bias[l], 'a_log': a_log[l], 'd_skip': d_skip[l], 'ssm_norm': ssm_norm[l],
              'q_norm': q_norm[l], 'k_norm': k_norm[l], 'sinks': sinks[l], 'w_out': w_out[l],
              'g_ffn2': g_ffn2[l], 'w1_b': w1_b[l], 'w3_b': w3_b[l], 'w2_b': w2_b[l],
              'g_ple': g_ple[l], 'w_ple_gate': w_ple_gate[l], 'w_ple_proj': w_ple_proj[l]}
        ssm0 = jnp.zeros((bp, N_SSM_HEADS, HEAD_DIM, D_STATE), jnp.float32)
        conv0 = jnp.zeros((bp, CONV_W - 1, CONV_DIM), dtp)
        kv0 = jnp.zeros((bp, WINDOW, N_KV_HEADS, HEAD_DIM), dtp)
        hp, s1, c1, k1, v1 = _layer(hp, p_prompt[l], ssm0, conv0, kv0, kv0, 0, lp)
        ssm_p.append(s1); conv_p.append(c1); k_p.append(k1); v_p.append(v1)
        hs, s2, c2, k2, v2 = _layer(hs, p_sample[l], state_ssm[l], state_conv[l], cache_k_win[l],
                                    cache_v_win[l], PAST_LEN, lp)
        ssm_s.append(s2); conv_s.append(c2); k_s.append(k2); v_s.append(v2)
    return (hp, hs, jnp.stack(ssm_p), jnp.stack(conv_p), jnp.stack(k_p), jnp.stack(v_p),
            jnp.stack(ssm_s), jnp.stack(conv_s), jnp.stack(k_s), jnp.stack(v_s))
```

```python
import contextlib
import numpy as np
import concourse.bass as bass
import concourse.mybir as mybir
from concourse.bass_utils import run_bass_kernel_spmd

F32 = mybir.dt.float32
BF16 = mybir.dt.bfloat16
AF = mybir.ActivationFunctionType
ALU = mybir.AluOpType
AX = mybir.AxisListType

D = 1024; DFF = 2752; DPROJ = 2312; DPLE = 256; DEPTH = 2
NCORES = 8
EPS = 1e-6
FT = [(i * 128, 128) for i in range(21)] + [(2688, 64)]
NFT = len(FT)
FCH = [list(range(i, min(i + 2, NFT))) for i in range(0, NFT, 2)]
W2CH = [list(range(i, min(i + 6, NFT))) for i in range(0, NFT, 6)]


DEBUG_MAP = None
SBUF_PEAK = [0, 0]
STAGE_LIMIT = None


class _Stop(Exception):
    pass


class Tok:
    __slots__ = ("w", "r")

    def __init__(self):
        self.w = None
        self.r = {}


class Eng:
    def __init__(self, name, h):
        self.name = name; self.h = h; self.sem = None; self.cnt = 0; self.waited = {}; self.own = set()


def _r32(n):
    return 32 if n <= 32 else (64 if n <= 64 else 128)


class PEProxy:
    def __init__(self, ctx, e):
        self.ctx = ctx; self.e = e; self.last = None

    def _mode(self, key):
        e = self.e
        if key != self.last and e.cnt > 0:
            k = id(e.sem)
            if e.waited.get(k, 0) < e.cnt:
                e.h.wait_ge(e.sem, e.cnt)
                e.waited[k] = e.cnt
        self.last = key

    def matmul(self, out, lhsT, rhs, start=True, stop=True):
        self._mode(("mm", str(lhsT.dtype), _r32(lhsT.shape[0]), _r32(int(np.prod(lhsT.shape[1:]))), out.base_partition()))
        return self.e.h.matmul(out, lhsT=lhsT, rhs=rhs, start=start, stop=stop)

    def transpose(self, out, in_, identity):
        self._mode(("tr", str(in_.dtype), _r32(in_.shape[0]), _r32(int(np.prod(in_.shape[1:]))), out.base_partition()))
        return self.e.h.transpose(out=out, in_=in_, identity=identity)


class Ctx:
    EPOCH = 12000

    def __init__(self, nc, es):
        self.nc = nc; self.es = es
        self.E = {"pe": Eng("pe", nc.tensor), "act": Eng("act", nc.scalar), "dve": Eng("dve", nc.vector),
                  "pool": Eng("pool", nc.gpsimd), "sp": Eng("sp", nc.sync)}
        self.nsem = 0
        for e in self.E.values():
            e.sem = self._newsem(); e.own.add(id(e.sem))
        self.slots = {"sp": [[self._newsem(), 0] for _ in range(10)],
                      "pool": [[self._newsem(), 0] for _ in range(10)]}
        self.slot_i = {"sp": 0, "pool": 0}
        self.semkey = {}
        self.stopped = False
        self.pe_proxy = PEProxy(self, self.E["pe"])
        self.pe_fast = False

    @contextlib.contextmanager
    def fast(self):
        old = self.pe_fast
        self.pe_fast = True; self.pe_proxy.last = "edge"
        try:
            yield
        finally:
            self.pe_fast = old; self.pe_proxy.last = "edge"

    def _newsem(self):
        self.nsem += 1
        return self.es.enter_context(self.nc.semaphore("s%d" % self.nsem))

    def _wait(self, e, ev):
        sem, val = ev
        k = id(sem)
        if e.name == "pe" and k in e.own and self.pe_fast:
            return
        if e.waited.get(k, 0) >= val:
            return
        e.h.wait_ge(sem, val)
        e.waited[k] = val

    def _sync(self, e, reads, writes):
        for t in reads:
            if t.w is not None:
                self._wait(e, t.w)
        for t in writes:
            if t.w is not None:
                self._wait(e, t.w)
            for ev in t.r.values():
                self._wait(e, ev)

    def _commit(self, ev, reads, writes):
        for t in writes:
            t.w = ev; t.r = {}
        for t in reads:
            k = id(ev[0])
            if k not in t.r or t.r[k][1] < ev[1]:
                t.r[k] = ev

    def op(self, eng, reads, writes, fn):
        if self.stopped:
            return None
        e = self.E[eng]
        self._sync(e, reads, writes)
        if e.cnt >= self.EPOCH:
            e.sem = self._newsem(); e.cnt = 0; e.own.add(id(e.sem))
        inst = fn(self.pe_proxy if eng == "pe" else e.h)
        e.cnt += 1
        inst.then_inc(e.sem, 1)
        if DEBUG_MAP is not None:
            import traceback
            nm = None
            for a in ("name", "inst", "instruction", "ins"):
                v = getattr(inst, a, None)
                if v is not None:
                    nm = getattr(v, "name", v) if a != "name" else v
                    break
            fr = traceback.extract_stack(limit=4)[-2]; fr0 = traceback.extract_stack(limit=4)[-3]
            DEBUG_MAP[str(nm)] = "%s:%d < %s:%d" % (fr.name, fr.lineno, fr0.name, fr0.lineno)
        ev = (e.sem, e.cnt)
        e.waited[id(e.sem)] = max(e.waited.get(id(e.sem), 0), 0)
        self._commit(ev, reads, writes)
        return ev

    def dma(self, eng, reads, writes, out, in_, **kw):
        if self.stopped:
            return None
        e = self.E[eng]
        sl = self.slots[eng][self.slot_i[eng]]
        self.slot_i[eng] = (self.slot_i[eng] + 1) % len(self.slots[eng])
        if sl[1] > 0:
            self._wait(e, (sl[0], sl[1]))
        self._sync(e, reads, writes)
        inst = e.h.dma_start(out=out, in_=in_, **kw)
        sl[1] += 16
        inst.then_inc(sl[0], 16)
        ev = (sl[0], sl[1])
        self._commit(ev, reads, writes)
        return ev

    def barrier(self):
        if self.stopped:
            return
        evs = []
        for e in self.E.values():
            if e.cnt > 0:
                evs.append((e.sem, e.cnt))
        for q in self.slots.values():
            for sl in q:
                if sl[1] > 0:
                    evs.append((sl[0], sl[1]))
        for e in self.E.values():
            for ev in evs:
                if ev[0] is e.sem:
                    continue
                self._wait(e, ev)

    def finish(self):
        self.stopped = False
        self.barrier()


def build(SEQ, NSS, NTG):
    NS = NSS * 4
    NG = SEQ // NTG
    nc = bass.Bass("TRN2", target_bir_lowering=False)
    di = lambda n, s: nc.dram_tensor(n, s, F32, kind="ExternalInput").ap()
    do = lambda n, s: nc.dram_tensor(n, s, F32, kind="ExternalOutput").ap()
    xp = di("xp", [SEQ, D]); pp = di("pp", [DEPTH, SEQ, DPLE]); xs = di("xs", [NS, D]); psm = di("psm", [DEPTH, NS, DPLE])
    sssm = di("sssm", [DEPTH, NSS, 512, 128]); sconv = di("sconv", [DEPTH, NSS * 3, 1024])
    ck = di("ck", [DEPTH, NSS, 128, 128]); cv = di("cv", [DEPTH, NSS, 128, 128])
    g_ffn1 = di("g_ffn1", [DEPTH, D]); g_mix = di("g_mix", [DEPTH, D]); g_ffn2 = di("g_ffn2", [DEPTH, D]); g_ple = di("g_ple", [DEPTH, D])
    w1a = di("w1_a", [DEPTH, D, DFF]); w3a = di("w3_a", [DEPTH, D, DFF]); w2a = di("w2_a", [DEPTH, DFF, D])
    w1b = di("w1_b", [DEPTH, D, DFF]); w3b = di("w3_b", [DEPTH, D, DFF]); w2b = di("w2_b", [DEPTH, DFF, D])
    w_in = di("w_in", [DEPTH, D, DPROJ]); w_out = di("w_out", [DEPTH, D, D])
    conv_w = di("conv_w", [DEPTH, 4, 1024]); conv_b = di("conv_b", [DEPTH, 1024])
    dt_bias = di("dt_bias", [DEPTH, 8]); a_log = di("a_log", [DEPTH, 8]); d_skip = di("d_skip", [DEPTH, 8])
    ssm_norm = di("ssm_norm", [DEPTH, 512]); q_norm = di("q_norm", [DEPTH, 64]); k_norm = di("k_norm", [DEPTH, 64])
    sinks = di("sinks", [DEPTH, 8]); w_pg = di("w_ple_gate", [DEPTH, D, D]); w_pp = di("w_ple_proj", [DEPTH, DPLE, D])
    c_ident = di("c_ident", [128, 128]); c_U = di("c_U", [128, 128]); c_SL = di("c_SL", [128, 128])
    c_Ubd = di("c_Ubd", [64, 64]); c_SLbd = di("c_SLbd", [64, 64]); c_BMt = di("c_BMt", [64, 16]); c_BM = di("c_BM", [128, 16 * 64])
    c_bones = di("c_bones", [128, 128]); c_Eprev = di("c_Eprev", [128, 1024]); c_Ecur = di("c_Ecur", [128, 1024])
    c_Ecache = di("c_Ecache", [128, 32]); c_Enew = di("c_Enew", [64, 512])
    yp = do("yp", [SEQ, D]); ys = do("ys", [NS, D])
    ossm_p = do("ossm_p", [DEPTH, 512, 128]); oconv_p = do("oconv_p", [DEPTH, 3, 1024])
    ock_p = do("ock_p", [DEPTH, 128, 128]); ocv_p = do("ocv_p", [DEPTH, 128, 128])
    ossm_s = do("ossm_s", [DEPTH, NSS, 512, 128]); oconv_s = do("oconv_s", [DEPTH, NSS * 3, 1024])
    ock_s = do("ock_s", [DEPTH, NSS, 128, 128]); ocv_s = do("ocv_s", [DEPTH, NSS, 128, 128])
    hscr = nc.dram_tensor("hscr", [8, 128, SEQ], F32, kind="Internal").ap()
    wsc13 = nc.dram_tensor("wsc13", [2, 2, len(FCH), 128, 2048], BF16, kind="Internal").ap()
    wsc2 = nc.dram_tensor("wsc2", [2, 2, 128, NFT * 512], BF16, kind="Internal").ap()
    wscm = nc.dram_tensor("wscm", [3, 128, 18560], BF16, kind="Internal").ap()
    t_sc13 = [[[Tok() for _ in FCH] for _ in range(2)] for _ in range(2)]
    t_sc2 = [[[Tok() for _ in W2CH] for _ in range(2)] for _ in range(2)]
    t_scm = [Tok() for _ in range(3)]
    t_hscr = [Tok() for _ in range(NG)]

    with contextlib.ExitStack() as es:
        cx = Ctx(nc, es)
        op = cx.op; dma = cx.dma

        uniq = [0]

        def SB(scope, name, shape, dt=F32):
            uniq[0] += 1
            t = scope.enter_context(nc.sbuf_tensor("%s_%d" % (name, uniq[0]), shape, dt))
            try:
                SBUF_PEAK[0] = max(SBUF_PEAK[0], int(nc.sbuf_base))
                SBUF_PEAK[1] = int(nc.sbuf_top)
            except Exception:
                pass
            return t

        ident = SB(es, "ident", [128, 128]); t_c = Tok()
        identb = SB(es, "identb", [128, 128], BF16)
        Um = SB(es, "Um", [128, 128]); SLm = SB(es, "SLm", [128, 128])
        Ubd = SB(es, "Ubd", [64, 64]); SLbd = SB(es, "SLbd", [64, 64]); BMt = SB(es, "BMt", [64, 16]); BMtb = SB(es, "BMtb", [64, 16], BF16)
        BMb = SB(es, "BMb", [128, 16 * 64], BF16)
        bones = SB(es, "bones", [128, 128], BF16); onesb = SB(es, "onesb", [128, 128], BF16); onesf = SB(es, "onesf", [128, 128])
        Eprev = SB(es, "Eprev", [128, 1024]); Ecur = SB(es, "Ecur", [128, 1024]); Ecache = SB(es, "Ecache", [128, 32]); Enew = SB(es, "Enew", [64, 512])
        gcol = SB(es, "gcol", [128, 4 * DEPTH * 8])
        lay = {}
        for nm, w in [("cw", 32), ("cb", 8), ("dtb", 8), ("aneg", 8), ("dsk", 8), ("esk", 8), ("gq", 1), ("gk", 1)]:
            lay[nm] = SB(es, "l_" + nm, [128, w])
        gssm = SB(es, "gssm", [128, 512]); esinkrow = SB(es, "esinkrow", [64, 1024])
        t_lay = Tok()
        hs = SB(es, "hs", [128, 8, 64]); t_hs = Tok()
        w13 = [[SB(es, "w13_%d_%d" % (m, b), [128, 8, 256], BF16) for b in range(2)] for m in range(2)]
        t_w13 = [[Tok() for _ in range(2)] for _ in range(2)]
        wreg = SB(es, "wreg", [128, 18560], BF16); t_wreg = Tok()
        hT = SB(es, "hT", [128, 512]); hTb = SB(es, "hTb", [128, 512], BF16); t_hT = Tok(); t_hTb = Tok()
        xhalo = SB(es, "xhalo", [128, 8, 3]); t_xhalo = Tok()
        khalo = SB(es, "khalo", [128, 128], BF16); vhalo = SB(es, "vhalo", [128, 128], BF16); t_kvh = Tok()
        psb = [es.enter_context(nc.psum_tensor("ps%d" % i, [128, 512], F32)) for i in range(8)]
        t_ps = [Tok() for _ in range(8)]
        psi = [0]

        held = set()

        def PS(hold=False):
            for _try in range(9):
                i = psi[0]; psi[0] = (i + 1) % 8
                if i not in held:
                    break
            else:
                raise RuntimeError("all PSUM banks held")
            if hold:
                held.add(i)
            return psb[i], t_ps[i]

        def REL(tok):
            held.discard(t_ps.index(tok))

        def ld(eng, dst, src, toks, **kw):
            dma(eng, [], toks, dst, src, **kw)
        ld("sp", ident[:], c_ident[:, :], [t_c]); ld("pool", identb[:], c_ident[:, :], [t_c])
        ld("sp", Um[:], c_U[:, :], [t_c]); ld("sp", SLm[:], c_SL[:, :], [t_c])
        ld("sp", Ubd[:], c_Ubd[:, :], [t_c]); ld("sp", SLbd[:], c_SLbd[:, :], [t_c]); ld("sp", BMt[:], c_BMt[:, :], [t_c])
        ld("pool", BMtb[:], c_BMt[:, :], [t_c]); ld("pool", BMb[:], c_BM[:, :], [t_c]); ld("pool", bones[:], c_bones[:, :], [t_c])
        ld("sp", Eprev[:], c_Eprev[:, :], [t_c]); ld("sp", Ecur[:], c_Ecur[:, :], [t_c]); ld("sp", Ecache[:], c_Ecache[:, :], [t_c]); ld("sp", Enew[:], c_Enew[:, :], [t_c])
        op("dve", [], [t_c], lambda v: v.memset(onesb[:], 1.0))
        op("dve", [], [t_c], lambda v: v.memset(onesf[:], 1.0))
        for ni, g in enumerate([g_ffn1, g_mix, g_ffn2, g_ple]):
            for l in range(DEPTH):
                o = (ni * DEPTH + l) * 8
                ld("sp", gcol[:, o:o + 8], g[l, :].rearrange("(j p) -> p j", p=128), [t_c], allow_slow_non_contiguous=True)
        op("dve", [t_c], [t_c], lambda v: v.tensor_scalar(out=gcol[:], in0=gcol[:], scalar1=32.0, scalar2=None, op0=ALU.mult))

        def load_layer_consts(l):
            T = [t_lay]
            for j in range(4):
                ld("sp", lay["cw"][:, j * 8:(j + 1) * 8], conv_w[l, j, :].rearrange("(c p) -> p c", p=128), T, allow_slow_non_contiguous=True)
            ld("sp", lay["cb"][:], conv_b[l, :].rearrange("(c p) -> p c", p=128), T, allow_slow_non_contiguous=True)
            ld("sp", lay["dtb"][:], dt_bias[l, :].partition_broadcast(128), T)
            ld("sp", lay["aneg"][:], a_log[l, :].partition_broadcast(128), T)
            ld("sp", lay["dsk"][:], d_skip[l, :].partition_broadcast(128), T)
            ld("sp", lay["esk"][:], sinks[l, :].partition_broadcast(128), T)
            for hh in range(2):
                ld("sp", lay["gq"][hh * 64:(hh + 1) * 64, :], q_norm[l, :].rearrange("(p o) -> p o", o=1), T, allow_slow_non_contiguous=True)
                ld("sp", lay["gk"][hh * 64:(hh + 1) * 64, :], k_norm[l, :].rearrange("(p o) -> p o", o=1), T, allow_slow_non_contiguous=True)
            ld("sp", gssm[:], ssm_norm[l, :].partition_broadcast(128), T)
            op("act", T, T, lambda a: a.activation(out=lay["aneg"][:], in_=lay["aneg"][:], func=AF.Exp))
            op("dve", T, T, lambda v: v.tensor_scalar(out=lay["aneg"][:], in0=lay["aneg"][:], scalar1=-1.0, scalar2=None, op0=ALU.mult))
            op("act", T, T, lambda a: a.activation(out=lay["esk"][:], in_=lay["esk"][:], func=AF.Exp))
            op("dve", T, T, lambda v: v.tensor_scalar(out=lay["gq"][:], in0=lay["gq"][:], scalar1=8.0, scalar2=None, op0=ALU.mult))
            op("dve", T, T, lambda v: v.tensor_scalar(out=lay["gk"][:], in0=lay["gk"][:], scalar1=8.0, scalar2=None, op0=ALU.mult))
            op("dve", T, T, lambda v: v.tensor_copy(out=esinkrow[:].rearrange("p (h q) -> p h q", h=8),
                                                     in_=lay["esk"][0:64, :].unsqueeze(2).broadcast_to([64, 8, 128])))

        def nblocks(NT):
            return [(n0, min(512, NT - n0)) for n0 in range(0, NT, 512)]

        def norm(sc, h, t_h, NT, ni, l, xn, t_xn):
            go = (ni * DEPTH + l) * 8
            sq = SB(sc, "sq", [128, 8, 512], BF16); t_sq = Tok()
            rstd = SB(sc, "rstd", [128, 512]); t_rstd = Tok()
            for (n0, nw) in nblocks(NT):
                for half in range(2):
                    op("act", [t_h], [t_sq], lambda a: a.activation(out=sq[:, half * 4:(half + 1) * 4, :nw], in_=h[:, half * 4:(half + 1) * 4, n0:n0 + nw], func=AF.Square))
                ps, tp = PS()
                for j in range(8):
                    op("pe", [t_sq, t_c], [tp], lambda p: p.matmul(ps[:, :nw], lhsT=onesb[:], rhs=sq[:, j, :nw], start=(j == 0), stop=(j == 7)))
                op("act", [tp], [t_rstd], lambda a: a.activation(out=rstd[:, :nw], in_=ps[:, :nw], func=AF.Ln, bias=1024.0 * EPS, scale=1.0))
                op("act", [t_rstd], [t_rstd], lambda a: a.activation(out=rstd[:, :nw], in_=rstd[:, :nw], func=AF.Exp, scale=-0.5))
                for j in range(8):
                    op("dve", [t_h, t_rstd, t_c], [t_xn[j]], lambda v: v.scalar_tensor_tensor(out=xn[:, j, n0:n0 + nw], in0=h[:, j, n0:n0 + nw], scalar=gcol[:, go + j:go + j + 1], in1=rstd[:, :nw], op0=ALU.mult, op1=ALU.mult))

        w13_next = [None]

        def issue_w13(W1, W3, l, ci, parity, ab, cached):
            cols = FCH[ci]; f0 = FT[cols[0]][0]; fw = sum(FT[c][1] for c in cols)
            for m, W in enumerate((W1, W3)):
                scr = wsc13[ab, m, ci, :, :].rearrange("p (k f) -> p k f", k=8)[:, :, :fw]
                if cached:
                    dma("pool", [t_sc13[ab][m][ci]], [t_w13[m][parity]], w13[m][parity][:, :, :fw], scr)
                else:
                    dma("pool", [], [t_w13[m][parity]], w13[m][parity][:, :, :fw], W[l, :, f0:f0 + fw].rearrange("(k p) f -> p k f", p=128))
                    dma("sp", [t_w13[m][parity]], [t_sc13[ab][m][ci]], scr, w13[m][parity][:, :, :fw])

        def ffn(sc, h, t_h, NT, xn, t_xn, W1, W3, W2, l, ab, cached):
            gT = SB(sc, "gT", [128, NFT, NT], BF16); t_g = [Tok() for _ in range(NFT)]
            s1 = [SB(sc, "s1_%d" % i, [128, 512]) for i in range(2)]; t_s1 = [Tok(), Tok()]
            w2 = SB(sc, "w2", [128, NFT, 512], BF16); t_w2 = [Tok() for _ in W2CH]
            si = 0
            issue_w13(W1, W3, l, 0, 0, ab, cached)
            for ci, cols in enumerate(FCH):
                par = ci % 2
                if ci + 1 < len(FCH):
                    issue_w13(W1, W3, l, ci + 1, (ci + 1) % 2, ab, cached)
                for fi, ft in enumerate(cols):
                    fw = FT[ft][1]; fo = fi * 128
                    for (n0, nw) in nblocks(NT):
                        p1, tp1 = PS(); p3, tp3 = PS()
                        for (pp_, tpp, m) in ((p1, tp1, 0), (p3, tp3, 1)):
                            for k in range(8):
                                op("pe", [t_w13[m][par], t_xn[k]], [tpp], lambda p: p.matmul(pp_[:fw, :nw], lhsT=w13[m][par][:, k, fo:fo + fw], rhs=xn[:, k, n0:n0 + nw], start=(k == 0), stop=(k == 7)))
                        sb_, ts_ = s1[si], t_s1[si]; si ^= 1
                        op("act", [tp1], [ts_], lambda a: a.activation(out=sb_[:fw, :nw], in_=p1[:fw, :nw], func=AF.Silu))
                        op("dve", [ts_, tp3], [t_g[ft]], lambda v: v.tensor_tensor(out=gT[:fw, ft, n0:n0 + nw], in0=sb_[:fw, :nw], in1=p3[:fw, :nw], op=ALU.mult))
            for half in range(2):
                for wi, rows in enumerate(W2CH):
                    r0 = FT[rows[0]][0]
                    nfull = [r for r in rows if FT[r][1] == 128]
                    scr2 = wsc2[ab, half, :, :].rearrange("p (f c) -> p f c", f=NFT)
                    if nfull:
                        sl = slice(nfull[0], nfull[-1] + 1)
                        if cached:
                            dma("pool", [t_sc2[ab][half][wi]], [t_w2[wi]], w2[:, sl, :], scr2[:, sl, :])
                        else:
                            dma("pool", [], [t_w2[wi]], w2[:, sl, :], W2[l, r0:r0 + 128 * len(nfull), half * 512:(half + 1) * 512].rearrange("(f p) c -> p f c", p=128))
                    for r in rows:
                        if FT[r][1] != 128:
                            if cached:
                                dma("pool", [t_sc2[ab][half][wi]], [t_w2[wi]], w2[:64, r, :], scr2[:64, r, :])
                            else:
                                dma("pool", [], [t_w2[wi]], w2[:64, r, :], W2[l, FT[r][0]:FT[r][0] + 64, half * 512:(half + 1) * 512])
                    if not cached:
                        if nfull:
                            dma("sp", [t_w2[wi]], [t_sc2[ab][half][wi]], scr2[:, sl, :], w2[:, sl, :])
                        for r in rows:
                            if FT[r][1] != 128:
                                dma("sp", [t_w2[wi]], [t_sc2[ab][half][wi]], scr2[:64, r, :], w2[:64, r, :])
                for (n0, nw) in nblocks(NT):
                    acc = [PS() for _ in range(4)]
                    for ft in range(NFT):
                        fw = FT[ft][1]
                        wi = [i for i, rows in enumerate(W2CH) if ft in rows][0]
                        for dj in range(4):
                            op("pe", [t_w2[wi], t_g[ft]], [acc[dj][1]], lambda p: p.matmul(acc[dj][0][:, :nw], lhsT=w2[:fw, ft, dj * 128:(dj + 1) * 128], rhs=gT[:fw, ft, n0:n0 + nw], start=(ft == 0), stop=(ft == NFT - 1)))
                    for dj in range(4):
                        d = half * 4 + dj
                        op("dve", [acc[dj][1], t_h], [t_h], lambda v: v.scalar_tensor_tensor(out=h[:, d, n0:n0 + nw], in0=acc[dj][0][:, :nw], scalar=0.5, in1=h[:, d, n0:n0 + nw], op0=ALU.mult, op1=ALU.add))

        def transpose_in(xtm, t_x, src, ntok, h, t_h, c0):
            dma("sp", [], [t_x], xtm[:ntok, :], src)
            for a in range(2):
                ps, tp = PS()
                for j in range(4):
                    op("pe", [t_x, t_c], [tp], lambda p: p.transpose(out=ps[:, j * 128:j * 128 + ntok], in_=xtm[:ntok, (a * 4 + j) * 128:(a * 4 + j + 1) * 128], identity=ident[:ntok, :ntok]))
                op("act", [tp], [t_h], lambda a_: a_.activation(out=h[:, a * 4:(a + 1) * 4, c0:c0 + ntok], in_=ps[:].rearrange("p (j t) -> p j t", j=4)[:, :, :ntok], func=AF.Copy))

        def transpose_out(ytm, t_y, h, t_h, c0, ntok, dst):
            for a in range(2):
                ps, tp = PS()
                for j in range(4):
                    op("pe", [t_h, t_c], [tp], lambda p: p.transpose(out=ps[:ntok, j * 128:(j + 1) * 128], in_=h[:, a * 4 + j, c0:c0 + ntok], identity=ident[:]))
                op("act", [tp], [t_y], lambda a_: a_.activation(out=ytm[:ntok, a * 512:(a + 1) * 512], in_=ps[:ntok, :], func=AF.Copy))
            dma("sp", [t_y], [], dst, ytm[:ntok, :])

        def mixer(sc, h, t_h, NT, l, sample, first_group, last_group, cached):
            xn = SB(sc, "xn", [128, 8, NT], BF16); t_xn = [Tok() for _ in range(8)]
            norm(sc, h, t_h, NT, 1, l, xn, t_xn)
            chk('m_norm')
            NTT = 64 if sample else 128
            ntile = NT // NTT
            o = 0
            def carve(n, shape_str, **kw):
                nonlocal o
                a = wreg[:, o:o + n]; o += n
                return a.rearrange(shape_str, **kw)
            wz = carve(8 * 512, "p (k c) -> p k c", k=8); wx = carve(8 * 1024, "p (k c) -> p k c", k=8)
            wq = carve(8 * 512, "p (k c) -> p k c", k=8); wk = carve(8 * 128, "p (k c) -> p k c", k=8)
            wv = carve(8 * 128, "p (k c) -> p k c", k=8); wdt = carve(8 * 8, "p (k c) -> p k c", k=8)
            wl = w_in[l, :, :].rearrange("(k p) c -> p k c", p=128)
            T = [t_wreg]
            if cached:
                dma("pool", [t_scm[0]], T, wreg[:, 0:18496], wscm[0, :, 0:18496])
            else:
                dma("pool", [], T, wz, wl[:, :, 0:512]); dma("pool", [], T, wx, wl[:, :, 512:1536]); dma("pool", [], T, wdt, wl[:, :, 1536:1544])
                for hh in range(4):
                    for g in range(2):
                        c = 1544 + g * 256 + hh * 64
                        dma("pool", [], T, wq[:, :, hh * 128 + g * 64:hh * 128 + g * 64 + 64], wl[:, :, c:c + 64])
                dma("pool", [], T, wk, wl[:, :, 2056:2184]); dma("pool", [], T, wv, wl[:, :, 2184:2312])
                dma("sp", T, [t_scm[0]], wscm[0, :, 0:18496], wreg[:, 0:18496])
            chk('m_wdma')
            HAL = 0 if sample else 3
            xin = SB(sc, "xin", [128, 8, NT + HAL]); t_xin = Tok()
            xc = SB(sc, "xc", [128, 8, NT], BF16); t_xc = Tok()
            qn = SB(sc, "qn", [128, 4, NT], BF16); t_qn = Tok()
            KH = 0 if sample else 128
            kn = SB(sc, "kn", [128, KH + NT], BF16); t_kn = Tok()
            knf = SB(sc, "knf", [128, NT]); t_knf = Tok()
            vb = SB(sc, "vb", [128, ntile + 1, 128], BF16); t_vb = Tok()
            vf = SB(sc, "vf", [128, 128]); t_vf = Tok()
            mixT = SB(sc, "mixT", [128, 4, NT], BF16); t_mixT = Tok()
            oT = SB(sc, "oT", [64, 8, NT], BF16); t_oT = Tok()
            tmpa = SB(sc, "tmpa", [128, 512]); t_tmpa = Tok()
            rq = SB(sc, "rq", [128, 512]); t_rq = Tok()
            sqb = SB(sc, "sqb", [128, 512], BF16); t_sqb = Tok()
            if not sample:
                op("dve", [t_xhalo], [t_xin], lambda v: v.tensor_copy(out=xin[:, :, 0:3], in_=xhalo[:]))
                op("dve", [t_kvh], [t_kn], lambda v: v.tensor_copy(out=kn[:, 0:128], in_=khalo[:]))
                op("dve", [t_kvh], [t_vb], lambda v: v.tensor_copy(out=vb[:, 0, :], in_=vhalo[:]))
            chk('m_halo')
            with cx.fast():
                for c in range(8):
                    for (n0, nw) in nblocks(NT):
                        ps, tp = PS()
                        for k in range(8):
                            op("pe", T + [t_xn[k]], [tp], lambda p: p.matmul(ps[:, :nw], lhsT=wx[:, k, c * 128:(c + 1) * 128], rhs=xn[:, k, n0:n0 + nw], start=(k == 0), stop=(k == 7)))
                        if not sample:
                            op("act", [tp], [t_xin], lambda a: a.activation(out=xin[:, c, HAL + n0:HAL + n0 + nw], in_=ps[:, :nw], func=AF.Copy))
                        else:
                            op("act", [tp], [t_xin], lambda a: a.activation(out=xin[:, c, n0:n0 + nw], in_=ps[:, :nw], func=AF.Copy))
                chk('m_xbc')
                for qi in range(5):
                    for (n0, nw) in nblocks(NT):
                        ps, tp = PS()
                        for k in range(8):
                            lw = wq[:, k, qi * 128:(qi + 1) * 128] if qi < 4 else wk[:, k, :]
                            op("pe", T + [t_xn[k]], [tp], lambda p: p.matmul(ps[:, :nw], lhsT=lw, rhs=xn[:, k, n0:n0 + nw], start=(k == 0), stop=(k == 7)))
                        op("act", [tp], [t_sqb], lambda a: a.activation(out=sqb[:, :nw], in_=ps[:, :nw], func=AF.Square))
                        ps2, tp2 = PS()
                        op("pe", [t_sqb, t_c], [tp2], lambda p: p.matmul(ps2[:, :nw], lhsT=bones[:], rhs=sqb[:, :nw], start=True, stop=True))
                        op("act", [tp2], [t_rq], lambda a: a.activation(out=rq[:, :nw], in_=ps2[:, :nw], func=AF.Ln, bias=64.0 * EPS, scale=1.0))
                        op("act", [t_rq], [t_rq], lambda a: a.activation(out=rq[:, :nw], in_=rq[:, :nw], func=AF.Exp, scale=-0.5))
                        if qi < 4:
                            op("dve", [tp, t_rq, t_lay], [t_qn], lambda v: v.scalar_tensor_tensor(out=qn[:, qi, n0:n0 + nw], in0=ps[:, :nw], scalar=lay["gq"][:, 0:1], in1=rq[:, :nw], op0=ALU.mult, op1=ALU.mult))
                        else:
                            op("dve", [tp, t_rq, t_lay], [t_knf], lambda v: v.scalar_tensor_tensor(out=knf[:, n0:n0 + nw], in0=ps[:, :nw], scalar=lay["gk"][:, 0:1], in1=rq[:, :nw], op0=ALU.mult, op1=ALU.mult))
                            op("act", [t_knf], [t_kn], lambda a: a.activation(out=kn[:, KH + n0:KH + n0 + nw], in_=knf[:, n0:n0 + nw], func=AF.Copy))
            chk('m_qk')
            cacc = SB(sc, "cacc", [128, 512]); t_cacc = Tok()
            if not sample:
                for c in range(8):
                    for (n0, nw) in nblocks(NT):
                        op("dve", [t_xin, t_lay], [t_cacc], lambda v: v.tensor_scalar(out=cacc[:, :nw], in0=xin[:, c, n0:n0 + nw], scalar1=lay["cw"][:, c:c + 1], scalar2=None, op0=ALU.mult))
                        for j in range(1, 4):
                            op("dve", [t_xin, t_lay, t_cacc], [t_cacc], lambda v: v.scalar_tensor_tensor(out=cacc[:, :nw], in0=xin[:, c, n0 + j:n0 + j + nw], scalar=lay["cw"][:, j * 8 + c:j * 8 + c + 1], in1=cacc[:, :nw], op0=ALU.mult, op1=ALU.add))
                        op("act", [t_cacc, t_lay], [t_xc], lambda a: a.activation(out=xc[:, c, n0:n0 + nw], in_=cacc[:, :nw], func=AF.Silu, bias=lay["cb"][:, c:c + 1], scale=1.0))
                op("dve", [t_xin], [t_xhalo], lambda v: v.tensor_copy(out=xhalo[:], in_=xin[:, :, NT:NT + 3]))
                if last_group:
                    cst = SB(sc, "cst", [128, 8, 4]); t_cst = Tok()
                    op("dve", [t_xin], [t_cst], lambda v: v.tensor_copy(out=cst[:, :, 0:3], in_=xin[:, :, NT:NT + 3]))
                    ps, tp = PS(); ps2, tp2 = PS()
                    for c in range(8):
                        pp_, tpp = (ps, tp) if c < 4 else (ps2, tp2)
                        op("pe", [t_cst, t_c], [tpp], lambda p: p.transpose(out=pp_[:3, (c % 4) * 128:(c % 4 + 1) * 128], in_=cst[:, c, 0:3], identity=ident[:]))
                    cso = SB(sc, "cso", [4, 1024]); t_cso = Tok()
                    op("act", [tp], [t_cso], lambda a: a.activation(out=cso[:3, 0:512], in_=ps[:3, :], func=AF.Copy))
                    op("act", [tp2], [t_cso], lambda a: a.activation(out=cso[:3, 512:1024], in_=ps2[:3, :], func=AF.Copy))
                    dma("sp", [t_cso], [], oconv_p[l, :, :], cso[:3, :])
            else:
                xfull = SB(sc, "xfull", [128, 8, NSS, 7]); t_xf = Tok()
                scm = SB(sc, "scm", [64, 1024]); t_scmb = Tok()
                dma("sp", [], [t_scmb], scm[:NSS * 3, :], sconv[l, :, :])
                for c in range(8):
                    ps, tp = PS()
                    op("pe", [t_scmb, t_c], [tp], lambda p: p.transpose(out=ps[:, :NSS * 3], in_=scm[:NSS * 3, c * 128:(c + 1) * 128], identity=ident[:NSS * 3, :NSS * 3]))
                    op("act", [tp], [t_xf], lambda a: a.activation(out=xfull[:, c, :, 0:3], in_=ps[:, :NSS * 3].rearrange("p (b j) -> p b j", j=3), func=AF.Copy))
                    op("dve", [t_xin], [t_xf], lambda v: v.tensor_copy(out=xfull[:, c, :, 3:7], in_=xin[:, c, :].rearrange("p (b t) -> p b t", t=4)))
                    ca = cacc[:, :NS].rearrange("p (b t) -> p b t", t=4)
                    op("dve", [t_xf, t_lay], [t_cacc], lambda v: v.tensor_scalar(out=ca, in0=xfull[:, c, :, 0:4], scalar1=lay["cw"][:, c:c + 1], scalar2=None, op0=ALU.mult))
                    for j in range(1, 4):
                        op("dve", [t_xf, t_lay, t_cacc], [t_cacc], lambda v: v.scalar_tensor_tensor(out=ca, in0=xfull[:, c, :, j:j + 4], scalar=lay["cw"][:, j * 8 + c:j * 8 + c + 1], in1=ca, op0=ALU.mult, op1=ALU.add))
                    op("act", [t_cacc, t_lay], [t_xc], lambda a: a.activation(out=xc[:, c, :], in_=cacc[:, :NS], func=AF.Silu, bias=lay["cb"][:, c:c + 1], scale=1.0))
                cso = SB(sc, "cso", [64, 1024]); t_cso = Tok()
                cst = SB(sc, "cst", [128, 8, NSS * 3]); t_cst = Tok()
                op("dve", [t_xf], [t_cst], lambda v: v.tensor_copy(out=cst[:].rearrange("p c (b j) -> p c b j", j=3), in_=xfull[:, :, :, 4:7]))
                for a_ in range(2):
                    ps, tp = PS()
                    for j in range(4):
                        op("pe", [t_cst, t_c], [tp], lambda p: p.transpose(out=ps[:NSS * 3, j * 128:(j + 1) * 128], in_=cst[:, a_ * 4 + j, :], identity=ident[:]))
                    op("act", [tp], [t_cso], lambda a: a.activation(out=cso[:NSS * 3, a_ * 512:(a_ + 1) * 512], in_=ps[:NSS * 3, :], func=AF.Copy))
                dma("sp", [t_cso], [], oconv_s[l, :, :], cso[:NSS * 3, :])

            chk('m_conv')
            Ut = Ubd if sample else Um; SLt = SLbd if sample else SLm
            P = NTT
            st = {}
            for nm, shp, dt in [("dt", [128, 8], F32), ("dta", [128, 8], F32), ("t8", [128, 8], F32), ("ecum", [128, 8], F32), ("etot", [128, 8], F32),
                                ("dend", [128, 8], F32), ("w2s", [128, 8], F32), ("DL", [128, 8, 128], F32), ("LT", [128, 8, 128], F32),
                                ("GM", [128, 2, 128], F32), ("MT", [128, 8, 128], BF16), ("xdt", [128, 512], BF16), ("xdd", [128, 512], BF16),
                                ("Btm", [128, 256], BF16), ("y1", [128, 512], F32), ("sz", [128, 512], F32), ("ysq", [128, 512], F32), ("xsk", [128, 512], F32), ("xtok", [128, 512], F32),
                                ("ss2", [128, 2], F32), ("ytm", [128, 512], BF16), ("pe0", [128, 512], F32), ("pe1", [128, 512], F32),
                                ("PT0", [128, 512], BF16), ("PT1", [128, 512], BF16), ("den", [64, 512], F32)]:
                st[nm] = (SB(sc, "st_" + nm, shp, dt), Tok())
            if sample:
                Snat = SB(sc, "Snat", [128, 8, 4, 128]); t_Sn = Tok()
                Sb1 = [SB(sc, "Sb1_%d" % i, [128, 4, 128], BF16) for i in range(2)]; t_Sb1 = [Tok(), Tok()]
                STb1 = [SB(sc, "STb1_%d" % i, [128, 512], BF16) for i in range(2)]; t_ST1 = [Tok(), Tok()]
                CTm = SB(sc, "CTm", [128, NSS, 2, 64], BF16); t_CT = Tok()
                xdm = [SB(sc, "xdm%d" % i, [64, 512], BF16) for i in range(2)]; t_xdm = [Tok(), Tok()]
                dtaE = SB(sc, "dtaE", [64, 512]); t_dE = Tok()
                cdT = SB(sc, "cdT", [128, 4, 16]); t_cd = Tok()
                Snew = [SB(sc, "Snew%d" % i, [128, 4, 128]) for i in range(2)]; t_Snew = [Tok(), Tok()]
                kcb = SB(sc, "kcb", [128, NSS, 128], BF16); vcb = SB(sc, "vcb", [128, NSS, 128], BF16); t_kcb = Tok(); t_vcb = Tok()
                KcT = SB(sc, "KcT", [128, NSS, 128], BF16); t_KcT = Tok()
                PTc = SB(sc, "PTc", [128, 512], BF16); t_PTc = Tok()
                for b4 in range(0, NSS, 4):
                    dma("pool", [], [t_kcb], kcb[:, b4:b4 + 4, :], ck[l, b4:b4 + 4, :, :].rearrange("b i c -> i b c"))
                    dma("pool", [], [t_vcb], vcb[:, b4:b4 + 4, :], cv[l, b4:b4 + 4, :, :].rearrange("b i c -> i b c"))
                for b4 in range(0, NSS, 4):
                    dma("sp", [], [], ock_s[l, b4:b4 + 4, 0:124, :], ck[l, b4:b4 + 4, 4:128, :])
                    dma("sp", [], [], ocv_s[l, b4:b4 + 4, 0:124, :], cv[l, b4:b4 + 4, 4:128, :])

            A = lambda nm: st[nm][0]
            K_ = lambda nm: st[nm][1]
            PSH = lambda: PS(hold=not sample)

            def RELH(tok):
                if not sample:
                    REL(tok)

            def tile_ssd(ti):
                c0 = ti * NTT
                cs = slice(c0, c0 + P)
                first_tile = first_group and ti == 0 and not sample
                pz, tpz = PSH(); pdv, tpdv = PSH()
                for k in range(8):
                    op("pe", T + [t_xn[k]], [tpz], lambda p: p.matmul(pz[:P, :], lhsT=xn[:, k, cs], rhs=wz[:, k, :], start=(k == 0), stop=(k == 7)))
                for k in range(8):
                    op("pe", T + [t_xn[k]], [tpdv], lambda p: p.matmul(pdv[:P, 0:128], lhsT=xn[:, k, cs], rhs=wv[:, k, :], start=(k == 0), stop=(k == 7)))
                for k in range(8):
                    op("pe", T + [t_xn[k]], [tpdv], lambda p: p.matmul(pdv[:P, 128:136], lhsT=xn[:, k, cs], rhs=wdt[:, k, :], start=(k == 0), stop=(k == 7)))
                op("act", [tpz], [st["sz"][1]], lambda a: a.activation(out=st["sz"][0][:P, :], in_=pz[:P, :], func=AF.Silu))
                RELH(tpz)
                op("act", [tpdv], [t_vb], lambda a: a.activation(out=vb[:P, ti + 1, :], in_=pdv[:P, 0:128], func=AF.Copy))
                need_vf = sample or (last_group and ti == ntile - 1)
                if need_vf:
                    op("act", [tpdv], [t_vf], lambda a: a.activation(out=vf[:P, :], in_=pdv[:P, 0:128], func=AF.Copy))
                chk('t_zdv')
                op("dve", [tpdv, t_lay], [K_("t8")], lambda v: v.tensor_tensor(out=A("t8")[:P, :], in0=pdv[:P, 128:136], in1=lay["dtb"][:P, :], op=ALU.add))
                RELH(tpdv)
                yield
                op("act", [K_("t8")], [K_("t8")], lambda a: a.activation(out=A("t8")[:P, :], in_=A("t8")[:P, :], func=AF.Exp))
                op("act", [K_("t8")], [K_("dt")], lambda a: a.activation(out=A("dt")[:P, :], in_=A("t8")[:P, :], func=AF.Ln, bias=1.0, scale=1.0))
                op("dve", [K_("dt"), t_lay], [K_("dta")], lambda v: v.tensor_tensor(out=A("dta")[:P, :], in0=A("dt")[:P, :], in1=lay["aneg"][:P, :], op=ALU.mult))
                op("dve", [K_("dta"), t_c], [K_("DL")], lambda v: v.tensor_tensor(out=A("DL")[:P, :, :P], in0=SLt[:P, :P].unsqueeze(1).broadcast_to([P, 8, P]), in1=A("dta")[:P, :].unsqueeze(2).broadcast_to([P, 8, P]), op=ALU.mult))
                yield
                pD0, tD0 = PSH(); pD1, tD1 = PSH(); pc, tpc = PSH()
                chk('t_dt')
                for hd in range(8):
                    pd_, td_ = (pD0, tD0) if hd < 4 else (pD1, tD1)
                    op("pe", [K_("DL"), t_c], [td_], lambda p: p.matmul(pd_[:P, (hd % 4) * 128:(hd % 4) * 128 + P], lhsT=A("DL")[:P, hd, :P], rhs=Ut[:P, :P], start=True, stop=True))
                op("pe", [K_("dta"), t_c], [tpc], lambda p: p.matmul(pc[:P, 0:8], lhsT=Ut[:P, :P], rhs=A("dta")[:P, :], start=True, stop=True))
                if not sample:
                    op("pe", [K_("dta"), t_c], [tpc], lambda p: p.matmul(pc[:, 8:16], lhsT=onesf[:, :], rhs=A("dta")[:, :], start=True, stop=True))
                else:
                    pass
                for hf, (pd_, td_) in enumerate(((pD0, tD0), (pD1, tD1))):
                    op("act", [td_], [K_("LT")], lambda a: a.activation(out=A("LT")[:P, hf * 4:(hf + 1) * 4, :P], in_=pd_[:P, :].rearrange("p (h t) -> p h t", h=4)[:, :, :P], func=AF.Exp))
                chk('u_LT')
                op("act", [tpc], [K_("ecum")], lambda a: a.activation(out=A("ecum")[:P, :], in_=pc[:P, 0:8], func=AF.Exp))
                RELH(tD0); RELH(tD1)
                chk('u_ecum')
                chk('t_D')
                pG, tpG = PSH()
                for g in range(2):
                    op("pe", [t_xc], [tpG], lambda p: p.matmul(pG[:P, g * 128:g * 128 + P], lhsT=xc[:, 4 + g, cs], rhs=xc[:, 6 + g, cs], start=True, stop=True))
                chk('u_pG')
                op("dve", [tpG, t_c], [K_("GM")], lambda v: v.tensor_tensor(out=A("GM")[:P, :, :P], in0=pG[:P, 0:256].rearrange("p (g t) -> p g t", g=2)[:, :, :P], in1=Ut[:P, :P].unsqueeze(1).broadcast_to([P, 2, P]), op=ALU.mult))
                RELH(tpG)
                chk('u_GM')
                for g in range(2):
                    op("dve", [K_("GM"), K_("LT")], [K_("MT")], lambda v: v.tensor_tensor(out=A("MT")[:P, g * 4:(g + 1) * 4, :P], in0=A("LT")[:P, g * 4:(g + 1) * 4, :P], in1=A("GM")[:P, g, :P].unsqueeze(1).broadcast_to([P, 4, P]), op=ALU.mult))
                chk('t_G')
                px, tpx = PSH(); pxb = px[:].bitcast(BF16)
                for j in range(4):
                    op("pe", [t_xc, t_c], [tpx], lambda p: p.transpose(out=pxb[:P, j * 128:(j + 1) * 128], in_=xc[:, j, cs], identity=identb[:]))
                for g in range(2):
                    op("pe", [t_xc, t_c], [tpx], lambda p: p.transpose(out=pxb[:P, 512 + g * 128:512 + (g + 1) * 128], in_=xc[:, 4 + g, cs], identity=identb[:]))
                chk('v_tr')
                op("act", [tpx], [K_("Btm")], lambda a: a.activation(out=A("Btm")[:P, :], in_=pxb[:P, 512:768], func=AF.Copy))
                op("act", [tpx], [K_("xtok")], lambda a: a.activation(out=A("xtok")[:P, :], in_=pxb[:P, 0:512], func=AF.Copy))
                RELH(tpx)
                chk('v_Btm')
                op("dve", [K_("xtok"), K_("dt")], [K_("xdt")], lambda v: v.tensor_tensor(out=A("xdt")[:P, :].rearrange("p (h d) -> p h d", h=8), in0=A("xtok")[:P, :].rearrange("p (h d) -> p h d", h=8), in1=A("dt")[:P, :].unsqueeze(2).broadcast_to([P, 8, 64]), op=ALU.mult))
                chk('v_xdt')
                op("dve", [K_("xtok"), t_lay], [K_("xsk")], lambda v: v.tensor_tensor(out=A("xsk")[:P, :].rearrange("p (h d) -> p h d", h=8), in0=A("xtok")[:P, :].rearrange("p (h d) -> p h d", h=8), in1=lay["dsk"][:P, :].unsqueeze(2).broadcast_to([P, 8, 64]), op=ALU.mult))
                yield
                chk('t_tr')
                py, tpy = PS(hold=True)
                for hd in range(8):
                    op("pe", [K_("MT"), K_("xdt")], [tpy], lambda p: p.matmul(py[:P, hd * 64:(hd + 1) * 64], lhsT=A("MT")[:P, hd, :P], rhs=A("xdt")[:P, hd * 64:(hd + 1) * 64], start=True, stop=True))
                pyo, tpyo = PS(hold=True)
                if sample:
                    pyo2, tpyo2 = PS(hold=True)
                if not sample:
                    op("act", [tpc], [K_("w2s")], lambda a: a.activation(out=A("w2s")[:, :], in_=pc[:, 0:8], func=AF.Copy))
                    op("dve", [tpc, K_("w2s")], [K_("t8")], lambda v: v.tensor_tensor(out=A("t8")[:, :], in0=pc[:, 8:16], in1=A("w2s")[:, :], op=ALU.subtract))
                    op("act", [K_("t8")], [K_("dend")], lambda a: a.activation(out=A("dend")[:, :], in_=A("t8")[:, :], func=AF.Exp))
                    op("act", [tpc], [K_("etot")], lambda a: a.activation(out=A("etot")[:, :], in_=pc[:, 8:16], func=AF.Exp))
                    RELH(tpc)
                    op("dve", [K_("dend"), K_("dt")], [K_("w2s")], lambda v: v.tensor_tensor(out=A("w2s")[:, :], in0=A("dend")[:, :], in1=A("dt")[:, :], op=ALU.mult))
                    op("dve", [K_("xtok"), K_("w2s")], [K_("xdd")], lambda v: v.tensor_tensor(out=A("xdd")[:, :].rearrange("p (h d) -> p h d", h=8), in0=A("xtok")[:, :].rearrange("p (h d) -> p h d", h=8), in1=A("w2s")[:, :].unsqueeze(2).broadcast_to([128, 8, 64]), op=ALU.mult))
                    for g in range(2):
                        op("pe", [t_xc, t_hTb], [tpyo], lambda p: p.matmul(pyo[:, g * 256:(g + 1) * 256], lhsT=xc[:, 6 + g, cs], rhs=hTb[:, g * 256:(g + 1) * 256], start=True, stop=True))
                    pst, tpst = PSH()
                    for g in range(2):
                        op("pe", [K_("Btm"), K_("xdd")], [tpst], lambda p: p.matmul(pst[:, g * 256:(g + 1) * 256], lhsT=A("Btm")[:, g * 128:(g + 1) * 128], rhs=A("xdd")[:, g * 256:(g + 1) * 256], start=True, stop=True))
                    op("dve", [t_hT, K_("etot")], [t_hT], lambda v: v.tensor_tensor(out=hT[:].rearrange("p (h d) -> p h d", h=8), in0=hT[:].rearrange("p (h d) -> p h d", h=8), in1=A("etot")[:, :].unsqueeze(2).broadcast_to([128, 8, 64]), op=ALU.mult))
                    op("dve", [t_hT, tpst], [t_hT], lambda v: v.tensor_tensor(out=hT[:], in0=hT[:], in1=pst[:, :], op=ALU.add))
                    RELH(tpst)
                    op("act", [t_hT], [t_hTb], lambda a: a.activation(out=hTb[:], in_=hT[:], func=AF.Copy))
                else:
                    op("pe", [K_("dta"), t_c], [tpc], lambda p: p.matmul(pc[:P, 8:16], lhsT=Ubd[:P, :P], rhs=A("dta")[:P, :], start=True, stop=False))
                    op("pe", [K_("dta"), t_c], [tpc], lambda p: p.matmul(pc[:P, 8:16], lhsT=SLbd[:P, :P], rhs=A("dta")[:P, :], start=False, stop=True))
                    op("act", [tpc], [K_("w2s")], lambda a: a.activation(out=A("w2s")[:P, :], in_=pc[:P, 0:8], func=AF.Copy))
                    op("dve", [tpc, K_("w2s")], [K_("t8")], lambda v: v.tensor_tensor(out=A("t8")[:P, :], in0=pc[:P, 8:16], in1=A("w2s")[:P, :], op=ALU.subtract))
                    op("act", [K_("t8")], [K_("dend")], lambda a: a.activation(out=A("dend")[:P, :], in_=A("t8")[:P, :], func=AF.Exp))
                    op("dve", [K_("dend"), K_("dt")], [K_("w2s")], lambda v: v.tensor_tensor(out=A("w2s")[:P, :], in0=A("dend")[:P, :], in1=A("dt")[:P, :], op=ALU.mult))
                    op("dve", [K_("xtok"), K_("w2s")], [K_("xdd")], lambda v: v.tensor_tensor(out=A("xdd")[:P, :].rearrange("p (h d) -> p h d", h=8), in0=A("xtok")[:P, :].rearrange("p (h d) -> p h d", h=8), in1=A("w2s")[:P, :].unsqueeze(2).broadcast_to([P, 8, 64]), op=ALU.mult))
                    for g in range(2):
                        op("dve", [t_xc, t_c], [t_CT], lambda v: v.tensor_tensor(out=CTm[:, :, g, :], in0=xc[:, 6 + g, :].unsqueeze(1).broadcast_to([128, NSS, 64]), in1=BMb[:].rearrange("p (b t) -> p b t", b=16)[:, :NSS, :], op=ALU.mult))
                    op("dve", [K_("dta")], [t_dE], lambda v: v.tensor_copy(out=dtaE[:].rearrange("p (h d) -> p h d", h=8), in_=A("dta")[:P, :].unsqueeze(2).broadcast_to([P, 8, 64])))
                    pcd, tpcd = PS()
                    for j in range(4):
                        op("pe", [t_dE, t_c], [tpcd], lambda p: p.matmul(pcd[:, j * 16:(j + 1) * 16], lhsT=dtaE[:, j * 128:(j + 1) * 128], rhs=BMt[:, :], start=True, stop=True))
                    op("act", [tpcd], [t_cd], lambda a: a.activation(out=cdT[:].rearrange("p j b -> p (j b)"), in_=pcd[:, 0:64], func=AF.Exp))
                    for b in range(NSS):
                        bl = b % 8
                        if bl == 0:
                            for b8 in range(8):
                                dma("sp", [], [t_Sn], Snat[:, b8, :, :], sssm[l, b + b8, :, :].rearrange("(j p) n -> p j n", p=128))
                        sb1, tsb1 = Sb1[b % 2], t_Sb1[b % 2]
                        stb, tstb = STb1[b % 2], t_ST1[b % 2]
                        op("act", [t_Sn], [tsb1], lambda a: a.activation(out=sb1[:], in_=Snat[:, bl, :, :], func=AF.Copy))
                        pt_, tpt = PS(); ptb = pt_[:].bitcast(BF16)
                        for j in range(4):
                            op("pe", [tsb1, t_c], [tpt], lambda p: p.transpose(out=ptb[:, j * 128:(j + 1) * 128], in_=sb1[:, j, :], identity=identb[:]))
                        op("act", [tpt], [tstb], lambda a: a.activation(out=stb[:, :], in_=ptb[:, 0:512], func=AF.Copy))
                        for g in range(2):
                            pq_, tq_ = (pyo, tpyo) if g == 0 else (pyo2, tpyo2)
                            op("pe", [t_CT, tstb], [tq_], lambda p: p.matmul(pq_[:P, 0:256], lhsT=CTm[:, b, g, :], rhs=stb[:, g * 256:(g + 1) * 256], start=(b == 0), stop=(b == NSS - 1)))
                        xm, txm = xdm[b % 2], t_xdm[b % 2]
                        op("dve", [K_("xdd"), t_c], [txm], lambda v: v.tensor_scalar(out=xm[:, :], in0=A("xdd")[:P, :], scalar1=BMt[:, b:b + 1], scalar2=None, op0=ALU.mult))
                        pst, tpst = PS()
                        for j in range(4):
                            op("pe", [txm, K_("Btm")], [tpst], lambda p: p.matmul(pst[:, j * 128:(j + 1) * 128], lhsT=xm[:, j * 128:(j + 1) * 128], rhs=A("Btm")[:P, (j // 2) * 128:(j // 2 + 1) * 128], start=True, stop=True))
                        sn, tsn = Snew[b % 2], t_Snew[b % 2]
                        for j in range(4):
                            op("dve", [t_Sn, t_cd, tpst], [tsn], lambda v: v.scalar_tensor_tensor(out=sn[:, j, :], in0=Snat[:, bl, j, :], scalar=cdT[:, j, b:b + 1], in1=pst[:, j * 128:(j + 1) * 128], op0=ALU.mult, op1=ALU.add))
                        dma("sp", [tsn], [], ossm_s[l, b, :, :].rearrange("(j p) n -> p j n", p=128), sn[:])
                yield
                chk('t_ssd')
                if not sample:
                    op("dve", [tpyo, K_("ecum")], [K_("y1")], lambda v: v.tensor_tensor(out=A("y1")[:P, :].rearrange("p (h d) -> p h d", h=8), in0=pyo[:P, :].rearrange("p (h d) -> p h d", h=8), in1=A("ecum")[:P, :].unsqueeze(2).broadcast_to([P, 8, 64]), op=ALU.mult))
                else:
                    for g, (pq_, tq_) in enumerate(((pyo, tpyo), (pyo2, tpyo2))):
                        op("dve", [tq_, K_("ecum")], [K_("y1")], lambda v: v.tensor_tensor(out=A("y1")[:P, g * 256:(g + 1) * 256].rearrange("p (h d) -> p h d", h=4), in0=pq_[:P, 0:256].rearrange("p (h d) -> p h d", h=4), in1=A("ecum")[:P, g * 4:(g + 1) * 4].unsqueeze(2).broadcast_to([P, 4, 64]), op=ALU.mult))
                op("dve", [K_("y1"), tpy], [K_("y1")], lambda v: v.tensor_tensor(out=A("y1")[:P, :], in0=A("y1")[:P, :], in1=py[:P, :], op=ALU.add))
                op("dve", [K_("y1"), K_("xsk")], [K_("y1")], lambda v: v.tensor_tensor(out=A("y1")[:P, :], in0=A("y1")[:P, :], in1=A("xsk")[:P, :], op=ALU.add))
                REL(tpy); REL(tpyo)
                if sample:
                    REL(tpyo2)
                op("dve", [K_("y1"), K_("sz")], [K_("y1")], lambda v: v.tensor_tensor(out=A("y1")[:P, :], in0=A("y1")[:P, :], in1=A("sz")[:P, :], op=ALU.mult))
                op("dve", [K_("y1")], [K_("ysq")], lambda v: v.tensor_tensor(out=A("ysq")[:P, :], in0=A("y1")[:P, :], in1=A("y1")[:P, :], op=ALU.mult))
                op("dve", [K_("ysq")], [K_("ss2")], lambda v: v.reduce_sum(out=A("ss2")[:P, :], in_=A("ysq")[:P, :].rearrange("p (g d) -> p g d", g=2), axis=AX.X))
                op("act", [K_("ss2")], [K_("ss2")], lambda a: a.activation(out=A("ss2")[:P, :], in_=A("ss2")[:P, :], func=AF.Ln, bias=256.0 * EPS, scale=1.0))
                op("act", [K_("ss2")], [K_("ss2")], lambda a: a.activation(out=A("ss2")[:P, :], in_=A("ss2")[:P, :], func=AF.Exp, scale=-0.5))
                op("dve", [K_("y1"), K_("ss2")], [K_("y1")], lambda v: v.tensor_tensor(out=A("y1")[:P, :].rearrange("p (g d) -> p g d", g=2), in0=A("y1")[:P, :].rearrange("p (g d) -> p g d", g=2), in1=A("ss2")[:P, :].unsqueeze(2).broadcast_to([P, 2, 256]), op=ALU.mult))
                op("dve", [K_("y1"), t_lay], [K_("ytm")], lambda v: v.scalar_tensor_tensor(out=A("ytm")[:P, :], in0=A("y1")[:P, :], scalar=16.0, in1=gssm[:P, :], op0=ALU.mult, op1=ALU.mult))
                yield
                pyt, tpyt = PSH(); pytb = pyt[:].bitcast(BF16)
                for j in range(4):
                    op("pe", [K_("ytm"), t_c], [tpyt], lambda p: p.transpose(out=pytb[:, j * 128:j * 128 + P], in_=A("ytm")[:P, j * 128:(j + 1) * 128], identity=identb[:P, :P]))
                op("act", [tpyt], [t_mixT], lambda a: a.activation(out=mixT[:, :, cs], in_=pytb[:, 0:512].rearrange("p (j t) -> p j t", j=4)[:, :, :P], func=AF.Copy))
                RELH(tpyt)

            def tile_attn(ti):
                c0 = ti * NTT
                cs = slice(c0, c0 + P)
                first_tile = first_group and ti == 0 and not sample
                chk('t_y')
                if not sample:
                    for g in range(2):
                        bs = slice(64 * g, 64 * g + 64)
                        kbs = [1] if first_tile else [0, 1]
                        for kb in kbs:
                            kcols = slice(c0 + kb * 128, c0 + kb * 128 + 128)
                            pS, tpS = PSH()
                            for hh in range(4):
                                op("pe", [t_kn, t_qn], [tpS], lambda p: p.matmul(pS[:, hh * 128:(hh + 1) * 128], lhsT=kn[bs, kcols], rhs=qn[bs, hh, cs], start=True, stop=True))
                            pe_, tpe = st["pe%d" % kb]; PT_, tPT = st["PT%d" % kb]
                            op("act", [tpS], [tpe], lambda a: a.activation(out=pe_[:], in_=pS[:], func=AF.Exp, scale=0.125))
                            RELH(tpS)
                            Et = Eprev if kb == 0 else Ecur
                            op("dve", [tpe, t_c], [tPT], lambda v: v.tensor_tensor(out=PT_[:], in0=pe_[:], in1=Et[:, g * 512:(g + 1) * 512], op=ALU.mult))
                        chk('a_S')
                        yield
                        po, tpo = PSH(); pdn, tpdn = PSH()
                        for ii, kb in enumerate(kbs):
                            PT_, tPT = st["PT%d" % kb]
                            op("pe", [t_vb, tPT], [tpo], lambda p: p.matmul(po[:64, :], lhsT=vb[:, ti + kb, g * 64:(g + 1) * 64], rhs=PT_[:], start=(ii == 0), stop=(ii == len(kbs) - 1)))
                        chk('a_po')
                        for ii, kb in enumerate(kbs):
                            PT_, tPT = st["PT%d" % kb]
                            op("pe", [t_c, tPT], [tpdn], lambda p: p.matmul(pdn[:64, :], lhsT=onesb[:, 0:64], rhs=PT_[:], start=(ii == 0), stop=(ii == len(kbs) - 1)))
                        chk('a_pdn')
                        op("dve", [tpdn, t_lay], [K_("den")], lambda v: v.tensor_tensor(out=A("den")[:, :], in0=pdn[:64, :], in1=esinkrow[:, g * 512:(g + 1) * 512], op=ALU.add))
                        RELH(tpdn)
                        op("dve", [K_("den")], [K_("den")], lambda v: v.reciprocal(out=A("den")[:, :], in_=A("den")[:, :]))
                        op("dve", [tpo, K_("den")], [t_oT], lambda v: v.tensor_tensor(out=oT[:, g * 4:(g + 1) * 4, cs], in0=po[:64, :].rearrange("p (h q) -> p h q", h=4), in1=A("den")[:, :].rearrange("p (h q) -> p h q", h=4), op=ALU.mult))
                        RELH(tpo)
                        yield
                else:
                    for b4 in range(0, NSS, 4):
                        pt_, tpt = PS(); ptb = pt_[:].bitcast(BF16)
                        for bb in range(4):
                            op("pe", [t_kcb, t_c], [tpt], lambda p: p.transpose(out=ptb[:, bb * 128:(bb + 1) * 128], in_=kcb[:, b4 + bb, :], identity=identb[:]))
                        op("act", [tpt], [t_KcT], lambda a: a.activation(out=KcT[:, b4:b4 + 4, :], in_=ptb[:, 0:512].rearrange("p (b i) -> p b i", b=4), func=AF.Copy))
                    pSc, tpSc = PS()
                    for b in range(NSS):
                        for g in range(2):
                            bs = slice(64 * g, 64 * g + 64)
                            for hh in range(4):
                                idx = (b * 8 + g * 4 + hh) * 4
                                op("pe", [t_KcT, t_qn], [tpSc], lambda p: p.matmul(pSc[:, idx:idx + 4], lhsT=KcT[bs, b, :], rhs=qn[bs, hh, b * 4:b * 4 + 4], start=True, stop=True))
                    pe_, tpe = st["pe0"]
                    op("act", [tpSc], [tpe], lambda a: a.activation(out=pe_[:, :NSS * 32], in_=pSc[:, :NSS * 32], func=AF.Exp, scale=0.125))
                    op("dve", [tpe, t_c], [t_PTc], lambda v: v.tensor_tensor(out=PTc[:, :NSS * 32].rearrange("p (b x) -> p b x", b=NSS), in0=pe_[:, :NSS * 32].rearrange("p (b x) -> p b x", b=NSS), in1=Ecache[:, :].unsqueeze(1).broadcast_to([128, NSS, 32]), op=ALU.mult))
                    pSn, tpSn = PS()
                    for g in range(2):
                        bs = slice(64 * g, 64 * g + 64)
                        for hh in range(4):
                            op("pe", [t_kn, t_qn], [tpSn], lambda p: p.matmul(pSn[:P, (g * 4 + hh) * 64:(g * 4 + hh) * 64 + P], lhsT=kn[bs, 0:P], rhs=qn[bs, hh, 0:P], start=True, stop=True))
                    pe1_, tpe1 = st["pe1"]; PT1_, tPT1 = st["PT1"]
                    op("act", [tpSn], [tpe1], lambda a: a.activation(out=pe1_[:P, :], in_=pSn[:P, :], func=AF.Exp, scale=0.125))
                    for g in range(2):
                        ov = PT1_[:P, g * 256:(g + 1) * 256].rearrange("p (b hh t) -> p hh b t", b=16, hh=4)
                        i0 = pe1_[:P, g * 256:(g + 1) * 256].rearrange("p (hh b t) -> p hh b t", hh=4, b=16)
                        i1 = Enew[:P, g * 256:(g + 1) * 256].rearrange("p (hh b t) -> p hh b t", hh=4, b=16)
                        op("dve", [tpe1, t_c], [tPT1], lambda v: v.tensor_tensor(out=ov, in0=i0, in1=i1, op=ALU.mult))
                    po, tpo = PS(); pdn, tpdn = PS()
                    for (pp_, tpp, use_v) in ((po, tpo, True), (pdn, tpdn, False)):
                        for g in range(2):
                            lw = vb[:P, 1, g * 64:(g + 1) * 64] if use_v else onesb[:P, 0:64]
                            op("pe", [t_vb, tPT1, t_c], [tpp], lambda p: p.matmul(pp_[:64, g * 256:(g + 1) * 256], lhsT=lw, rhs=PT1_[:P, g * 256:(g + 1) * 256], start=True, stop=False))
                            for b in range(NSS):
                                lw2 = vcb[:, b, g * 64:(g + 1) * 64] if use_v else onesb[:, 0:64]
                                op("pe", [t_vcb, t_PTc, t_c], [tpp], lambda p: p.matmul(pp_[:64, g * 256 + b * 16:g * 256 + b * 16 + 16], lhsT=lw2, rhs=PTc[:, b * 32 + g * 16:b * 32 + g * 16 + 16], start=False, stop=(b == NSS - 1)))
                    for g in range(2):
                        dv = A("den")[:, g * 256:(g + 1) * 256].rearrange("p (b hh t) -> p hh b t", b=16, hh=4)
                        ek = esinkrow[:, :].rearrange("p (h q) -> p h q", h=8)[:, g * 4:(g + 1) * 4, 0:64].rearrange("p hh (b t) -> p hh b t", t=4)
                        op("dve", [tpdn, t_lay], [K_("den")], lambda v: v.tensor_tensor(out=dv, in0=pdn[:64, g * 256:(g + 1) * 256].rearrange("p (b hh t) -> p hh b t", b=16, hh=4), in1=ek, op=ALU.add))
                        op("dve", [K_("den")], [K_("den")], lambda v: v.reciprocal(out=A("den")[:, g * 256:(g + 1) * 256], in_=A("den")[:, g * 256:(g + 1) * 256]))
                        op("dve", [tpo, K_("den")], [t_oT], lambda v: v.tensor_tensor(out=oT[:, g * 4:(g + 1) * 4, 0:64].rearrange("p hh (b t) -> p hh b t", t=4), in0=po[:64, g * 256:(g + 1) * 256].rearrange("p (b hh t) -> p hh b t", b=16, hh=4), in1=dv, op=ALU.mult))

                yield

            for ti in range(ntile):
                if sample:
                    for _ in tile_ssd(ti):
                        pass
                    for _ in tile_attn(ti):
                        pass
                else:
                    gens = [tile_ssd(ti), tile_attn(ti)]
                    while gens:
                        for g_ in list(gens):
                            try:
                                next(g_)
                            except StopIteration:
                                gens.remove(g_)
            chk('m_tiles')
            if sample or last_group:
                ktm = SB(sc, "ktm", [128, 128]); t_ktm = Tok()
                pk, tpk = PS()
                lastc = slice(NT - P, NT)
                op("pe", [t_knf, t_c], [tpk], lambda p: p.transpose(out=pk[:P, 0:128], in_=knf[:, lastc], identity=ident[:]))
                op("act", [tpk], [t_ktm], lambda a: a.activation(out=ktm[:P, :], in_=pk[:P, 0:128], func=AF.Copy))
                if sample:
                    for b in range(NSS):
                        dma("sp", [t_ktm], [], ock_s[l, b, 124:128, :], ktm[b * 4:(b + 1) * 4, :])
                        dma("sp", [t_vf], [], ocv_s[l, b, 124:128, :], vf[b * 4:(b + 1) * 4, :])
                else:
                    dma("sp", [t_ktm], [], ock_p[l, :, :], ktm[:, :])
                    dma("sp", [t_vf], [], ocv_p[l, :, :], vf[:, :])
                    hTo = SB(sc, "hTo", [128, 4, 128]); t_hTo = Tok()
                    ph, tph = PS()
                    for j in range(4):
                        op("pe", [t_hT, t_c], [tph], lambda p: p.transpose(out=ph[:, j * 128:(j + 1) * 128], in_=hT[:, j * 128:(j + 1) * 128], identity=ident[:]))
                    op("act", [tph], [t_hTo], lambda a: a.activation(out=hTo[:].rearrange("p j n -> p (j n)"), in_=ph[:, :], func=AF.Copy))
                    dma("sp", [t_hTo], [], ossm_p[l, :, :].rearrange("(j p) n -> p j n", p=128), hTo[:])
            if not sample:
                op("dve", [t_kn], [t_kvh], lambda v: v.tensor_copy(out=khalo[:], in_=kn[:, NT:NT + 128]))
                op("dve", [t_vb], [t_kvh], lambda v: v.tensor_copy(out=vhalo[:], in_=vb[:, ntile, :]))

            chk('m_outs')
            woT = wreg[:, 0:4096].rearrange("p (k c) -> p k c", k=4)
            woA = wreg[:64, 4096:4096 + 8192].rearrange("p (k c) -> p k c", k=8)
            if cached:
                dma("pool", [t_scm[1]], T, wreg[:, 0:4096], wscm[1, :, 0:4096])
                dma("pool", [t_scm[1]], T, wreg[:64, 4096:12288], wscm[1, :64, 4096:12288])
            else:
                dma("pool", [], T, woT, w_out[l, 0:512, :].rearrange("(k p) c -> p k c", p=128))
                dma("pool", [], T, woA, w_out[l, 512:1024, :].rearrange("(hd p) c -> p hd c", p=64))
                dma("sp", T, [t_scm[1]], wscm[1, :, 0:4096], wreg[:, 0:4096])
                dma("sp", T, [t_scm[1]], wscm[1, :64, 4096:12288], wreg[:64, 4096:12288])
            with cx.fast():
                for d in range(8):
                    for (n0, nw) in nblocks(NT):
                        ps, tp = PS()
                        for k in range(4):
                            op("pe", T + [t_mixT], [tp], lambda p: p.matmul(ps[:, :nw], lhsT=woT[:, k, d * 128:(d + 1) * 128], rhs=mixT[:, k, n0:n0 + nw], start=(k == 0), stop=False))
                        for hd in range(8):
                            op("pe", T + [t_oT], [tp], lambda p: p.matmul(ps[:, :nw], lhsT=woA[:, hd, d * 128:(d + 1) * 128], rhs=oT[:, hd, n0:n0 + nw], start=False, stop=(hd == 7)))
                        op("dve", [tp, t_h], [t_h], lambda v: v.tensor_tensor(out=h[:, d, n0:n0 + nw], in0=h[:, d, n0:n0 + nw], in1=ps[:, :nw], op=ALU.add))

        def ple(sc, h, t_h, NT, l, psrc, ntok_tile, cached):
            xn = SB(sc, "xn", [128, 8, NT], BF16); t_xn = [Tok() for _ in range(8)]
            norm(sc, h, t_h, NT, 3, l, xn, t_xn)
            wg = wreg[:, 0:8192].rearrange("p (k c) -> p k c", k=8)
            wp = wreg[:, 8192:8192 + 2048].rearrange("p (k c) -> p k c", k=2)
            T = [t_wreg]
            if cached:
                dma("pool", [t_scm[2]], T, wreg[:, 0:10240], wscm[2, :, 0:10240])
            else:
                dma("pool", [], T, wg, w_pg[l, :, :].rearrange("(k p) c -> p k c", p=128))
                dma("pool", [], T, wp, w_pp[l, :, :].rearrange("(k p) c -> p k c", p=128))
                dma("sp", T, [t_scm[2]], wscm[2, :, 0:10240], wreg[:, 0:10240])
            peT = SB(sc, "peT", [128, 2, NT], BF16); t_peT = Tok()
            ptm = [SB(sc, "ptm%d" % i, [128, 256]) for i in range(2)]; t_ptm = [Tok(), Tok()]
            P = ntok_tile
            for ti in range(NT // P):
                pt_, tpt_ = ptm[ti % 2], t_ptm[ti % 2]
                dma("sp", [], [tpt_], pt_[:P, :], psrc[ti * P:(ti + 1) * P, :])
                ps, tp = PS()
                for j in range(2):
                    op("pe", [tpt_, t_c], [tp], lambda p: p.transpose(out=ps[:, j * 128:j * 128 + P], in_=pt_[:P, j * 128:(j + 1) * 128], identity=ident[:P, :P]))
                op("act", [tp], [t_peT], lambda a: a.activation(out=peT[:, :, ti * P:(ti + 1) * P], in_=ps[:, 0:256].rearrange("p (j t) -> p j t", j=2)[:, :, :P], func=AF.Copy))
            with cx.fast():
                sg = SB(sc, "sg", [128, 512]); t_sg = Tok()
                for d in range(8):
                    for (n0, nw) in nblocks(NT):
                        pg, tpg = PS(); pq, tpq = PS()
                        for k in range(8):
                            op("pe", T + [t_xn[k]], [tpg], lambda p: p.matmul(pg[:, :nw], lhsT=wg[:, k, d * 128:(d + 1) * 128], rhs=xn[:, k, n0:n0 + nw], start=(k == 0), stop=(k == 7)))
                        for k in range(2):
                            op("pe", T + [t_peT], [tpq], lambda p: p.matmul(pq[:, :nw], lhsT=wp[:, k, d * 128:(d + 1) * 128], rhs=peT[:, k, n0:n0 + nw], start=(k == 0), stop=(k == 1)))
                        op("act", [tpg], [t_sg], lambda a: a.activation(out=sg[:, :nw], in_=pg[:, :nw], func=AF.Sigmoid))
                        op("dve", [t_sg, tpq], [t_sg], lambda v: v.tensor_tensor(out=sg[:, :nw], in0=sg[:, :nw], in1=pq[:, :nw], op=ALU.mult))
                        op("dve", [t_sg, t_h], [t_h], lambda v: v.tensor_tensor(out=h[:, d, n0:n0 + nw], in0=h[:, d, n0:n0 + nw], in1=sg[:, :nw], op=ALU.add))

        stage = [0]

        def chk(name):
            stage[0] += 1
            if STAGE_LIMIT is not None and stage[0] >= STAGE_LIMIT:
                if not cx.stopped:
                    print("STOP at stage", stage[0], name)
                cx.stopped = True

        try:
          _main_body = True
          with contextlib.ExitStack() as sc:
              xtm0 = SB(sc, "xtm", [128, 1024])
              transpose_in(xtm0, Tok(), xs[:, :], NS, hs, t_hs, 0)
              cx.barrier()
          chk('sample_load')
          for l in range(DEPTH):
              load_layer_consts(l)
              chk('layer_consts')
              op("dve", [], [t_hT], lambda v: v.memset(hT[:], 0.0))
              op("dve", [], [t_hTb], lambda v: v.memset(hTb[:], 0.0))
              op("dve", [], [t_xhalo], lambda v: v.memset(xhalo[:], 0.0))
              op("dve", [], [t_kvh], lambda v: v.memset(khalo[:], 0.0))
              op("dve", [], [t_kvh], lambda v: v.memset(vhalo[:], 0.0))
              for gi in range(NG + 1):
                  sample = gi == NG
                  NT = NS if sample else NTG
                  with contextlib.ExitStack() as gsc:
                      if sample:
                          h, t_h = hs, t_hs
                      else:
                          h = SB(gsc, "hgrp", [128, 8, NTG]); t_h = Tok()
                          if l == 0:
                              with contextlib.ExitStack() as sc:
                                  xtms = [(SB(sc, "xtm", [128, 1024]), Tok()) for _ in range(2)]
                                  for ti in range(NTG // 128):
                                      transpose_in(xtms[ti % 2][0], xtms[ti % 2][1], xp[gi * NTG + ti * 128:gi * NTG + (ti + 1) * 128, :], 128, h, t_h, ti * 128)
                                  cx.barrier()
                          else:
                              dma("sp", [t_hscr[gi]], [t_h], h[:], hscr[:, :, gi * NTG:(gi + 1) * NTG].rearrange("j p t -> p j t"))
                      cx.barrier()
                      chk('group_load')
                      with contextlib.ExitStack() as sc:
                          xn = SB(sc, "xn", [128, 8, NT], BF16); t_xn = [Tok() for _ in range(8)]
                          with cx.fast():
                              norm(sc, h, t_h, NT, 0, l, xn, t_xn)
                              chk('norm')
                              ffn(sc, h, t_h, NT, xn, t_xn, w1a, w3a, w2a, l, 0, gi > 0)
                          cx.barrier()
                          chk('ffn_a')
                      with contextlib.ExitStack() as sc:
                          mixer(sc, h, t_h, NT, l, sample, gi == 0, gi == NG - 1, gi > 0)
                          cx.barrier()
                          chk('mixer')
                      with contextlib.ExitStack() as sc:
                          xn = SB(sc, "xn", [128, 8, NT], BF16); t_xn = [Tok() for _ in range(8)]
                          with cx.fast():
                              norm(sc, h, t_h, NT, 2, l, xn, t_xn)
                              ffn(sc, h, t_h, NT, xn, t_xn, w1b, w3b, w2b, l, 1, gi > 0)
                          cx.barrier()
                          chk('ffn_b')
                      with contextlib.ExitStack() as sc:
                          if sample:
                              ple(sc, h, t_h, NT, l, psm[l, :, :], 64, gi > 0)
                          else:
                              ple(sc, h, t_h, NT, l, pp[l, gi * NTG:(gi + 1) * NTG, :], 128, gi > 0)
                          cx.barrier()
                      if not sample:
                          if l == 0:
                              dma("sp", [t_h], [t_hscr[gi]], hscr[:, :, gi * NTG:(gi + 1) * NTG].rearrange("j p t -> p j t"), h[:])
                          else:
                              with contextlib.ExitStack() as sc:
                                  ytms = [(SB(sc, "ytm", [128, 1024]), Tok()) for _ in range(2)]
                                  for ti in range(NTG // 128):
                                      transpose_out(ytms[ti % 2][0], ytms[ti % 2][1], h, t_h, ti * 128, 128, yp[gi * NTG + ti * 128:gi * NTG + (ti + 1) * 128, :])
                                  cx.barrier()
                      elif l == DEPTH - 1:
                          with contextlib.ExitStack() as sc:
                              ytm0 = SB(sc, "ytm", [128, 1024])
                              transpose_out(ytm0, Tok(), h, t_h, 0, NS, ys[:, :])
                              cx.barrier()
                      cx.barrier()
        except _Stop:
            pass
        cx.finish()
    return nc


def make_consts():
    c = {}
    c["c_ident"] = np.eye(128, dtype=np.float32)
    i = np.arange(128)
    c["c_U"] = (i[:, None] <= i[None, :]).astype(np.float32)
    c["c_SL"] = (i[:, None] > i[None, :]).astype(np.float32)
    j = np.arange(64); same = (j[:, None] // 4) == (j[None, :] // 4)
    c["c_Ubd"] = (same & (j[:, None] <= j[None, :])).astype(np.float32)
    c["c_SLbd"] = (same & (j[:, None] > j[None, :])).astype(np.float32)
    c["c_BMt"] = ((j[:, None] // 4) == np.arange(16)[None, :]).astype(np.float32)
    bm = ((np.arange(16)[:, None]) == (j[None, :] // 4)).astype(np.float32)
    c["c_BM"] = np.broadcast_to(bm.reshape(1, 16 * 64), (128, 16 * 64)).copy()
    bo = np.zeros((128, 128), np.float32); bo[:64, :64] = 1; bo[64:, 64:] = 1
    c["c_bones"] = bo
    slopes = np.power(np.float32(2.0), -8.0 * np.arange(1, 9, dtype=np.float32) / 8).astype(np.float32)
    s = i[:, None, None]; q = i[None, None, :]; sl = slopes[None, :, None]
    ecur = np.where(q >= s, np.exp(-sl * (q - s).astype(np.float32)), 0.0)
    eprev = np.where(s > q, np.exp(-sl * (q - s + 128).astype(np.float32)), 0.0)
    c["c_Ecur"] = ecur.astype(np.float32).reshape(128, 1024)
    c["c_Eprev"] = eprev.astype(np.float32).reshape(128, 1024)
    t = np.arange(4)[None, None, :]
    ecache = np.where(s > t, np.exp(-sl * (128 + t - s).astype(np.float32)), 0.0)
    c["c_Ecache"] = ecache.astype(np.float32).reshape(128, 32)
    sj = j[:, None, None]; qj = j[None, None, :]
    enew = np.where(((sj // 4) == (qj // 4)) & (sj <= qj), np.exp(-sl * (qj - sj).astype(np.float32)), 0.0)
    c["c_Enew"] = enew.astype(np.float32).reshape(64, 512)
    return c


_WNAMES = ["g_ffn1", "w1_a", "w3_a", "w2_a", "g_mix", "w_in", "conv_w", "conv_b", "dt_bias", "a_log", "d_skip", "ssm_norm",
           "q_norm", "k_norm", "sinks", "w_out", "g_ffn2", "w1_b", "w3_b", "w2_b", "g_ple", "w_ple_gate", "w_ple_proj"]


def run(inputs, SEQ, NSS, NTG, n_prompt, ncores):
    f = lambda a: np.ascontiguousarray(np.asarray(a, dtype=np.float32))
    nc = build(SEQ, NSS, NTG)
    consts = make_consts()
    wts = {n: f(inputs[n]) for n in _WNAMES}
    xpr = f(inputs["x_prompt"]); ppr = f(inputs["p_prompt"]); xsm = f(inputs["x_sample"]); psm = f(inputs["p_sample"])
    sssm = f(inputs["state_ssm"]); sconv = f(inputs["state_conv"]); ck = f(inputs["cache_k_win"]); cv = f(inputs["cache_v_win"])
    in_maps = []
    for c in range(ncores):
        b = c % n_prompt
        bs = slice(c * NSS, (c + 1) * NSS)
        m = dict(wts); m.update(consts)
        m["xp"] = f(xpr[b]); m["pp"] = f(ppr[:, b])
        m["xs"] = f(xsm[bs].reshape(NSS * 4, D)); m["psm"] = f(psm[:, bs].reshape(DEPTH, NSS * 4, DPLE))
        m["sssm"] = f(sssm[:, bs].reshape(DEPTH, NSS, 512, 128)); m["sconv"] = f(sconv[:, bs].reshape(DEPTH, NSS * 3, 1024))
        m["ck"] = f(ck[:, bs].reshape(DEPTH, NSS, 128, 128)); m["cv"] = f(cv[:, bs].reshape(DEPTH, NSS, 128, 128))
        in_maps.append(m)
    res = run_bass_kernel_spmd(nc, in_maps, core_ids=list(range(ncores))).results
    P = n_prompt
    y_p = np.stack([res[b]["yp"] for b in range(P)])
    y_s = np.concatenate([res[c]["ys"].reshape(NSS, 4, D) for c in range(ncores)])
    ssm_p = np.stack([res[b]["ossm_p"].reshape(DEPTH, 8, 64, 128) for b in range(P)], axis=1)
    conv_p = np.stack([res[b]["oconv_p"] for b in range(P)], axis=1)
    k_p = np.stack([res[b]["ock_p"].reshape(DEPTH, 128, 2, 64) for b in range(P)], axis=1)
    v_p = np.stack([res[b]["ocv_p"].reshape(DEPTH, 128, 2, 64) for b in range(P)], axis=1)
    ssm_s = np.concatenate([res[c]["ossm_s"].reshape(DEPTH, NSS, 8, 64, 128) for c in range(ncores)], axis=1)
    conv_s = np.concatenate([res[c]["oconv_s"].reshape(DEPTH, NSS, 3, 1024) for c in range(ncores)], axis=1)
    k_s = np.concatenate([res[c]["ock_s"].reshape(DEPTH, NSS, 128, 2, 64) for c in range(ncores)], axis=1)
    v_s = np.concatenate([res[c]["ocv_s"].reshape(DEPTH, NSS, 128, 2, 64) for c in range(ncores)], axis=1)
    return tuple(np.ascontiguousarray(a, dtype=np.float32) for a in (y_p, y_s, ssm_p, conv_p, k_p, v_p, ssm_s, conv_s, k_s, v_s))


def kernel(**inputs):
    return run(inputs, SEQ=4096, NSS=16, NTG=512, n_prompt=4, ncores=NCORES)
```

```python
import contextlib
import numpy as np
import concourse.bass as bass
import concourse.mybir as mybir
from concourse.bass_utils import run_bass_kernel_spmd

F32 = mybir.dt.float32
BF16 = mybir.dt.bfloat16
AF = mybir.ActivationFunctionType
ALU = mybir.AluOpType
AX = mybir.AxisListType

D = 1024; DFF = 2752; DPROJ = 2312; DPLE = 256; DEPTH = 2
NCORES = 8
EPS = 1e-6
FT = [(i * 128, 128) for i in range(21)] + [(2688, 64)]
NFT = len(FT)
FCH = [list(range(i, min(i + 2, NFT))) for i in range(0, NFT, 2)]
W2CH = [list(range(i, min(i + 6, NFT))) for i in range(0, NFT, 6)]


DEBUG_MAP = None
SBUF_PEAK = [0, 0]
STAGE_LIMIT = None


class _Stop(Exception):
    pass


class Tok:
    __slots__ = ("w", "r")

    def __init__(self):
        self.w = None
        self.r = {}


class Eng:
    def __init__(self, name, h):
        self.name = name; self.h = h; self.sem = None; self.cnt = 0; self.waited = {}; self.own = set()


def _r32(n):
    return 32 if n <= 32 else (64 if n <= 64 else 128)


class PEProxy:
    def __init__(self, ctx, e):
        self.ctx = ctx; self.e = e; self.last = None

    def _mode(self, key):
        e = self.e
        if key != self.last and e.cnt > 0:
            k = id(e.sem)
            if e.waited.get(k, 0) < e.cnt:
                e.h.wait_ge(e.sem, e.cnt)
                e.waited[k] = e.cnt
        self.last = key

    def matmul(self, out, lhsT, rhs, start=True, stop=True):
        self._mode(("mm", str(lhsT.dtype), _r32(lhsT.shape[0]), _r32(int(np.prod(lhsT.shape[1:]))), out.base_partition()))
        return self.e.h.matmul(out, lhsT=lhsT, rhs=rhs, start=start, stop=stop)

    def transpose(self, out, in_, identity):
        self._mode(("tr", str(in_.dtype), _r32(in_.shape[0]), _r32(int(np.prod(in_.shape[1:]))), out.base_partition()))
        return self.e.h.transpose(out=out, in_=in_, identity=identity)


class Ctx:
    EPOCH = 12000

    def __init__(self, nc, es):
        self.nc = nc; self.es = es
        self.E = {"pe": Eng("pe", nc.tensor), "act": Eng("act", nc.scalar), "dve": Eng("dve", nc.vector),
                  "pool": Eng("pool", nc.gpsimd), "sp": Eng("sp", nc.sync)}
        self.nsem = 0
        for e in self.E.values():
            e.sem = self._newsem(); e.own.add(id(e.sem))
        self.slots = {"sp": [[self._newsem(), 0] for _ in range(10)],
                      "pool": [[self._newsem(), 0] for _ in range(10)]}
        self.slot_i = {"sp": 0, "pool": 0}
        self.semkey = {}
        self.stopped = False
        self.pe_proxy = PEProxy(self, self.E["pe"])
        self.pe_fast = False

    @contextlib.contextmanager
    def fast(self):
        old = self.pe_fast
        self.pe_fast = True; self.pe_proxy.last = "edge"
        try:
            yield
        finally:
            self.pe_fast = old; self.pe_proxy.last = "edge"

    def _newsem(self):
        self.nsem += 1
        return self.es.enter_context(self.nc.semaphore("s%d" % self.nsem))

    def _wait(self, e, ev):
        sem, val = ev
        k = id(sem)
        if e.name == "pe" and k in e.own and self.pe_fast:
            return
        if e.waited.get(k, 0) >= val:
            return
        e.h.wait_ge(sem, val)
        e.waited[k] = val

    def _sync(self, e, reads, writes):
        for t in reads:
            if t.w is not None:
                self._wait(e, t.w)
        for t in writes:
            if t.w is not None:
                self._wait(e, t.w)
            for ev in t.r.values():
                self._wait(e, ev)

    def _commit(self, ev, reads, writes):
        for t in writes:
            t.w = ev; t.r = {}
        for t in reads:
            k = id(ev[0])
            if k not in t.r or t.r[k][1] < ev[1]:
                t.r[k] = ev

    def op(self, eng, reads, writes, fn):
        if self.stopped:
            return None
        e = self.E[eng]
        self._sync(e, reads, writes)
        if e.cnt >= self.EPOCH:
            e.sem = self._newsem(); e.cnt = 0; e.own.add(id(e.sem))
        inst = fn(self.pe_proxy if eng == "pe" else e.h)
        e.cnt += 1
        inst.then_inc(e.sem, 1)
        if DEBUG_MAP is not None:
            import traceback
            nm = None
            for a in ("name", "inst", "instruction", "ins"):
                v = getattr(inst, a, None)
                if v is not None:
                    nm = getattr(v, "name", v) if a != "name" else v
                    break
            fr = traceback.extract_stack(limit=4)[-2]; fr0 = traceback.extract_stack(limit=4)[-3]
            DEBUG_MAP[str(nm)] = "%s:%d < %s:%d" % (fr.name, fr.lineno, fr0.name, fr0.lineno)
        ev = (e.sem, e.cnt)
        e.waited[id(e.sem)] = max(e.waited.get(id(e.sem), 0), 0)
        self._commit(ev, reads, writes)
        return ev

    def dma(self, eng, reads, writes, out, in_, **kw):
        if self.stopped:
            return None
        e = self.E[eng]
        sl = self.slots[eng][self.slot_i[eng]]
        self.slot_i[eng] = (self.slot_i[eng] + 1) % len(self.slots[eng])
        if sl[1] > 0:
            self._wait(e, (sl[0], sl[1]))
        self._sync(e, reads, writes)
        inst = e.h.dma_start(out=out, in_=in_, **kw)
        sl[1] += 16
        inst.then_inc(sl[0], 16)
        ev = (sl[0], sl[1])
        self._commit(ev, reads, writes)
        return ev

    def barrier(self):
        if self.stopped:
            return
        evs = []
        for e in self.E.values():
            if e.cnt > 0:
                evs.append((e.sem, e.cnt))
        for q in self.slots.values():
            for sl in q:
                if sl[1] > 0:
                    evs.append((sl[0], sl[1]))
        for e in self.E.values():
            for ev in evs:
                if ev[0] is e.sem:
                    continue
                self._wait(e, ev)

    def finish(self):
        self.stopped = False
        self.barrier()


def build(SEQ, NSS, NTG):
    NS = NSS * 4
    NG = SEQ // NTG
    nc = bass.Bass("TRN2", target_bir_lowering=False)
    di = lambda n, s: nc.dram_tensor(n, s, F32, kind="ExternalInput").ap()
    do = lambda n, s: nc.dram_tensor(n, s, F32, kind="ExternalOutput").ap()
    xp = di("xp", [SEQ, D]); pp = di("pp", [DEPTH, SEQ, DPLE]); xs = di("xs", [NS, D]); psm = di("psm", [DEPTH, NS, DPLE])
    sssm = di("sssm", [DEPTH, NSS, 512, 128]); sconv = di("sconv", [DEPTH, NSS * 3, 1024])
    ck = di("ck", [DEPTH, NSS, 128, 128]); cv = di("cv", [DEPTH, NSS, 128, 128])
    g_ffn1 = di("g_ffn1", [DEPTH, D]); g_mix = di("g_mix", [DEPTH, D]); g_ffn2 = di("g_ffn2", [DEPTH, D]); g_ple = di("g_ple", [DEPTH, D])
    w1a = di("w1_a", [DEPTH, D, DFF]); w3a = di("w3_a", [DEPTH, D, DFF]); w2a = di("w2_a", [DEPTH, DFF, D])
    w1b = di("w1_b", [DEPTH, D, DFF]); w3b = di("w3_b", [DEPTH, D, DFF]); w2b = di("w2_b", [DEPTH, DFF, D])
    w_in = di("w_in", [DEPTH, D, DPROJ]); w_out = di("w_out", [DEPTH, D, D])
    conv_w = di("conv_w", [DEPTH, 4, 1024]); conv_b = di("conv_b", [DEPTH, 1024])
    dt_bias = di("dt_bias", [DEPTH, 8]); a_log = di("a_log", [DEPTH, 8]); d_skip = di("d_skip", [DEPTH, 8])
    ssm_norm = di("ssm_norm", [DEPTH, 512]); q_norm = di("q_norm", [DEPTH, 64]); k_norm = di("k_norm", [DEPTH, 64])
    sinks = di("sinks", [DEPTH, 8]); w_pg = di("w_ple_gate", [DEPTH, D, D]); w_pp = di("w_ple_proj", [DEPTH, DPLE, D])
    c_ident = di("c_ident", [128, 128]); c_U = di("c_U", [128, 128]); c_SL = di("c_SL", [128, 128])
    c_Ubd = di("c_Ubd", [64, 64]); c_SLbd = di("c_SLbd", [64, 64]); c_BMt = di("c_BMt", [64, 16]); c_BM = di("c_BM", [128, 16 * 64])
    c_bones = di("c_bones", [128, 128]); c_Eprev = di("c_Eprev", [128, 1024]); c_Ecur = di("c_Ecur", [128, 1024])
    c_Ecache = di("c_Ecache", [128, 32]); c_Enew = di("c_Enew", [64, 512])
    yp = do("yp", [SEQ, D]); ys = do("ys", [NS, D])
    ossm_p = do("ossm_p", [DEPTH, 512, 128]); oconv_p = do("oconv_p", [DEPTH, 3, 1024])
    ock_p = do("ock_p", [DEPTH, 128, 128]); ocv_p = do("ocv_p", [DEPTH, 128, 128])
    ossm_s = do("ossm_s", [DEPTH, NSS, 512, 128]); oconv_s = do("oconv_s", [DEPTH, NSS * 3, 1024])
    ock_s = do("ock_s", [DEPTH, NSS, 128, 128]); ocv_s = do("ocv_s", [DEPTH, NSS, 128, 128])
    hscr = nc.dram_tensor("hscr", [8, 128, SEQ], F32, kind="Internal").ap()
    wsc13 = nc.dram_tensor("wsc13", [2, 2, len(FCH), 128, 2048], BF16, kind="Internal").ap()
    wsc2 = nc.dram_tensor("wsc2", [2, 2, 128, NFT * 512], BF16, kind="Internal").ap()
    wscm = nc.dram_tensor("wscm", [3, 128, 18560], BF16, kind="Internal").ap()
    t_sc13 = [[[Tok() for _ in FCH] for _ in range(2)] for _ in range(2)]
    t_sc2 = [[[Tok() for _ in W2CH] for _ in range(2)] for _ in range(2)]
    t_scm = [Tok() for _ in range(3)]
    t_hscr = [Tok() for _ in range(NG)]

    with contextlib.ExitStack() as es:
        cx = Ctx(nc, es)
        op = cx.op; dma = cx.dma

        uniq = [0]

        def SB(scope, name, shape, dt=F32):
            uniq[0] += 1
            t = scope.enter_context(nc.sbuf_tensor("%s_%d" % (name, uniq[0]), shape, dt))
            try:
                SBUF_PEAK[0] = max(SBUF_PEAK[0], int(nc.sbuf_base))
                SBUF_PEAK[1] = int(nc.sbuf_top)
            except Exception:
                pass
            return t

        ident = SB(es, "ident", [128, 128]); t_c = Tok()
        identb = SB(es, "identb", [128, 128], BF16)
        Um = SB(es, "Um", [128, 128]); SLm = SB(es, "SLm", [128, 128])
        Ubd = SB(es, "Ubd", [64, 64]); SLbd = SB(es, "SLbd", [64, 64]); BMt = SB(es, "BMt", [64, 16]); BMtb = SB(es, "BMtb", [64, 16], BF16)
        BMb = SB(es, "BMb", [128, 16 * 64], BF16)
        bones = SB(es, "bones", [128, 128], BF16); onesb = SB(es, "onesb", [128, 128], BF16); onesf = SB(es, "onesf", [128, 128])
        Eprev = SB(es, "Eprev", [128, 1024]); Ecur = SB(es, "Ecur", [128, 1024]); Ecache = SB(es, "Ecache", [128, 32]); Enew = SB(es, "Enew", [64, 512])
        gcol = SB(es, "gcol", [128, 4 * DEPTH * 8])
        lay = {}
        for nm, w in [("cw", 32), ("cb", 8), ("dtb", 8), ("aneg", 8), ("dsk", 8), ("esk", 8), ("gq", 1), ("gk", 1)]:
            lay[nm] = SB(es, "l_" + nm, [128, w])
        gssm = SB(es, "gssm", [128, 512]); esinkrow = SB(es, "esinkrow", [64, 1024])
        t_lay = Tok()
        hs = SB(es, "hs", [128, 8, 64]); t_hs = Tok()
        w13 = [[SB(es, "w13_%d_%d" % (m, b), [128, 8, 256], BF16) for b in range(2)] for m in range(2)]
        t_w13 = [[Tok() for _ in range(2)] for _ in range(2)]
        wreg = SB(es, "wreg", [128, 18560], BF16); t_wreg = Tok()
        hT = SB(es, "hT", [128, 512]); hTb = SB(es, "hTb", [128, 512], BF16); t_hT = Tok(); t_hTb = Tok()
        xhalo = SB(es, "xhalo", [128, 8, 3]); t_xhalo = Tok()
        khalo = SB(es, "khalo", [128, 128], BF16); vhalo = SB(es, "vhalo", [128, 128], BF16); t_kvh = Tok()
        psb = [es.enter_context(nc.psum_tensor("ps%d" % i, [128, 512], F32)) for i in range(8)]
        t_ps = [Tok() for _ in range(8)]
        psi = [0]

        held = set()

        def PS(hold=False):
            for _try in range(9):
                i = psi[0]; psi[0] = (i + 1) % 8
                if i not in held:
                    break
            else:
                raise RuntimeError("all PSUM banks held")
            if hold:
                held.add(i)
            return psb[i], t_ps[i]

        def REL(tok):
            held.discard(t_ps.index(tok))

        def ld(eng, dst, src, toks, **kw):
            dma(eng, [], toks, dst, src, **kw)
        ld("sp", ident[:], c_ident[:, :], [t_c]); ld("pool", identb[:], c_ident[:, :], [t_c])
        ld("sp", Um[:], c_U[:, :], [t_c]); ld("sp", SLm[:], c_SL[:, :], [t_c])
        ld("sp", Ubd[:], c_Ubd[:, :], [t_c]); ld("sp", SLbd[:], c_SLbd[:, :], [t_c]); ld("sp", BMt[:], c_BMt[:, :], [t_c])
        ld("pool", BMtb[:], c_BMt[:, :], [t_c]); ld("pool", BMb[:], c_BM[:, :], [t_c]); ld("pool", bones[:], c_bones[:, :], [t_c])
        ld("sp", Eprev[:], c_Eprev[:, :], [t_c]); ld("sp", Ecur[:], c_Ecur[:, :], [t_c]); ld("sp", Ecache[:], c_Ecache[:, :], [t_c]); ld("sp", Enew[:], c_Enew[:, :], [t_c])
        op("dve", [], [t_c], lambda v: v.memset(onesb[:], 1.0))
        op("dve", [], [t_c], lambda v: v.memset(onesf[:], 1.0))
        for ni, g in enumerate([g_ffn1, g_mix, g_ffn2, g_ple]):
            for l in range(DEPTH):
                o = (ni * DEPTH + l) * 8
                ld("sp", gcol[:, o:o + 8], g[l, :].rearrange("(j p) -> p j", p=128), [t_c], allow_slow_non_contiguous=True)
        op("dve", [t_c], [t_c], lambda v: v.tensor_scalar(out=gcol[:], in0=gcol[:], scalar1=32.0, scalar2=None, op0=ALU.mult))

        def load_layer_consts(l):
            T = [t_lay]
            for j in range(4):
                ld("sp", lay["cw"][:, j * 8:(j + 1) * 8], conv_w[l, j, :].rearrange("(c p) -> p c", p=128), T, allow_slow_non_contiguous=True)
            ld("sp", lay["cb"][:], conv_b[l, :].rearrange("(c p) -> p c", p=128), T, allow_slow_non_contiguous=True)
            ld("sp", lay["dtb"][:], dt_bias[l, :].partition_broadcast(128), T)
            ld("sp", lay["aneg"][:], a_log[l, :].partition_broadcast(128), T)
            ld("sp", lay["dsk"][:], d_skip[l, :].partition_broadcast(128), T)
            ld("sp", lay["esk"][:], sinks[l, :].partition_broadcast(128), T)
            for hh in range(2):
                ld("sp", lay["gq"][hh * 64:(hh + 1) * 64, :], q_norm[l, :].rearrange("(p o) -> p o", o=1), T, allow_slow_non_contiguous=True)
                ld("sp", lay["gk"][hh * 64:(hh + 1) * 64, :], k_norm[l, :].rearrange("(p o) -> p o", o=1), T, allow_slow_non_contiguous=True)
            ld("sp", gssm[:], ssm_norm[l, :].partition_broadcast(128), T)
            op("act", T, T, lambda a: a.activation(out=lay["aneg"][:], in_=lay["aneg"][:], func=AF.Exp))
            op("dve", T, T, lambda v: v.tensor_scalar(out=lay["aneg"][:], in0=lay["aneg"][:], scalar1=-1.0, scalar2=None, op0=ALU.mult))
            op("act", T, T, lambda a: a.activation(out=lay["esk"][:], in_=lay["esk"][:], func=AF.Exp))
            op("dve", T, T, lambda v: v.tensor_scalar(out=lay["gq"][:], in0=lay["gq"][:], scalar1=8.0, scalar2=None, op0=ALU.mult))
            op("dve", T, T, lambda v: v.tensor_scalar(out=lay["gk"][:], in0=lay["gk"][:], scalar1=8.0, scalar2=None, op0=ALU.mult))
            op("dve", T, T, lambda v: v.tensor_copy(out=esinkrow[:].rearrange("p (h q) -> p h q", h=8),
                                                     in_=lay["esk"][0:64, :].unsqueeze(2).broadcast_to([64, 8, 128])))

        def nblocks(NT):
            return [(n0, min(512, NT - n0)) for n0 in range(0, NT, 512)]

        def norm(sc, h, t_h, NT, ni, l, xn, t_xn):
            go = (ni * DEPTH + l) * 8
            sq = SB(sc, "sq", [128, 8, 512], BF16); t_sq = Tok()
            rstd = SB(sc, "rstd", [128, 512]); t_rstd = Tok()
            for (n0, nw) in nblocks(NT):
                for half in range(2):
                    op("act", [t_h], [t_sq], lambda a: a.activation(out=sq[:, half * 4:(half + 1) * 4, :nw], in_=h[:, half * 4:(half + 1) * 4, n0:n0 + nw], func=AF.Square))
                ps, tp = PS()
                for j in range(8):
                    op("pe", [t_sq, t_c], [tp], lambda p: p.matmul(ps[:, :nw], lhsT=onesb[:], rhs=sq[:, j, :nw], start=(j == 0), stop=(j == 7)))
                op("act", [tp], [t_rstd], lambda a: a.activation(out=rstd[:, :nw], in_=ps[:, :nw], func=AF.Ln, bias=1024.0 * EPS, scale=1.0))
                op("act", [t_rstd], [t_rstd], lambda a: a.activation(out=rstd[:, :nw], in_=rstd[:, :nw], func=AF.Exp, scale=-0.5))
                for j in range(8):
                    op("dve", [t_h, t_rstd, t_c], [t_xn[j]], lambda v: v.scalar_tensor_tensor(out=xn[:, j, n0:n0 + nw], in0=h[:, j, n0:n0 + nw], scalar=gcol[:, go + j:go + j + 1], in1=rstd[:, :nw], op0=ALU.mult, op1=ALU.mult))

        w13_next = [None]

        def issue_w13(W1, W3, l, ci, parity, ab, cached):
            cols = FCH[ci]; f0 = FT[cols[0]][0]; fw = sum(FT[c][1] for c in cols)
            for m, W in enumerate((W1, W3)):
                scr = wsc13[ab, m, ci, :, :].rearrange("p (k f) -> p k f", k=8)[:, :, :fw]
                if cached:
                    dma("pool", [t_sc13[ab][m][ci]], [t_w13[m][parity]], w13[m][parity][:, :, :fw], scr)
                else:
                    dma("pool", [], [t_w13[m][parity]], w13[m][parity][:, :, :fw], W[l, :, f0:f0 + fw].rearrange("(k p) f -> p k f", p=128))
                    dma("sp", [t_w13[m][parity]], [t_sc13[ab][m][ci]], scr, w13[m][parity][:, :, :fw])

        def ffn(sc, h, t_h, NT, xn, t_xn, W1, W3, W2, l, ab, cached):
            gT = SB(sc, "gT", [128, NFT, NT], BF16); t_g = [Tok() for _ in range(NFT)]
            s1 = [SB(sc, "s1_%d" % i, [128, 512]) for i in range(2)]; t_s1 = [Tok(), Tok()]
            w2 = SB(sc, "w2", [128, NFT, 512], BF16); t_w2 = [Tok() for _ in W2CH]
            si = 0
            issue_w13(W1, W3, l, 0, 0, ab, cached)
            for ci, cols in enumerate(FCH):
                par = ci % 2
                if ci + 1 < len(FCH):
                    issue_w13(W1, W3, l, ci + 1, (ci + 1) % 2, ab, cached)
                for fi, ft in enumerate(cols):
                    fw = FT[ft][1]; fo = fi * 128
                    for (n0, nw) in nblocks(NT):
                        p1, tp1 = PS(); p3, tp3 = PS()
                        for (pp_, tpp, m) in ((p1, tp1, 0), (p3, tp3, 1)):
                            for k in range(8):
                                op("pe", [t_w13[m][par], t_xn[k]], [tpp], lambda p: p.matmul(pp_[:fw, :nw], lhsT=w13[m][par][:, k, fo:fo + fw], rhs=xn[:, k, n0:n0 + nw], start=(k == 0), stop=(k == 7)))
                        sb_, ts_ = s1[si], t_s1[si]; si ^= 1
                        op("act", [tp1], [ts_], lambda a: a.activation(out=sb_[:fw, :nw], in_=p1[:fw, :nw], func=AF.Silu))
                        op("dve", [ts_, tp3], [t_g[ft]], lambda v: v.tensor_tensor(out=gT[:fw, ft, n0:n0 + nw], in0=sb_[:fw, :nw], in1=p3[:fw, :nw], op=ALU.mult))
            for half in range(2):
                for wi, rows in enumerate(W2CH):
                    r0 = FT[rows[0]][0]
                    nfull = [r for r in rows if FT[r][1] == 128]
                    scr2 = wsc2[ab, half, :, :].rearrange("p (f c) -> p f c", f=NFT)
                    if nfull:
                        sl = slice(nfull[0], nfull[-1] + 1)
                        if cached:
                            dma("pool", [t_sc2[ab][half][wi]], [t_w2[wi]], w2[:, sl, :], scr2[:, sl, :])
                        else:
                            dma("pool", [], [t_w2[wi]], w2[:, sl, :], W2[l, r0:r0 + 128 * len(nfull), half * 512:(half + 1) * 512].rearrange("(f p) c -> p f c", p=128))
                    for r in rows:
                        if FT[r][1] != 128:
                            if cached:
                                dma("pool", [t_sc2[ab][half][wi]], [t_w2[wi]], w2[:64, r, :], scr2[:64, r, :])
                            else:
                                dma("pool", [], [t_w2[wi]], w2[:64, r, :], W2[l, FT[r][0]:FT[r][0] + 64, half * 512:(half + 1) * 512])
                    if not cached:
                        if nfull:
                            dma("sp", [t_w2[wi]], [t_sc2[ab][half][wi]], scr2[:, sl, :], w2[:, sl, :])
                        for r in rows:
                            if FT[r][1] != 128:
                                dma("sp", [t_w2[wi]], [t_sc2[ab][half][wi]], scr2[:64, r, :], w2[:64, r, :])
                for (n0, nw) in nblocks(NT):
                    acc = [PS() for _ in range(4)]
                    for ft in range(NFT):
                        fw = FT[ft][1]
                        wi = [i for i, rows in enumerate(W2CH) if ft in rows][0]
                        for dj in range(4):
                            op("pe", [t_w2[wi], t_g[ft]], [acc[dj][1]], lambda p: p.matmul(acc[dj][0][:, :nw], lhsT=w2[:fw, ft, dj * 128:(dj + 1) * 128], rhs=gT[:fw, ft, n0:n0 + nw], start=(ft == 0), stop=(ft == NFT - 1)))
                    for dj in range(4):
                        d = half * 4 + dj
                        op("dve", [acc[dj][1], t_h], [t_h], lambda v: v.scalar_tensor_tensor(out=h[:, d, n0:n0 + nw], in0=acc[dj][0][:, :nw], scalar=0.5, in1=h[:, d, n0:n0 + nw], op0=ALU.mult, op1=ALU.add))

        def transpose_in(xtm, t_x, src, ntok, h, t_h, c0):
            dma("sp", [], [t_x], xtm[:ntok, :], src)
            for a in range(2):
                ps, tp = PS()
                for j in range(4):
                    op("pe", [t_x, t_c], [tp], lambda p: p.transpose(out=ps[:, j * 128:j * 128 + ntok], in_=xtm[:ntok, (a * 4 + j) * 128:(a * 4 + j + 1) * 128], identity=ident[:ntok, :ntok]))
                op("act", [tp], [t_h], lambda a_: a_.activation(out=h[:, a * 4:(a + 1) * 4, c0:c0 + ntok], in_=ps[:].rearrange("p (j t) -> p j t", j=4)[:, :, :ntok], func=AF.Copy))

        def transpose_out(ytm, t_y, h, t_h, c0, ntok, dst):
            for a in range(2):
                ps, tp = PS()
                for j in range(4):
                    op("pe", [t_h, t_c], [tp], lambda p: p.transpose(out=ps[:ntok, j * 128:(j + 1) * 128], in_=h[:, a * 4 + j, c0:c0 + ntok], identity=ident[:]))
                op("act", [tp], [t_y], lambda a_: a_.activation(out=ytm[:ntok, a * 512:(a + 1) * 512], in_=ps[:ntok, :], func=AF.Copy))
            dma("sp", [t_y], [], dst, ytm[:ntok, :])

        def mixer(sc, h, t_h, NT, l, sample, first_group, last_group, cached):
            xn = SB(sc, "xn", [128, 8, NT], BF16); t_xn = [Tok() for _ in range(8)]
            norm(sc, h, t_h, NT, 1, l, xn, t_xn)
            chk('m_norm')
            NTT = 64 if sample else 128
            ntile = NT // NTT
            o = 0
            def carve(n, shape_str, **kw):
                nonlocal o
                a = wreg[:, o:o + n]; o += n
                return a.rearrange(shape_str, **kw)
            wz = carve(8 * 512, "p (k c) -> p k c", k=8); wx = carve(8 * 1024, "p (k c) -> p k c", k=8)
            wq = carve(8 * 512, "p (k c) -> p k c", k=8); wk = carve(8 * 128, "p (k c) -> p k c", k=8)
            wv = carve(8 * 128, "p (k c) -> p k c", k=8); wdt = carve(8 * 8, "p (k c) -> p k c", k=8)
            wl = w_in[l, :, :].rearrange("(k p) c -> p k c", p=128)
            T = [t_wreg]
            if cached:
                dma("pool", [t_scm[0]], T, wreg[:, 0:18496], wscm[0, :, 0:18496])
            else:
                dma("pool", [], T, wz, wl[:, :, 0:512]); dma("pool", [], T, wx, wl[:, :, 512:1536]); dma("pool", [], T, wdt, wl[:, :, 1536:1544])
                for hh in range(4):
                    for g in range(2):
                        c = 1544 + g * 256 + hh * 64
                        dma("pool", [], T, wq[:, :, hh * 128 + g * 64:hh * 128 + g * 64 + 64], wl[:, :, c:c + 64])
                dma("pool", [], T, wk, wl[:, :, 2056:2184]); dma("pool", [], T, wv, wl[:, :, 2184:2312])
                dma("sp", T, [t_scm[0]], wscm[0, :, 0:18496], wreg[:, 0:18496])
            chk('m_wdma')
            HAL = 0 if sample else 3
            xin = SB(sc, "xin", [128, 8, NT + HAL]); t_xin = Tok()
            xc = SB(sc, "xc", [128, 8, NT], BF16); t_xc = Tok()
            qn = SB(sc, "qn", [128, 4, NT], BF16); t_qn = Tok()
            KH = 0 if sample else 128
            kn = SB(sc, "kn", [128, KH + NT], BF16); t_kn = Tok()
            knf = SB(sc, "knf", [128, NT]); t_knf = Tok()
            vb = SB(sc, "vb", [128, ntile + 1, 128], BF16); t_vb = Tok()
            vf = SB(sc, "vf", [128, 128]); t_vf = Tok()
            mixT = SB(sc, "mixT", [128, 4, NT], BF16); t_mixT = Tok()
            oT = SB(sc, "oT", [64, 8, NT], BF16); t_oT = Tok()
            tmpa = SB(sc, "tmpa", [128, 512]); t_tmpa = Tok()
            rq = SB(sc, "rq", [128, 512]); t_rq = Tok()
            sqb = SB(sc, "sqb", [128, 512], BF16); t_sqb = Tok()
            if not sample:
                op("dve", [t_xhalo], [t_xin], lambda v: v.tensor_copy(out=xin[:, :, 0:3], in_=xhalo[:]))
                op("dve", [t_kvh], [t_kn], lambda v: v.tensor_copy(out=kn[:, 0:128], in_=khalo[:]))
                op("dve", [t_kvh], [t_vb], lambda v: v.tensor_copy(out=vb[:, 0, :], in_=vhalo[:]))
            chk('m_halo')
            with cx.fast():
                for c in range(8):
                    for (n0, nw) in nblocks(NT):
                        ps, tp = PS()
                        for k in range(8):
                            op("pe", T + [t_xn[k]], [tp], lambda p: p.matmul(ps[:, :nw], lhsT=wx[:, k, c * 128:(c + 1) * 128], rhs=xn[:, k, n0:n0 + nw], start=(k == 0), stop=(k == 7)))
                        if not sample:
                            op("act", [tp], [t_xin], lambda a: a.activation(out=xin[:, c, HAL + n0:HAL + n0 + nw], in_=ps[:, :nw], func=AF.Copy))
                        else:
                            op("act", [tp], [t_xin], lambda a: a.activation(out=xin[:, c, n0:n0 + nw], in_=ps[:, :nw], func=AF.Copy))
                chk('m_xbc')
                for qi in range(5):
                    for (n0, nw) in nblocks(NT):
                        ps, tp = PS()
                        for k in range(8):
                            lw = wq[:, k, qi * 128:(qi + 1) * 128] if qi < 4 else wk[:, k, :]
                            op("pe", T + [t_xn[k]], [tp], lambda p: p.matmul(ps[:, :nw], lhsT=lw, rhs=xn[:, k, n0:n0 + nw], start=(k == 0), stop=(k == 7)))
                        op("act", [tp], [t_sqb], lambda a: a.activation(out=sqb[:, :nw], in_=ps[:, :nw], func=AF.Square))
                        ps2, tp2 = PS()
                        op("pe", [t_sqb, t_c], [tp2], lambda p: p.matmul(ps2[:, :nw], lhsT=bones[:], rhs=sqb[:, :nw], start=True, stop=True))
                        op("act", [tp2], [t_rq], lambda a: a.activation(out=rq[:, :nw], in_=ps2[:, :nw], func=AF.Ln, bias=64.0 * EPS, scale=1.0))
                        op("act", [t_rq], [t_rq], lambda a: a.activation(out=rq[:, :nw], in_=rq[:, :nw], func=AF.Exp, scale=-0.5))
                        if qi < 4:
                            op("dve", [tp, t_rq, t_lay], [t_qn], lambda v: v.scalar_tensor_tensor(out=qn[:, qi, n0:n0 + nw], in0=ps[:, :nw], scalar=lay["gq"][:, 0:1], in1=rq[:, :nw], op0=ALU.mult, op1=ALU.mult))
                        else:
                            op("dve", [tp, t_rq, t_lay], [t_knf], lambda v: v.scalar_tensor_tensor(out=knf[:, n0:n0 + nw], in0=ps[:, :nw], scalar=lay["gk"][:, 0:1], in1=rq[:, :nw], op0=ALU.mult, op1=ALU.mult))
                            op("act", [t_knf], [t_kn], lambda a: a.activation(out=kn[:, KH + n0:KH + n0 + nw], in_=knf[:, n0:n0 + nw], func=AF.Copy))
            chk('m_qk')
            cacc = SB(sc, "cacc", [128, 512]); t_cacc = Tok()
            if not sample:
                for c in range(8):
                    for (n0, nw) in nblocks(NT):
                        op("dve", [t_xin, t_lay], [t_cacc], lambda v: v.tensor_scalar(out=cacc[:, :nw], in0=xin[:, c, n0:n0 + nw], scalar1=lay["cw"][:, c:c + 1], scalar2=None, op0=ALU.mult))
                        for j in range(1, 4):
                            op("dve", [t_xin, t_lay, t_cacc], [t_cacc], lambda v: v.scalar_tensor_tensor(out=cacc[:, :nw], in0=xin[:, c, n0 + j:n0 + j + nw], scalar=lay["cw"][:, j * 8 + c:j * 8 + c + 1], in1=cacc[:, :nw], op0=ALU.mult, op1=ALU.add))
                        op("act", [t_cacc, t_lay], [t_xc], lambda a: a.activation(out=xc[:, c, n0:n0 + nw], in_=cacc[:, :nw], func=AF.Silu, bias=lay["cb"][:, c:c + 1], scale=1.0))
                op("dve", [t_xin], [t_xhalo], lambda v: v.tensor_copy(out=xhalo[:], in_=xin[:, :, NT:NT + 3]))
                if last_group:
                    cst = SB(sc, "cst", [128, 8, 4]); t_cst = Tok()
                    op("dve", [t_xin], [t_cst], lambda v: v.tensor_copy(out=cst[:, :, 0:3], in_=xin[:, :, NT:NT + 3]))
                    ps, tp = PS(); ps2, tp2 = PS()
                    for c in range(8):
                        pp_, tpp = (ps, tp) if c < 4 else (ps2, tp2)
                        op("pe", [t_cst, t_c], [tpp], lambda p: p.transpose(out=pp_[:3, (c % 4) * 128:(c % 4 + 1) * 128], in_=cst[:, c, 0:3], identity=ident[:]))
                    cso = SB(sc, "cso", [4, 1024]); t_cso = Tok()
                    op("act", [tp], [t_cso], lambda a: a.activation(out=cso[:3, 0:512], in_=ps[:3, :], func=AF.Copy))
                    op("act", [tp2], [t_cso], lambda a: a.activation(out=cso[:3, 512:1024], in_=ps2[:3, :], func=AF.Copy))
                    dma("sp", [t_cso], [], oconv_p[l, :, :], cso[:3, :])
            else:
                xfull = SB(sc, "xfull", [128, 8, NSS, 7]); t_xf = Tok()
                scm = SB(sc, "scm", [64, 1024]); t_scmb = Tok()
                dma("sp", [], [t_scmb], scm[:NSS * 3, :], sconv[l, :, :])
                for c in range(8):
                    ps, tp = PS()
                    op("pe", [t_scmb, t_c], [tp], lambda p: p.transpose(out=ps[:, :NSS * 3], in_=scm[:NSS * 3, c * 128:(c + 1) * 128], identity=ident[:NSS * 3, :NSS * 3]))
                    op("act", [tp], [t_xf], lambda a: a.activation(out=xfull[:, c, :, 0:3], in_=ps[:, :NSS * 3].rearrange("p (b j) -> p b j", j=3), func=AF.Copy))
                    op("dve", [t_xin], [t_xf], lambda v: v.tensor_copy(out=xfull[:, c, :, 3:7], in_=xin[:, c, :].rearrange("p (b t) -> p b t", t=4)))
                    ca = cacc[:, :NS].rearrange("p (b t) -> p b t", t=4)
                    op("dve", [t_xf, t_lay], [t_cacc], lambda v: v.tensor_scalar(out=ca, in0=xfull[:, c, :, 0:4], scalar1=lay["cw"][:, c:c + 1], scalar2=None, op0=ALU.mult))
                    for j in range(1, 4):
                        op("dve", [t_xf, t_lay, t_cacc], [t_cacc], lambda v: v.scalar_tensor_tensor(out=ca, in0=xfull[:, c, :, j:j + 4], scalar=lay["cw"][:, j * 8 + c:j * 8 + c + 1], in1=ca, op0=ALU.mult, op1=ALU.add))
                    op("act", [t_cacc, t_lay], [t_xc], lambda a: a.activation(out=xc[:, c, :], in_=cacc[:, :NS], func=AF.Silu, bias=lay["cb"][:, c:c + 1], scale=1.0))
                cso = SB(sc, "cso", [64, 1024]); t_cso = Tok()
                cst = SB(sc, "cst", [128, 8, NSS * 3]); t_cst = Tok()
                op("dve", [t_xf], [t_cst], lambda v: v.tensor_copy(out=cst[:].rearrange("p c (b j) -> p c b j", j=3), in_=xfull[:, :, :, 4:7]))
                for a_ in range(2):
                    ps, tp = PS()
                    for j in range(4):
                        op("pe", [t_cst, t_c], [tp], lambda p: p.transpose(out=ps[:NSS * 3, j * 128:(j + 1) * 128], in_=cst[:, a_ * 4 + j, :], identity=ident[:]))
                    op("act", [tp], [t_cso], lambda a: a.activation(out=cso[:NSS * 3, a_ * 512:(a_ + 1) * 512], in_=ps[:NSS * 3, :], func=AF.Copy))
                dma("sp", [t_cso], [], oconv_s[l, :, :], cso[:NSS * 3, :])

            chk('m_conv')
            Ut = Ubd if sample else Um; SLt = SLbd if sample else SLm
            P = NTT
            st = {}
            for nm, shp, dt in [("dt", [128, 8], F32), ("dta", [128, 8], F32), ("t8", [128, 8], F32), ("ecum", [128, 8], F32), ("etot", [128, 8], F32),
                                ("dend", [128, 8], F32), ("w2s", [128, 8], F32), ("DL", [128, 8, 128], F32), ("LT", [128, 8, 128], F32),
                                ("GM", [128, 2, 128], F32), ("MT", [128, 8, 128], BF16), ("xdt", [128, 512], BF16), ("xdd", [128, 512], BF16),
                                ("Btm", [128, 256], BF16), ("y1", [128, 512], F32), ("sz", [128, 512], F32), ("ysq", [128, 512], F32), ("xsk", [128, 512], F32), ("xtok", [128, 512], F32),
                                ("ss2", [128, 2], F32), ("ytm", [128, 512], BF16), ("pe0", [128, 512], F32), ("pe1", [128, 512], F32),
                                ("PT0", [128, 512], BF16), ("PT1", [128, 512], BF16), ("den", [64, 512], F32)]:
                st[nm] = (SB(sc, "st_" + nm, shp, dt), Tok())
            if sample:
                Snat = SB(sc, "Snat", [128, 8, 4, 128]); t_Sn = Tok()
                Sb1 = [SB(sc, "Sb1_%d" % i, [128, 4, 128], BF16) for i in range(2)]; t_Sb1 = [Tok(), Tok()]
                STb1 = [SB(sc, "STb1_%d" % i, [128, 512], BF16) for i in range(2)]; t_ST1 = [Tok(), Tok()]
                CTm = SB(sc, "CTm", [128, NSS, 2, 64], BF16); t_CT = Tok()
                xdm = [SB(sc, "xdm%d" % i, [64, 512], BF16) for i in range(2)]; t_xdm = [Tok(), Tok()]
                dtaE = SB(sc, "dtaE", [64, 512]); t_dE = Tok()
                cdT = SB(sc, "cdT", [128, 4, 16]); t_cd = Tok()
                Snew = [SB(sc, "Snew%d" % i, [128, 4, 128]) for i in range(2)]; t_Snew = [Tok(), Tok()]
                kcb = SB(sc, "kcb", [128, NSS, 128], BF16); vcb = SB(sc, "vcb", [128, NSS, 128], BF16); t_kcb = Tok(); t_vcb = Tok()
                KcT = SB(sc, "KcT", [128, NSS, 128], BF16); t_KcT = Tok()
                PTc = SB(sc, "PTc", [128, 512], BF16); t_PTc = Tok()
                for b4 in range(0, NSS, 4):
                    dma("pool", [], [t_kcb], kcb[:, b4:b4 + 4, :], ck[l, b4:b4 + 4, :, :].rearrange("b i c -> i b c"))
                    dma("pool", [], [t_vcb], vcb[:, b4:b4 + 4, :], cv[l, b4:b4 + 4, :, :].rearrange("b i c -> i b c"))
                for b4 in range(0, NSS, 4):
                    dma("sp", [], [], ock_s[l, b4:b4 + 4, 0:124, :], ck[l, b4:b4 + 4, 4:128, :])
                    dma("sp", [], [], ocv_s[l, b4:b4 + 4, 0:124, :], cv[l, b4:b4 + 4, 4:128, :])

            A = lambda nm: st[nm][0]
            K_ = lambda nm: st[nm][1]
            PSH = lambda: PS(hold=not sample)

            def RELH(tok):
                if not sample:
                    REL(tok)

            def tile_ssd(ti):
                c0 = ti * NTT
                cs = slice(c0, c0 + P)
                first_tile = first_group and ti == 0 and not sample
                pz, tpz = PSH(); pdv, tpdv = PSH()
                with cx.fast():
                    for k in range(8):
                        op("pe", T + [t_xn[k]], [tpz], lambda p: p.matmul(pz[:P, :], lhsT=xn[:, k, cs], rhs=wz[:, k, :], start=(k == 0), stop=(k == 7)))
                    for k in range(8):
                        op("pe", T + [t_xn[k]], [tpdv], lambda p: p.matmul(pdv[:P, 0:128], lhsT=xn[:, k, cs], rhs=wv[:, k, :], start=(k == 0), stop=(k == 7)))
                    for k in range(8):
                        op("pe", T + [t_xn[k]], [tpdv], lambda p: p.matmul(pdv[:P, 128:136], lhsT=xn[:, k, cs], rhs=wdt[:, k, :], start=(k == 0), stop=(k == 7)))
                op("act", [tpz], [st["sz"][1]], lambda a: a.activation(out=st["sz"][0][:P, :], in_=pz[:P, :], func=AF.Silu))
                RELH(tpz)
                op("act", [tpdv], [t_vb], lambda a: a.activation(out=vb[:P, ti + 1, :], in_=pdv[:P, 0:128], func=AF.Copy))
                need_vf = sample or (last_group and ti == ntile - 1)
                if need_vf:
                    op("act", [tpdv], [t_vf], lambda a: a.activation(out=vf[:P, :], in_=pdv[:P, 0:128], func=AF.Copy))
                chk('t_zdv')
                op("dve", [tpdv, t_lay], [K_("t8")], lambda v: v.tensor_tensor(out=A("t8")[:P, :], in0=pdv[:P, 128:136], in1=lay["dtb"][:P, :], op=ALU.add))
                RELH(tpdv)
                yield
                op("act", [K_("t8")], [K_("t8")], lambda a: a.activation(out=A("t8")[:P, :], in_=A("t8")[:P, :], func=AF.Exp))
                op("act", [K_("t8")], [K_("dt")], lambda a: a.activation(out=A("dt")[:P, :], in_=A("t8")[:P, :], func=AF.Ln, bias=1.0, scale=1.0))
                op("dve", [K_("dt"), t_lay], [K_("dta")], lambda v: v.tensor_tensor(out=A("dta")[:P, :], in0=A("dt")[:P, :], in1=lay["aneg"][:P, :], op=ALU.mult))
                op("dve", [K_("dta"), t_c], [K_("DL")], lambda v: v.tensor_tensor(out=A("DL")[:P, :, :P], in0=SLt[:P, :P].unsqueeze(1).broadcast_to([P, 8, P]), in1=A("dta")[:P, :].unsqueeze(2).broadcast_to([P, 8, P]), op=ALU.mult))
                yield
                pD0, tD0 = PSH(); pD1, tD1 = PSH(); pc, tpc = PSH()
                chk('t_dt')
                for hd in range(8):
                    pd_, td_ = (pD0, tD0) if hd < 4 else (pD1, tD1)
                    op("pe", [K_("DL"), t_c], [td_], lambda p: p.matmul(pd_[:P, (hd % 4) * 128:(hd % 4) * 128 + P], lhsT=A("DL")[:P, hd, :P], rhs=Ut[:P, :P], start=True, stop=True))
                op("pe", [K_("dta"), t_c], [tpc], lambda p: p.matmul(pc[:P, 0:8], lhsT=Ut[:P, :P], rhs=A("dta")[:P, :], start=True, stop=True))
                if not sample:
                    op("pe", [K_("dta"), t_c], [tpc], lambda p: p.matmul(pc[:, 8:16], lhsT=onesf[:, :], rhs=A("dta")[:, :], start=True, stop=True))
                else:
                    pass
                for hf, (pd_, td_) in enumerate(((pD0, tD0), (pD1, tD1))):
                    op("act", [td_], [K_("LT")], lambda a: a.activation(out=A("LT")[:P, hf * 4:(hf + 1) * 4, :P], in_=pd_[:P, :].rearrange("p (h t) -> p h t", h=4)[:, :, :P], func=AF.Exp))
                chk('u_LT')
                op("act", [tpc], [K_("ecum")], lambda a: a.activation(out=A("ecum")[:P, :], in_=pc[:P, 0:8], func=AF.Exp))
                RELH(tD0); RELH(tD1)
                chk('u_ecum')
                chk('t_D')
                pG, tpG = PSH()
                for g in range(2):
                    op("pe", [t_xc], [tpG], lambda p: p.matmul(pG[:P, g * 128:g * 128 + P], lhsT=xc[:, 4 + g, cs], rhs=xc[:, 6 + g, cs], start=True, stop=True))
                chk('u_pG')
                op("dve", [tpG, t_c], [K_("GM")], lambda v: v.tensor_tensor(out=A("GM")[:P, :, :P], in0=pG[:P, 0:256].rearrange("p (g t) -> p g t", g=2)[:, :, :P], in1=Ut[:P, :P].unsqueeze(1).broadcast_to([P, 2, P]), op=ALU.mult))
                RELH(tpG)
                chk('u_GM')
                for g in range(2):
                    op("dve", [K_("GM"), K_("LT")], [K_("MT")], lambda v: v.tensor_tensor(out=A("MT")[:P, g * 4:(g + 1) * 4, :P], in0=A("LT")[:P, g * 4:(g + 1) * 4, :P], in1=A("GM")[:P, g, :P].unsqueeze(1).broadcast_to([P, 4, P]), op=ALU.mult))
                chk('t_G')
                px, tpx = PSH(); pxb = px[:].bitcast(BF16)
                with cx.fast():
                    for j in range(4):
                        op("pe", [t_xc, t_c], [tpx], lambda p: p.transpose(out=pxb[:P, j * 128:(j + 1) * 128], in_=xc[:, j, cs], identity=identb[:]))
                    for g in range(2):
                        op("pe", [t_xc, t_c], [tpx], lambda p: p.transpose(out=pxb[:P, 512 + g * 128:512 + (g + 1) * 128], in_=xc[:, 4 + g, cs], identity=identb[:]))
                    chk('v_tr')
                op("act", [tpx], [K_("Btm")], lambda a: a.activation(out=A("Btm")[:P, :], in_=pxb[:P, 512:768], func=AF.Copy))
                op("act", [tpx], [K_("xtok")], lambda a: a.activation(out=A("xtok")[:P, :], in_=pxb[:P, 0:512], func=AF.Copy))
                RELH(tpx)
                chk('v_Btm')
                op("dve", [K_("xtok"), K_("dt")], [K_("xdt")], lambda v: v.tensor_tensor(out=A("xdt")[:P, :].rearrange("p (h d) -> p h d", h=8), in0=A("xtok")[:P, :].rearrange("p (h d) -> p h d", h=8), in1=A("dt")[:P, :].unsqueeze(2).broadcast_to([P, 8, 64]), op=ALU.mult))
                chk('v_xdt')
                op("dve", [K_("xtok"), t_lay], [K_("xsk")], lambda v: v.tensor_tensor(out=A("xsk")[:P, :].rearrange("p (h d) -> p h d", h=8), in0=A("xtok")[:P, :].rearrange("p (h d) -> p h d", h=8), in1=lay["dsk"][:P, :].unsqueeze(2).broadcast_to([P, 8, 64]), op=ALU.mult))
                yield
                chk('t_tr')
                py, tpy = PS(hold=True)
                with cx.fast():
                    for hd in range(8):
                        op("pe", [K_("MT"), K_("xdt")], [tpy], lambda p: p.matmul(py[:P, hd * 64:(hd + 1) * 64], lhsT=A("MT")[:P, hd, :P], rhs=A("xdt")[:P, hd * 64:(hd + 1) * 64], start=True, stop=True))
                pyo, tpyo = PS(hold=True)
                if sample:
                    pyo2, tpyo2 = PS(hold=True)
                if not sample:
                    op("act", [tpc], [K_("w2s")], lambda a: a.activation(out=A("w2s")[:, :], in_=pc[:, 0:8], func=AF.Copy))
                    op("dve", [tpc, K_("w2s")], [K_("t8")], lambda v: v.tensor_tensor(out=A("t8")[:, :], in0=pc[:, 8:16], in1=A("w2s")[:, :], op=ALU.subtract))
                    op("act", [K_("t8")], [K_("dend")], lambda a: a.activation(out=A("dend")[:, :], in_=A("t8")[:, :], func=AF.Exp))
                    op("act", [tpc], [K_("etot")], lambda a: a.activation(out=A("etot")[:, :], in_=pc[:, 8:16], func=AF.Exp))
                    RELH(tpc)
                    op("dve", [K_("dend"), K_("dt")], [K_("w2s")], lambda v: v.tensor_tensor(out=A("w2s")[:, :], in0=A("dend")[:, :], in1=A("dt")[:, :], op=ALU.mult))
                    op("dve", [K_("xtok"), K_("w2s")], [K_("xdd")], lambda v: v.tensor_tensor(out=A("xdd")[:, :].rearrange("p (h d) -> p h d", h=8), in0=A("xtok")[:, :].rearrange("p (h d) -> p h d", h=8), in1=A("w2s")[:, :].unsqueeze(2).broadcast_to([128, 8, 64]), op=ALU.mult))
                    for g in range(2):
                        op("pe", [t_xc, t_hTb], [tpyo], lambda p: p.matmul(pyo[:, g * 256:(g + 1) * 256], lhsT=xc[:, 6 + g, cs], rhs=hTb[:, g * 256:(g + 1) * 256], start=True, stop=True))
                    pst, tpst = PSH()
                    for g in range(2):
                        op("pe", [K_("Btm"), K_("xdd")], [tpst], lambda p: p.matmul(pst[:, g * 256:(g + 1) * 256], lhsT=A("Btm")[:, g * 128:(g + 1) * 128], rhs=A("xdd")[:, g * 256:(g + 1) * 256], start=True, stop=True))
                    op("dve", [t_hT, K_("etot")], [t_hT], lambda v: v.tensor_tensor(out=hT[:].rearrange("p (h d) -> p h d", h=8), in0=hT[:].rearrange("p (h d) -> p h d", h=8), in1=A("etot")[:, :].unsqueeze(2).broadcast_to([128, 8, 64]), op=ALU.mult))
                    op("dve", [t_hT, tpst], [t_hT], lambda v: v.tensor_tensor(out=hT[:], in0=hT[:], in1=pst[:, :], op=ALU.add))
                    RELH(tpst)
                    op("act", [t_hT], [t_hTb], lambda a: a.activation(out=hTb[:], in_=hT[:], func=AF.Copy))
                else:
                    op("pe", [K_("dta"), t_c], [tpc], lambda p: p.matmul(pc[:P, 8:16], lhsT=Ubd[:P, :P], rhs=A("dta")[:P, :], start=True, stop=False))
                    op("pe", [K_("dta"), t_c], [tpc], lambda p: p.matmul(pc[:P, 8:16], lhsT=SLbd[:P, :P], rhs=A("dta")[:P, :], start=False, stop=True))
                    op("act", [tpc], [K_("w2s")], lambda a: a.activation(out=A("w2s")[:P, :], in_=pc[:P, 0:8], func=AF.Copy))
                    op("dve", [tpc, K_("w2s")], [K_("t8")], lambda v: v.tensor_tensor(out=A("t8")[:P, :], in0=pc[:P, 8:16], in1=A("w2s")[:P, :], op=ALU.subtract))
                    op("act", [K_("t8")], [K_("dend")], lambda a: a.activation(out=A("dend")[:P, :], in_=A("t8")[:P, :], func=AF.Exp))
                    op("dve", [K_("dend"), K_("dt")], [K_("w2s")], lambda v: v.tensor_tensor(out=A("w2s")[:P, :], in0=A("dend")[:P, :], in1=A("dt")[:P, :], op=ALU.mult))
                    op("dve", [K_("xtok"), K_("w2s")], [K_("xdd")], lambda v: v.tensor_tensor(out=A("xdd")[:P, :].rearrange("p (h d) -> p h d", h=8), in0=A("xtok")[:P, :].rearrange("p (h d) -> p h d", h=8), in1=A("w2s")[:P, :].unsqueeze(2).broadcast_to([P, 8, 64]), op=ALU.mult))
                    for g in range(2):
                        op("dve", [t_xc, t_c], [t_CT], lambda v: v.tensor_tensor(out=CTm[:, :, g, :], in0=xc[:, 6 + g, :].unsqueeze(1).broadcast_to([128, NSS, 64]), in1=BMb[:].rearrange("p (b t) -> p b t", b=16)[:, :NSS, :], op=ALU.mult))
                    op("dve", [K_("dta")], [t_dE], lambda v: v.tensor_copy(out=dtaE[:].rearrange("p (h d) -> p h d", h=8), in_=A("dta")[:P, :].unsqueeze(2).broadcast_to([P, 8, 64])))
                    pcd, tpcd = PS()
                    for j in range(4):
                        op("pe", [t_dE, t_c], [tpcd], lambda p: p.matmul(pcd[:, j * 16:(j + 1) * 16], lhsT=dtaE[:, j * 128:(j + 1) * 128], rhs=BMt[:, :], start=True, stop=True))
                    op("act", [tpcd], [t_cd], lambda a: a.activation(out=cdT[:].rearrange("p j b -> p (j b)"), in_=pcd[:, 0:64], func=AF.Exp))
                    for b in range(NSS):
                        bl = b % 8
                        if bl == 0:
                            for b8 in range(8):
                                dma("sp", [], [t_Sn], Snat[:, b8, :, :], sssm[l, b + b8, :, :].rearrange("(j p) n -> p j n", p=128))
                        sb1, tsb1 = Sb1[b % 2], t_Sb1[b % 2]
                        stb, tstb = STb1[b % 2], t_ST1[b % 2]
                        op("act", [t_Sn], [tsb1], lambda a: a.activation(out=sb1[:], in_=Snat[:, bl, :, :], func=AF.Copy))
                        pt_, tpt = PS(); ptb = pt_[:].bitcast(BF16)
                        for j in range(4):
                            op("pe", [tsb1, t_c], [tpt], lambda p: p.transpose(out=ptb[:, j * 128:(j + 1) * 128], in_=sb1[:, j, :], identity=identb[:]))
                        op("act", [tpt], [tstb], lambda a: a.activation(out=stb[:, :], in_=ptb[:, 0:512], func=AF.Copy))
                        for g in range(2):
                            pq_, tq_ = (pyo, tpyo) if g == 0 else (pyo2, tpyo2)
                            op("pe", [t_CT, tstb], [tq_], lambda p: p.matmul(pq_[:P, 0:256], lhsT=CTm[:, b, g, :], rhs=stb[:, g * 256:(g + 1) * 256], start=(b == 0), stop=(b == NSS - 1)))
                        xm, txm = xdm[b % 2], t_xdm[b % 2]
                        op("dve", [K_("xdd"), t_c], [txm], lambda v: v.tensor_scalar(out=xm[:, :], in0=A("xdd")[:P, :], scalar1=BMt[:, b:b + 1], scalar2=None, op0=ALU.mult))
                        pst, tpst = PS()
                        for j in range(4):
                            op("pe", [txm, K_("Btm")], [tpst], lambda p: p.matmul(pst[:, j * 128:(j + 1) * 128], lhsT=xm[:, j * 128:(j + 1) * 128], rhs=A("Btm")[:P, (j // 2) * 128:(j // 2 + 1) * 128], start=True, stop=True))
                        sn, tsn = Snew[b % 2], t_Snew[b % 2]
                        for j in range(4):
                            op("dve", [t_Sn, t_cd, tpst], [tsn], lambda v: v.scalar_tensor_tensor(out=sn[:, j, :], in0=Snat[:, bl, j, :], scalar=cdT[:, j, b:b + 1], in1=pst[:, j * 128:(j + 1) * 128], op0=ALU.mult, op1=ALU.add))
                        dma("sp", [tsn], [], ossm_s[l, b, :, :].rearrange("(j p) n -> p j n", p=128), sn[:])
                yield
                chk('t_ssd')
                if not sample:
                    op("dve", [tpyo, K_("ecum")], [K_("y1")], lambda v: v.tensor_tensor(out=A("y1")[:P, :].rearrange("p (h d) -> p h d", h=8), in0=pyo[:P, :].rearrange("p (h d) -> p h d", h=8), in1=A("ecum")[:P, :].unsqueeze(2).broadcast_to([P, 8, 64]), op=ALU.mult))
                else:
                    for g, (pq_, tq_) in enumerate(((pyo, tpyo), (pyo2, tpyo2))):
                        op("dve", [tq_, K_("ecum")], [K_("y1")], lambda v: v.tensor_tensor(out=A("y1")[:P, g * 256:(g + 1) * 256].rearrange("p (h d) -> p h d", h=4), in0=pq_[:P, 0:256].rearrange("p (h d) -> p h d", h=4), in1=A("ecum")[:P, g * 4:(g + 1) * 4].unsqueeze(2).broadcast_to([P, 4, 64]), op=ALU.mult))
                op("dve", [K_("y1"), tpy], [K_("y1")], lambda v: v.tensor_tensor(out=A("y1")[:P, :], in0=A("y1")[:P, :], in1=py[:P, :], op=ALU.add))
                op("dve", [K_("y1"), K_("xsk")], [K_("y1")], lambda v: v.tensor_tensor(out=A("y1")[:P, :], in0=A("y1")[:P, :], in1=A("xsk")[:P, :], op=ALU.add))
                REL(tpy); REL(tpyo)
                if sample:
                    REL(tpyo2)
                op("dve", [K_("y1"), K_("sz")], [K_("y1")], lambda v: v.tensor_tensor(out=A("y1")[:P, :], in0=A("y1")[:P, :], in1=A("sz")[:P, :], op=ALU.mult))
                op("dve", [K_("y1")], [K_("ysq")], lambda v: v.tensor_tensor(out=A("ysq")[:P, :], in0=A("y1")[:P, :], in1=A("y1")[:P, :], op=ALU.mult))
                op("dve", [K_("ysq")], [K_("ss2")], lambda v: v.reduce_sum(out=A("ss2")[:P, :], in_=A("ysq")[:P, :].rearrange("p (g d) -> p g d", g=2), axis=AX.X))
                op("act", [K_("ss2")], [K_("ss2")], lambda a: a.activation(out=A("ss2")[:P, :], in_=A("ss2")[:P, :], func=AF.Ln, bias=256.0 * EPS, scale=1.0))
                op("act", [K_("ss2")], [K_("ss2")], lambda a: a.activation(out=A("ss2")[:P, :], in_=A("ss2")[:P, :], func=AF.Exp, scale=-0.5))
                op("dve", [K_("y1"), K_("ss2")], [K_("y1")], lambda v: v.tensor_tensor(out=A("y1")[:P, :].rearrange("p (g d) -> p g d", g=2), in0=A("y1")[:P, :].rearrange("p (g d) -> p g d", g=2), in1=A("ss2")[:P, :].unsqueeze(2).broadcast_to([P, 2, 256]), op=ALU.mult))
                op("dve", [K_("y1"), t_lay], [K_("ytm")], lambda v: v.scalar_tensor_tensor(out=A("ytm")[:P, :], in0=A("y1")[:P, :], scalar=16.0, in1=gssm[:P, :], op0=ALU.mult, op1=ALU.mult))
                yield
                pyt, tpyt = PSH(); pytb = pyt[:].bitcast(BF16)
                for j in range(4):
                    op("pe", [K_("ytm"), t_c], [tpyt], lambda p: p.transpose(out=pytb[:, j * 128:j * 128 + P], in_=A("ytm")[:P, j * 128:(j + 1) * 128], identity=identb[:P, :P]))
                op("act", [tpyt], [t_mixT], lambda a: a.activation(out=mixT[:, :, cs], in_=pytb[:, 0:512].rearrange("p (j t) -> p j t", j=4)[:, :, :P], func=AF.Copy))
                RELH(tpyt)

            def tile_attn(ti):
                c0 = ti * NTT
                cs = slice(c0, c0 + P)
                first_tile = first_group and ti == 0 and not sample
                chk('t_y')
                if not sample:
                    for g in range(2):
                        bs = slice(64 * g, 64 * g + 64)
                        kbs = [1] if first_tile else [0, 1]
                        for kb in kbs:
                            kcols = slice(c0 + kb * 128, c0 + kb * 128 + 128)
                            pS, tpS = PSH()
                            for hh in range(4):
                                op("pe", [t_kn, t_qn], [tpS], lambda p: p.matmul(pS[:, hh * 128:(hh + 1) * 128], lhsT=kn[bs, kcols], rhs=qn[bs, hh, cs], start=True, stop=True))
                            pe_, tpe = st["pe%d" % kb]; PT_, tPT = st["PT%d" % kb]
                            op("act", [tpS], [tpe], lambda a: a.activation(out=pe_[:], in_=pS[:], func=AF.Exp, scale=0.125))
                            RELH(tpS)
                            Et = Eprev if kb == 0 else Ecur
                            op("dve", [tpe, t_c], [tPT], lambda v: v.tensor_tensor(out=PT_[:], in0=pe_[:], in1=Et[:, g * 512:(g + 1) * 512], op=ALU.mult))
                        chk('a_S')
                        yield
                        po, tpo = PSH(); pdn, tpdn = PSH()
                        for ii, kb in enumerate(kbs):
                            PT_, tPT = st["PT%d" % kb]
                            op("pe", [t_vb, tPT], [tpo], lambda p: p.matmul(po[:64, :], lhsT=vb[:, ti + kb, g * 64:(g + 1) * 64], rhs=PT_[:], start=(ii == 0), stop=(ii == len(kbs) - 1)))
                        chk('a_po')
                        for ii, kb in enumerate(kbs):
                            PT_, tPT = st["PT%d" % kb]
                            op("pe", [t_c, tPT], [tpdn], lambda p: p.matmul(pdn[:64, :], lhsT=onesb[:, 0:64], rhs=PT_[:], start=(ii == 0), stop=(ii == len(kbs) - 1)))
                        chk('a_pdn')
                        op("dve", [tpdn, t_lay], [K_("den")], lambda v: v.tensor_tensor(out=A("den")[:, :], in0=pdn[:64, :], in1=esinkrow[:, g * 512:(g + 1) * 512], op=ALU.add))
                        RELH(tpdn)
                        op("act", [K_("den")], [K_("den")], lambda a: a.activation(out=A("den")[:, :], in_=A("den")[:, :], func=AF.Ln))
                        op("act", [K_("den")], [K_("den")], lambda a: a.activation(out=A("den")[:, :], in_=A("den")[:, :], func=AF.Exp, scale=-1.0))
                        op("dve", [tpo, K_("den")], [t_oT], lambda v: v.tensor_tensor(out=oT[:, g * 4:(g + 1) * 4, cs], in0=po[:64, :].rearrange("p (h q) -> p h q", h=4), in1=A("den")[:, :].rearrange("p (h q) -> p h q", h=4), op=ALU.mult))
                        RELH(tpo)
                        yield
                else:
                    for b4 in range(0, NSS, 4):
                        pt_, tpt = PS(); ptb = pt_[:].bitcast(BF16)
                        for bb in range(4):
                            op("pe", [t_kcb, t_c], [tpt], lambda p: p.transpose(out=ptb[:, bb * 128:(bb + 1) * 128], in_=kcb[:, b4 + bb, :], identity=identb[:]))
                        op("act", [tpt], [t_KcT], lambda a: a.activation(out=KcT[:, b4:b4 + 4, :], in_=ptb[:, 0:512].rearrange("p (b i) -> p b i", b=4), func=AF.Copy))
                    pSc, tpSc = PS()
                    for b in range(NSS):
                        for g in range(2):
                            bs = slice(64 * g, 64 * g + 64)
                            for hh in range(4):
                                idx = (b * 8 + g * 4 + hh) * 4
                                op("pe", [t_KcT, t_qn], [tpSc], lambda p: p.matmul(pSc[:, idx:idx + 4], lhsT=KcT[bs, b, :], rhs=qn[bs, hh, b * 4:b * 4 + 4], start=True, stop=True))
                    pe_, tpe = st["pe0"]
                    op("act", [tpSc], [tpe], lambda a: a.activation(out=pe_[:, :NSS * 32], in_=pSc[:, :NSS * 32], func=AF.Exp, scale=0.125))
                    op("dve", [tpe, t_c], [t_PTc], lambda v: v.tensor_tensor(out=PTc[:, :NSS * 32].rearrange("p (b x) -> p b x", b=NSS), in0=pe_[:, :NSS * 32].rearrange("p (b x) -> p b x", b=NSS), in1=Ecache[:, :].unsqueeze(1).broadcast_to([128, NSS, 32]), op=ALU.mult))
                    pSn, tpSn = PS()
                    for g in range(2):
                        bs = slice(64 * g, 64 * g + 64)
                        for hh in range(4):
                            op("pe", [t_kn, t_qn], [tpSn], lambda p: p.matmul(pSn[:P, (g * 4 + hh) * 64:(g * 4 + hh) * 64 + P], lhsT=kn[bs, 0:P], rhs=qn[bs, hh, 0:P], start=True, stop=True))
                    pe1_, tpe1 = st["pe1"]; PT1_, tPT1 = st["PT1"]
                    op("act", [tpSn], [tpe1], lambda a: a.activation(out=pe1_[:P, :], in_=pSn[:P, :], func=AF.Exp, scale=0.125))
                    for g in range(2):
                        ov = PT1_[:P, g * 256:(g + 1) * 256].rearrange("p (b hh t) -> p hh b t", b=16, hh=4)
                        i0 = pe1_[:P, g * 256:(g + 1) * 256].rearrange("p (hh b t) -> p hh b t", hh=4, b=16)
                        i1 = Enew[:P, g * 256:(g + 1) * 256].rearrange("p (hh b t) -> p hh b t", hh=4, b=16)
                        op("dve", [tpe1, t_c], [tPT1], lambda v: v.tensor_tensor(out=ov, in0=i0, in1=i1, op=ALU.mult))
                    po, tpo = PS(); pdn, tpdn = PS()
                    for (pp_, tpp, use_v) in ((po, tpo, True), (pdn, tpdn, False)):
                        for g in range(2):
                            lw = vb[:P, 1, g * 64:(g + 1) * 64] if use_v else onesb[:P, 0:64]
                            op("pe", [t_vb, tPT1, t_c], [tpp], lambda p: p.matmul(pp_[:64, g * 256:(g + 1) * 256], lhsT=lw, rhs=PT1_[:P, g * 256:(g + 1) * 256], start=True, stop=False))
                            for b in range(NSS):
                                lw2 = vcb[:, b, g * 64:(g + 1) * 64] if use_v else onesb[:, 0:64]
                                op("pe", [t_vcb, t_PTc, t_c], [tpp], lambda p: p.matmul(pp_[:64, g * 256 + b * 16:g * 256 + b * 16 + 16], lhsT=lw2, rhs=PTc[:, b * 32 + g * 16:b * 32 + g * 16 + 16], start=False, stop=(b == NSS - 1)))
                    for g in range(2):
                        dv = A("den")[:, g * 256:(g + 1) * 256].rearrange("p (b hh t) -> p hh b t", b=16, hh=4)
                        ek = esinkrow[:, :].rearrange("p (h q) -> p h q", h=8)[:, g * 4:(g + 1) * 4, 0:64].rearrange("p hh (b t) -> p hh b t", t=4)
                        op("dve", [tpdn, t_lay], [K_("den")], lambda v: v.tensor_tensor(out=dv, in0=pdn[:64, g * 256:(g + 1) * 256].rearrange("p (b hh t) -> p hh b t", b=16, hh=4), in1=ek, op=ALU.add))
                        op("dve", [K_("den")], [K_("den")], lambda v: v.reciprocal(out=A("den")[:, g * 256:(g + 1) * 256], in_=A("den")[:, g * 256:(g + 1) * 256]))
                        op("dve", [tpo, K_("den")], [t_oT], lambda v: v.tensor_tensor(out=oT[:, g * 4:(g + 1) * 4, 0:64].rearrange("p hh (b t) -> p hh b t", t=4), in0=po[:64, g * 256:(g + 1) * 256].rearrange("p (b hh t) -> p hh b t", b=16, hh=4), in1=dv, op=ALU.mult))

                yield

            for ti in range(ntile):
                if sample:
                    for _ in tile_ssd(ti):
                        pass
                    for _ in tile_attn(ti):
                        pass
                else:
                    gens = [tile_ssd(ti), tile_attn(ti)]
                    while gens:
                        for g_ in list(gens):
                            try:
                                next(g_)
                            except StopIteration:
                                gens.remove(g_)
            chk('m_tiles')
            if sample or last_group:
                ktm = SB(sc, "ktm", [128, 128]); t_ktm = Tok()
                pk, tpk = PS()
                lastc = slice(NT - P, NT)
                op("pe", [t_knf, t_c], [tpk], lambda p: p.transpose(out=pk[:P, 0:128], in_=knf[:, lastc], identity=ident[:]))
                op("act", [tpk], [t_ktm], lambda a: a.activation(out=ktm[:P, :], in_=pk[:P, 0:128], func=AF.Copy))
                if sample:
                    for b in range(NSS):
                        dma("sp", [t_ktm], [], ock_s[l, b, 124:128, :], ktm[b * 4:(b + 1) * 4, :])
                        dma("sp", [t_vf], [], ocv_s[l, b, 124:128, :], vf[b * 4:(b + 1) * 4, :])
                else:
                    dma("sp", [t_ktm], [], ock_p[l, :, :], ktm[:, :])
                    dma("sp", [t_vf], [], ocv_p[l, :, :], vf[:, :])
                    hTo = SB(sc, "hTo", [128, 4, 128]); t_hTo = Tok()
                    ph, tph = PS()
                    for j in range(4):
                        op("pe", [t_hT, t_c], [tph], lambda p: p.transpose(out=ph[:, j * 128:(j + 1) * 128], in_=hT[:, j * 128:(j + 1) * 128], identity=ident[:]))
                    op("act", [tph], [t_hTo], lambda a: a.activation(out=hTo[:].rearrange("p j n -> p (j n)"), in_=ph[:, :], func=AF.Copy))
                    dma("sp", [t_hTo], [], ossm_p[l, :, :].rearrange("(j p) n -> p j n", p=128), hTo[:])
            if not sample:
                op("dve", [t_kn], [t_kvh], lambda v: v.tensor_copy(out=khalo[:], in_=kn[:, NT:NT + 128]))
                op("dve", [t_vb], [t_kvh], lambda v: v.tensor_copy(out=vhalo[:], in_=vb[:, ntile, :]))

            chk('m_outs')
            woT = wreg[:, 0:4096].rearrange("p (k c) -> p k c", k=4)
            woA = wreg[:64, 4096:4096 + 8192].rearrange("p (k c) -> p k c", k=8)
            if cached:
                dma("pool", [t_scm[1]], T, wreg[:, 0:4096], wscm[1, :, 0:4096])
                dma("pool", [t_scm[1]], T, wreg[:64, 4096:12288], wscm[1, :64, 4096:12288])
            else:
                dma("pool", [], T, woT, w_out[l, 0:512, :].rearrange("(k p) c -> p k c", p=128))
                dma("pool", [], T, woA, w_out[l, 512:1024, :].rearrange("(hd p) c -> p hd c", p=64))
                dma("sp", T, [t_scm[1]], wscm[1, :, 0:4096], wreg[:, 0:4096])
                dma("sp", T, [t_scm[1]], wscm[1, :64, 4096:12288], wreg[:64, 4096:12288])
            with cx.fast():
                for d in range(8):
                    for (n0, nw) in nblocks(NT):
                        ps, tp = PS()
                        for k in range(4):
                            op("pe", T + [t_mixT], [tp], lambda p: p.matmul(ps[:, :nw], lhsT=woT[:, k, d * 128:(d + 1) * 128], rhs=mixT[:, k, n0:n0 + nw], start=(k == 0), stop=False))
                        for hd in range(8):
                            op("pe", T + [t_oT], [tp], lambda p: p.matmul(ps[:, :nw], lhsT=woA[:, hd, d * 128:(d + 1) * 128], rhs=oT[:, hd, n0:n0 + nw], start=False, stop=(hd == 7)))
                        op("dve", [tp, t_h], [t_h], lambda v: v.tensor_tensor(out=h[:, d, n0:n0 + nw], in0=h[:, d, n0:n0 + nw], in1=ps[:, :nw], op=ALU.add))

        def ple(sc, h, t_h, NT, l, psrc, ntok_tile, cached):
            xn = SB(sc, "xn", [128, 8, NT], BF16); t_xn = [Tok() for _ in range(8)]
            norm(sc, h, t_h, NT, 3, l, xn, t_xn)
            wg = wreg[:, 0:8192].rearrange("p (k c) -> p k c", k=8)
            wp = wreg[:, 8192:8192 + 2048].rearrange("p (k c) -> p k c", k=2)
            T = [t_wreg]
            if cached:
                dma("pool", [t_scm[2]], T, wreg[:, 0:10240], wscm[2, :, 0:10240])
            else:
                dma("pool", [], T, wg, w_pg[l, :, :].rearrange("(k p) c -> p k c", p=128))
                dma("pool", [], T, wp, w_pp[l, :, :].rearrange("(k p) c -> p k c", p=128))
                dma("sp", T, [t_scm[2]], wscm[2, :, 0:10240], wreg[:, 0:10240])
            peT = SB(sc, "peT", [128, 2, NT], BF16); t_peT = Tok()
            ptm = [SB(sc, "ptm%d" % i, [128, 256]) for i in range(2)]; t_ptm = [Tok(), Tok()]
            P = ntok_tile
            for ti in range(NT // P):
                pt_, tpt_ = ptm[ti % 2], t_ptm[ti % 2]
                dma("sp", [], [tpt_], pt_[:P, :], psrc[ti * P:(ti + 1) * P, :])
                ps, tp = PS()
                for j in range(2):
                    op("pe", [tpt_, t_c], [tp], lambda p: p.transpose(out=ps[:, j * 128:j * 128 + P], in_=pt_[:P, j * 128:(j + 1) * 128], identity=ident[:P, :P]))
                op("act", [tp], [t_peT], lambda a: a.activation(out=peT[:, :, ti * P:(ti + 1) * P], in_=ps[:, 0:256].rearrange("p (j t) -> p j t", j=2)[:, :, :P], func=AF.Copy))
            with cx.fast():
                sg = SB(sc, "sg", [128, 512]); t_sg = Tok()
                for d in range(8):
                    for (n0, nw) in nblocks(NT):
                        pg, tpg = PS(); pq, tpq = PS()
                        for k in range(8):
                            op("pe", T + [t_xn[k]], [tpg], lambda p: p.matmul(pg[:, :nw], lhsT=wg[:, k, d * 128:(d + 1) * 128], rhs=xn[:, k, n0:n0 + nw], start=(k == 0), stop=(k == 7)))
                        for k in range(2):
                            op("pe", T + [t_peT], [tpq], lambda p: p.matmul(pq[:, :nw], lhsT=wp[:, k, d * 128:(d + 1) * 128], rhs=peT[:, k, n0:n0 + nw], start=(k == 0), stop=(k == 1)))
                        op("act", [tpg], [t_sg], lambda a: a.activation(out=sg[:, :nw], in_=pg[:, :nw], func=AF.Sigmoid))
                        op("dve", [t_sg, tpq], [t_sg], lambda v: v.tensor_tensor(out=sg[:, :nw], in0=sg[:, :nw], in1=pq[:, :nw], op=ALU.mult))
                        op("dve", [t_sg, t_h], [t_h], lambda v: v.tensor_tensor(out=h[:, d, n0:n0 + nw], in0=h[:, d, n0:n0 + nw], in1=sg[:, :nw], op=ALU.add))

        stage = [0]

        def chk(name):
            stage[0] += 1
            if STAGE_LIMIT is not None and stage[0] >= STAGE_LIMIT:
                if not cx.stopped:
                    print("STOP at stage", stage[0], name)
                cx.stopped = True

        try:
          _main_body = True
          with contextlib.ExitStack() as sc:
              xtm0 = SB(sc, "xtm", [128, 1024])
              transpose_in(xtm0, Tok(), xs[:, :], NS, hs, t_hs, 0)
              cx.barrier()
          chk('sample_load')
          for l in range(DEPTH):
              load_layer_consts(l)
              chk('layer_consts')
              op("dve", [], [t_hT], lambda v: v.memset(hT[:], 0.0))
              op("dve", [], [t_hTb], lambda v: v.memset(hTb[:], 0.0))
              op("dve", [], [t_xhalo], lambda v: v.memset(xhalo[:], 0.0))
              op("dve", [], [t_kvh], lambda v: v.memset(khalo[:], 0.0))
              op("dve", [], [t_kvh], lambda v: v.memset(vhalo[:], 0.0))
              for gi in range(NG + 1):
                  sample = gi == NG
                  NT = NS if sample else NTG
                  with contextlib.ExitStack() as gsc:
                      if sample:
                          h, t_h = hs, t_hs
                      else:
                          h = SB(gsc, "hgrp", [128, 8, NTG]); t_h = Tok()
                          if l == 0:
                              with contextlib.ExitStack() as sc:
                                  xtms = [(SB(sc, "xtm", [128, 1024]), Tok()) for _ in range(2)]
                                  for ti in range(NTG // 128):
                                      transpose_in(xtms[ti % 2][0], xtms[ti % 2][1], xp[gi * NTG + ti * 128:gi * NTG + (ti + 1) * 128, :], 128, h, t_h, ti * 128)
                                  cx.barrier()
                          else:
                              dma("sp", [t_hscr[gi]], [t_h], h[:], hscr[:, :, gi * NTG:(gi + 1) * NTG].rearrange("j p t -> p j t"))
                      cx.barrier()
                      chk('group_load')
                      with contextlib.ExitStack() as sc:
                          xn = SB(sc, "xn", [128, 8, NT], BF16); t_xn = [Tok() for _ in range(8)]
                          with cx.fast():
                              norm(sc, h, t_h, NT, 0, l, xn, t_xn)
                              chk('norm')
                              ffn(sc, h, t_h, NT, xn, t_xn, w1a, w3a, w2a, l, 0, gi > 0)
                          cx.barrier()
                          chk('ffn_a')
                      with contextlib.ExitStack() as sc:
                          mixer(sc, h, t_h, NT, l, sample, gi == 0, gi == NG - 1, gi > 0)
                          cx.barrier()
                          chk('mixer')
                      with contextlib.ExitStack() as sc:
                          xn = SB(sc, "xn", [128, 8, NT], BF16); t_xn = [Tok() for _ in range(8)]
                          with cx.fast():
                              norm(sc, h, t_h, NT, 2, l, xn, t_xn)
                              ffn(sc, h, t_h, NT, xn, t_xn, w1b, w3b, w2b, l, 1, gi > 0)
                          cx.barrier()
                          chk('ffn_b')
                      with contextlib.ExitStack() as sc:
                          if sample:
                              ple(sc, h, t_h, NT, l, psm[l, :, :], 64, gi > 0)
                          else:
                              ple(sc, h, t_h, NT, l, pp[l, gi * NTG:(gi + 1) * NTG, :], 128, gi > 0)
                          cx.barrier()
                      if not sample:
                          if l == 0:
                              dma("sp", [t_h], [t_hscr[gi]], hscr[:, :, gi * NTG:(gi + 1) * NTG].rearrange("j p t -> p j t"), h[:])
                          else:
                              with contextlib.ExitStack() as sc:
                                  ytms = [(SB(sc, "ytm", [128, 1024]), Tok()) for _ in range(2)]
                                  for ti in range(NTG // 128):
                                      transpose_out(ytms[ti % 2][0], ytms[ti % 2][1], h, t_h, ti * 128, 128, yp[gi * NTG + ti * 128:gi * NTG + (ti + 1) * 128, :])
                                  cx.barrier()
                      elif l == DEPTH - 1:
                          with contextlib.ExitStack() as sc:
                              ytm0 = SB(sc, "ytm", [128, 1024])
                              transpose_out(ytm0, Tok(), h, t_h, 0, NS, ys[:, :])
                              cx.barrier()
                      cx.barrier()
        except _Stop:
            pass
        cx.finish()
    return nc


def make_consts():
    c = {}
    c["c_ident"] = np.eye(128, dtype=np.float32)
    i = np.arange(128)
    c["c_U"] = (i[:, None] <= i[None, :]).astype(np.float32)
    c["c_SL"] = (i[:, None] > i[None, :]).astype(np.float32)
    j = np.arange(64); same = (j[:, None] // 4) == (j[None, :] // 4)
    c["c_Ubd"] = (same & (j[:, None] <= j[None, :])).astype(np.float32)
    c["c_SLbd"] = (same & (j[:, None] > j[None, :])).astype(np.float32)
    c["c_BMt"] = ((j[:, None] // 4) == np.arange(16)[None, :]).astype(np.float32)
    bm = ((np.arange(16)[:, None]) == (j[None, :] // 4)).astype(np.float32)
    c["c_BM"] = np.broadcast_to(bm.reshape(1, 16 * 64), (128, 16 * 64)).copy()
    bo = np.zeros((128, 128), np.float32); bo[:64, :64] = 1; bo[64:, 64:] = 1
    c["c_bones"] = bo
    slopes = np.power(np.float32(2.0), -8.0 * np.arange(1, 9, dtype=np.float32) / 8).astype(np.float32)
    s = i[:, None, None]; q = i[None, None, :]; sl = slopes[None, :, None]
    ecur = np.where(q >= s, np.exp(-sl * (q - s).astype(np.float32)), 0.0)
    eprev = np.where(s > q, np.exp(-sl * (q - s + 128).astype(np.float32)), 0.0)
    c["c_Ecur"] = ecur.astype(np.float32).reshape(128, 1024)
    c["c_Eprev"] = eprev.astype(np.float32).reshape(128, 1024)
    t = np.arange(4)[None, None, :]
    ecache = np.where(s > t, np.exp(-sl * (128 + t - s).astype(np.float32)), 0.0)
    c["c_Ecache"] = ecache.astype(np.float32).reshape(128, 32)
    sj = j[:, None, None]; qj = j[None, None, :]
    enew = np.where(((sj // 4) == (qj // 4)) & (sj <= qj), np.exp(-sl * (qj - sj).astype(np.float32)), 0.0)
    c["c_Enew"] = enew.astype(np.float32).reshape(64, 512)
    return c


_WNAMES = ["g_ffn1", "w1_a", "w3_a", "w2_a", "g_mix", "w_in", "conv_w", "conv_b", "dt_bias", "a_log", "d_skip", "ssm_norm",
           "q_norm", "k_norm", "sinks", "w_out", "g_ffn2", "w1_b", "w3_b", "w2_b", "g_ple", "w_ple_gate", "w_ple_proj"]


def run(inputs, SEQ, NSS, NTG, n_prompt, ncores):
    f = lambda a: np.ascontiguousarray(np.asarray(a, dtype=np.float32))
    nc = build(SEQ, NSS, NTG)
    consts = make_consts()
    wts = {n: f(inputs[n]) for n in _WNAMES}
    xpr = f(inputs["x_prompt"]); ppr = f(inputs["p_prompt"]); xsm = f(inputs["x_sample"]); psm = f(inputs["p_sample"])
    sssm = f(inputs["state_ssm"]); sconv = f(inputs["state_conv"]); ck = f(inputs["cache_k_win"]); cv = f(inputs["cache_v_win"])
    in_maps = []
    for c in range(ncores):
        b = c % n_prompt
        bs = slice(c * NSS, (c + 1) * NSS)
        m = dict(wts); m.update(consts)
        m["xp"] = f(xpr[b]); m["pp"] = f(ppr[:, b])
        m["xs"] = f(xsm[bs].reshape(NSS * 4, D)); m["psm"] = f(psm[:, bs].reshape(DEPTH, NSS * 4, DPLE))
        m["sssm"] = f(sssm[:, bs].reshape(DEPTH, NSS, 512, 128)); m["sconv"] = f(sconv[:, bs].reshape(DEPTH, NSS * 3, 1024))
        m["ck"] = f(ck[:, bs].reshape(DEPTH, NSS, 128, 128)); m["cv"] = f(cv[:, bs].reshape(DEPTH, NSS, 128, 128))
        in_maps.append(m)
    res = run_bass_kernel_spmd(nc, in_maps, core_ids=list(range(ncores))).results
    P = n_prompt
    y_p = np.stack([res[b]["yp"] for b in range(P)])
    y_s = np.concatenate([res[c]["ys"].reshape(NSS, 4, D) for c in range(ncores)])
    ssm_p = np.stack([res[b]["ossm_p"].reshape(DEPTH, 8, 64, 128) for b in range(P)], axis=1)
    conv_p = np.stack([res[b]["oconv_p"] for b in range(P)], axis=1)
    k_p = np.stack([res[b]["ock_p"].reshape(DEPTH, 128, 2, 64) for b in range(P)], axis=1)
    v_p = np.stack([res[b]["ocv_p"].reshape(DEPTH, 128, 2, 64) for b in range(P)], axis=1)
    ssm_s = np.concatenate([res[c]["ossm_s"].reshape(DEPTH, NSS, 8, 64, 128) for c in range(ncores)], axis=1)
    conv_s = np.concatenate([res[c]["oconv_s"].reshape(DEPTH, NSS, 3, 1024) for c in range(ncores)], axis=1)
    k_s = np.concatenate([res[c]["ock_s"].reshape(DEPTH, NSS, 128, 2, 64) for c in range(ncores)], axis=1)
    v_s = np.concatenate([res[c]["ocv_s"].reshape(DEPTH, NSS, 128, 2, 64) for c in range(ncores)], axis=1)
    return tuple(np.ascontiguousarray(a, dtype=np.float32) for a in (y_p, y_s, ssm_p, conv_p, k_p, v_p, ssm_s, conv_s, k_s, v_s))


def kernel(**inputs):
    return run(inputs, SEQ=4096, NSS=16, NTG=512, n_prompt=4, ncores=NCORES)
```

```python
import contextlib
import numpy as np
import concourse.bass as bass
import concourse.mybir as mybir
from concourse.bass_utils import run_bass_kernel_spmd

F32 = mybir.dt.float32
BF16 = mybir.dt.bfloat16
AF = mybir.ActivationFunctionType
ALU = mybir.AluOpType
AX = mybir.AxisListType

D = 1024; DFF = 2752; DPROJ = 2312; DPLE = 256; DEPTH = 2
NCORES = 8
EPS = 1e-6
FT = [(i * 128, 128) for i in range(21)] + [(2688, 64)]
NFT = len(FT)
FCH = [list(range(i, min(i + 2, NFT))) for i in range(0, NFT, 2)]
W2CH = [list(range(i, min(i + 6, NFT))) for i in range(0, NFT, 6)]


DEBUG_MAP = None
SBUF_PEAK = [0, 0]
STAGE_LIMIT = None


class _Stop(Exception):
    pass


class Tok:
    __slots__ = ("w", "r")

    def __init__(self):
        self.w = None
        self.r = {}


class Eng:
    def __init__(self, name, h):
        self.name = name; self.h = h; self.sem = None; self.cnt = 0; self.waited = {}; self.own = set()


def _r32(n):
    return 32 if n <= 32 else (64 if n <= 64 else 128)


class PEProxy:
    def __init__(self, ctx, e):
        self.ctx = ctx; self.e = e; self.last = None

    def _mode(self, key):
        e = self.e
        if key != self.last and e.cnt > 0:
            k = id(e.sem)
            if e.waited.get(k, 0) < e.cnt:
                e.h.wait_ge(e.sem, e.cnt)
                e.waited[k] = e.cnt
        self.last = key

    def matmul(self, out, lhsT, rhs, start=True, stop=True):
        self._mode(("mm", str(lhsT.dtype), _r32(lhsT.shape[0]), _r32(int(np.prod(lhsT.shape[1:]))), out.base_partition()))
        return self.e.h.matmul(out, lhsT=lhsT, rhs=rhs, start=start, stop=stop)

    def transpose(self, out, in_, identity):
        self._mode(("tr", str(in_.dtype), _r32(in_.shape[0]), _r32(int(np.prod(in_.shape[1:]))), out.base_partition()))
        return self.e.h.transpose(out=out, in_=in_, identity=identity)


class Ctx:
    EPOCH = 12000

    def __init__(self, nc, es):
        self.nc = nc; self.es = es
        self.E = {"pe": Eng("pe", nc.tensor), "act": Eng("act", nc.scalar), "dve": Eng("dve", nc.vector),
                  "pool": Eng("pool", nc.gpsimd), "sp": Eng("sp", nc.sync)}
        self.nsem = 0
        for e in self.E.values():
            e.sem = self._newsem(); e.own.add(id(e.sem))
        self.slots = {"sp": [[self._newsem(), 0] for _ in range(10)],
                      "pool": [[self._newsem(), 0] for _ in range(10)]}
        self.slot_i = {"sp": 0, "pool": 0}
        self.semkey = {}
        self.stopped = False
        self.pe_proxy = PEProxy(self, self.E["pe"])
        self.pe_fast = False

    @contextlib.contextmanager
    def fast(self):
        old = self.pe_fast
        self.pe_fast = True; self.pe_proxy.last = "edge"
        try:
            yield
        finally:
            self.pe_fast = old; self.pe_proxy.last = "edge"

    def _newsem(self):
        self.nsem += 1
        return self.es.enter_context(self.nc.semaphore("s%d" % self.nsem))

    def _wait(self, e, ev):
        sem, val = ev
        k = id(sem)
        if e.name == "pe" and k in e.own and self.pe_fast:
            return
        if e.waited.get(k, 0) >= val:
            return
        e.h.wait_ge(sem, val)
        e.waited[k] = val

    def _sync(self, e, reads, writes):
        for t in reads:
            if t.w is not None:
                self._wait(e, t.w)
        for t in writes:
            if t.w is not None:
                self._wait(e, t.w)
            for ev in t.r.values():
                self._wait(e, ev)

    def _commit(self, ev, reads, writes):
        for t in writes:
            t.w = ev; t.r = {}
        for t in reads:
            k = id(ev[0])
            if k not in t.r or t.r[k][1] < ev[1]:
                t.r[k] = ev

    def op(self, eng, reads, writes, fn):
        if self.stopped:
            return None
        e = self.E[eng]
        self._sync(e, reads, writes)
        if e.cnt >= self.EPOCH:
            e.sem = self._newsem(); e.cnt = 0; e.own.add(id(e.sem))
        inst = fn(self.pe_proxy if eng == "pe" else e.h)
        e.cnt += 1
        inst.then_inc(e.sem, 1)
        if DEBUG_MAP is not None:
            import traceback
            nm = None
            for a in ("name", "inst", "instruction", "ins"):
                v = getattr(inst, a, None)
                if v is not None:
                    nm = getattr(v, "name", v) if a != "name" else v
                    break
            fr = traceback.extract_stack(limit=4)[-2]; fr0 = traceback.extract_stack(limit=4)[-3]
            DEBUG_MAP[str(nm)] = "%s:%d < %s:%d" % (fr.name, fr.lineno, fr0.name, fr0.lineno)
        ev = (e.sem, e.cnt)
        e.waited[id(e.sem)] = max(e.waited.get(id(e.sem), 0), 0)
        self._commit(ev, reads, writes)
        return ev

    def dma(self, eng, reads, writes, out, in_, **kw):
        if self.stopped:
            return None
        e = self.E[eng]
        sl = self.slots[eng][self.slot_i[eng]]
        self.slot_i[eng] = (self.slot_i[eng] + 1) % len(self.slots[eng])
        if sl[1] > 0:
            self._wait(e, (sl[0], sl[1]))
        self._sync(e, reads, writes)
        inst = e.h.dma_start(out=out, in_=in_, **kw)
        sl[1] += 16
        inst.then_inc(sl[0], 16)
        ev = (sl[0], sl[1])
        self._commit(ev, reads, writes)
        return ev

    def barrier(self):
        if self.stopped:
            return
        evs = []
        for e in self.E.values():
            if e.cnt > 0:
                evs.append((e.sem, e.cnt))
        for q in self.slots.values():
            for sl in q:
                if sl[1] > 0:
                    evs.append((sl[0], sl[1]))
        for e in self.E.values():
            for ev in evs:
                if ev[0] is e.sem:
                    continue
                self._wait(e, ev)

    def finish(self):
        self.stopped = False
        self.barrier()


def build(SEQ, NSS, NTG):
    NS = NSS * 4
    NG = SEQ // NTG
    nc = bass.Bass("TRN2", target_bir_lowering=False)
    di = lambda n, s: nc.dram_tensor(n, s, F32, kind="ExternalInput").ap()
    do = lambda n, s: nc.dram_tensor(n, s, F32, kind="ExternalOutput").ap()
    xp = di("xp", [SEQ, D]); pp = di("pp", [DEPTH, SEQ, DPLE]); xs = di("xs", [NS, D]); psm = di("psm", [DEPTH, NS, DPLE])
    sssm = di("sssm", [DEPTH, NSS, 512, 128]); sconv = di("sconv", [DEPTH, NSS * 3, 1024])
    ck = di("ck", [DEPTH, NSS, 128, 128]); cv = di("cv", [DEPTH, NSS, 128, 128])
    g_ffn1 = di("g_ffn1", [DEPTH, D]); g_mix = di("g_mix", [DEPTH, D]); g_ffn2 = di("g_ffn2", [DEPTH, D]); g_ple = di("g_ple", [DEPTH, D])
    w1a = di("w1_a", [DEPTH, D, DFF]); w3a = di("w3_a", [DEPTH, D, DFF]); w2a = di("w2_a", [DEPTH, DFF, D])
    w1b = di("w1_b", [DEPTH, D, DFF]); w3b = di("w3_b", [DEPTH, D, DFF]); w2b = di("w2_b", [DEPTH, DFF, D])
    w_in = di("w_in", [DEPTH, D, DPROJ]); w_out = di("w_out", [DEPTH, D, D])
    conv_w = di("conv_w", [DEPTH, 4, 1024]); conv_b = di("conv_b", [DEPTH, 1024])
    dt_bias = di("dt_bias", [DEPTH, 8]); a_log = di("a_log", [DEPTH, 8]); d_skip = di("d_skip", [DEPTH, 8])
    ssm_norm = di("ssm_norm", [DEPTH, 512]); q_norm = di("q_norm", [DEPTH, 64]); k_norm = di("k_norm", [DEPTH, 64])
    sinks = di("sinks", [DEPTH, 8]); w_pg = di("w_ple_gate", [DEPTH, D, D]); w_pp = di("w_ple_proj", [DEPTH, DPLE, D])
    c_ident = di("c_ident", [128, 128]); c_U = di("c_U", [128, 128]); c_SL = di("c_SL", [128, 128])
    c_Ubd = di("c_Ubd", [64, 64]); c_SLbd = di("c_SLbd", [64, 64]); c_BMt = di("c_BMt", [64, 16]); c_BM = di("c_BM", [128, 16 * 64])
    c_bones = di("c_bones", [128, 128]); c_Eprev = di("c_Eprev", [128, 1024]); c_Ecur = di("c_Ecur", [128, 1024])
    c_Ecache = di("c_Ecache", [128, 32]); c_Enew = di("c_Enew", [64, 512])
    yp = do("yp", [SEQ, D]); ys = do("ys", [NS, D])
    ossm_p = do("ossm_p", [DEPTH, 512, 128]); oconv_p = do("oconv_p", [DEPTH, 3, 1024])
    ock_p = do("ock_p", [DEPTH, 128, 128]); ocv_p = do("ocv_p", [DEPTH, 128, 128])
    ossm_s = do("ossm_s", [DEPTH, NSS, 512, 128]); oconv_s = do("oconv_s", [DEPTH, NSS * 3, 1024])
    ock_s = do("ock_s", [DEPTH, NSS, 128, 128]); ocv_s = do("ocv_s", [DEPTH, NSS, 128, 128])
    hscr = nc.dram_tensor("hscr", [8, 128, SEQ], F32, kind="Internal").ap()
    wsc13 = nc.dram_tensor("wsc13", [2, 2, len(FCH), 128, 2048], BF16, kind="Internal").ap()
    wsc2 = nc.dram_tensor("wsc2", [2, 2, 128, NFT * 512], BF16, kind="Internal").ap()
    wscm = nc.dram_tensor("wscm", [3, 128, 18560], BF16, kind="Internal").ap()
    t_sc13 = [[[Tok() for _ in FCH] for _ in range(2)] for _ in range(2)]
    t_sc2 = [[[Tok() for _ in W2CH] for _ in range(2)] for _ in range(2)]
    t_scm = [Tok() for _ in range(3)]
    t_hscr = [Tok() for _ in range(NG)]

    with contextlib.ExitStack() as es:
        cx = Ctx(nc, es)
        op = cx.op; dma = cx.dma

        uniq = [0]

        def SB(scope, name, shape, dt=F32):
            uniq[0] += 1
            t = scope.enter_context(nc.sbuf_tensor("%s_%d" % (name, uniq[0]), shape, dt))
            try:
                SBUF_PEAK[0] = max(SBUF_PEAK[0], int(nc.sbuf_base))
                SBUF_PEAK[1] = int(nc.sbuf_top)
            except Exception:
                pass
            return t

        ident = SB(es, "ident", [128, 128]); t_c = Tok()
        identb = SB(es, "identb", [128, 128], BF16)
        Um = SB(es, "Um", [128, 128]); SLm = SB(es, "SLm", [128, 128])
        Ubd = SB(es, "Ubd", [64, 64]); SLbd = SB(es, "SLbd", [64, 64]); BMt = SB(es, "BMt", [64, 16]); BMtb = SB(es, "BMtb", [64, 16], BF16)
        BMb = SB(es, "BMb", [128, 16 * 64], BF16)
        bones = SB(es, "bones", [128, 128], BF16); onesb = SB(es, "onesb", [128, 128], BF16); onesf = SB(es, "onesf", [128, 128])
        Eprev = SB(es, "Eprev", [128, 1024]); Ecur = SB(es, "Ecur", [128, 1024]); Ecache = SB(es, "Ecache", [128, 32]); Enew = SB(es, "Enew", [64, 512])
        gcol = SB(es, "gcol", [128, 4 * DEPTH * 8])
        lay = {}
        for nm, w in [("cw", 32), ("cb", 8), ("dtb", 8), ("aneg", 8), ("dsk", 8), ("esk", 8), ("gq", 1), ("gk", 1)]:
            lay[nm] = SB(es, "l_" + nm, [128, w])
        gssm = SB(es, "gssm", [128, 512]); esinkrow = SB(es, "esinkrow", [64, 1024])
        t_lay = Tok()
        hs = SB(es, "hs", [128, 8, 64]); t_hs = Tok()
        w13 = [[SB(es, "w13_%d_%d" % (m, b), [128, 8, 256], BF16) for b in range(2)] for m in range(2)]
        t_w13 = [[Tok() for _ in range(2)] for _ in range(2)]
        wreg = SB(es, "wreg", [128, 18560], BF16); t_wreg = Tok()
        hT = SB(es, "hT", [128, 512]); hTb = SB(es, "hTb", [128, 512], BF16); t_hT = Tok(); t_hTb = Tok()
        xhalo = SB(es, "xhalo", [128, 8, 3]); t_xhalo = Tok()
        khalo = SB(es, "khalo", [128, 128], BF16); vhalo = SB(es, "vhalo", [128, 128], BF16); t_kvh = Tok()
        psb = [es.enter_context(nc.psum_tensor("ps%d" % i, [128, 512], F32)) for i in range(8)]
        t_ps = [Tok() for _ in range(8)]
        psi = [0]

        held = set()

        def PS(hold=False):
            for _try in range(9):
                i = psi[0]; psi[0] = (i + 1) % 8
                if i not in held:
                    break
            else:
                raise RuntimeError("all PSUM banks held")
            if hold:
                held.add(i)
            return psb[i], t_ps[i]

        def REL(tok):
            held.discard(t_ps.index(tok))

        def ld(eng, dst, src, toks, **kw):
            dma(eng, [], toks, dst, src, **kw)
        ld("sp", ident[:], c_ident[:, :], [t_c]); ld("pool", identb[:], c_ident[:, :], [t_c])
        ld("sp", Um[:], c_U[:, :], [t_c]); ld("sp", SLm[:], c_SL[:, :], [t_c])
        ld("sp", Ubd[:], c_Ubd[:, :], [t_c]); ld("sp", SLbd[:], c_SLbd[:, :], [t_c]); ld("sp", BMt[:], c_BMt[:, :], [t_c])
        ld("pool", BMtb[:], c_BMt[:, :], [t_c]); ld("pool", BMb[:], c_BM[:, :], [t_c]); ld("pool", bones[:], c_bones[:, :], [t_c])
        ld("sp", Eprev[:], c_Eprev[:, :], [t_c]); ld("sp", Ecur[:], c_Ecur[:, :], [t_c]); ld("sp", Ecache[:], c_Ecache[:, :], [t_c]); ld("sp", Enew[:], c_Enew[:, :], [t_c])
        op("dve", [], [t_c], lambda v: v.memset(onesb[:], 1.0))
        op("dve", [], [t_c], lambda v: v.memset(onesf[:], 1.0))
        for ni, g in enumerate([g_ffn1, g_mix, g_ffn2, g_ple]):
            for l in range(DEPTH):
                o = (ni * DEPTH + l) * 8
                ld("sp", gcol[:, o:o + 8], g[l, :].rearrange("(j p) -> p j", p=128), [t_c], allow_slow_non_contiguous=True)
        op("dve", [t_c], [t_c], lambda v: v.tensor_scalar(out=gcol[:], in0=gcol[:], scalar1=32.0, scalar2=None, op0=ALU.mult))

        def load_layer_consts(l):
            T = [t_lay]
            for j in range(4):
                ld("sp", lay["cw"][:, j * 8:(j + 1) * 8], conv_w[l, j, :].rearrange("(c p) -> p c", p=128), T, allow_slow_non_contiguous=True)
            ld("sp", lay["cb"][:], conv_b[l, :].rearrange("(c p) -> p c", p=128), T, allow_slow_non_contiguous=True)
            ld("sp", lay["dtb"][:], dt_bias[l, :].partition_broadcast(128), T)
            ld("sp", lay["aneg"][:], a_log[l, :].partition_broadcast(128), T)
            ld("sp", lay["dsk"][:], d_skip[l, :].partition_broadcast(128), T)
            ld("sp", lay["esk"][:], sinks[l, :].partition_broadcast(128), T)
            for hh in range(2):
                ld("sp", lay["gq"][hh * 64:(hh + 1) * 64, :], q_norm[l, :].rearrange("(p o) -> p o", o=1), T, allow_slow_non_contiguous=True)
                ld("sp", lay["gk"][hh * 64:(hh + 1) * 64, :], k_norm[l, :].rearrange("(p o) -> p o", o=1), T, allow_slow_non_contiguous=True)
            ld("sp", gssm[:], ssm_norm[l, :].partition_broadcast(128), T)
            op("act", T, T, lambda a: a.activation(out=lay["aneg"][:], in_=lay["aneg"][:], func=AF.Exp))
            op("dve", T, T, lambda v: v.tensor_scalar(out=lay["aneg"][:], in0=lay["aneg"][:], scalar1=-1.0, scalar2=None, op0=ALU.mult))
            op("act", T, T, lambda a: a.activation(out=lay["esk"][:], in_=lay["esk"][:], func=AF.Exp))
            op("dve", T, T, lambda v: v.tensor_scalar(out=lay["gq"][:], in0=lay["gq"][:], scalar1=8.0, scalar2=None, op0=ALU.mult))
            op("dve", T, T, lambda v: v.tensor_scalar(out=lay["gk"][:], in0=lay["gk"][:], scalar1=8.0, scalar2=None, op0=ALU.mult))
            op("dve", T, T, lambda v: v.tensor_copy(out=esinkrow[:].rearrange("p (h q) -> p h q", h=8),
                                                     in_=lay["esk"][0:64, :].unsqueeze(2).broadcast_to([64, 8, 128])))

        def nblocks(NT):
            return [(n0, min(512, NT - n0)) for n0 in range(0, NT, 512)]

        def norm(sc, h, t_h, NT, ni, l, xn, t_xn):
            go = (ni * DEPTH + l) * 8
            sq = SB(sc, "sq", [128, 8, 512], BF16); t_sq = Tok()
            rstd = SB(sc, "rstd", [128, 512]); t_rstd = Tok()
            for (n0, nw) in nblocks(NT):
                for half in range(2):
                    op("act", [t_h], [t_sq], lambda a: a.activation(out=sq[:, half * 4:(half + 1) * 4, :nw], in_=h[:, half * 4:(half + 1) * 4, n0:n0 + nw], func=AF.Square))
                ps, tp = PS()
                for j in range(8):
                    op("pe", [t_sq, t_c], [tp], lambda p: p.matmul(ps[:, :nw], lhsT=onesb[:], rhs=sq[:, j, :nw], start=(j == 0), stop=(j == 7)))
                op("act", [tp], [t_rstd], lambda a: a.activation(out=rstd[:, :nw], in_=ps[:, :nw], func=AF.Ln, bias=1024.0 * EPS, scale=1.0))
                op("act", [t_rstd], [t_rstd], lambda a: a.activation(out=rstd[:, :nw], in_=rstd[:, :nw], func=AF.Exp, scale=-0.5))
                for j in range(8):
                    op("dve", [t_h, t_rstd, t_c], [t_xn[j]], lambda v: v.scalar_tensor_tensor(out=xn[:, j, n0:n0 + nw], in0=h[:, j, n0:n0 + nw], scalar=gcol[:, go + j:go + j + 1], in1=rstd[:, :nw], op0=ALU.mult, op1=ALU.mult))

        w13_next = [None]

        def issue_w13(W1, W3, l, ci, parity, ab, cached):
            cols = FCH[ci]; f0 = FT[cols[0]][0]; fw = sum(FT[c][1] for c in cols)
            for m, W in enumerate((W1, W3)):
                scr = wsc13[ab, m, ci, :, :].rearrange("p (k f) -> p k f", k=8)[:, :, :fw]
                if cached:
                    dma("pool", [t_sc13[ab][m][ci]], [t_w13[m][parity]], w13[m][parity][:, :, :fw], scr)
                else:
                    dma("pool", [], [t_w13[m][parity]], w13[m][parity][:, :, :fw], W[l, :, f0:f0 + fw].rearrange("(k p) f -> p k f", p=128))
                    dma("sp", [t_w13[m][parity]], [t_sc13[ab][m][ci]], scr, w13[m][parity][:, :, :fw])

        def ffn(sc, h, t_h, NT, xn, t_xn, W1, W3, W2, l, ab, cached):
            gT = SB(sc, "gT", [128, NFT, NT], BF16); t_g = [Tok() for _ in range(NFT)]
            s1 = [SB(sc, "s1_%d" % i, [128, 512]) for i in range(2)]; t_s1 = [Tok(), Tok()]
            w2 = SB(sc, "w2", [128, NFT, 512], BF16); t_w2 = [Tok() for _ in W2CH]
            si = 0
            issue_w13(W1, W3, l, 0, 0, ab, cached)
            for ci, cols in enumerate(FCH):
                par = ci % 2
                if ci + 1 < len(FCH):
                    issue_w13(W1, W3, l, ci + 1, (ci + 1) % 2, ab, cached)
                for fi, ft in enumerate(cols):
                    fw = FT[ft][1]; fo = fi * 128
                    for (n0, nw) in nblocks(NT):
                        p1, tp1 = PS(); p3, tp3 = PS()
                        for (pp_, tpp, m) in ((p1, tp1, 0), (p3, tp3, 1)):
                            for k in range(8):
                                op("pe", [t_w13[m][par], t_xn[k]], [tpp], lambda p: p.matmul(pp_[:fw, :nw], lhsT=w13[m][par][:, k, fo:fo + fw], rhs=xn[:, k, n0:n0 + nw], start=(k == 0), stop=(k == 7)))
                        sb_, ts_ = s1[si], t_s1[si]; si ^= 1
                        op("act", [tp1], [ts_], lambda a: a.activation(out=sb_[:fw, :nw], in_=p1[:fw, :nw], func=AF.Silu))
                        op("dve", [ts_, tp3], [t_g[ft]], lambda v: v.tensor_tensor(out=gT[:fw, ft, n0:n0 + nw], in0=sb_[:fw, :nw], in1=p3[:fw, :nw], op=ALU.mult))
            for half in range(2):
                for wi, rows in enumerate(W2CH):
                    r0 = FT[rows[0]][0]
                    nfull = [r for r in rows if FT[r][1] == 128]
                    scr2 = wsc2[ab, half, :, :].rearrange("p (f c) -> p f c", f=NFT)
                    if nfull:
                        sl = slice(nfull[0], nfull[-1] + 1)
                        if cached:
                            dma("pool", [t_sc2[ab][half][wi]], [t_w2[wi]], w2[:, sl, :], scr2[:, sl, :])
                        else:
                            dma("pool", [], [t_w2[wi]], w2[:, sl, :], W2[l, r0:r0 + 128 * len(nfull), half * 512:(half + 1) * 512].rearrange("(f p) c -> p f c", p=128))
                    for r in rows:
                        if FT[r][1] != 128:
                            if cached:
                                dma("pool", [t_sc2[ab][half][wi]], [t_w2[wi]], w2[:64, r, :], scr2[:64, r, :])
                            else:
                                dma("pool", [], [t_w2[wi]], w2[:64, r, :], W2[l, FT[r][0]:FT[r][0] + 64, half * 512:(half + 1) * 512])
                    if not cached:
                        if nfull:
                            dma("sp", [t_w2[wi]], [t_sc2[ab][half][wi]], scr2[:, sl, :], w2[:, sl, :])
                        for r in rows:
                            if FT[r][1] != 128:
                                dma("sp", [t_w2[wi]], [t_sc2[ab][half][wi]], scr2[:64, r, :], w2[:64, r, :])
                for (n0, nw) in nblocks(NT):
                    acc = [PS() for _ in range(4)]
                    for ft in range(NFT):
                        fw = FT[ft][1]
                        wi = [i for i, rows in enumerate(W2CH) if ft in rows][0]
                        for dj in range(4):
                            op("pe", [t_w2[wi], t_g[ft]], [acc[dj][1]], lambda p: p.matmul(acc[dj][0][:, :nw], lhsT=w2[:fw, ft, dj * 128:(dj + 1) * 128], rhs=gT[:fw, ft, n0:n0 + nw], start=(ft == 0), stop=(ft == NFT - 1)))
                    for dj in range(4):
                        d = half * 4 + dj
                        op("dve", [acc[dj][1], t_h], [t_h], lambda v: v.scalar_tensor_tensor(out=h[:, d, n0:n0 + nw], in0=acc[dj][0][:, :nw], scalar=0.5, in1=h[:, d, n0:n0 + nw], op0=ALU.mult, op1=ALU.add))

        def transpose_in(xtm, t_x, src, ntok, h, t_h, c0):
            dma("sp", [], [t_x], xtm[:ntok, :], src)
            for a in range(2):
                ps, tp = PS()
                for j in range(4):
                    op("pe", [t_x, t_c], [tp], lambda p: p.transpose(out=ps[:, j * 128:j * 128 + ntok], in_=xtm[:ntok, (a * 4 + j) * 128:(a * 4 + j + 1) * 128], identity=ident[:ntok, :ntok]))
                op("act", [tp], [t_h], lambda a_: a_.activation(out=h[:, a * 4:(a + 1) * 4, c0:c0 + ntok], in_=ps[:].rearrange("p (j t) -> p j t", j=4)[:, :, :ntok], func=AF.Copy))

        def transpose_out(ytm, t_y, h, t_h, c0, ntok, dst):
            for a in range(2):
                ps, tp = PS()
                for j in range(4):
                    op("pe", [t_h, t_c], [tp], lambda p: p.transpose(out=ps[:ntok, j * 128:(j + 1) * 128], in_=h[:, a * 4 + j, c0:c0 + ntok], identity=ident[:]))
                op("act", [tp], [t_y], lambda a_: a_.activation(out=ytm[:ntok, a * 512:(a + 1) * 512], in_=ps[:ntok, :], func=AF.Copy))
            dma("sp", [t_y], [], dst, ytm[:ntok, :])

        def mixer(sc, h, t_h, NT, l, sample, first_group, last_group, cached):
            xn = SB(sc, "xn", [128, 8, NT], BF16); t_xn = [Tok() for _ in range(8)]
            norm(sc, h, t_h, NT, 1, l, xn, t_xn)
            chk('m_norm')
            NTT = 64 if sample else 128
            ntile = NT // NTT
            o = 0
            def carve(n, shape_str, **kw):
                nonlocal o
                a = wreg[:, o:o + n]; o += n
                return a.rearrange(shape_str, **kw)
            wz = carve(8 * 512, "p (k c) -> p k c", k=8); wx = carve(8 * 1024, "p (k c) -> p k c", k=8)
            wq = carve(8 * 512, "p (k c) -> p k c", k=8); wk = carve(8 * 128, "p (k c) -> p k c", k=8)
            wv = carve(8 * 128, "p (k c) -> p k c", k=8); wdt = carve(8 * 8, "p (k c) -> p k c", k=8)
            wl = w_in[l, :, :].rearrange("(k p) c -> p k c", p=128)
            T = [t_wreg]
            if cached:
                dma("pool", [t_scm[0]], T, wreg[:, 0:18496], wscm[0, :, 0:18496])
            else:
                dma("pool", [], T, wz, wl[:, :, 0:512]); dma("pool", [], T, wx, wl[:, :, 512:1536]); dma("pool", [], T, wdt, wl[:, :, 1536:1544])
                for hh in range(4):
                    for g in range(2):
                        c = 1544 + g * 256 + hh * 64
                        dma("pool", [], T, wq[:, :, hh * 128 + g * 64:hh * 128 + g * 64 + 64], wl[:, :, c:c + 64])
                dma("pool", [], T, wk, wl[:, :, 2056:2184]); dma("pool", [], T, wv, wl[:, :, 2184:2312])
                dma("sp", T, [t_scm[0]], wscm[0, :, 0:18496], wreg[:, 0:18496])
            chk('m_wdma')
            HAL = 0 if sample else 3
            xin = SB(sc, "xin", [128, 8, NT + HAL]); t_xin = Tok()
            xc = SB(sc, "xc", [128, 8, NT], BF16); t_xc = Tok()
            qn = SB(sc, "qn", [128, 4, NT], BF16); t_qn = Tok()
            KH = 0 if sample else 128
            kn = SB(sc, "kn", [128, KH + NT], BF16); t_kn = Tok()
            knf = SB(sc, "knf", [128, NT]); t_knf = Tok()
            vb = SB(sc, "vb", [128, ntile + 1, 128], BF16); t_vb = Tok()
            vf = SB(sc, "vf", [128, 128]); t_vf = Tok()
            mixT = SB(sc, "mixT", [128, 4, NT], BF16); t_mixT = Tok()
            oT = SB(sc, "oT", [64, 8, NT], BF16); t_oT = Tok()
            tmpa = SB(sc, "tmpa", [128, 512]); t_tmpa = Tok()
            rq = SB(sc, "rq", [128, 512]); t_rq = Tok()
            sqb = SB(sc, "sqb", [128, 512], BF16); t_sqb = Tok()
            if not sample:
                op("dve", [t_xhalo], [t_xin], lambda v: v.tensor_copy(out=xin[:, :, 0:3], in_=xhalo[:]))
                op("dve", [t_kvh], [t_kn], lambda v: v.tensor_copy(out=kn[:, 0:128], in_=khalo[:]))
                op("dve", [t_kvh], [t_vb], lambda v: v.tensor_copy(out=vb[:, 0, :], in_=vhalo[:]))
            chk('m_halo')
            with cx.fast():
                for c in range(8):
                    for (n0, nw) in nblocks(NT):
                        ps, tp = PS()
                        for k in range(8):
                            op("pe", T + [t_xn[k]], [tp], lambda p: p.matmul(ps[:, :nw], lhsT=wx[:, k, c * 128:(c + 1) * 128], rhs=xn[:, k, n0:n0 + nw], start=(k == 0), stop=(k == 7)))
                        if not sample:
                            op("act", [tp], [t_xin], lambda a: a.activation(out=xin[:, c, HAL + n0:HAL + n0 + nw], in_=ps[:, :nw], func=AF.Copy))
                        else:
                            op("act", [tp], [t_xin], lambda a: a.activation(out=xin[:, c, n0:n0 + nw], in_=ps[:, :nw], func=AF.Copy))
                chk('m_xbc')
                for qi in range(5):
                    for (n0, nw) in nblocks(NT):
                        ps, tp = PS()
                        for k in range(8):
                            lw = wq[:, k, qi * 128:(qi + 1) * 128] if qi < 4 else wk[:, k, :]
                            op("pe", T + [t_xn[k]], [tp], lambda p: p.matmul(ps[:, :nw], lhsT=lw, rhs=xn[:, k, n0:n0 + nw], start=(k == 0), stop=(k == 7)))
                        op("act", [tp], [t_sqb], lambda a: a.activation(out=sqb[:, :nw], in_=ps[:, :nw], func=AF.Square))
                        ps2, tp2 = PS()
                        op("pe", [t_sqb, t_c], [tp2], lambda p: p.matmul(ps2[:, :nw], lhsT=bones[:], rhs=sqb[:, :nw], start=True, stop=True))
                        op("act", [tp2], [t_rq], lambda a: a.activation(out=rq[:, :nw], in_=ps2[:, :nw], func=AF.Ln, bias=64.0 * EPS, scale=1.0))
                        op("act", [t_rq], [t_rq], lambda a: a.activation(out=rq[:, :nw], in_=rq[:, :nw], func=AF.Exp, scale=-0.5))
                        if qi < 4:
                            op("dve", [tp, t_rq, t_lay], [t_qn], lambda v: v.scalar_tensor_tensor(out=qn[:, qi, n0:n0 + nw], in0=ps[:, :nw], scalar=lay["gq"][:, 0:1], in1=rq[:, :nw], op0=ALU.mult, op1=ALU.mult))
                        else:
                            op("dve", [tp, t_rq, t_lay], [t_knf], lambda v: v.scalar_tensor_tensor(out=knf[:, n0:n0 + nw], in0=ps[:, :nw], scalar=lay["gk"][:, 0:1], in1=rq[:, :nw], op0=ALU.mult, op1=ALU.mult))
                            op("act", [t_knf], [t_kn], lambda a: a.activation(out=kn[:, KH + n0:KH + n0 + nw], in_=knf[:, n0:n0 + nw], func=AF.Copy))
            chk('m_qk')
            cacc = SB(sc, "cacc", [128, 512]); t_cacc = Tok()
            if not sample:
                for c in range(8):
                    for (n0, nw) in nblocks(NT):
                        op("dve", [t_xin, t_lay], [t_cacc], lambda v: v.tensor_scalar(out=cacc[:, :nw], in0=xin[:, c, n0:n0 + nw], scalar1=lay["cw"][:, c:c + 1], scalar2=None, op0=ALU.mult))
                        for j in range(1, 4):
                            op("dve", [t_xin, t_lay, t_cacc], [t_cacc], lambda v: v.scalar_tensor_tensor(out=cacc[:, :nw], in0=xin[:, c, n0 + j:n0 + j + nw], scalar=lay["cw"][:, j * 8 + c:j * 8 + c + 1], in1=cacc[:, :nw], op0=ALU.mult, op1=ALU.add))
                        op("act", [t_cacc, t_lay], [t_xc], lambda a: a.activation(out=xc[:, c, n0:n0 + nw], in_=cacc[:, :nw], func=AF.Silu, bias=lay["cb"][:, c:c + 1], scale=1.0))
                op("dve", [t_xin], [t_xhalo], lambda v: v.tensor_copy(out=xhalo[:], in_=xin[:, :, NT:NT + 3]))
                if last_group:
                    cst = SB(sc, "cst", [128, 8, 4]); t_cst = Tok()
                    op("dve", [t_xin], [t_cst], lambda v: v.tensor_copy(out=cst[:, :, 0:3], in_=xin[:, :, NT:NT + 3]))
                    ps, tp = PS(); ps2, tp2 = PS()
                    for c in range(8):
                        pp_, tpp = (ps, tp) if c < 4 else (ps2, tp2)
                        op("pe", [t_cst, t_c], [tpp], lambda p: p.transpose(out=pp_[:3, (c % 4) * 128:(c % 4 + 1) * 128], in_=cst[:, c, 0:3], identity=ident[:]))
                    cso = SB(sc, "cso", [4, 1024]); t_cso = Tok()
                    op("act", [tp], [t_cso], lambda a: a.activation(out=cso[:3, 0:512], in_=ps[:3, :], func=AF.Copy))
                    op("act", [tp2], [t_cso], lambda a: a.activation(out=cso[:3, 512:1024], in_=ps2[:3, :], func=AF.Copy))
                    dma("sp", [t_cso], [], oconv_p[l, :, :], cso[:3, :])
            else:
                xfull = SB(sc, "xfull", [128, 8, NSS, 7]); t_xf = Tok()
                scm = SB(sc, "scm", [64, 1024]); t_scmb = Tok()
                dma("sp", [], [t_scmb], scm[:NSS * 3, :], sconv[l, :, :])
                for c in range(8):
                    ps, tp = PS()
                    op("pe", [t_scmb, t_c], [tp], lambda p: p.transpose(out=ps[:, :NSS * 3], in_=scm[:NSS * 3, c * 128:(c + 1) * 128], identity=ident[:NSS * 3, :NSS * 3]))
                    op("act", [tp], [t_xf], lambda a: a.activation(out=xfull[:, c, :, 0:3], in_=ps[:, :NSS * 3].rearrange("p (b j) -> p b j", j=3), func=AF.Copy))
                    op("dve", [t_xin], [t_xf], lambda v: v.tensor_copy(out=xfull[:, c, :, 3:7], in_=xin[:, c, :].rearrange("p (b t) -> p b t", t=4)))
                    ca = cacc[:, :NS].rearrange("p (b t) -> p b t", t=4)
                    op("dve", [t_xf, t_lay], [t_cacc], lambda v: v.tensor_scalar(out=ca, in0=xfull[:, c, :, 0:4], scalar1=lay["cw"][:, c:c + 1], scalar2=None, op0=ALU.mult))
                    for j in range(1, 4):
                        op("dve", [t_xf, t_lay, t_cacc], [t_cacc], lambda v: v.scalar_tensor_tensor(out=ca, in0=xfull[:, c, :, j:j + 4], scalar=lay["cw"][:, j * 8 + c:j * 8 + c + 1], in1=ca, op0=ALU.mult, op1=ALU.add))
                    op("act", [t_cacc, t_lay], [t_xc], lambda a: a.activation(out=xc[:, c, :], in_=cacc[:, :NS], func=AF.Silu, bias=lay["cb"][:, c:c + 1], scale=1.0))
                cso = SB(sc, "cso", [64, 1024]); t_cso = Tok()
                cst = SB(sc, "cst", [128, 8, NSS * 3]); t_cst = Tok()
                op("dve", [t_xf], [t_cst], lambda v: v.tensor_copy(out=cst[:].rearrange("p c (b j) -> p c b j", j=3), in_=xfull[:, :, :, 4:7]))
                for a_ in range(2):
                    ps, tp = PS()
                    for j in range(4):
                        op("pe", [t_cst, t_c], [tp], lambda p: p.transpose(out=ps[:NSS * 3, j * 128:(j + 1) * 128], in_=cst[:, a_ * 4 + j, :], identity=ident[:]))
                    op("act", [tp], [t_cso], lambda a: a.activation(out=cso[:NSS * 3, a_ * 512:(a_ + 1) * 512], in_=ps[:NSS * 3, :], func=AF.Copy))
                dma("sp", [t_cso], [], oconv_s[l, :, :], cso[:NSS * 3, :])

            chk('m_conv')
            Ut = Ubd if sample else Um; SLt = SLbd if sample else SLm
            P = NTT
            st = {}
            for nm, shp, dt in [("dt", [128, 8], F32), ("dta", [128, 8], F32), ("t8", [128, 8], F32), ("ecum", [128, 8], F32), ("etot", [128, 8], F32),
                                ("dend", [128, 8], F32), ("w2s", [128, 8], F32), ("DL", [128, 8, 128], F32), ("LT", [128, 8, 128], F32),
                                ("GM", [128, 2, 128], F32), ("MT", [128, 8, 128], BF16), ("xdt", [128, 512], BF16), ("xdd", [128, 512], BF16),
                                ("Btm", [128, 256], BF16), ("y1", [128, 512], F32), ("sz", [128, 512], F32), ("ysq", [128, 512], F32), ("xsk", [128, 512], F32), ("xtok", [128, 512], F32),
                                ("ss2", [128, 2], F32), ("ytm", [128, 512], BF16), ("pe0", [128, 512], F32), ("pe1", [128, 512], F32),
                                ("PT0", [128, 512], BF16), ("PT1", [128, 512], BF16), ("den", [64, 512], F32)]:
                st[nm] = (SB(sc, "st_" + nm, shp, dt), Tok())
            if sample:
                Snat = SB(sc, "Snat", [128, 8, 4, 128]); t_Sn = Tok()
                Sb1 = [SB(sc, "Sb1_%d" % i, [128, 4, 128], BF16) for i in range(2)]; t_Sb1 = [Tok(), Tok()]
                STb1 = [SB(sc, "STb1_%d" % i, [128, 512], BF16) for i in range(2)]; t_ST1 = [Tok(), Tok()]
                CTm = SB(sc, "CTm", [128, NSS, 2, 64], BF16); t_CT = Tok()
                xdm = [SB(sc, "xdm%d" % i, [64, 512], BF16) for i in range(2)]; t_xdm = [Tok(), Tok()]
                dtaE = SB(sc, "dtaE", [64, 512]); t_dE = Tok()
                cdT = SB(sc, "cdT", [128, 4, 16]); t_cd = Tok()
                Snew = [SB(sc, "Snew%d" % i, [128, 4, 128]) for i in range(2)]; t_Snew = [Tok(), Tok()]
                kcb = SB(sc, "kcb", [128, NSS, 128], BF16); vcb = SB(sc, "vcb", [128, NSS, 128], BF16); t_kcb = Tok(); t_vcb = Tok()
                KcT = SB(sc, "KcT", [128, NSS, 128], BF16); t_KcT = Tok()
                PTc = SB(sc, "PTc", [128, 512], BF16); t_PTc = Tok()
                for b4 in range(0, NSS, 4):
                    dma("pool", [], [t_kcb], kcb[:, b4:b4 + 4, :], ck[l, b4:b4 + 4, :, :].rearrange("b i c -> i b c"))
                    dma("pool", [], [t_vcb], vcb[:, b4:b4 + 4, :], cv[l, b4:b4 + 4, :, :].rearrange("b i c -> i b c"))
                for b4 in range(0, NSS, 4):
                    dma("sp", [], [], ock_s[l, b4:b4 + 4, 0:124, :], ck[l, b4:b4 + 4, 4:128, :])
                    dma("sp", [], [], ocv_s[l, b4:b4 + 4, 0:124, :], cv[l, b4:b4 + 4, 4:128, :])

            A = lambda nm: st[nm][0]
            K_ = lambda nm: st[nm][1]
            PSH = lambda: PS(hold=not sample)

            def RELH(tok):
                if not sample:
                    REL(tok)

            def tile_ssd(ti):
                c0 = ti * NTT
                cs = slice(c0, c0 + P)
                first_tile = first_group and ti == 0 and not sample
                pz, tpz = PSH(); pdv, tpdv = PSH()
                with cx.fast():
                    for k in range(8):
                        op("pe", T + [t_xn[k]], [tpz], lambda p: p.matmul(pz[:P, :], lhsT=xn[:, k, cs], rhs=wz[:, k, :], start=(k == 0), stop=(k == 7)))
                    for k in range(8):
                        op("pe", T + [t_xn[k]], [tpdv], lambda p: p.matmul(pdv[:P, 0:128], lhsT=xn[:, k, cs], rhs=wv[:, k, :], start=(k == 0), stop=(k == 7)))
                    for k in range(8):
                        op("pe", T + [t_xn[k]], [tpdv], lambda p: p.matmul(pdv[:P, 128:136], lhsT=xn[:, k, cs], rhs=wdt[:, k, :], start=(k == 0), stop=(k == 7)))
                op("act", [tpz], [st["sz"][1]], lambda a: a.activation(out=st["sz"][0][:P, :], in_=pz[:P, :], func=AF.Silu))
                RELH(tpz)
                op("act", [tpdv], [t_vb], lambda a: a.activation(out=vb[:P, ti + 1, :], in_=pdv[:P, 0:128], func=AF.Copy))
                need_vf = sample or (last_group and ti == ntile - 1)
                if need_vf:
                    op("act", [tpdv], [t_vf], lambda a: a.activation(out=vf[:P, :], in_=pdv[:P, 0:128], func=AF.Copy))
                chk('t_zdv')
                op("dve", [tpdv, t_lay], [K_("t8")], lambda v: v.tensor_tensor(out=A("t8")[:P, :], in0=pdv[:P, 128:136], in1=lay["dtb"][:P, :], op=ALU.add))
                RELH(tpdv)
                yield
                op("act", [K_("t8")], [K_("t8")], lambda a: a.activation(out=A("t8")[:P, :], in_=A("t8")[:P, :], func=AF.Exp))
                op("act", [K_("t8")], [K_("dt")], lambda a: a.activation(out=A("dt")[:P, :], in_=A("t8")[:P, :], func=AF.Ln, bias=1.0, scale=1.0))
                op("dve", [K_("dt"), t_lay], [K_("dta")], lambda v: v.tensor_tensor(out=A("dta")[:P, :], in0=A("dt")[:P, :], in1=lay["aneg"][:P, :], op=ALU.mult))
                op("dve", [K_("dta"), t_c], [K_("DL")], lambda v: v.tensor_tensor(out=A("DL")[:P, :, :P], in0=SLt[:P, :P].unsqueeze(1).broadcast_to([P, 8, P]), in1=A("dta")[:P, :].unsqueeze(2).broadcast_to([P, 8, P]), op=ALU.mult))
                yield
                pD0, tD0 = PSH(); pD1, tD1 = PSH(); pc, tpc = PSH()
                chk('t_dt')
                for hd in range(8):
                    pd_, td_ = (pD0, tD0) if hd < 4 else (pD1, tD1)
                    op("pe", [K_("DL"), t_c], [td_], lambda p: p.matmul(pd_[:P, (hd % 4) * 128:(hd % 4) * 128 + P], lhsT=A("DL")[:P, hd, :P], rhs=Ut[:P, :P], start=True, stop=True))
                op("pe", [K_("dta"), t_c], [tpc], lambda p: p.matmul(pc[:P, 0:8], lhsT=Ut[:P, :P], rhs=A("dta")[:P, :], start=True, stop=True))
                if not sample:
                    op("pe", [K_("dta"), t_c], [tpc], lambda p: p.matmul(pc[:, 8:16], lhsT=onesf[:, :], rhs=A("dta")[:, :], start=True, stop=True))
                else:
                    pass
                for hf, (pd_, td_) in enumerate(((pD0, tD0), (pD1, tD1))):
                    op("act", [td_], [K_("LT")], lambda a: a.activation(out=A("LT")[:P, hf * 4:(hf + 1) * 4, :P], in_=pd_[:P, :].rearrange("p (h t) -> p h t", h=4)[:, :, :P], func=AF.Exp))
                chk('u_LT')
                op("act", [tpc], [K_("ecum")], lambda a: a.activation(out=A("ecum")[:P, :], in_=pc[:P, 0:8], func=AF.Exp))
                RELH(tD0); RELH(tD1)
                chk('u_ecum')
                chk('t_D')
                pG, tpG = PSH()
                with cx.fast():
                    for g in range(2):
                        op("pe", [t_xc], [tpG], lambda p: p.matmul(pG[:P, g * 128:g * 128 + P], lhsT=xc[:, 4 + g, cs], rhs=xc[:, 6 + g, cs], start=True, stop=True))
                    chk('u_pG')
                op("dve", [tpG, t_c], [K_("GM")], lambda v: v.tensor_tensor(out=A("GM")[:P, :, :P], in0=pG[:P, 0:256].rearrange("p (g t) -> p g t", g=2)[:, :, :P], in1=Ut[:P, :P].unsqueeze(1).broadcast_to([P, 2, P]), op=ALU.mult))
                RELH(tpG)
                chk('u_GM')
                for g in range(2):
                    op("dve", [K_("GM"), K_("LT")], [K_("MT")], lambda v: v.tensor_tensor(out=A("MT")[:P, g * 4:(g + 1) * 4, :P], in0=A("LT")[:P, g * 4:(g + 1) * 4, :P], in1=A("GM")[:P, g, :P].unsqueeze(1).broadcast_to([P, 4, P]), op=ALU.mult))
                chk('t_G')
                px, tpx = PSH(); pxb = px[:].bitcast(BF16)
                with cx.fast():
                    for j in range(4):
                        op("pe", [t_xc, t_c], [tpx], lambda p: p.transpose(out=pxb[:P, j * 128:(j + 1) * 128], in_=xc[:, j, cs], identity=identb[:]))
                    for g in range(2):
                        op("pe", [t_xc, t_c], [tpx], lambda p: p.transpose(out=pxb[:P, 512 + g * 128:512 + (g + 1) * 128], in_=xc[:, 4 + g, cs], identity=identb[:]))
                    chk('v_tr')
                op("act", [tpx], [K_("Btm")], lambda a: a.activation(out=A("Btm")[:P, :], in_=pxb[:P, 512:768], func=AF.Copy))
                op("act", [tpx], [K_("xtok")], lambda a: a.activation(out=A("xtok")[:P, :], in_=pxb[:P, 0:512], func=AF.Copy))
                RELH(tpx)
                chk('v_Btm')
                op("dve", [K_("xtok"), K_("dt")], [K_("xdt")], lambda v: v.tensor_tensor(out=A("xdt")[:P, :].rearrange("p (h d) -> p h d", h=8), in0=A("xtok")[:P, :].rearrange("p (h d) -> p h d", h=8), in1=A("dt")[:P, :].unsqueeze(2).broadcast_to([P, 8, 64]), op=ALU.mult))
                chk('v_xdt')
                op("dve", [K_("xtok"), t_lay], [K_("xsk")], lambda v: v.tensor_tensor(out=A("xsk")[:P, :].rearrange("p (h d) -> p h d", h=8), in0=A("xtok")[:P, :].rearrange("p (h d) -> p h d", h=8), in1=lay["dsk"][:P, :].unsqueeze(2).broadcast_to([P, 8, 64]), op=ALU.mult))
                yield
                chk('t_tr')
                py, tpy = PS(hold=True)
                with cx.fast():
                    for hd in range(8):
                        op("pe", [K_("MT"), K_("xdt")], [tpy], lambda p: p.matmul(py[:P, hd * 64:(hd + 1) * 64], lhsT=A("MT")[:P, hd, :P], rhs=A("xdt")[:P, hd * 64:(hd + 1) * 64], start=True, stop=True))
                pyo, tpyo = PS(hold=True)
                if sample:
                    pyo2, tpyo2 = PS(hold=True)
                if not sample:
                    op("act", [tpc], [K_("w2s")], lambda a: a.activation(out=A("w2s")[:, :], in_=pc[:, 0:8], func=AF.Copy))
                    op("dve", [tpc, K_("w2s")], [K_("t8")], lambda v: v.tensor_tensor(out=A("t8")[:, :], in0=pc[:, 8:16], in1=A("w2s")[:, :], op=ALU.subtract))
                    op("act", [K_("t8")], [K_("dend")], lambda a: a.activation(out=A("dend")[:, :], in_=A("t8")[:, :], func=AF.Exp))
                    op("act", [tpc], [K_("etot")], lambda a: a.activation(out=A("etot")[:, :], in_=pc[:, 8:16], func=AF.Exp))
                    RELH(tpc)
                    op("dve", [K_("dend"), K_("dt")], [K_("w2s")], lambda v: v.tensor_tensor(out=A("w2s")[:, :], in0=A("dend")[:, :], in1=A("dt")[:, :], op=ALU.mult))
                    op("dve", [K_("xtok"), K_("w2s")], [K_("xdd")], lambda v: v.tensor_tensor(out=A("xdd")[:, :].rearrange("p (h d) -> p h d", h=8), in0=A("xtok")[:, :].rearrange("p (h d) -> p h d", h=8), in1=A("w2s")[:, :].unsqueeze(2).broadcast_to([128, 8, 64]), op=ALU.mult))
                    for g in range(2):
                        op("pe", [t_xc, t_hTb], [tpyo], lambda p: p.matmul(pyo[:, g * 256:(g + 1) * 256], lhsT=xc[:, 6 + g, cs], rhs=hTb[:, g * 256:(g + 1) * 256], start=True, stop=True))
                    pst, tpst = PSH()
                    for g in range(2):
                        op("pe", [K_("Btm"), K_("xdd")], [tpst], lambda p: p.matmul(pst[:, g * 256:(g + 1) * 256], lhsT=A("Btm")[:, g * 128:(g + 1) * 128], rhs=A("xdd")[:, g * 256:(g + 1) * 256], start=True, stop=True))
                    op("dve", [t_hT, K_("etot")], [t_hT], lambda v: v.tensor_tensor(out=hT[:].rearrange("p (h d) -> p h d", h=8), in0=hT[:].rearrange("p (h d) -> p h d", h=8), in1=A("etot")[:, :].unsqueeze(2).broadcast_to([128, 8, 64]), op=ALU.mult))
                    op("dve", [t_hT, tpst], [t_hT], lambda v: v.tensor_tensor(out=hT[:], in0=hT[:], in1=pst[:, :], op=ALU.add))
                    RELH(tpst)
                    op("act", [t_hT], [t_hTb], lambda a: a.activation(out=hTb[:], in_=hT[:], func=AF.Copy))
                else:
                    op("pe", [K_("dta"), t_c], [tpc], lambda p: p.matmul(pc[:P, 8:16], lhsT=Ubd[:P, :P], rhs=A("dta")[:P, :], start=True, stop=False))
                    op("pe", [K_("dta"), t_c], [tpc], lambda p: p.matmul(pc[:P, 8:16], lhsT=SLbd[:P, :P], rhs=A("dta")[:P, :], start=False, stop=True))
                    op("act", [tpc], [K_("w2s")], lambda a: a.activation(out=A("w2s")[:P, :], in_=pc[:P, 0:8], func=AF.Copy))
                    op("dve", [tpc, K_("w2s")], [K_("t8")], lambda v: v.tensor_tensor(out=A("t8")[:P, :], in0=pc[:P, 8:16], in1=A("w2s")[:P, :], op=ALU.subtract))
                    op("act", [K_("t8")], [K_("dend")], lambda a: a.activation(out=A("dend")[:P, :], in_=A("t8")[:P, :], func=AF.Exp))
                    op("dve", [K_("dend"), K_("dt")], [K_("w2s")], lambda v: v.tensor_tensor(out=A("w2s")[:P, :], in0=A("dend")[:P, :], in1=A("dt")[:P, :], op=ALU.mult))
                    op("dve", [K_("xtok"), K_("w2s")], [K_("xdd")], lambda v: v.tensor_tensor(out=A("xdd")[:P, :].rearrange("p (h d) -> p h d", h=8), in0=A("xtok")[:P, :].rearrange("p (h d) -> p h d", h=8), in1=A("w2s")[:P, :].unsqueeze(2).broadcast_to([P, 8, 64]), op=ALU.mult))
                    for g in range(2):
                        op("dve", [t_xc, t_c], [t_CT], lambda v: v.tensor_tensor(out=CTm[:, :, g, :], in0=xc[:, 6 + g, :].unsqueeze(1).broadcast_to([128, NSS, 64]), in1=BMb[:].rearrange("p (b t) -> p b t", b=16)[:, :NSS, :], op=ALU.mult))
                    op("dve", [K_("dta")], [t_dE], lambda v: v.tensor_copy(out=dtaE[:].rearrange("p (h d) -> p h d", h=8), in_=A("dta")[:P, :].unsqueeze(2).broadcast_to([P, 8, 64])))
                    pcd, tpcd = PS()
                    for j in range(4):
                        op("pe", [t_dE, t_c], [tpcd], lambda p: p.matmul(pcd[:, j * 16:(j + 1) * 16], lhsT=dtaE[:, j * 128:(j + 1) * 128], rhs=BMt[:, :], start=True, stop=True))
                    op("act", [tpcd], [t_cd], lambda a: a.activation(out=cdT[:].rearrange("p j b -> p (j b)"), in_=pcd[:, 0:64], func=AF.Exp))
                    for b in range(NSS):
                        bl = b % 8
                        if bl == 0:
                            for b8 in range(8):
                                dma("sp", [], [t_Sn], Snat[:, b8, :, :], sssm[l, b + b8, :, :].rearrange("(j p) n -> p j n", p=128))
                        sb1, tsb1 = Sb1[b % 2], t_Sb1[b % 2]
                        stb, tstb = STb1[b % 2], t_ST1[b % 2]
                        op("act", [t_Sn], [tsb1], lambda a: a.activation(out=sb1[:], in_=Snat[:, bl, :, :], func=AF.Copy))
                        pt_, tpt = PS(); ptb = pt_[:].bitcast(BF16)
                        for j in range(4):
                            op("pe", [tsb1, t_c], [tpt], lambda p: p.transpose(out=ptb[:, j * 128:(j + 1) * 128], in_=sb1[:, j, :], identity=identb[:]))
                        op("act", [tpt], [tstb], lambda a: a.activation(out=stb[:, :], in_=ptb[:, 0:512], func=AF.Copy))
                        for g in range(2):
                            pq_, tq_ = (pyo, tpyo) if g == 0 else (pyo2, tpyo2)
                            op("pe", [t_CT, tstb], [tq_], lambda p: p.matmul(pq_[:P, 0:256], lhsT=CTm[:, b, g, :], rhs=stb[:, g * 256:(g + 1) * 256], start=(b == 0), stop=(b == NSS - 1)))
                        xm, txm = xdm[b % 2], t_xdm[b % 2]
                        op("dve", [K_("xdd"), t_c], [txm], lambda v: v.tensor_scalar(out=xm[:, :], in0=A("xdd")[:P, :], scalar1=BMt[:, b:b + 1], scalar2=None, op0=ALU.mult))
                        pst, tpst = PS()
                        for j in range(4):
                            op("pe", [txm, K_("Btm")], [tpst], lambda p: p.matmul(pst[:, j * 128:(j + 1) * 128], lhsT=xm[:, j * 128:(j + 1) * 128], rhs=A("Btm")[:P, (j // 2) * 128:(j // 2 + 1) * 128], start=True, stop=True))
                        sn, tsn = Snew[b % 2], t_Snew[b % 2]
                        for j in range(4):
                            op("dve", [t_Sn, t_cd, tpst], [tsn], lambda v: v.scalar_tensor_tensor(out=sn[:, j, :], in0=Snat[:, bl, j, :], scalar=cdT[:, j, b:b + 1], in1=pst[:, j * 128:(j + 1) * 128], op0=ALU.mult, op1=ALU.add))
                        dma("sp", [tsn], [], ossm_s[l, b, :, :].rearrange("(j p) n -> p j n", p=128), sn[:])
                yield
                chk('t_ssd')
                if not sample:
                    op("dve", [tpyo, K_("ecum")], [K_("y1")], lambda v: v.tensor_tensor(out=A("y1")[:P, :].rearrange("p (h d) -> p h d", h=8), in0=pyo[:P, :].rearrange("p (h d) -> p h d", h=8), in1=A("ecum")[:P, :].unsqueeze(2).broadcast_to([P, 8, 64]), op=ALU.mult))
                else:
                    for g, (pq_, tq_) in enumerate(((pyo, tpyo), (pyo2, tpyo2))):
                        op("dve", [tq_, K_("ecum")], [K_("y1")], lambda v: v.tensor_tensor(out=A("y1")[:P, g * 256:(g + 1) * 256].rearrange("p (h d) -> p h d", h=4), in0=pq_[:P, 0:256].rearrange("p (h d) -> p h d", h=4), in1=A("ecum")[:P, g * 4:(g + 1) * 4].unsqueeze(2).broadcast_to([P, 4, 64]), op=ALU.mult))
                op("dve", [K_("y1"), tpy], [K_("y1")], lambda v: v.tensor_tensor(out=A("y1")[:P, :], in0=A("y1")[:P, :], in1=py[:P, :], op=ALU.add))
                op("dve", [K_("y1"), K_("xsk")], [K_("y1")], lambda v: v.tensor_tensor(out=A("y1")[:P, :], in0=A("y1")[:P, :], in1=A("xsk")[:P, :], op=ALU.add))
                REL(tpy); REL(tpyo)
                if sample:
                    REL(tpyo2)
                op("dve", [K_("y1"), K_("sz")], [K_("y1")], lambda v: v.tensor_tensor(out=A("y1")[:P, :], in0=A("y1")[:P, :], in1=A("sz")[:P, :], op=ALU.mult))
                op("dve", [K_("y1")], [K_("ysq")], lambda v: v.tensor_tensor(out=A("ysq")[:P, :], in0=A("y1")[:P, :], in1=A("y1")[:P, :], op=ALU.mult))
                op("dve", [K_("ysq")], [K_("ss2")], lambda v: v.reduce_sum(out=A("ss2")[:P, :], in_=A("ysq")[:P, :].rearrange("p (g d) -> p g d", g=2), axis=AX.X))
                op("act", [K_("ss2")], [K_("ss2")], lambda a: a.activation(out=A("ss2")[:P, :], in_=A("ss2")[:P, :], func=AF.Ln, bias=256.0 * EPS, scale=1.0))
                op("act", [K_("ss2")], [K_("ss2")], lambda a: a.activation(out=A("ss2")[:P, :], in_=A("ss2")[:P, :], func=AF.Exp, scale=-0.5))
                op("dve", [K_("y1"), K_("ss2")], [K_("y1")], lambda v: v.tensor_tensor(out=A("y1")[:P, :].rearrange("p (g d) -> p g d", g=2), in0=A("y1")[:P, :].rearrange("p (g d) -> p g d", g=2), in1=A("ss2")[:P, :].unsqueeze(2).broadcast_to([P, 2, 256]), op=ALU.mult))
                op("dve", [K_("y1"), t_lay], [K_("ytm")], lambda v: v.scalar_tensor_tensor(out=A("ytm")[:P, :], in0=A("y1")[:P, :], scalar=16.0, in1=gssm[:P, :], op0=ALU.mult, op1=ALU.mult))
                yield
                pyt, tpyt = PSH(); pytb = pyt[:].bitcast(BF16)
                with cx.fast():
                    for j in range(4):
                        op("pe", [K_("ytm"), t_c], [tpyt], lambda p: p.transpose(out=pytb[:, j * 128:j * 128 + P], in_=A("ytm")[:P, j * 128:(j + 1) * 128], identity=identb[:P, :P]))
                op("act", [tpyt], [t_mixT], lambda a: a.activation(out=mixT[:, :, cs], in_=pytb[:, 0:512].rearrange("p (j t) -> p j t", j=4)[:, :, :P], func=AF.Copy))
                RELH(tpyt)

            def tile_attn(ti):
                c0 = ti * NTT
                cs = slice(c0, c0 + P)
                first_tile = first_group and ti == 0 and not sample
                chk('t_y')
                if not sample:
                    for g in range(2):
                        bs = slice(64 * g, 64 * g + 64)
                        kbs = [1] if first_tile else [0, 1]
                        for kb in kbs:
                            kcols = slice(c0 + kb * 128, c0 + kb * 128 + 128)
                            pS, tpS = PSH()
                            for hh in range(4):
                                op("pe", [t_kn, t_qn], [tpS], lambda p: p.matmul(pS[:, hh * 128:(hh + 1) * 128], lhsT=kn[bs, kcols], rhs=qn[bs, hh, cs], start=True, stop=True))
                            pe_, tpe = st["pe%d" % kb]; PT_, tPT = st["PT%d" % kb]
                            op("act", [tpS], [tpe], lambda a: a.activation(out=pe_[:], in_=pS[:], func=AF.Exp, scale=0.125))
                            RELH(tpS)
                            Et = Eprev if kb == 0 else Ecur
                            op("dve", [tpe, t_c], [tPT], lambda v: v.tensor_tensor(out=PT_[:], in0=pe_[:], in1=Et[:, g * 512:(g + 1) * 512], op=ALU.mult))
                        chk('a_S')
                        yield
                        po, tpo = PSH(); pdn, tpdn = PSH()
                        for ii, kb in enumerate(kbs):
                            PT_, tPT = st["PT%d" % kb]
                            op("pe", [t_vb, tPT], [tpo], lambda p: p.matmul(po[:64, :], lhsT=vb[:, ti + kb, g * 64:(g + 1) * 64], rhs=PT_[:], start=(ii == 0), stop=(ii == len(kbs) - 1)))
                        chk('a_po')
                        for ii, kb in enumerate(kbs):
                            PT_, tPT = st["PT%d" % kb]
                            op("pe", [t_c, tPT], [tpdn], lambda p: p.matmul(pdn[:64, :], lhsT=onesb[:, 0:64], rhs=PT_[:], start=(ii == 0), stop=(ii == len(kbs) - 1)))
                        chk('a_pdn')
                        op("dve", [tpdn, t_lay], [K_("den")], lambda v: v.tensor_tensor(out=A("den")[:, :], in0=pdn[:64, :], in1=esinkrow[:, g * 512:(g + 1) * 512], op=ALU.add))
                        RELH(tpdn)
                        op("act", [K_("den")], [K_("den")], lambda a: a.activation(out=A("den")[:, :], in_=A("den")[:, :], func=AF.Ln))
                        op("act", [K_("den")], [K_("den")], lambda a: a.activation(out=A("den")[:, :], in_=A("den")[:, :], func=AF.Exp, scale=-1.0))
                        op("dve", [tpo, K_("den")], [t_oT], lambda v: v.tensor_tensor(out=oT[:, g * 4:(g + 1) * 4, cs], in0=po[:64, :].rearrange("p (h q) -> p h q", h=4), in1=A("den")[:, :].rearrange("p (h q) -> p h q", h=4), op=ALU.mult))
                        RELH(tpo)
                        yield
                else:
                    for b4 in range(0, NSS, 4):
                        pt_, tpt = PS(); ptb = pt_[:].bitcast(BF16)
                        for bb in range(4):
                            op("pe", [t_kcb, t_c], [tpt], lambda p: p.transpose(out=ptb[:, bb * 128:(bb + 1) * 128], in_=kcb[:, b4 + bb, :], identity=identb[:]))
                        op("act", [tpt], [t_KcT], lambda a: a.activation(out=KcT[:, b4:b4 + 4, :], in_=ptb[:, 0:512].rearrange("p (b i) -> p b i", b=4), func=AF.Copy))
                    pSc, tpSc = PS()
                    for b in range(NSS):
                        for g in range(2):
                            bs = slice(64 * g, 64 * g + 64)
                            for hh in range(4):
                                idx = (b * 8 + g * 4 + hh) * 4
                                op("pe", [t_KcT, t_qn], [tpSc], lambda p: p.matmul(pSc[:, idx:idx + 4], lhsT=KcT[bs, b, :], rhs=qn[bs, hh, b * 4:b * 4 + 4], start=True, stop=True))
                    pe_, tpe = st["pe0"]
                    op("act", [tpSc], [tpe], lambda a: a.activation(out=pe_[:, :NSS * 32], in_=pSc[:, :NSS * 32], func=AF.Exp, scale=0.125))
                    op("dve", [tpe, t_c], [t_PTc], lambda v: v.tensor_tensor(out=PTc[:, :NSS * 32].rearrange("p (b x) -> p b x", b=NSS), in0=pe_[:, :NSS * 32].rearrange("p (b x) -> p b x", b=NSS), in1=Ecache[:, :].unsqueeze(1).broadcast_to([128, NSS, 32]), op=ALU.mult))
                    pSn, tpSn = PS()
                    for g in range(2):
                        bs = slice(64 * g, 64 * g + 64)
                        for hh in range(4):
                            op("pe", [t_kn, t_qn], [tpSn], lambda p: p.matmul(pSn[:P, (g * 4 + hh) * 64:(g * 4 + hh) * 64 + P], lhsT=kn[bs, 0:P], rhs=qn[bs, hh, 0:P], start=True, stop=True))
                    pe1_, tpe1 = st["pe1"]; PT1_, tPT1 = st["PT1"]
                    op("act", [tpSn], [tpe1], lambda a: a.activation(out=pe1_[:P, :], in_=pSn[:P, :], func=AF.Exp, scale=0.125))
                    for g in range(2):
                        ov = PT1_[:P, g * 256:(g + 1) * 256].rearrange("p (b hh t) -> p hh b t", b=16, hh=4)
                        i0 = pe1_[:P, g * 256:(g + 1) * 256].rearrange("p (hh b t) -> p hh b t", hh=4, b=16)
                        i1 = Enew[:P, g * 256:(g + 1) * 256].rearrange("p (hh b t) -> p hh b t", hh=4, b=16)
                        op("dve", [tpe1, t_c], [tPT1], lambda v: v.tensor_tensor(out=ov, in0=i0, in1=i1, op=ALU.mult))
                    po, tpo = PS(); pdn, tpdn = PS()
                    for (pp_, tpp, use_v) in ((po, tpo, True), (pdn, tpdn, False)):
                        for g in range(2):
                            lw = vb[:P, 1, g * 64:(g + 1) * 64] if use_v else onesb[:P, 0:64]
                            op("pe", [t_vb, tPT1, t_c], [tpp], lambda p: p.matmul(pp_[:64, g * 256:(g + 1) * 256], lhsT=lw, rhs=PT1_[:P, g * 256:(g + 1) * 256], start=True, stop=False))
                            for b in range(NSS):
                                lw2 = vcb[:, b, g * 64:(g + 1) * 64] if use_v else onesb[:, 0:64]
                                op("pe", [t_vcb, t_PTc, t_c], [tpp], lambda p: p.matmul(pp_[:64, g * 256 + b * 16:g * 256 + b * 16 + 16], lhsT=lw2, rhs=PTc[:, b * 32 + g * 16:b * 32 + g * 16 + 16], start=False, stop=(b == NSS - 1)))
                    for g in range(2):
                        dv = A("den")[:, g * 256:(g + 1) * 256].rearrange("p (b hh t) -> p hh b t", b=16, hh=4)
                        ek = esinkrow[:, :].rearrange("p (h q) -> p h q", h=8)[:, g * 4:(g + 1) * 4, 0:64].rearrange("p hh (b t) -> p hh b t", t=4)
                        op("dve", [tpdn, t_lay], [K_("den")], lambda v: v.tensor_tensor(out=dv, in0=pdn[:64, g * 256:(g + 1) * 256].rearrange("p (b hh t) -> p hh b t", b=16, hh=4), in1=ek, op=ALU.add))
                        op("dve", [K_("den")], [K_("den")], lambda v: v.reciprocal(out=A("den")[:, g * 256:(g + 1) * 256], in_=A("den")[:, g * 256:(g + 1) * 256]))
                        op("dve", [tpo, K_("den")], [t_oT], lambda v: v.tensor_tensor(out=oT[:, g * 4:(g + 1) * 4, 0:64].rearrange("p hh (b t) -> p hh b t", t=4), in0=po[:64, g * 256:(g + 1) * 256].rearrange("p (b hh t) -> p hh b t", b=16, hh=4), in1=dv, op=ALU.mult))

                yield

            for ti in range(ntile):
                if sample:
                    for _ in tile_ssd(ti):
                        pass
                    for _ in tile_attn(ti):
                        pass
                else:
                    gens = [tile_ssd(ti), tile_attn(ti)]
                    while gens:
                        for g_ in list(gens):
                            try:
                                next(g_)
                            except StopIteration:
                                gens.remove(g_)
            chk('m_tiles')
            if sample or last_group:
                ktm = SB(sc, "ktm", [128, 128]); t_ktm = Tok()
                pk, tpk = PS()
                lastc = slice(NT - P, NT)
                op("pe", [t_knf, t_c], [tpk], lambda p: p.transpose(out=pk[:P, 0:128], in_=knf[:, lastc], identity=ident[:]))
                op("act", [tpk], [t_ktm], lambda a: a.activation(out=ktm[:P, :], in_=pk[:P, 0:128], func=AF.Copy))
                if sample:
                    for b in range(NSS):
                        dma("sp", [t_ktm], [], ock_s[l, b, 124:128, :], ktm[b * 4:(b + 1) * 4, :])
                        dma("sp", [t_vf], [], ocv_s[l, b, 124:128, :], vf[b * 4:(b + 1) * 4, :])
                else:
                    dma("sp", [t_ktm], [], ock_p[l, :, :], ktm[:, :])
                    dma("sp", [t_vf], [], ocv_p[l, :, :], vf[:, :])
                    hTo = SB(sc, "hTo", [128, 4, 128]); t_hTo = Tok()
                    ph, tph = PS()
                    for j in range(4):
                        op("pe", [t_hT, t_c], [tph], lambda p: p.transpose(out=ph[:, j * 128:(j + 1) * 128], in_=hT[:, j * 128:(j + 1) * 128], identity=ident[:]))
                    op("act", [tph], [t_hTo], lambda a: a.activation(out=hTo[:].rearrange("p j n -> p (j n)"), in_=ph[:, :], func=AF.Copy))
                    dma("sp", [t_hTo], [], ossm_p[l, :, :].rearrange("(j p) n -> p j n", p=128), hTo[:])
            if not sample:
                op("dve", [t_kn], [t_kvh], lambda v: v.tensor_copy(out=khalo[:], in_=kn[:, NT:NT + 128]))
                op("dve", [t_vb], [t_kvh], lambda v: v.tensor_copy(out=vhalo[:], in_=vb[:, ntile, :]))

            chk('m_outs')
            woT = wreg[:, 0:4096].rearrange("p (k c) -> p k c", k=4)
            woA = wreg[:64, 4096:4096 + 8192].rearrange("p (k c) -> p k c", k=8)
            if cached:
                dma("pool", [t_scm[1]], T, wreg[:, 0:4096], wscm[1, :, 0:4096])
                dma("pool", [t_scm[1]], T, wreg[:64, 4096:12288], wscm[1, :64, 4096:12288])
            else:
                dma("pool", [], T, woT, w_out[l, 0:512, :].rearrange("(k p) c -> p k c", p=128))
                dma("pool", [], T, woA, w_out[l, 512:1024, :].rearrange("(hd p) c -> p hd c", p=64))
                dma("sp", T, [t_scm[1]], wscm[1, :, 0:4096], wreg[:, 0:4096])
                dma("sp", T, [t_scm[1]], wscm[1, :64, 4096:12288], wreg[:64, 4096:12288])
            with cx.fast():
                for d in range(8):
                    for (n0, nw) in nblocks(NT):
                        ps, tp = PS()
                        for k in range(4):
                            op("pe", T + [t_mixT], [tp], lambda p: p.matmul(ps[:, :nw], lhsT=woT[:, k, d * 128:(d + 1) * 128], rhs=mixT[:, k, n0:n0 + nw], start=(k == 0), stop=False))
                        for hd in range(8):
                            op("pe", T + [t_oT], [tp], lambda p: p.matmul(ps[:, :nw], lhsT=woA[:, hd, d * 128:(d + 1) * 128], rhs=oT[:, hd, n0:n0 + nw], start=False, stop=(hd == 7)))
                        op("dve", [tp, t_h], [t_h], lambda v: v.tensor_tensor(out=h[:, d, n0:n0 + nw], in0=h[:, d, n0:n0 + nw], in1=ps[:, :nw], op=ALU.add))

        def ple(sc, h, t_h, NT, l, psrc, ntok_tile, cached):
            xn = SB(sc, "xn", [128, 8, NT], BF16); t_xn = [Tok() for _ in range(8)]
            norm(sc, h, t_h, NT, 3, l, xn, t_xn)
            wg = wreg[:, 0:8192].rearrange("p (k c) -> p k c", k=8)
            wp = wreg[:, 8192:8192 + 2048].rearrange("p (k c) -> p k c", k=2)
            T = [t_wreg]
            if cached:
                dma("pool", [t_scm[2]], T, wreg[:, 0:10240], wscm[2, :, 0:10240])
            else:
                dma("pool", [], T, wg, w_pg[l, :, :].rearrange("(k p) c -> p k c", p=128))
                dma("pool", [], T, wp, w_pp[l, :, :].rearrange("(k p) c -> p k c", p=128))
                dma("sp", T, [t_scm[2]], wscm[2, :, 0:10240], wreg[:, 0:10240])
            peT = SB(sc, "peT", [128, 2, NT], BF16); t_peT = Tok()
            ptm = [SB(sc, "ptm%d" % i, [128, 256]) for i in range(2)]; t_ptm = [Tok(), Tok()]
            P = ntok_tile
            for ti in range(NT // P):
                pt_, tpt_ = ptm[ti % 2], t_ptm[ti % 2]
                dma("sp", [], [tpt_], pt_[:P, :], psrc[ti * P:(ti + 1) * P, :])
                ps, tp = PS()
                for j in range(2):
                    op("pe", [tpt_, t_c], [tp], lambda p: p.transpose(out=ps[:, j * 128:j * 128 + P], in_=pt_[:P, j * 128:(j + 1) * 128], identity=ident[:P, :P]))
                op("act", [tp], [t_peT], lambda a: a.activation(out=peT[:, :, ti * P:(ti + 1) * P], in_=ps[:, 0:256].rearrange("p (j t) -> p j t", j=2)[:, :, :P], func=AF.Copy))
            with cx.fast():
                sg = SB(sc, "sg", [128, 512]); t_sg = Tok()
                for d in range(8):
                    for (n0, nw) in nblocks(NT):
                        pg, tpg = PS(); pq, tpq = PS()
                        for k in range(8):
                            op("pe", T + [t_xn[k]], [tpg], lambda p: p.matmul(pg[:, :nw], lhsT=wg[:, k, d * 128:(d + 1) * 128], rhs=xn[:, k, n0:n0 + nw], start=(k == 0), stop=(k == 7)))
                        for k in range(2):
                            op("pe", T + [t_peT], [tpq], lambda p: p.matmul(pq[:, :nw], lhsT=wp[:, k, d * 128:(d + 1) * 128], rhs=peT[:, k, n0:n0 + nw], start=(k == 0), stop=(k == 1)))
                        op("act", [tpg], [t_sg], lambda a: a.activation(out=sg[:, :nw], in_=pg[:, :nw], func=AF.Sigmoid))
                        op("dve", [t_sg, tpq], [t_sg], lambda v: v.tensor_tensor(out=sg[:, :nw], in0=sg[:, :nw], in1=pq[:, :nw], op=ALU.mult))
                        op("dve", [t_sg, t_h], [t_h], lambda v: v.tensor_tensor(out=h[:, d, n0:n0 + nw], in0=h[:, d, n0:n0 + nw], in1=sg[:, :nw], op=ALU.add))

        stage = [0]

        def chk(name):
            stage[0] += 1
            if STAGE_LIMIT is not None and stage[0] >= STAGE_LIMIT:
                if not cx.stopped:
                    print("STOP at stage", stage[0], name)
                cx.stopped = True

        try:
          _main_body = True
          with contextlib.ExitStack() as sc:
              xtm0 = SB(sc, "xtm", [128, 1024])
              transpose_in(xtm0, Tok(), xs[:, :], NS, hs, t_hs, 0)
              cx.barrier()
          chk('sample_load')
          for l in range(DEPTH):
              load_layer_consts(l)
              chk('layer_consts')
              op("dve", [], [t_hT], lambda v: v.memset(hT[:], 0.0))
              op("dve", [], [t_hTb], lambda v: v.memset(hTb[:], 0.0))
              op("dve", [], [t_xhalo], lambda v: v.memset(xhalo[:], 0.0))
              op("dve", [], [t_kvh], lambda v: v.memset(khalo[:], 0.0))
              op("dve", [], [t_kvh], lambda v: v.memset(vhalo[:], 0.0))
              for gi in range(NG + 1):
                  sample = gi == NG
                  NT = NS if sample else NTG
                  with contextlib.ExitStack() as gsc:
                      if sample:
                          h, t_h = hs, t_hs
                      else:
                          h = SB(gsc, "hgrp", [128, 8, NTG]); t_h = Tok()
                          if l == 0:
                              with contextlib.ExitStack() as sc:
                                  xtms = [(SB(sc, "xtm", [128, 1024]), Tok()) for _ in range(2)]
                                  for ti in range(NTG // 128):
                                      transpose_in(xtms[ti % 2][0], xtms[ti % 2][1], xp[gi * NTG + ti * 128:gi * NTG + (ti + 1) * 128, :], 128, h, t_h, ti * 128)
                                  cx.barrier()
                          else:
                              dma("sp", [t_hscr[gi]], [t_h], h[:], hscr[:, :, gi * NTG:(gi + 1) * NTG].rearrange("j p t -> p j t"))
                      cx.barrier()
                      chk('group_load')
                      with contextlib.ExitStack() as sc:
                          xn = SB(sc, "xn", [128, 8, NT], BF16); t_xn = [Tok() for _ in range(8)]
                          with cx.fast():
                              norm(sc, h, t_h, NT, 0, l, xn, t_xn)
                              chk('norm')
                              ffn(sc, h, t_h, NT, xn, t_xn, w1a, w3a, w2a, l, 0, gi > 0)
                          cx.barrier()
                          chk('ffn_a')
                      with contextlib.ExitStack() as sc:
                          mixer(sc, h, t_h, NT, l, sample, gi == 0, gi == NG - 1, gi > 0)
                          cx.barrier()
                          chk('mixer')
                      with contextlib.ExitStack() as sc:
                          xn = SB(sc, "xn", [128, 8, NT], BF16); t_xn = [Tok() for _ in range(8)]
                          with cx.fast():
                              norm(sc, h, t_h, NT, 2, l, xn, t_xn)
                              ffn(sc, h, t_h, NT, xn, t_xn, w1b, w3b, w2b, l, 1, gi > 0)
                          cx.barrier()
                          chk('ffn_b')
                      with contextlib.ExitStack() as sc:
                          if sample:
                              ple(sc, h, t_h, NT, l, psm[l, :, :], 64, gi > 0)
                          else:
                              ple(sc, h, t_h, NT, l, pp[l, gi * NTG:(gi + 1) * NTG, :], 128, gi > 0)
                          cx.barrier()
                      if not sample:
                          if l == 0:
                              dma("sp", [t_h], [t_hscr[gi]], hscr[:, :, gi * NTG:(gi + 1) * NTG].rearrange("j p t -> p j t"), h[:])
                          else:
                              with contextlib.ExitStack() as sc:
                                  ytms = [(SB(sc, "ytm", [128, 1024]), Tok()) for _ in range(2)]
                                  for ti in range(NTG // 128):
                                      transpose_out(ytms[ti % 2][0], ytms[ti % 2][1], h, t_h, ti * 128, 128, yp[gi * NTG + ti * 128:gi * NTG + (ti + 1) * 128, :])
                                  cx.barrier()
                      elif l == DEPTH - 1:
                          with contextlib.ExitStack() as sc:
                              ytm0 = SB(sc, "ytm", [128, 1024])
                              transpose_out(ytm0, Tok(), h, t_h, 0, NS, ys[:, :])
                              cx.barrier()
                      cx.barrier()
        except _Stop:
            pass
        cx.finish()
    return nc


def make_consts():
    c = {}
    c["c_ident"] = np.eye(128, dtype=np.float32)
    i = np.arange(128)
    c["c_U"] = (i[:, None] <= i[None, :]).astype(np.float32)
    c["c_SL"] = (i[:, None] > i[None, :]).astype(np.float32)
    j = np.arange(64); same = (j[:, None] // 4) == (j[None, :] // 4)
    c["c_Ubd"] = (same & (j[:, None] <= j[None, :])).astype(np.float32)
    c["c_SLbd"] = (same & (j[:, None] > j[None, :])).astype(np.float32)
    c["c_BMt"] = ((j[:, None] // 4) == np.arange(16)[None, :]).astype(np.float32)
    bm = ((np.arange(16)[:, None]) == (j[None, :] // 4)).astype(np.float32)
    c["c_BM"] = np.broadcast_to(bm.reshape(1, 16 * 64), (128, 16 * 64)).copy()
    bo = np.zeros((128, 128), np.float32); bo[:64, :64] = 1; bo[64:, 64:] = 1
    c["c_bones"] = bo
    slopes = np.power(np.float32(2.0), -8.0 * np.arange(1, 9, dtype=np.float32) / 8).astype(np.float32)
    s = i[:, None, None]; q = i[None, None, :]; sl = slopes[None, :, None]
    ecur = np.where(q >= s, np.exp(-sl * (q - s).astype(np.float32)), 0.0)
    eprev = np.where(s > q, np.exp(-sl * (q - s + 128).astype(np.float32)), 0.0)
    c["c_Ecur"] = ecur.astype(np.float32).reshape(128, 1024)
    c["c_Eprev"] = eprev.astype(np.float32).reshape(128, 1024)
    t = np.arange(4)[None, None, :]
    ecache = np.where(s > t, np.exp(-sl * (128 + t - s).astype(np.float32)), 0.0)
    c["c_Ecache"] = ecache.astype(np.float32).reshape(128, 32)
    sj = j[:, None, None]; qj = j[None, None, :]
    enew = np.where(((sj // 4) == (qj // 4)) & (sj <= qj), np.exp(-sl * (qj - sj).astype(np.float32)), 0.0)
    c["c_Enew"] = enew.astype(np.float32).reshape(64, 512)
    return c


_WNAMES = ["g_ffn1", "w1_a", "w3_a", "w2_a", "g_mix", "w_in", "conv_w", "conv_b", "dt_bias", "a_log", "d_skip", "ssm_norm",
           "q_norm", "k_norm", "sinks", "w_out", "g_ffn2", "w1_b", "w3_b", "w2_b", "g_ple", "w_ple_gate", "w_ple_proj"]


def run(inputs, SEQ, NSS, NTG, n_prompt, ncores):
    f = lambda a: np.ascontiguousarray(np.asarray(a, dtype=np.float32))
    nc = build(SEQ, NSS, NTG)
    consts = make_consts()
    wts = {n: f(inputs[n]) for n in _WNAMES}
    xpr = f(inputs["x_prompt"]); ppr = f(inputs["p_prompt"]); xsm = f(inputs["x_sample"]); psm = f(inputs["p_sample"])
    sssm = f(inputs["state_ssm"]); sconv = f(inputs["state_conv"]); ck = f(inputs["cache_k_win"]); cv = f(inputs["cache_v_win"])
    in_maps = []
    for c in range(ncores):
        b = c % n_prompt
        bs = slice(c * NSS, (c + 1) * NSS)
        m = dict(wts); m.update(consts)
        m["xp"] = f(xpr[b]); m["pp"] = f(ppr[:, b])
        m["xs"] = f(xsm[bs].reshape(NSS * 4, D)); m["psm"] = f(psm[:, bs].reshape(DEPTH, NSS * 4, DPLE))
        m["sssm"] = f(sssm[:, bs].reshape(DEPTH, NSS, 512, 128)); m["sconv"] = f(sconv[:, bs].reshape(DEPTH, NSS * 3, 1024))
        m["ck"] = f(ck[:, bs].reshape(DEPTH, NSS, 128, 128)); m["cv"] = f(cv[:, bs].reshape(DEPTH, NSS, 128, 128))
        in_maps.append(m)
    res = run_bass_kernel_spmd(nc, in_maps, core_ids=list(range(ncores))).results
    P = n_prompt
    y_p = np.stack([res[b]["yp"] for b in range(P)])
    y_s = np.concatenate([res[c]["ys"].reshape(NSS, 4, D) for c in range(ncores)])
    ssm_p = np.stack([res[b]["ossm_p"].reshape(DEPTH, 8, 64, 128) for b in range(P)], axis=1)
    conv_p = np.stack([res[b]["oconv_p"] for b in range(P)], axis=1)
    k_p = np.stack([res[b]["ock_p"].reshape(DEPTH, 128, 2, 64) for b in range(P)], axis=1)
    v_p = np.stack([res[b]["ocv_p"].reshape(DEPTH, 128, 2, 64) for b in range(P)], axis=1)
    ssm_s = np.concatenate([res[c]["ossm_s"].reshape(DEPTH, NSS, 8, 64, 128) for c in range(ncores)], axis=1)
    conv_s = np.concatenate([res[c]["oconv_s"].reshape(DEPTH, NSS, 3, 1024) for c in range(ncores)], axis=1)
    k_s = np.concatenate([res[c]["ock_s"].reshape(DEPTH, NSS, 128, 2, 64) for c in range(ncores)], axis=1)
    v_s = np.concatenate([res[c]["ocv_s"].reshape(DEPTH, NSS, 128, 2, 64) for c in range(ncores)], axis=1)
    return tuple(np.ascontiguousarray(a, dtype=np.float32) for a in (y_p, y_s, ssm_p, conv_p, k_p, v_p, ssm_s, conv_s, k_s, v_s))


def kernel(**inputs):
    return run(inputs, SEQ=4096, NSS=16, NTG=512, n_prompt=4, ncores=NCORES)
```

```python
import contextlib
import numpy as np
import concourse.bass as bass
import concourse.mybir as mybir
from concourse.bass_utils import run_bass_kernel_spmd

F32 = mybir.dt.float32
BF16 = mybir.dt.bfloat16
AF = mybir.ActivationFunctionType
ALU = mybir.AluOpType
AX = mybir.AxisListType

D = 1024; DFF = 2752; DPROJ = 2312; DPLE = 256; DEPTH = 2
NCORES = 8
EPS = 1e-6
FT = [(i * 128, 128) for i in range(21)] + [(2688, 64)]
NFT = len(FT)
FCH = [list(range(i, min(i + 2, NFT))) for i in range(0, NFT, 2)]
W2CH = [list(range(i, min(i + 6, NFT))) for i in range(0, NFT, 6)]


DEBUG_MAP = None
SBUF_PEAK = [0, 0]
STAGE_LIMIT = None


class _Stop(Exception):
    pass


class Tok:
    __slots__ = ("w", "r")

    def __init__(self):
        self.w = None
        self.r = {}


class Eng:
    def __init__(self, name, h):
        self.name = name; self.h = h; self.sem = None; self.cnt = 0; self.waited = {}; self.own = set()


def _r32(n):
    return 32 if n <= 32 else (64 if n <= 64 else 128)


class PEProxy:
    def __init__(self, ctx, e):
        self.ctx = ctx; self.e = e; self.last = None

    def _mode(self, key):
        e = self.e
        if key != self.last and e.cnt > 0:
            k = id(e.sem)
            if e.waited.get(k, 0) < e.cnt:
                e.h.wait_ge(e.sem, e.cnt)
                e.waited[k] = e.cnt
        self.last = key

    def matmul(self, out, lhsT, rhs, start=True, stop=True):
        self._mode(("mm", str(lhsT.dtype), _r32(lhsT.shape[0]), _r32(int(np.prod(lhsT.shape[1:]))), out.base_partition()))
        return self.e.h.matmul(out, lhsT=lhsT, rhs=rhs, start=start, stop=stop)

    def transpose(self, out, in_, identity):
        self._mode(("tr", str(in_.dtype), _r32(in_.shape[0]), _r32(int(np.prod(in_.shape[1:]))), out.base_partition()))
        return self.e.h.transpose(out=out, in_=in_, identity=identity)


class Ctx:
    EPOCH = 12000

    def __init__(self, nc, es):
        self.nc = nc; self.es = es
        self.E = {"pe": Eng("pe", nc.tensor), "act": Eng("act", nc.scalar), "dve": Eng("dve", nc.vector),
                  "pool": Eng("pool", nc.gpsimd), "sp": Eng("sp", nc.sync)}
        self.nsem = 0
        for e in self.E.values():
            e.sem = self._newsem(); e.own.add(id(e.sem))
        self.slots = {"sp": [[self._newsem(), 0] for _ in range(10)],
                      "pool": [[self._newsem(), 0] for _ in range(10)]}
        self.slot_i = {"sp": 0, "pool": 0}
        self.semkey = {}
        self.stopped = False
        self.pe_proxy = PEProxy(self, self.E["pe"])
        self.pe_fast = False

    @contextlib.contextmanager
    def fast(self):
        old = self.pe_fast
        self.pe_fast = True; self.pe_proxy.last = "edge"
        try:
            yield
        finally:
            self.pe_fast = old; self.pe_proxy.last = "edge"

    def _newsem(self):
        self.nsem += 1
        return self.es.enter_context(self.nc.semaphore("s%d" % self.nsem))

    def _wait(self, e, ev):
        sem, val = ev
        k = id(sem)
        if e.name == "pe" and k in e.own and self.pe_fast:
            return
        if e.waited.get(k, 0) >= val:
            return
        e.h.wait_ge(sem, val)
        e.waited[k] = val

    def _sync(self, e, reads, writes):
        for t in reads:
            if t.w is not None:
                self._wait(e, t.w)
        for t in writes:
            if t.w is not None:
                self._wait(e, t.w)
            for ev in t.r.values():
                self._wait(e, ev)

    def _commit(self, ev, reads, writes):
        for t in writes:
            t.w = ev; t.r = {}
        for t in reads:
            k = id(ev[0])
            if k not in t.r or t.r[k][1] < ev[1]:
                t.r[k] = ev

    def op(self, eng, reads, writes, fn):
        if self.stopped:
            return None
        e = self.E[eng]
        self._sync(e, reads, writes)
        if e.cnt >= self.EPOCH:
            e.sem = self._newsem(); e.cnt = 0; e.own.add(id(e.sem))
        inst = fn(self.pe_proxy if eng == "pe" else e.h)
        e.cnt += 1
        inst.then_inc(e.sem, 1)
        if DEBUG_MAP is not None:
            import traceback
            nm = None
            for a in ("name", "inst", "instruction", "ins"):
                v = getattr(inst, a, None)
                if v is not None:
                    nm = getattr(v, "name", v) if a != "name" else v
                    break
            fr = traceback.extract_stack(limit=4)[-2]; fr0 = traceback.extract_stack(limit=4)[-3]
            DEBUG_MAP[str(nm)] = "%s:%d < %s:%d" % (fr.name, fr.lineno, fr0.name, fr0.lineno)
        ev = (e.sem, e.cnt)
        e.waited[id(e.sem)] = max(e.waited.get(id(e.sem), 0), 0)
        self._commit(ev, reads, writes)
        return ev

    def dma(self, eng, reads, writes, out, in_, **kw):
        if self.stopped:
            return None
        e = self.E[eng]
        sl = self.slots[eng][self.slot_i[eng]]
        self.slot_i[eng] = (self.slot_i[eng] + 1) % len(self.slots[eng])
        if sl[1] > 0:
            self._wait(e, (sl[0], sl[1]))
        self._sync(e, reads, writes)
        inst = e.h.dma_start(out=out, in_=in_, **kw)
        sl[1] += 16
        inst.then_inc(sl[0], 16)
        ev = (sl[0], sl[1])
        self._commit(ev, reads, writes)
        return ev

    def barrier(self):
        if self.stopped:
            return
        evs = []
        for e in self.E.values():
            if e.cnt > 0:
                evs.append((e.sem, e.cnt))
        for q in self.slots.values():
            for sl in q:
                if sl[1] > 0:
                    evs.append((sl[0], sl[1]))
        for e in self.E.values():
            for ev in evs:
                if ev[0] is e.sem:
                    continue
                self._wait(e, ev)

    def finish(self):
        self.stopped = False
        self.barrier()


def build(SEQ, NSS, NTG):
    NS = NSS * 4
    NG = SEQ // NTG
    nc = bass.Bass("TRN2", target_bir_lowering=False)
    di = lambda n, s: nc.dram_tensor(n, s, F32, kind="ExternalInput").ap()
    do = lambda n, s: nc.dram_tensor(n, s, F32, kind="ExternalOutput").ap()
    xp = di("xp", [SEQ, D]); pp = di("pp", [DEPTH, SEQ, DPLE]); xs = di("xs", [NS, D]); psm = di("psm", [DEPTH, NS, DPLE])
    sssm = di("sssm", [DEPTH, NSS, 512, 128]); sconv = di("sconv", [DEPTH, NSS * 3, 1024])
    ck = di("ck", [DEPTH, NSS, 128, 128]); cv = di("cv", [DEPTH, NSS, 128, 128])
    g_ffn1 = di("g_ffn1", [DEPTH, D]); g_mix = di("g_mix", [DEPTH, D]); g_ffn2 = di("g_ffn2", [DEPTH, D]); g_ple = di("g_ple", [DEPTH, D])
    w1a = di("w1_a", [DEPTH, D, DFF]); w3a = di("w3_a", [DEPTH, D, DFF]); w2a = di("w2_a", [DEPTH, DFF, D])
    w1b = di("w1_b", [DEPTH, D, DFF]); w3b = di("w3_b", [DEPTH, D, DFF]); w2b = di("w2_b", [DEPTH, DFF, D])
    w_in = di("w_in", [DEPTH, D, DPROJ]); w_out = di("w_out", [DEPTH, D, D])
    conv_w = di("conv_w", [DEPTH, 4, 1024]); conv_b = di("conv_b", [DEPTH, 1024])
    dt_bias = di("dt_bias", [DEPTH, 8]); a_log = di("a_log", [DEPTH, 8]); d_skip = di("d_skip", [DEPTH, 8])
    ssm_norm = di("ssm_norm", [DEPTH, 512]); q_norm = di("q_norm", [DEPTH, 64]); k_norm = di("k_norm", [DEPTH, 64])
    sinks = di("sinks", [DEPTH, 8]); w_pg = di("w_ple_gate", [DEPTH, D, D]); w_pp = di("w_ple_proj", [DEPTH, DPLE, D])
    c_ident = di("c_ident", [128, 128]); c_U = di("c_U", [128, 128]); c_SL = di("c_SL", [128, 128])
    c_Ubd = di("c_Ubd", [64, 64]); c_SLbd = di("c_SLbd", [64, 64]); c_BMt = di("c_BMt", [64, 16]); c_BM = di("c_BM", [128, 16 * 64])
    c_bones = di("c_bones", [128, 128]); c_Eprev = di("c_Eprev", [128, 1024]); c_Ecur = di("c_Ecur", [128, 1024])
    c_Ecache = di("c_Ecache", [128, 32]); c_Enew = di("c_Enew", [64, 512])
    yp = do("yp", [SEQ, D]); ys = do("ys", [NS, D])
    ossm_p = do("ossm_p", [DEPTH, 512, 128]); oconv_p = do("oconv_p", [DEPTH, 3, 1024])
    ock_p = do("ock_p", [DEPTH, 128, 128]); ocv_p = do("ocv_p", [DEPTH, 128, 128])
    ossm_s = do("ossm_s", [DEPTH, NSS, 512, 128]); oconv_s = do("oconv_s", [DEPTH, NSS * 3, 1024])
    ock_s = do("ock_s", [DEPTH, NSS, 128, 128]); ocv_s = do("ocv_s", [DEPTH, NSS, 128, 128])
    hscr = nc.dram_tensor("hscr", [8, 128, SEQ], F32, kind="Internal").ap()
    wsc13 = nc.dram_tensor("wsc13", [2, 2, len(FCH), 128, 2048], BF16, kind="Internal").ap()
    wsc2 = nc.dram_tensor("wsc2", [2, 2, 128, NFT * 512], BF16, kind="Internal").ap()
    wscm = nc.dram_tensor("wscm", [3, 128, 18560], BF16, kind="Internal").ap()
    t_sc13 = [[[Tok() for _ in FCH] for _ in range(2)] for _ in range(2)]
    t_sc2 = [[[Tok() for _ in W2CH] for _ in range(2)] for _ in range(2)]
    t_scm = [Tok() for _ in range(3)]
    t_hscr = [Tok() for _ in range(NG)]

    with contextlib.ExitStack() as es:
        cx = Ctx(nc, es)
        op = cx.op; dma = cx.dma

        uniq = [0]

        def SB(scope, name, shape, dt=F32):
            uniq[0] += 1
            t = scope.enter_context(nc.sbuf_tensor("%s_%d" % (name, uniq[0]), shape, dt))
            try:
                SBUF_PEAK[0] = max(SBUF_PEAK[0], int(nc.sbuf_base))
                SBUF_PEAK[1] = int(nc.sbuf_top)
            except Exception:
                pass
            return t

        ident = SB(es, "ident", [128, 128]); t_c = Tok()
        identb = SB(es, "identb", [128, 128], BF16)
        Um = SB(es, "Um", [128, 128]); SLm = SB(es, "SLm", [128, 128])
        Ubd = SB(es, "Ubd", [64, 64]); SLbd = SB(es, "SLbd", [64, 64]); BMt = SB(es, "BMt", [64, 16]); BMtb = SB(es, "BMtb", [64, 16], BF16)
        BMb = SB(es, "BMb", [128, 16 * 64], BF16)
        bones = SB(es, "bones", [128, 128], BF16); onesb = SB(es, "onesb", [128, 128], BF16); onesf = SB(es, "onesf", [128, 128])
        Eprev = SB(es, "Eprev", [128, 1024]); Ecur = SB(es, "Ecur", [128, 1024]); Ecache = SB(es, "Ecache", [128, 32]); Enew = SB(es, "Enew", [64, 512])
        gcol = SB(es, "gcol", [128, 4 * DEPTH * 8])
        lay = {}
        for nm, w in [("cw", 32), ("cb", 8), ("dtb", 8), ("aneg", 8), ("dsk", 8), ("esk", 8), ("gq", 1), ("gk", 1)]:
            lay[nm] = SB(es, "l_" + nm, [128, w])
        gssm = SB(es, "gssm", [128, 512]); esinkrow = SB(es, "esinkrow", [64, 1024])
        t_lay = Tok()
        hs = SB(es, "hs", [128, 8, 64]); t_hs = Tok()
        w13 = [[SB(es, "w13_%d_%d" % (m, b), [128, 8, 256], BF16) for b in range(2)] for m in range(2)]
        t_w13 = [[Tok() for _ in range(2)] for _ in range(2)]
        wreg = SB(es, "wreg", [128, 18560], BF16); t_wreg = Tok()
        hT = SB(es, "hT", [128, 512]); hTb = SB(es, "hTb", [128, 512], BF16); t_hT = Tok(); t_hTb = Tok()
        xhalo = SB(es, "xhalo", [128, 8, 3]); t_xhalo = Tok()
        khalo = SB(es, "khalo", [128, 128], BF16); vhalo = SB(es, "vhalo", [128, 128], BF16); t_kvh = Tok()
        psb = [es.enter_context(nc.psum_tensor("ps%d" % i, [128, 512], F32)) for i in range(8)]
        t_ps = [Tok() for _ in range(8)]
        psi = [0]

        held = set()

        def PS(hold=False):
            for _try in range(9):
                i = psi[0]; psi[0] = (i + 1) % 8
                if i not in held:
                    break
            else:
                raise RuntimeError("all PSUM banks held")
            if hold:
                held.add(i)
            return psb[i], t_ps[i]

        def REL(tok):
            held.discard(t_ps.index(tok))

        def ld(eng, dst, src, toks, **kw):
            dma(eng, [], toks, dst, src, **kw)
        ld("sp", ident[:], c_ident[:, :], [t_c]); ld("pool", identb[:], c_ident[:, :], [t_c])
        ld("sp", Um[:], c_U[:, :], [t_c]); ld("sp", SLm[:], c_SL[:, :], [t_c])
        ld("sp", Ubd[:], c_Ubd[:, :], [t_c]); ld("sp", SLbd[:], c_SLbd[:, :], [t_c]); ld("sp", BMt[:], c_BMt[:, :], [t_c])
        ld("pool", BMtb[:], c_BMt[:, :], [t_c]); ld("pool", BMb[:], c_BM[:, :], [t_c]); ld("pool", bones[:], c_bones[:, :], [t_c])
        ld("sp", Eprev[:], c_Eprev[:, :], [t_c]); ld("sp", Ecur[:], c_Ecur[:, :], [t_c]); ld("sp", Ecache[:], c_Ecache[:, :], [t_c]); ld("sp", Enew[:], c_Enew[:, :], [t_c])
        op("dve", [], [t_c], lambda v: v.memset(onesb[:], 1.0))
        op("dve", [], [t_c], lambda v: v.memset(onesf[:], 1.0))
        for ni, g in enumerate([g_ffn1, g_mix, g_ffn2, g_ple]):
            for l in range(DEPTH):
                o = (ni * DEPTH + l) * 8
                ld("sp", gcol[:, o:o + 8], g[l, :].rearrange("(j p) -> p j", p=128), [t_c], allow_slow_non_contiguous=True)
        op("dve", [t_c], [t_c], lambda v: v.tensor_scalar(out=gcol[:], in0=gcol[:], scalar1=32.0, scalar2=None, op0=ALU.mult))

        def load_layer_consts(l):
            T = [t_lay]
            for j in range(4):
                ld("sp", lay["cw"][:, j * 8:(j + 1) * 8], conv_w[l, j, :].rearrange("(c p) -> p c", p=128), T, allow_slow_non_contiguous=True)
            ld("sp", lay["cb"][:], conv_b[l, :].rearrange("(c p) -> p c", p=128), T, allow_slow_non_contiguous=True)
            ld("sp", lay["dtb"][:], dt_bias[l, :].partition_broadcast(128), T)
            ld("sp", lay["aneg"][:], a_log[l, :].partition_broadcast(128), T)
            ld("sp", lay["dsk"][:], d_skip[l, :].partition_broadcast(128), T)
            ld("sp", lay["esk"][:], sinks[l, :].partition_broadcast(128), T)
            for hh in range(2):
                ld("sp", lay["gq"][hh * 64:(hh + 1) * 64, :], q_norm[l, :].rearrange("(p o) -> p o", o=1), T, allow_slow_non_contiguous=True)
                ld("sp", lay["gk"][hh * 64:(hh + 1) * 64, :], k_norm[l, :].rearrange("(p o) -> p o", o=1), T, allow_slow_non_contiguous=True)
            ld("sp", gssm[:], ssm_norm[l, :].partition_broadcast(128), T)
            op("act", T, T, lambda a: a.activation(out=lay["aneg"][:], in_=lay["aneg"][:], func=AF.Exp))
            op("dve", T, T, lambda v: v.tensor_scalar(out=lay["aneg"][:], in0=lay["aneg"][:], scalar1=-1.0, scalar2=None, op0=ALU.mult))
            op("act", T, T, lambda a: a.activation(out=lay["esk"][:], in_=lay["esk"][:], func=AF.Exp))
            op("dve", T, T, lambda v: v.tensor_scalar(out=lay["gq"][:], in0=lay["gq"][:], scalar1=8.0, scalar2=None, op0=ALU.mult))
            op("dve", T, T, lambda v: v.tensor_scalar(out=lay["gk"][:], in0=lay["gk"][:], scalar1=8.0, scalar2=None, op0=ALU.mult))
            op("dve", T, T, lambda v: v.tensor_copy(out=esinkrow[:].rearrange("p (h q) -> p h q", h=8),
                                                     in_=lay["esk"][0:64, :].unsqueeze(2).broadcast_to([64, 8, 128])))

        def nblocks(NT):
            return [(n0, min(512, NT - n0)) for n0 in range(0, NT, 512)]

        def norm(sc, h, t_h, NT, ni, l, xn, t_xn):
            go = (ni * DEPTH + l) * 8
            sq = SB(sc, "sq", [128, 8, 512], BF16); t_sq = Tok()
            rstd = SB(sc, "rstd", [128, 512]); t_rstd = Tok()
            for (n0, nw) in nblocks(NT):
                for half in range(2):
                    op("act", [t_h], [t_sq], lambda a: a.activation(out=sq[:, half * 4:(half + 1) * 4, :nw], in_=h[:, half * 4:(half + 1) * 4, n0:n0 + nw], func=AF.Square))
                ps, tp = PS()
                for j in range(8):
                    op("pe", [t_sq, t_c], [tp], lambda p: p.matmul(ps[:, :nw], lhsT=onesb[:], rhs=sq[:, j, :nw], start=(j == 0), stop=(j == 7)))
                op("act", [tp], [t_rstd], lambda a: a.activation(out=rstd[:, :nw], in_=ps[:, :nw], func=AF.Ln, bias=1024.0 * EPS, scale=1.0))
                op("act", [t_rstd], [t_rstd], lambda a: a.activation(out=rstd[:, :nw], in_=rstd[:, :nw], func=AF.Exp, scale=-0.5))
                for j in range(8):
                    op("dve", [t_h, t_rstd, t_c], [t_xn[j]], lambda v: v.scalar_tensor_tensor(out=xn[:, j, n0:n0 + nw], in0=h[:, j, n0:n0 + nw], scalar=gcol[:, go + j:go + j + 1], in1=rstd[:, :nw], op0=ALU.mult, op1=ALU.mult))

        w13_next = [None]

        def issue_w13(W1, W3, l, ci, parity, ab, cached):
            cols = FCH[ci]; f0 = FT[cols[0]][0]; fw = sum(FT[c][1] for c in cols)
            for m, W in enumerate((W1, W3)):
                scr = wsc13[ab, m, ci, :, :].rearrange("p (k f) -> p k f", k=8)[:, :, :fw]
                if cached:
                    dma("pool", [t_sc13[ab][m][ci]], [t_w13[m][parity]], w13[m][parity][:, :, :fw], scr)
                else:
                    dma("pool", [], [t_w13[m][parity]], w13[m][parity][:, :, :fw], W[l, :, f0:f0 + fw].rearrange("(k p) f -> p k f", p=128))
                    dma("sp", [t_w13[m][parity]], [t_sc13[ab][m][ci]], scr, w13[m][parity][:, :, :fw])

        def ffn(sc, h, t_h, NT, xn, t_xn, W1, W3, W2, l, ab, cached):
            gT = SB(sc, "gT", [128, NFT, NT], BF16); t_g = [Tok() for _ in range(NFT)]
            s1 = [SB(sc, "s1_%d" % i, [128, 512]) for i in range(2)]; t_s1 = [Tok(), Tok()]
            w2 = SB(sc, "w2", [128, NFT, 512], BF16); t_w2 = [Tok() for _ in W2CH]
            si = 0
            issue_w13(W1, W3, l, 0, 0, ab, cached)
            for ci, cols in enumerate(FCH):
                par = ci % 2
                if ci + 1 < len(FCH):
                    issue_w13(W1, W3, l, ci + 1, (ci + 1) % 2, ab, cached)
                for fi, ft in enumerate(cols):
                    fw = FT[ft][1]; fo = fi * 128
                    for (n0, nw) in nblocks(NT):
                        p1, tp1 = PS(); p3, tp3 = PS()
                        for (pp_, tpp, m) in ((p1, tp1, 0), (p3, tp3, 1)):
                            for k in range(8):
                                op("pe", [t_w13[m][par], t_xn[k]], [tpp], lambda p: p.matmul(pp_[:fw, :nw], lhsT=w13[m][par][:, k, fo:fo + fw], rhs=xn[:, k, n0:n0 + nw], start=(k == 0), stop=(k == 7)))
                        sb_, ts_ = s1[si], t_s1[si]; si ^= 1
                        op("act", [tp1], [ts_], lambda a: a.activation(out=sb_[:fw, :nw], in_=p1[:fw, :nw], func=AF.Silu))
                        op("dve", [ts_, tp3], [t_g[ft]], lambda v: v.tensor_tensor(out=gT[:fw, ft, n0:n0 + nw], in0=sb_[:fw, :nw], in1=p3[:fw, :nw], op=ALU.mult))
            for half in range(2):
                for wi, rows in enumerate(W2CH):
                    r0 = FT[rows[0]][0]
                    nfull = [r for r in rows if FT[r][1] == 128]
                    scr2 = wsc2[ab, half, :, :].rearrange("p (f c) -> p f c", f=NFT)
                    if nfull:
                        sl = slice(nfull[0], nfull[-1] + 1)
                        if cached:
                            dma("pool", [t_sc2[ab][half][wi]], [t_w2[wi]], w2[:, sl, :], scr2[:, sl, :])
                        else:
                            dma("pool", [], [t_w2[wi]], w2[:, sl, :], W2[l, r0:r0 + 128 * len(nfull), half * 512:(half + 1) * 512].rearrange("(f p) c -> p f c", p=128))
                    for r in rows:
                        if FT[r][1] != 128:
                            if cached:
                                dma("pool", [t_sc2[ab][half][wi]], [t_w2[wi]], w2[:64, r, :], scr2[:64, r, :])
                            else:
                                dma("pool", [], [t_w2[wi]], w2[:64, r, :], W2[l, FT[r][0]:FT[r][0] + 64, half * 512:(half + 1) * 512])
                    if not cached:
                        if nfull:
                            dma("sp", [t_w2[wi]], [t_sc2[ab][half][wi]], scr2[:, sl, :], w2[:, sl, :])
                        for r in rows:
                            if FT[r][1] != 128:
                                dma("sp", [t_w2[wi]], [t_sc2[ab][half][wi]], scr2[:64, r, :], w2[:64, r, :])
                for (n0, nw) in nblocks(NT):
                    acc = [PS() for _ in range(4)]
                    for ft in range(NFT):
                        fw = FT[ft][1]
                        wi = [i for i, rows in enumerate(W2CH) if ft in rows][0]
                        for dj in range(4):
                            op("pe", [t_w2[wi], t_g[ft]], [acc[dj][1]], lambda p: p.matmul(acc[dj][0][:, :nw], lhsT=w2[:fw, ft, dj * 128:(dj + 1) * 128], rhs=gT[:fw, ft, n0:n0 + nw], start=(ft == 0), stop=(ft == NFT - 1)))
                    for dj in range(4):
                        d = half * 4 + dj
                        op("dve", [acc[dj][1], t_h], [t_h], lambda v: v.scalar_tensor_tensor(out=h[:, d, n0:n0 + nw], in0=acc[dj][0][:, :nw], scalar=0.5, in1=h[:, d, n0:n0 + nw], op0=ALU.mult, op1=ALU.add))

        def transpose_in(xtm, t_x, src, ntok, h, t_h, c0):
            dma("sp", [], [t_x], xtm[:ntok, :], src)
            for a in range(2):
                ps, tp = PS()
                for j in range(4):
                    op("pe", [t_x, t_c], [tp], lambda p: p.transpose(out=ps[:, j * 128:j * 128 + ntok], in_=xtm[:ntok, (a * 4 + j) * 128:(a * 4 + j + 1) * 128], identity=ident[:ntok, :ntok]))
                op("act", [tp], [t_h], lambda a_: a_.activation(out=h[:, a * 4:(a + 1) * 4, c0:c0 + ntok], in_=ps[:].rearrange("p (j t) -> p j t", j=4)[:, :, :ntok], func=AF.Copy))

        def transpose_out(ytm, t_y, h, t_h, c0, ntok, dst):
            for a in range(2):
                ps, tp = PS()
                for j in range(4):
                    op("pe", [t_h, t_c], [tp], lambda p: p.transpose(out=ps[:ntok, j * 128:(j + 1) * 128], in_=h[:, a * 4 + j, c0:c0 + ntok], identity=ident[:]))
                op("act", [tp], [t_y], lambda a_: a_.activation(out=ytm[:ntok, a * 512:(a + 1) * 512], in_=ps[:ntok, :], func=AF.Copy))
            dma("sp", [t_y], [], dst, ytm[:ntok, :])

        def mixer(sc, h, t_h, NT, l, sample, first_group, last_group, cached):
            xn = SB(sc, "xn", [128, 8, NT], BF16); t_xn = [Tok() for _ in range(8)]
            norm(sc, h, t_h, NT, 1, l, xn, t_xn)
            chk('m_norm')
            NTT = 64 if sample else 128
            ntile = NT // NTT
            o = 0
            def carve(n, shape_str, **kw):
                nonlocal o
                a = wreg[:, o:o + n]; o += n
                return a.rearrange(shape_str, **kw)
            wz = carve(8 * 512, "p (k c) -> p k c", k=8); wx = carve(8 * 1024, "p (k c) -> p k c", k=8)
            wq = carve(8 * 512, "p (k c) -> p k c", k=8); wk = carve(8 * 128, "p (k c) -> p k c", k=8)
            wv = carve(8 * 128, "p (k c) -> p k c", k=8); wdt = carve(8 * 8, "p (k c) -> p k c", k=8)
            wl = w_in[l, :, :].rearrange("(k p) c -> p k c", p=128)
            T = [t_wreg]
            if cached:
                dma("pool", [t_scm[0]], T, wreg[:, 0:18496], wscm[0, :, 0:18496])
            else:
                dma("pool", [], T, wz, wl[:, :, 0:512]); dma("pool", [], T, wx, wl[:, :, 512:1536]); dma("pool", [], T, wdt, wl[:, :, 1536:1544])
                for hh in range(4):
                    for g in range(2):
                        c = 1544 + g * 256 + hh * 64
                        dma("pool", [], T, wq[:, :, hh * 128 + g * 64:hh * 128 + g * 64 + 64], wl[:, :, c:c + 64])
                dma("pool", [], T, wk, wl[:, :, 2056:2184]); dma("pool", [], T, wv, wl[:, :, 2184:2312])
                dma("sp", T, [t_scm[0]], wscm[0, :, 0:18496], wreg[:, 0:18496])
            chk('m_wdma')
            HAL = 0 if sample else 3
            xin = SB(sc, "xin", [128, 8, NT + HAL]); t_xin = Tok()
            xc = SB(sc, "xc", [128, 8, NT], BF16); t_xc = Tok()
            qn = SB(sc, "qn", [128, 4, NT], BF16); t_qn = Tok()
            KH = 0 if sample else 128
            kn = SB(sc, "kn", [128, KH + NT], BF16); t_kn = Tok()
            knf = SB(sc, "knf", [128, NT]); t_knf = Tok()
            vb = SB(sc, "vb", [128, ntile + 1, 128], BF16); t_vb = Tok()
            vf = SB(sc, "vf", [128, 128]); t_vf = Tok()
            mixT = SB(sc, "mixT", [128, 4, NT], BF16); t_mixT = Tok()
            oT = SB(sc, "oT", [64, 8, NT], BF16); t_oT = Tok()
            tmpa = SB(sc, "tmpa", [128, 512]); t_tmpa = Tok()
            rq = SB(sc, "rq", [128, 512]); t_rq = Tok()
            sqb = SB(sc, "sqb", [128, 512], BF16); t_sqb = Tok()
            if not sample:
                op("dve", [t_xhalo], [t_xin], lambda v: v.tensor_copy(out=xin[:, :, 0:3], in_=xhalo[:]))
                op("dve", [t_kvh], [t_kn], lambda v: v.tensor_copy(out=kn[:, 0:128], in_=khalo[:]))
                op("dve", [t_kvh], [t_vb], lambda v: v.tensor_copy(out=vb[:, 0, :], in_=vhalo[:]))
            chk('m_halo')
            with cx.fast():
                for c in range(8):
                    for (n0, nw) in nblocks(NT):
                        ps, tp = PS()
                        for k in range(8):
                            op("pe", T + [t_xn[k]], [tp], lambda p: p.matmul(ps[:, :nw], lhsT=wx[:, k, c * 128:(c + 1) * 128], rhs=xn[:, k, n0:n0 + nw], start=(k == 0), stop=(k == 7)))
                        if not sample:
                            op("act", [tp], [t_xin], lambda a: a.activation(out=xin[:, c, HAL + n0:HAL + n0 + nw], in_=ps[:, :nw], func=AF.Copy))
                        else:
                            op("act", [tp], [t_xin], lambda a: a.activation(out=xin[:, c, n0:n0 + nw], in_=ps[:, :nw], func=AF.Copy))
                chk('m_xbc')
                for qi in range(5):
                    for (n0, nw) in nblocks(NT):
                        ps, tp = PS()
                        for k in range(8):
                            lw = wq[:, k, qi * 128:(qi + 1) * 128] if qi < 4 else wk[:, k, :]
                            op("pe", T + [t_xn[k]], [tp], lambda p: p.matmul(ps[:, :nw], lhsT=lw, rhs=xn[:, k, n0:n0 + nw], start=(k == 0), stop=(k == 7)))
                        op("act", [tp], [t_sqb], lambda a: a.activation(out=sqb[:, :nw], in_=ps[:, :nw], func=AF.Square))
                        ps2, tp2 = PS()
                        op("pe", [t_sqb, t_c], [tp2], lambda p: p.matmul(ps2[:, :nw], lhsT=bones[:], rhs=sqb[:, :nw], start=True, stop=True))
                        op("act", [tp2], [t_rq], lambda a: a.activation(out=rq[:, :nw], in_=ps2[:, :nw], func=AF.Ln, bias=64.0 * EPS, scale=1.0))
                        op("act", [t_rq], [t_rq], lambda a: a.activation(out=rq[:, :nw], in_=rq[:, :nw], func=AF.Exp, scale=-0.5))
                        if qi < 4:
                            op("dve", [tp, t_rq, t_lay], [t_qn], lambda v: v.scalar_tensor_tensor(out=qn[:, qi, n0:n0 + nw], in0=ps[:, :nw], scalar=lay["gq"][:, 0:1], in1=rq[:, :nw], op0=ALU.mult, op1=ALU.mult))
                        else:
                            op("dve", [tp, t_rq, t_lay], [t_knf], lambda v: v.scalar_tensor_tensor(out=knf[:, n0:n0 + nw], in0=ps[:, :nw], scalar=lay["gk"][:, 0:1], in1=rq[:, :nw], op0=ALU.mult, op1=ALU.mult))
                            op("act", [t_knf], [t_kn], lambda a: a.activation(out=kn[:, KH + n0:KH + n0 + nw], in_=knf[:, n0:n0 + nw], func=AF.Copy))
            chk('m_qk')
            cacc = SB(sc, "cacc", [128, 512]); t_cacc = Tok()
            if not sample:
                for c in range(8):
                    for (n0, nw) in nblocks(NT):
                        op("dve", [t_xin, t_lay], [t_cacc], lambda v: v.tensor_scalar(out=cacc[:, :nw], in0=xin[:, c, n0:n0 + nw], scalar1=lay["cw"][:, c:c + 1], scalar2=None, op0=ALU.mult))
                        for j in range(1, 4):
                            op("dve", [t_xin, t_lay, t_cacc], [t_cacc], lambda v: v.scalar_tensor_tensor(out=cacc[:, :nw], in0=xin[:, c, n0 + j:n0 + j + nw], scalar=lay["cw"][:, j * 8 + c:j * 8 + c + 1], in1=cacc[:, :nw], op0=ALU.mult, op1=ALU.add))
                        op("act", [t_cacc, t_lay], [t_xc], lambda a: a.activation(out=xc[:, c, n0:n0 + nw], in_=cacc[:, :nw], func=AF.Silu, bias=lay["cb"][:, c:c + 1], scale=1.0))
                op("dve", [t_xin], [t_xhalo], lambda v: v.tensor_copy(out=xhalo[:], in_=xin[:, :, NT:NT + 3]))
                if last_group:
                    cst = SB(sc, "cst", [128, 8, 4]); t_cst = Tok()
                    op("dve", [t_xin], [t_cst], lambda v: v.tensor_copy(out=cst[:, :, 0:3], in_=xin[:, :, NT:NT + 3]))
                    ps, tp = PS(); ps2, tp2 = PS()
                    for c in range(8):
                        pp_, tpp = (ps, tp) if c < 4 else (ps2, tp2)
                        op("pe", [t_cst, t_c], [tpp], lambda p: p.transpose(out=pp_[:3, (c % 4) * 128:(c % 4 + 1) * 128], in_=cst[:, c, 0:3], identity=ident[:]))
                    cso = SB(sc, "cso", [4, 1024]); t_cso = Tok()
                    op("act", [tp], [t_cso], lambda a: a.activation(out=cso[:3, 0:512], in_=ps[:3, :], func=AF.Copy))
                    op("act", [tp2], [t_cso], lambda a: a.activation(out=cso[:3, 512:1024], in_=ps2[:3, :], func=AF.Copy))
                    dma("sp", [t_cso], [], oconv_p[l, :, :], cso[:3, :])
            else:
                xfull = SB(sc, "xfull", [128, 8, NSS, 7]); t_xf = Tok()
                scm = SB(sc, "scm", [64, 1024]); t_scmb = Tok()
                dma("sp", [], [t_scmb], scm[:NSS * 3, :], sconv[l, :, :])
                for c in range(8):
                    ps, tp = PS()
                    op("pe", [t_scmb, t_c], [tp], lambda p: p.transpose(out=ps[:, :NSS * 3], in_=scm[:NSS * 3, c * 128:(c + 1) * 128], identity=ident[:NSS * 3, :NSS * 3]))
                    op("act", [tp], [t_xf], lambda a: a.activation(out=xfull[:, c, :, 0:3], in_=ps[:, :NSS * 3].rearrange("p (b j) -> p b j", j=3), func=AF.Copy))
                    op("dve", [t_xin], [t_xf], lambda v: v.tensor_copy(out=xfull[:, c, :, 3:7], in_=xin[:, c, :].rearrange("p (b t) -> p b t", t=4)))
                    ca = cacc[:, :NS].rearrange("p (b t) -> p b t", t=4)
                    op("dve", [t_xf, t_lay], [t_cacc], lambda v: v.tensor_scalar(out=ca, in0=xfull[:, c, :, 0:4], scalar1=lay["cw"][:, c:c + 1], scalar2=None, op0=ALU.mult))
                    for j in range(1, 4):
                        op("dve", [t_xf, t_lay, t_cacc], [t_cacc], lambda v: v.scalar_tensor_tensor(out=ca, in0=xfull[:, c, :, j:j + 4], scalar=lay["cw"][:, j * 8 + c:j * 8 + c + 1], in1=ca, op0=ALU.mult, op1=ALU.add))
                    op("act", [t_cacc, t_lay], [t_xc], lambda a: a.activation(out=xc[:, c, :], in_=cacc[:, :NS], func=AF.Silu, bias=lay["cb"][:, c:c + 1], scale=1.0))
                cso = SB(sc, "cso", [64, 1024]); t_cso = Tok()
                cst = SB(sc, "cst", [128, 8, NSS * 3]); t_cst = Tok()
                op("dve", [t_xf], [t_cst], lambda v: v.tensor_copy(out=cst[:].rearrange("p c (b j) -> p c b j", j=3), in_=xfull[:, :, :, 4:7]))
                for a_ in range(2):
                    ps, tp = PS()
                    for j in range(4):
                        op("pe", [t_cst, t_c], [tp], lambda p: p.transpose(out=ps[:NSS * 3, j * 128:(j + 1) * 128], in_=cst[:, a_ * 4 + j, :], identity=ident[:]))
                    op("act", [tp], [t_cso], lambda a: a.activation(out=cso[:NSS * 3, a_ * 512:(a_ + 1) * 512], in_=ps[:NSS * 3, :], func=AF.Copy))
                dma("sp", [t_cso], [], oconv_s[l, :, :], cso[:NSS * 3, :])

            chk('m_conv')
            Ut = Ubd if sample else Um; SLt = SLbd if sample else SLm
            P = NTT
            st = {}
            for nm, shp, dt in [("dt", [128, 8], F32), ("dta", [128, 8], F32), ("t8", [128, 8], F32), ("ecum", [128, 8], F32), ("etot", [128, 8], F32),
                                ("dend", [128, 8], F32), ("w2s", [128, 8], F32), ("DL", [128, 8, 128], F32), ("LT", [128, 8, 128], F32),
                                ("GM", [128, 2, 128], F32), ("MT", [128, 8, 128], BF16), ("xdt", [128, 512], BF16), ("xdd", [128, 512], BF16),
                                ("Btm", [128, 256], BF16), ("y1", [128, 512], F32), ("sz", [128, 512], F32), ("ysq", [128, 512], F32), ("xsk", [128, 512], F32), ("xtok", [128, 512], F32),
                                ("ss2", [128, 2], F32), ("ytm", [128, 512], BF16), ("pe0", [128, 512], F32), ("pe1", [128, 512], F32),
                                ("PT0", [128, 512], BF16), ("PT1", [128, 512], BF16), ("den", [64, 512], F32)]:
                st[nm] = (SB(sc, "st_" + nm, shp, dt), Tok())
            if sample:
                Snat = SB(sc, "Snat", [128, 8, 4, 128]); t_Sn = Tok()
                Sb1 = [SB(sc, "Sb1_%d" % i, [128, 4, 128], BF16) for i in range(2)]; t_Sb1 = [Tok(), Tok()]
                STb1 = [SB(sc, "STb1_%d" % i, [128, 512], BF16) for i in range(2)]; t_ST1 = [Tok(), Tok()]
                CTm = SB(sc, "CTm", [128, NSS, 2, 64], BF16); t_CT = Tok()
                xdm = [SB(sc, "xdm%d" % i, [64, 512], BF16) for i in range(2)]; t_xdm = [Tok(), Tok()]
                dtaE = SB(sc, "dtaE", [64, 512]); t_dE = Tok()
                cdT = SB(sc, "cdT", [128, 4, 16]); t_cd = Tok()
                Snew = [SB(sc, "Snew%d" % i, [128, 4, 128]) for i in range(2)]; t_Snew = [Tok(), Tok()]
                kcb = SB(sc, "kcb", [128, NSS, 128], BF16); vcb = SB(sc, "vcb", [128, NSS, 128], BF16); t_kcb = Tok(); t_vcb = Tok()
                KcT = SB(sc, "KcT", [128, NSS, 128], BF16); t_KcT = Tok()
                PTc = SB(sc, "PTc", [128, 512], BF16); t_PTc = Tok()
                for b4 in range(0, NSS, 4):
                    dma("pool", [], [t_kcb], kcb[:, b4:b4 + 4, :], ck[l, b4:b4 + 4, :, :].rearrange("b i c -> i b c"))
                    dma("pool", [], [t_vcb], vcb[:, b4:b4 + 4, :], cv[l, b4:b4 + 4, :, :].rearrange("b i c -> i b c"))
                for b4 in range(0, NSS, 4):
                    dma("sp", [], [], ock_s[l, b4:b4 + 4, 0:124, :], ck[l, b4:b4 + 4, 4:128, :])
                    dma("sp", [], [], ocv_s[l, b4:b4 + 4, 0:124, :], cv[l, b4:b4 + 4, 4:128, :])

            A = lambda nm: st[nm][0]
            K_ = lambda nm: st[nm][1]
            PSH = lambda: PS(hold=not sample)

            def RELH(tok):
                if not sample:
                    REL(tok)

            def tile_ssd(ti):
                c0 = ti * NTT
                cs = slice(c0, c0 + P)
                first_tile = first_group and ti == 0 and not sample
                pz, tpz = PSH(); pdv, tpdv = PSH()
                with cx.fast():
                    for k in range(8):
                        op("pe", T + [t_xn[k]], [tpz], lambda p: p.matmul(pz[:P, :], lhsT=xn[:, k, cs], rhs=wz[:, k, :], start=(k == 0), stop=(k == 7)))
                    for k in range(8):
                        op("pe", T + [t_xn[k]], [tpdv], lambda p: p.matmul(pdv[:P, 0:128], lhsT=xn[:, k, cs], rhs=wv[:, k, :], start=(k == 0), stop=(k == 7)))
                    for k in range(8):
                        op("pe", T + [t_xn[k]], [tpdv], lambda p: p.matmul(pdv[:P, 128:136], lhsT=xn[:, k, cs], rhs=wdt[:, k, :], start=(k == 0), stop=(k == 7)))
                op("act", [tpz], [st["sz"][1]], lambda a: a.activation(out=st["sz"][0][:P, :], in_=pz[:P, :], func=AF.Silu))
                RELH(tpz)
                op("act", [tpdv], [t_vb], lambda a: a.activation(out=vb[:P, ti + 1, :], in_=pdv[:P, 0:128], func=AF.Copy))
                need_vf = sample or (last_group and ti == ntile - 1)
                if need_vf:
                    op("act", [tpdv], [t_vf], lambda a: a.activation(out=vf[:P, :], in_=pdv[:P, 0:128], func=AF.Copy))
                chk('t_zdv')
                op("dve", [tpdv, t_lay], [K_("t8")], lambda v: v.tensor_tensor(out=A("t8")[:P, :], in0=pdv[:P, 128:136], in1=lay["dtb"][:P, :], op=ALU.add))
                RELH(tpdv)
                yield
                op("act", [K_("t8")], [K_("t8")], lambda a: a.activation(out=A("t8")[:P, :], in_=A("t8")[:P, :], func=AF.Exp))
                op("act", [K_("t8")], [K_("dt")], lambda a: a.activation(out=A("dt")[:P, :], in_=A("t8")[:P, :], func=AF.Ln, bias=1.0, scale=1.0))
                op("dve", [K_("dt"), t_lay], [K_("dta")], lambda v: v.tensor_tensor(out=A("dta")[:P, :], in0=A("dt")[:P, :], in1=lay["aneg"][:P, :], op=ALU.mult))
                op("dve", [K_("dta"), t_c], [K_("DL")], lambda v: v.tensor_tensor(out=A("DL")[:P, :, :P], in0=SLt[:P, :P].unsqueeze(1).broadcast_to([P, 8, P]), in1=A("dta")[:P, :].unsqueeze(2).broadcast_to([P, 8, P]), op=ALU.mult))
                yield
                pD0, tD0 = PSH(); pD1, tD1 = PSH(); pc, tpc = PSH()
                chk('t_dt')
                for hd in range(8):
                    pd_, td_ = (pD0, tD0) if hd < 4 else (pD1, tD1)
                    op("pe", [K_("DL"), t_c], [td_], lambda p: p.matmul(pd_[:P, (hd % 4) * 128:(hd % 4) * 128 + P], lhsT=A("DL")[:P, hd, :P], rhs=Ut[:P, :P], start=True, stop=True))
                op("pe", [K_("dta"), t_c], [tpc], lambda p: p.matmul(pc[:P, 0:8], lhsT=Ut[:P, :P], rhs=A("dta")[:P, :], start=True, stop=True))
                if not sample:
                    op("pe", [K_("dta"), t_c], [tpc], lambda p: p.matmul(pc[:, 8:16], lhsT=onesf[:, :], rhs=A("dta")[:, :], start=True, stop=True))
                else:
                    pass
                for hf, (pd_, td_) in enumerate(((pD0, tD0), (pD1, tD1))):
                    op("act", [td_], [K_("LT")], lambda a: a.activation(out=A("LT")[:P, hf * 4:(hf + 1) * 4, :P], in_=pd_[:P, :].rearrange("p (h t) -> p h t", h=4)[:, :, :P], func=AF.Exp))
                chk('u_LT')
                op("act", [tpc], [K_("ecum")], lambda a: a.activation(out=A("ecum")[:P, :], in_=pc[:P, 0:8], func=AF.Exp))
                RELH(tD0); RELH(tD1)
                chk('u_ecum')
                chk('t_D')
                pG, tpG = PSH()
                with cx.fast():
                    for g in range(2):
                        op("pe", [t_xc], [tpG], lambda p: p.matmul(pG[:P, g * 128:g * 128 + P], lhsT=xc[:, 4 + g, cs], rhs=xc[:, 6 + g, cs], start=True, stop=True))
                    chk('u_pG')
                op("dve", [tpG, t_c], [K_("GM")], lambda v: v.tensor_tensor(out=A("GM")[:P, :, :P], in0=pG[:P, 0:256].rearrange("p (g t) -> p g t", g=2)[:, :, :P], in1=Ut[:P, :P].unsqueeze(1).broadcast_to([P, 2, P]), op=ALU.mult))
                RELH(tpG)
                chk('u_GM')
                for g in range(2):
                    op("dve", [K_("GM"), K_("LT")], [K_("MT")], lambda v: v.tensor_tensor(out=A("MT")[:P, g * 4:(g + 1) * 4, :P], in0=A("LT")[:P, g * 4:(g + 1) * 4, :P], in1=A("GM")[:P, g, :P].unsqueeze(1).broadcast_to([P, 4, P]), op=ALU.mult))
                chk('t_G')
                px, tpx = PSH(); pxb = px[:].bitcast(BF16)
                with cx.fast():
                    for j in range(4):
                        op("pe", [t_xc, t_c], [tpx], lambda p: p.transpose(out=pxb[:P, j * 128:(j + 1) * 128], in_=xc[:, j, cs], identity=identb[:]))
                    for g in range(2):
                        op("pe", [t_xc, t_c], [tpx], lambda p: p.transpose(out=pxb[:P, 512 + g * 128:512 + (g + 1) * 128], in_=xc[:, 4 + g, cs], identity=identb[:]))
                    chk('v_tr')
                op("act", [tpx], [K_("Btm")], lambda a: a.activation(out=A("Btm")[:P, :], in_=pxb[:P, 512:768], func=AF.Copy))
                op("act", [tpx], [K_("xtok")], lambda a: a.activation(out=A("xtok")[:P, :], in_=pxb[:P, 0:512], func=AF.Copy))
                RELH(tpx)
                chk('v_Btm')
                op("dve", [K_("xtok"), K_("dt")], [K_("xdt")], lambda v: v.tensor_tensor(out=A("xdt")[:P, :].rearrange("p (h d) -> p h d", h=8), in0=A("xtok")[:P, :].rearrange("p (h d) -> p h d", h=8), in1=A("dt")[:P, :].unsqueeze(2).broadcast_to([P, 8, 64]), op=ALU.mult))
                chk('v_xdt')
                op("dve", [K_("xtok"), t_lay], [K_("xsk")], lambda v: v.tensor_tensor(out=A("xsk")[:P, :].rearrange("p (h d) -> p h d", h=8), in0=A("xtok")[:P, :].rearrange("p (h d) -> p h d", h=8), in1=lay["dsk"][:P, :].unsqueeze(2).broadcast_to([P, 8, 64]), op=ALU.mult))
                yield
                chk('t_tr')
                py, tpy = PS(hold=True)
                with cx.fast():
                    for hd in range(8):
                        op("pe", [K_("MT"), K_("xdt")], [tpy], lambda p: p.matmul(py[:P, hd * 64:(hd + 1) * 64], lhsT=A("MT")[:P, hd, :P], rhs=A("xdt")[:P, hd * 64:(hd + 1) * 64], start=True, stop=True))
                pyo, tpyo = PS(hold=True)
                if sample:
                    pyo2, tpyo2 = PS(hold=True)
                if not sample:
                    op("act", [tpc], [K_("w2s")], lambda a: a.activation(out=A("w2s")[:, :], in_=pc[:, 0:8], func=AF.Copy))
                    op("dve", [tpc, K_("w2s")], [K_("t8")], lambda v: v.tensor_tensor(out=A("t8")[:, :], in0=pc[:, 8:16], in1=A("w2s")[:, :], op=ALU.subtract))
                    op("act", [K_("t8")], [K_("dend")], lambda a: a.activation(out=A("dend")[:, :], in_=A("t8")[:, :], func=AF.Exp))
                    op("act", [tpc], [K_("etot")], lambda a: a.activation(out=A("etot")[:, :], in_=pc[:, 8:16], func=AF.Exp))
                    RELH(tpc)
                    op("dve", [K_("dend"), K_("dt")], [K_("w2s")], lambda v: v.tensor_tensor(out=A("w2s")[:, :], in0=A("dend")[:, :], in1=A("dt")[:, :], op=ALU.mult))
                    op("dve", [K_("xtok"), K_("w2s")], [K_("xdd")], lambda v: v.tensor_tensor(out=A("xdd")[:, :].rearrange("p (h d) -> p h d", h=8), in0=A("xtok")[:, :].rearrange("p (h d) -> p h d", h=8), in1=A("w2s")[:, :].unsqueeze(2).broadcast_to([128, 8, 64]), op=ALU.mult))
                    with cx.fast():
                        for g in range(2):
                            op("pe", [t_xc, t_hTb], [tpyo], lambda p: p.matmul(pyo[:, g * 256:(g + 1) * 256], lhsT=xc[:, 6 + g, cs], rhs=hTb[:, g * 256:(g + 1) * 256], start=True, stop=True))
                        pst, tpst = PSH()
                        for g in range(2):
                            op("pe", [K_("Btm"), K_("xdd")], [tpst], lambda p: p.matmul(pst[:, g * 256:(g + 1) * 256], lhsT=A("Btm")[:, g * 128:(g + 1) * 128], rhs=A("xdd")[:, g * 256:(g + 1) * 256], start=True, stop=True))
                    op("dve", [t_hT, K_("etot")], [t_hT], lambda v: v.tensor_tensor(out=hT[:].rearrange("p (h d) -> p h d", h=8), in0=hT[:].rearrange("p (h d) -> p h d", h=8), in1=A("etot")[:, :].unsqueeze(2).broadcast_to([128, 8, 64]), op=ALU.mult))
                    op("dve", [t_hT, tpst], [t_hT], lambda v: v.tensor_tensor(out=hT[:], in0=hT[:], in1=pst[:, :], op=ALU.add))
                    RELH(tpst)
                    op("act", [t_hT], [t_hTb], lambda a: a.activation(out=hTb[:], in_=hT[:], func=AF.Copy))
                else:
                    op("pe", [K_("dta"), t_c], [tpc], lambda p: p.matmul(pc[:P, 8:16], lhsT=Ubd[:P, :P], rhs=A("dta")[:P, :], start=True, stop=False))
                    op("pe", [K_("dta"), t_c], [tpc], lambda p: p.matmul(pc[:P, 8:16], lhsT=SLbd[:P, :P], rhs=A("dta")[:P, :], start=False, stop=True))
                    op("act", [tpc], [K_("w2s")], lambda a: a.activation(out=A("w2s")[:P, :], in_=pc[:P, 0:8], func=AF.Copy))
                    op("dve", [tpc, K_("w2s")], [K_("t8")], lambda v: v.tensor_tensor(out=A("t8")[:P, :], in0=pc[:P, 8:16], in1=A("w2s")[:P, :], op=ALU.subtract))
                    op("act", [K_("t8")], [K_("dend")], lambda a: a.activation(out=A("dend")[:P, :], in_=A("t8")[:P, :], func=AF.Exp))
                    op("dve", [K_("dend"), K_("dt")], [K_("w2s")], lambda v: v.tensor_tensor(out=A("w2s")[:P, :], in0=A("dend")[:P, :], in1=A("dt")[:P, :], op=ALU.mult))
                    op("dve", [K_("xtok"), K_("w2s")], [K_("xdd")], lambda v: v.tensor_tensor(out=A("xdd")[:P, :].rearrange("p (h d) -> p h d", h=8), in0=A("xtok")[:P, :].rearrange("p (h d) -> p h d", h=8), in1=A("w2s")[:P, :].unsqueeze(2).broadcast_to([P, 8, 64]), op=ALU.mult))
                    for g in range(2):
                        op("dve", [t_xc, t_c], [t_CT], lambda v: v.tensor_tensor(out=CTm[:, :, g, :], in0=xc[:, 6 + g, :].unsqueeze(1).broadcast_to([128, NSS, 64]), in1=BMb[:].rearrange("p (b t) -> p b t", b=16)[:, :NSS, :], op=ALU.mult))
                    op("dve", [K_("dta")], [t_dE], lambda v: v.tensor_copy(out=dtaE[:].rearrange("p (h d) -> p h d", h=8), in_=A("dta")[:P, :].unsqueeze(2).broadcast_to([P, 8, 64])))
                    pcd, tpcd = PS()
                    for j in range(4):
                        op("pe", [t_dE, t_c], [tpcd], lambda p: p.matmul(pcd[:, j * 16:(j + 1) * 16], lhsT=dtaE[:, j * 128:(j + 1) * 128], rhs=BMt[:, :], start=True, stop=True))
                    op("act", [tpcd], [t_cd], lambda a: a.activation(out=cdT[:].rearrange("p j b -> p (j b)"), in_=pcd[:, 0:64], func=AF.Exp))
                    for b in range(NSS):
                        bl = b % 8
                        if bl == 0:
                            for b8 in range(8):
                                dma("sp", [], [t_Sn], Snat[:, b8, :, :], sssm[l, b + b8, :, :].rearrange("(j p) n -> p j n", p=128))
                        sb1, tsb1 = Sb1[b % 2], t_Sb1[b % 2]
                        stb, tstb = STb1[b % 2], t_ST1[b % 2]
                        op("act", [t_Sn], [tsb1], lambda a: a.activation(out=sb1[:], in_=Snat[:, bl, :, :], func=AF.Copy))
                        pt_, tpt = PS(); ptb = pt_[:].bitcast(BF16)
                        for j in range(4):
                            op("pe", [tsb1, t_c], [tpt], lambda p: p.transpose(out=ptb[:, j * 128:(j + 1) * 128], in_=sb1[:, j, :], identity=identb[:]))
                        op("act", [tpt], [tstb], lambda a: a.activation(out=stb[:, :], in_=ptb[:, 0:512], func=AF.Copy))
                        for g in range(2):
                            pq_, tq_ = (pyo, tpyo) if g == 0 else (pyo2, tpyo2)
                            op("pe", [t_CT, tstb], [tq_], lambda p: p.matmul(pq_[:P, 0:256], lhsT=CTm[:, b, g, :], rhs=stb[:, g * 256:(g + 1) * 256], start=(b == 0), stop=(b == NSS - 1)))
                        xm, txm = xdm[b % 2], t_xdm[b % 2]
                        op("dve", [K_("xdd"), t_c], [txm], lambda v: v.tensor_scalar(out=xm[:, :], in0=A("xdd")[:P, :], scalar1=BMt[:, b:b + 1], scalar2=None, op0=ALU.mult))
                        pst, tpst = PS()
                        for j in range(4):
                            op("pe", [txm, K_("Btm")], [tpst], lambda p: p.matmul(pst[:, j * 128:(j + 1) * 128], lhsT=xm[:, j * 128:(j + 1) * 128], rhs=A("Btm")[:P, (j // 2) * 128:(j // 2 + 1) * 128], start=True, stop=True))
                        sn, tsn = Snew[b % 2], t_Snew[b % 2]
                        for j in range(4):
                            op("dve", [t_Sn, t_cd, tpst], [tsn], lambda v: v.scalar_tensor_tensor(out=sn[:, j, :], in0=Snat[:, bl, j, :], scalar=cdT[:, j, b:b + 1], in1=pst[:, j * 128:(j + 1) * 128], op0=ALU.mult, op1=ALU.add))
                        dma("sp", [tsn], [], ossm_s[l, b, :, :].rearrange("(j p) n -> p j n", p=128), sn[:])
                yield
                chk('t_ssd')
                if not sample:
                    op("dve", [tpyo, K_("ecum")], [K_("y1")], lambda v: v.tensor_tensor(out=A("y1")[:P, :].rearrange("p (h d) -> p h d", h=8), in0=pyo[:P, :].rearrange("p (h d) -> p h d", h=8), in1=A("ecum")[:P, :].unsqueeze(2).broadcast_to([P, 8, 64]), op=ALU.mult))
                else:
                    for g, (pq_, tq_) in enumerate(((pyo, tpyo), (pyo2, tpyo2))):
                        op("dve", [tq_, K_("ecum")], [K_("y1")], lambda v: v.tensor_tensor(out=A("y1")[:P, g * 256:(g + 1) * 256].rearrange("p (h d) -> p h d", h=4), in0=pq_[:P, 0:256].rearrange("p (h d) -> p h d", h=4), in1=A("ecum")[:P, g * 4:(g + 1) * 4].unsqueeze(2).broadcast_to([P, 4, 64]), op=ALU.mult))
                op("dve", [K_("y1"), tpy], [K_("y1")], lambda v: v.tensor_tensor(out=A("y1")[:P, :], in0=A("y1")[:P, :], in1=py[:P, :], op=ALU.add))
                op("dve", [K_("y1"), K_("xsk")], [K_("y1")], lambda v: v.tensor_tensor(out=A("y1")[:P, :], in0=A("y1")[:P, :], in1=A("xsk")[:P, :], op=ALU.add))
                REL(tpy); REL(tpyo)
                if sample:
                    REL(tpyo2)
                op("dve", [K_("y1"), K_("sz")], [K_("y1")], lambda v: v.tensor_tensor(out=A("y1")[:P, :], in0=A("y1")[:P, :], in1=A("sz")[:P, :], op=ALU.mult))
                op("dve", [K_("y1")], [K_("ysq")], lambda v: v.tensor_tensor(out=A("ysq")[:P, :], in0=A("y1")[:P, :], in1=A("y1")[:P, :], op=ALU.mult))
                op("dve", [K_("ysq")], [K_("ss2")], lambda v: v.reduce_sum(out=A("ss2")[:P, :], in_=A("ysq")[:P, :].rearrange("p (g d) -> p g d", g=2), axis=AX.X))
                op("act", [K_("ss2")], [K_("ss2")], lambda a: a.activation(out=A("ss2")[:P, :], in_=A("ss2")[:P, :], func=AF.Ln, bias=256.0 * EPS, scale=1.0))
                op("act", [K_("ss2")], [K_("ss2")], lambda a: a.activation(out=A("ss2")[:P, :], in_=A("ss2")[:P, :], func=AF.Exp, scale=-0.5))
                op("dve", [K_("y1"), K_("ss2")], [K_("y1")], lambda v: v.tensor_tensor(out=A("y1")[:P, :].rearrange("p (g d) -> p g d", g=2), in0=A("y1")[:P, :].rearrange("p (g d) -> p g d", g=2), in1=A("ss2")[:P, :].unsqueeze(2).broadcast_to([P, 2, 256]), op=ALU.mult))
                op("dve", [K_("y1"), t_lay], [K_("ytm")], lambda v: v.scalar_tensor_tensor(out=A("ytm")[:P, :], in0=A("y1")[:P, :], scalar=16.0, in1=gssm[:P, :], op0=ALU.mult, op1=ALU.mult))
                yield
                pyt, tpyt = PSH(); pytb = pyt[:].bitcast(BF16)
                with cx.fast():
                    for j in range(4):
                        op("pe", [K_("ytm"), t_c], [tpyt], lambda p: p.transpose(out=pytb[:, j * 128:j * 128 + P], in_=A("ytm")[:P, j * 128:(j + 1) * 128], identity=identb[:P, :P]))
                op("act", [tpyt], [t_mixT], lambda a: a.activation(out=mixT[:, :, cs], in_=pytb[:, 0:512].rearrange("p (j t) -> p j t", j=4)[:, :, :P], func=AF.Copy))
                RELH(tpyt)

            def tile_attn(ti):
                c0 = ti * NTT
                cs = slice(c0, c0 + P)
                first_tile = first_group and ti == 0 and not sample
                chk('t_y')
                if not sample:
                    for g in range(2):
                        bs = slice(64 * g, 64 * g + 64)
                        kbs = [1] if first_tile else [0, 1]
                        for kb in kbs:
                            kcols = slice(c0 + kb * 128, c0 + kb * 128 + 128)
                            pS, tpS = PSH()
                            for hh in range(4):
                                op("pe", [t_kn, t_qn], [tpS], lambda p: p.matmul(pS[:, hh * 128:(hh + 1) * 128], lhsT=kn[bs, kcols], rhs=qn[bs, hh, cs], start=True, stop=True))
                            pe_, tpe = st["pe%d" % kb]; PT_, tPT = st["PT%d" % kb]
                            op("act", [tpS], [tpe], lambda a: a.activation(out=pe_[:], in_=pS[:], func=AF.Exp, scale=0.125))
                            RELH(tpS)
                            Et = Eprev if kb == 0 else Ecur
                            op("dve", [tpe, t_c], [tPT], lambda v: v.tensor_tensor(out=PT_[:], in0=pe_[:], in1=Et[:, g * 512:(g + 1) * 512], op=ALU.mult))
                        chk('a_S')
                        yield
                        po, tpo = PSH(); pdn, tpdn = PSH()
                        for ii, kb in enumerate(kbs):
                            PT_, tPT = st["PT%d" % kb]
                            op("pe", [t_vb, tPT], [tpo], lambda p: p.matmul(po[:64, :], lhsT=vb[:, ti + kb, g * 64:(g + 1) * 64], rhs=PT_[:], start=(ii == 0), stop=(ii == len(kbs) - 1)))
                        chk('a_po')
                        for ii, kb in enumerate(kbs):
                            PT_, tPT = st["PT%d" % kb]
                            op("pe", [t_c, tPT], [tpdn], lambda p: p.matmul(pdn[:64, :], lhsT=onesb[:, 0:64], rhs=PT_[:], start=(ii == 0), stop=(ii == len(kbs) - 1)))
                        chk('a_pdn')
                        op("dve", [tpdn, t_lay], [K_("den")], lambda v: v.tensor_tensor(out=A("den")[:, :], in0=pdn[:64, :], in1=esinkrow[:, g * 512:(g + 1) * 512], op=ALU.add))
                        RELH(tpdn)
                        op("act", [K_("den")], [K_("den")], lambda a: a.activation(out=A("den")[:, :], in_=A("den")[:, :], func=AF.Ln))
                        op("act", [K_("den")], [K_("den")], lambda a: a.activation(out=A("den")[:, :], in_=A("den")[:, :], func=AF.Exp, scale=-1.0))
                        op("dve", [tpo, K_("den")], [t_oT], lambda v: v.tensor_tensor(out=oT[:, g * 4:(g + 1) * 4, cs], in0=po[:64, :].rearrange("p (h q) -> p h q", h=4), in1=A("den")[:, :].rearrange("p (h q) -> p h q", h=4), op=ALU.mult))
                        RELH(tpo)
                        yield
                else:
                    for b4 in range(0, NSS, 4):
                        pt_, tpt = PS(); ptb = pt_[:].bitcast(BF16)
                        for bb in range(4):
                            op("pe", [t_kcb, t_c], [tpt], lambda p: p.transpose(out=ptb[:, bb * 128:(bb + 1) * 128], in_=kcb[:, b4 + bb, :], identity=identb[:]))
                        op("act", [tpt], [t_KcT], lambda a: a.activation(out=KcT[:, b4:b4 + 4, :], in_=ptb[:, 0:512].rearrange("p (b i) -> p b i", b=4), func=AF.Copy))
                    pSc, tpSc = PS()
                    for b in range(NSS):
                        for g in range(2):
                            bs = slice(64 * g, 64 * g + 64)
                            for hh in range(4):
                                idx = (b * 8 + g * 4 + hh) * 4
                                op("pe", [t_KcT, t_qn], [tpSc], lambda p: p.matmul(pSc[:, idx:idx + 4], lhsT=KcT[bs, b, :], rhs=qn[bs, hh, b * 4:b * 4 + 4], start=True, stop=True))
                    pe_, tpe = st["pe0"]
                    op("act", [tpSc], [tpe], lambda a: a.activation(out=pe_[:, :NSS * 32], in_=pSc[:, :NSS * 32], func=AF.Exp, scale=0.125))
                    op("dve", [tpe, t_c], [t_PTc], lambda v: v.tensor_tensor(out=PTc[:, :NSS * 32].rearrange("p (b x) -> p b x", b=NSS), in0=pe_[:, :NSS * 32].rearrange("p (b x) -> p b x", b=NSS), in1=Ecache[:, :].unsqueeze(1).broadcast_to([128, NSS, 32]), op=ALU.mult))
                    pSn, tpSn = PS()
                    for g in range(2):
                        bs = slice(64 * g, 64 * g + 64)
                        for hh in range(4):
                            op("pe", [t_kn, t_qn], [tpSn], lambda p: p.matmul(pSn[:P, (g * 4 + hh) * 64:(g * 4 + hh) * 64 + P], lhsT=kn[bs, 0:P], rhs=qn[bs, hh, 0:P], start=True, stop=True))
                    pe1_, tpe1 = st["pe1"]; PT1_, tPT1 = st["PT1"]
                    op("act", [tpSn], [tpe1], lambda a: a.activation(out=pe1_[:P, :], in_=pSn[:P, :], func=AF.Exp, scale=0.125))
                    for g in range(2):
                        ov = PT1_[:P, g * 256:(g + 1) * 256].rearrange("p (b hh t) -> p hh b t", b=16, hh=4)
                        i0 = pe1_[:P, g * 256:(g + 1) * 256].rearrange("p (hh b t) -> p hh b t", hh=4, b=16)
                        i1 = Enew[:P, g * 256:(g + 1) * 256].rearrange("p (hh b t) -> p hh b t", hh=4, b=16)
                        op("dve", [tpe1, t_c], [tPT1], lambda v: v.tensor_tensor(out=ov, in0=i0, in1=i1, op=ALU.mult))
                    po, tpo = PS(); pdn, tpdn = PS()
                    for (pp_, tpp, use_v) in ((po, tpo, True), (pdn, tpdn, False)):
                        for g in range(2):
                            lw = vb[:P, 1, g * 64:(g + 1) * 64] if use_v else onesb[:P, 0:64]
                            op("pe", [t_vb, tPT1, t_c], [tpp], lambda p: p.matmul(pp_[:64, g * 256:(g + 1) * 256], lhsT=lw, rhs=PT1_[:P, g * 256:(g + 1) * 256], start=True, stop=False))
                            for b in range(NSS):
                                lw2 = vcb[:, b, g * 64:(g + 1) * 64] if use_v else onesb[:, 0:64]
                                op("pe", [t_vcb, t_PTc, t_c], [tpp], lambda p: p.matmul(pp_[:64, g * 256 + b * 16:g * 256 + b * 16 + 16], lhsT=lw2, rhs=PTc[:, b * 32 + g * 16:b * 32 + g * 16 + 16], start=False, stop=(b == NSS - 1)))
                    for g in range(2):
                        dv = A("den")[:, g * 256:(g + 1) * 256].rearrange("p (b hh t) -> p hh b t", b=16, hh=4)
                        ek = esinkrow[:, :].rearrange("p (h q) -> p h q", h=8)[:, g * 4:(g + 1) * 4, 0:64].rearrange("p hh (b t) -> p hh b t", t=4)
                        op("dve", [tpdn, t_lay], [K_("den")], lambda v: v.tensor_tensor(out=dv, in0=pdn[:64, g * 256:(g + 1) * 256].rearrange("p (b hh t) -> p hh b t", b=16, hh=4), in1=ek, op=ALU.add))
                        op("dve", [K_("den")], [K_("den")], lambda v: v.reciprocal(out=A("den")[:, g * 256:(g + 1) * 256], in_=A("den")[:, g * 256:(g + 1) * 256]))
                        op("dve", [tpo, K_("den")], [t_oT], lambda v: v.tensor_tensor(out=oT[:, g * 4:(g + 1) * 4, 0:64].rearrange("p hh (b t) -> p hh b t", t=4), in0=po[:64, g * 256:(g + 1) * 256].rearrange("p (b hh t) -> p hh b t", b=16, hh=4), in1=dv, op=ALU.mult))

                yield

            for ti in range(ntile):
                if sample:
                    for _ in tile_ssd(ti):
                        pass
                    for _ in tile_attn(ti):
                        pass
                else:
                    gens = [tile_ssd(ti), tile_attn(ti)]
                    while gens:
                        for g_ in list(gens):
                            try:
                                next(g_)
                            except StopIteration:
                                gens.remove(g_)
            chk('m_tiles')
            if sample or last_group:
                ktm = SB(sc, "ktm", [128, 128]); t_ktm = Tok()
                pk, tpk = PS()
                lastc = slice(NT - P, NT)
                op("pe", [t_knf, t_c], [tpk], lambda p: p.transpose(out=pk[:P, 0:128], in_=knf[:, lastc], identity=ident[:]))
                op("act", [tpk], [t_ktm], lambda a: a.activation(out=ktm[:P, :], in_=pk[:P, 0:128], func=AF.Copy))
                if sample:
                    for b in range(NSS):
                        dma("sp", [t_ktm], [], ock_s[l, b, 124:128, :], ktm[b * 4:(b + 1) * 4, :])
                        dma("sp", [t_vf], [], ocv_s[l, b, 124:128, :], vf[b * 4:(b + 1) * 4, :])
                else:
                    dma("sp", [t_ktm], [], ock_p[l, :, :], ktm[:, :])
                    dma("sp", [t_vf], [], ocv_p[l, :, :], vf[:, :])
                    hTo = SB(sc, "hTo", [128, 4, 128]); t_hTo = Tok()
                    ph, tph = PS()
                    for j in range(4):
                        op("pe", [t_hT, t_c], [tph], lambda p: p.transpose(out=ph[:, j * 128:(j + 1) * 128], in_=hT[:, j * 128:(j + 1) * 128], identity=ident[:]))
                    op("act", [tph], [t_hTo], lambda a: a.activation(out=hTo[:].rearrange("p j n -> p (j n)"), in_=ph[:, :], func=AF.Copy))
                    dma("sp", [t_hTo], [], ossm_p[l, :, :].rearrange("(j p) n -> p j n", p=128), hTo[:])
            if not sample:
                op("dve", [t_kn], [t_kvh], lambda v: v.tensor_copy(out=khalo[:], in_=kn[:, NT:NT + 128]))
                op("dve", [t_vb], [t_kvh], lambda v: v.tensor_copy(out=vhalo[:], in_=vb[:, ntile, :]))

            chk('m_outs')
            woT = wreg[:, 0:4096].rearrange("p (k c) -> p k c", k=4)
            woA = wreg[:64, 4096:4096 + 8192].rearrange("p (k c) -> p k c", k=8)
            if cached:
                dma("pool", [t_scm[1]], T, wreg[:, 0:4096], wscm[1, :, 0:4096])
                dma("pool", [t_scm[1]], T, wreg[:64, 4096:12288], wscm[1, :64, 4096:12288])
            else:
                dma("pool", [], T, woT, w_out[l, 0:512, :].rearrange("(k p) c -> p k c", p=128))
                dma("pool", [], T, woA, w_out[l, 512:1024, :].rearrange("(hd p) c -> p hd c", p=64))
                dma("sp", T, [t_scm[1]], wscm[1, :, 0:4096], wreg[:, 0:4096])
                dma("sp", T, [t_scm[1]], wscm[1, :64, 4096:12288], wreg[:64, 4096:12288])
            with cx.fast():
                for d in range(8):
                    for (n0, nw) in nblocks(NT):
                        ps, tp = PS()
                        for k in range(4):
                            op("pe", T + [t_mixT], [tp], lambda p: p.matmul(ps[:, :nw], lhsT=woT[:, k, d * 128:(d + 1) * 128], rhs=mixT[:, k, n0:n0 + nw], start=(k == 0), stop=False))
                        for hd in range(8):
                            op("pe", T + [t_oT], [tp], lambda p: p.matmul(ps[:, :nw], lhsT=woA[:, hd, d * 128:(d + 1) * 128], rhs=oT[:, hd, n0:n0 + nw], start=False, stop=(hd == 7)))
                        op("dve", [tp, t_h], [t_h], lambda v: v.tensor_tensor(out=h[:, d, n0:n0 + nw], in0=h[:, d, n0:n0 + nw], in1=ps[:, :nw], op=ALU.add))

        def ple(sc, h, t_h, NT, l, psrc, ntok_tile, cached):
            xn = SB(sc, "xn", [128, 8, NT], BF16); t_xn = [Tok() for _ in range(8)]
            norm(sc, h, t_h, NT, 3, l, xn, t_xn)
            wg = wreg[:, 0:8192].rearrange("p (k c) -> p k c", k=8)
            wp = wreg[:, 8192:8192 + 2048].rearrange("p (k c) -> p k c", k=2)
            T = [t_wreg]
            if cached:
                dma("pool", [t_scm[2]], T, wreg[:, 0:10240], wscm[2, :, 0:10240])
            else:
                dma("pool", [], T, wg, w_pg[l, :, :].rearrange("(k p) c -> p k c", p=128))
                dma("pool", [], T, wp, w_pp[l, :, :].rearrange("(k p) c -> p k c", p=128))
                dma("sp", T, [t_scm[2]], wscm[2, :, 0:10240], wreg[:, 0:10240])
            peT = SB(sc, "peT", [128, 2, NT], BF16); t_peT = Tok()
            ptm = [SB(sc, "ptm%d" % i, [128, 256]) for i in range(2)]; t_ptm = [Tok(), Tok()]
            P = ntok_tile
            for ti in range(NT // P):
                pt_, tpt_ = ptm[ti % 2], t_ptm[ti % 2]
                dma("sp", [], [tpt_], pt_[:P, :], psrc[ti * P:(ti + 1) * P, :])
                ps, tp = PS()
                for j in range(2):
                    op("pe", [tpt_, t_c], [tp], lambda p: p.transpose(out=ps[:, j * 128:j * 128 + P], in_=pt_[:P, j * 128:(j + 1) * 128], identity=ident[:P, :P]))
                op("act", [tp], [t_peT], lambda a: a.activation(out=peT[:, :, ti * P:(ti + 1) * P], in_=ps[:, 0:256].rearrange("p (j t) -> p j t", j=2)[:, :, :P], func=AF.Copy))
            with cx.fast():
                sg = SB(sc, "sg", [128, 512]); t_sg = Tok()
                for d in range(8):
                    for (n0, nw) in nblocks(NT):
                        pg, tpg = PS(); pq, tpq = PS()
                        for k in range(8):
                            op("pe", T + [t_xn[k]], [tpg], lambda p: p.matmul(pg[:, :nw], lhsT=wg[:, k, d * 128:(d + 1) * 128], rhs=xn[:, k, n0:n0 + nw], start=(k == 0), stop=(k == 7)))
                        for k in range(2):
                            op("pe", T + [t_peT], [tpq], lambda p: p.matmul(pq[:, :nw], lhsT=wp[:, k, d * 128:(d + 1) * 128], rhs=peT[:, k, n0:n0 + nw], start=(k == 0), stop=(k == 1)))
                        op("act", [tpg], [t_sg], lambda a: a.activation(out=sg[:, :nw], in_=pg[:, :nw], func=AF.Sigmoid))
                        op("dve", [t_sg, tpq], [t_sg], lambda v: v.tensor_tensor(out=sg[:, :nw], in0=sg[:, :nw], in1=pq[:, :nw], op=ALU.mult))
                        op("dve", [t_sg, t_h], [t_h], lambda v: v.tensor_tensor(out=h[:, d, n0:n0 + nw], in0=h[:, d, n0:n0 + nw], in1=sg[:, :nw], op=ALU.add))

        stage = [0]

        def chk(name):
            stage[0] += 1
            if STAGE_LIMIT is not None and stage[0] >= STAGE_LIMIT:
                if not cx.stopped:
                    print("STOP at stage", stage[0], name)
                cx.stopped = True

        try:
          _main_body = True
          with contextlib.ExitStack() as sc:
              xtm0 = SB(sc, "xtm", [128, 1024])
              transpose_in(xtm0, Tok(), xs[:, :], NS, hs, t_hs, 0)
              cx.barrier()
          chk('sample_load')
          for l in range(DEPTH):
              load_layer_consts(l)
              chk('layer_consts')
              op("dve", [], [t_hT], lambda v: v.memset(hT[:], 0.0))
              op("dve", [], [t_hTb], lambda v: v.memset(hTb[:], 0.0))
              op("dve", [], [t_xhalo], lambda v: v.memset(xhalo[:], 0.0))
              op("dve", [], [t_kvh], lambda v: v.memset(khalo[:], 0.0))
              op("dve", [], [t_kvh], lambda v: v.memset(vhalo[:], 0.0))
              for gi in range(NG + 1):
                  sample = gi == NG
                  NT = NS if sample else NTG
                  with contextlib.ExitStack() as gsc:
                      if sample:
                          h, t_h = hs, t_hs
                      else:
                          h = SB(gsc, "hgrp", [128, 8, NTG]); t_h = Tok()
                          if l == 0:
                              with contextlib.ExitStack() as sc:
                                  xtms = [(SB(sc, "xtm", [128, 1024]), Tok()) for _ in range(2)]
                                  for ti in range(NTG // 128):
                                      transpose_in(xtms[ti % 2][0], xtms[ti % 2][1], xp[gi * NTG + ti * 128:gi * NTG + (ti + 1) * 128, :], 128, h, t_h, ti * 128)
                                  cx.barrier()
                          else:
                              dma("sp", [t_hscr[gi]], [t_h], h[:], hscr[:, :, gi * NTG:(gi + 1) * NTG].rearrange("j p t -> p j t"))
                      cx.barrier()
                      chk('group_load')
                      with contextlib.ExitStack() as sc:
                          xn = SB(sc, "xn", [128, 8, NT], BF16); t_xn = [Tok() for _ in range(8)]
                          with cx.fast():
                              norm(sc, h, t_h, NT, 0, l, xn, t_xn)
                              chk('norm')
                              ffn(sc, h, t_h, NT, xn, t_xn, w1a, w3a, w2a, l, 0, gi > 0)
                          cx.barrier()
                          chk('ffn_a')
                      with contextlib.ExitStack() as sc:
                          mixer(sc, h, t_h, NT, l, sample, gi == 0, gi == NG - 1, gi > 0)
                          cx.barrier()
                          chk('mixer')
                      with contextlib.ExitStack() as sc:
                          xn = SB(sc, "xn", [128, 8, NT], BF16); t_xn = [Tok() for _ in range(8)]
                          with cx.fast():
                              norm(sc, h, t_h, NT, 2, l, xn, t_xn)
                              ffn(sc, h, t_h, NT, xn, t_xn, w1b, w3b, w2b, l, 1, gi > 0)
                          cx.barrier()
                          chk('ffn_b')
                      with contextlib.ExitStack() as sc:
                          if sample:
                              ple(sc, h, t_h, NT, l, psm[l, :, :], 64, gi > 0)
                          else:
                              ple(sc, h, t_h, NT, l, pp[l, gi * NTG:(gi + 1) * NTG, :], 128, gi > 0)
                          cx.barrier()
                      if not sample:
                          if l == 0:
                              dma("sp", [t_h], [t_hscr[gi]], hscr[:, :, gi * NTG:(gi + 1) * NTG].rearrange("j p t -> p j t"), h[:])
                          else:
                              with contextlib.ExitStack() as sc:
                                  ytms = [(SB(sc, "ytm", [128, 1024]), Tok()) for _ in range(2)]
                                  for ti in range(NTG // 128):
                                      transpose_out(ytms[ti % 2][0], ytms[ti % 2][1], h, t_h, ti * 128, 128, yp[gi * NTG + ti * 128:gi * NTG + (ti + 1) * 128, :])
                                  cx.barrier()
                      elif l == DEPTH - 1:
                          with contextlib.ExitStack() as sc:
                              ytm0 = SB(sc, "ytm", [128, 1024])
                              transpose_out(ytm0, Tok(), h, t_h, 0, NS, ys[:, :])
                              cx.barrier()
                      cx.barrier()
        except _Stop:
            pass
        cx.finish()
    return nc


def make_consts():
    c = {}
    c["c_ident"] = np.eye(128, dtype=np.float32)
    i = np.arange(128)
    c["c_U"] = (i[:, None] <= i[None, :]).astype(np.float32)
    c["c_SL"] = (i[:, None] > i[None, :]).astype(np.float32)
    j = np.arange(64); same = (j[:, None] // 4) == (j[None, :] // 4)
    c["c_Ubd"] = (same & (j[:, None] <= j[None, :])).astype(np.float32)
    c["c_SLbd"] = (same & (j[:, None] > j[None, :])).astype(np.float32)
    c["c_BMt"] = ((j[:, None] // 4) == np.arange(16)[None, :]).astype(np.float32)
    bm = ((np.arange(16)[:, None]) == (j[None, :] // 4)).astype(np.float32)
    c["c_BM"] = np.broadcast_to(bm.reshape(1, 16 * 64), (128, 16 * 64)).copy()
    bo = np.zeros((128, 128), np.float32); bo[:64, :64] = 1; bo[64:, 64:] = 1
    c["c_bones"] = bo
    slopes = np.power(np.float32(2.0), -8.0 * np.arange(1, 9, dtype=np.float32) / 8).astype(np.float32)
    s = i[:, None, None]; q = i[None, None, :]; sl = slopes[None, :, None]
    ecur = np.where(q >= s, np.exp(-sl * (q - s).astype(np.float32)), 0.0)
    eprev = np.where(s > q, np.exp(-sl * (q - s + 128).astype(np.float32)), 0.0)
    c["c_Ecur"] = ecur.astype(np.float32).reshape(128, 1024)
    c["c_Eprev"] = eprev.astype(np.float32).reshape(128, 1024)
    t = np.arange(4)[None, None, :]
    ecache = np.where(s > t, np.exp(-sl * (128 + t - s).astype(np.float32)), 0.0)
    c["c_Ecache"] = ecache.astype(np.float32).reshape(128, 32)
    sj = j[:, None, None]; qj = j[None, None, :]
    enew = np.where(((sj // 4) == (qj // 4)) & (sj <= qj), np.exp(-sl * (qj - sj).astype(np.float32)), 0.0)
    c["c_Enew"] = enew.astype(np.float32).reshape(64, 512)
    return c


_WNAMES = ["g_ffn1", "w1_a", "w3_a", "w2_a", "g_mix", "w_in", "conv_w", "conv_b", "dt_bias", "a_log", "d_skip", "ssm_norm",
           "q_norm", "k_norm", "sinks", "w_out", "g_ffn2", "w1_b", "w3_b", "w2_b", "g_ple", "w_ple_gate", "w_ple_proj"]


def run(inputs, SEQ, NSS, NTG, n_prompt, ncores):
    f = lambda a: np.ascontiguousarray(np.asarray(a, dtype=np.float32))
    nc = build(SEQ, NSS, NTG)
    consts = make_consts()
    wts = {n: f(inputs[n]) for n in _WNAMES}
    xpr = f(inputs["x_prompt"]); ppr = f(inputs["p_prompt"]); xsm = f(inputs["x_sample"]); psm = f(inputs["p_sample"])
    sssm = f(inputs["state_ssm"]); sconv = f(inputs["state_conv"]); ck = f(inputs["cache_k_win"]); cv = f(inputs["cache_v_win"])
    in_maps = []
    for c in range(ncores):
        b = c % n_prompt
        bs = slice(c * NSS, (c + 1) * NSS)
        m = dict(wts); m.update(consts)
        m["xp"] = f(xpr[b]); m["pp"] = f(ppr[:, b])
        m["xs"] = f(xsm[bs].reshape(NSS * 4, D)); m["psm"] = f(psm[:, bs].reshape(DEPTH, NSS * 4, DPLE))
        m["sssm"] = f(sssm[:, bs].reshape(DEPTH, NSS, 512, 128)); m["sconv"] = f(sconv[:, bs].reshape(DEPTH, NSS * 3, 1024))
        m["ck"] = f(ck[:, bs].reshape(DEPTH, NSS, 128, 128)); m["cv"] = f(cv[:, bs].reshape(DEPTH, NSS, 128, 128))
        in_maps.append(m)
    res = run_bass_kernel_spmd(nc, in_maps, core_ids=list(range(ncores))).results
    P = n_prompt
    y_p = np.stack([res[b]["yp"] for b in range(P)])
    y_s = np.concatenate([res[c]["ys"].reshape(NSS, 4, D) for c in range(ncores)])
    ssm_p = np.stack([res[b]["ossm_p"].reshape(DEPTH, 8, 64, 128) for b in range(P)], axis=1)
    conv_p = np.stack([res[b]["oconv_p"] for b in range(P)], axis=1)
    k_p = np.stack([res[b]["ock_p"].reshape(DEPTH, 128, 2, 64) for b in range(P)], axis=1)
    v_p = np.stack([res[b]["ocv_p"].reshape(DEPTH, 128, 2, 64) for b in range(P)], axis=1)
    ssm_s = np.concatenate([res[c]["ossm_s"].reshape(DEPTH, NSS, 8, 64, 128) for c in range(ncores)], axis=1)
    conv_s = np.concatenate([res[c]["oconv_s"].reshape(DEPTH, NSS, 3, 1024) for c in range(ncores)], axis=1)
    k_s = np.concatenate([res[c]["ock_s"].reshape(DEPTH, NSS, 128, 2, 64) for c in range(ncores)], axis=1)
    v_s = np.concatenate([res[c]["ocv_s"].reshape(DEPTH, NSS, 128, 2, 64) for c in range(ncores)], axis=1)
    return tuple(np.ascontiguousarray(a, dtype=np.float32) for a in (y_p, y_s, ssm_p, conv_p, k_p, v_p, ssm_s, conv_s, k_s, v_s))


def kernel(**inputs):
    return run(inputs, SEQ=4096, NSS=16, NTG=512, n_prompt=4, ncores=NCORES)
```

```python
import contextlib
import numpy as np
import concourse.bass as bass
import concourse.mybir as mybir
from concourse.bass_utils import run_bass_kernel_spmd

F32 = mybir.dt.float32
BF16 = mybir.dt.bfloat16
AF = mybir.ActivationFunctionType
ALU = mybir.AluOpType
AX = mybir.AxisListType

D = 1024; DFF = 2752; DPROJ = 2312; DPLE = 256; DEPTH = 2
NCORES = 8
EPS = 1e-6
FT = [(i * 128, 128) for i in range(21)] + [(2688, 64)]
NFT = len(FT)
FCH = [list(range(i, min(i + 2, NFT))) for i in range(0, NFT, 2)]
W2CH = [list(range(i, min(i + 6, NFT))) for i in range(0, NFT, 6)]


DEBUG_MAP = None
SBUF_PEAK = [0, 0]
STAGE_LIMIT = None


class _Stop(Exception):
    pass


class Tok:
    __slots__ = ("w", "r")

    def __init__(self):
        self.w = None
        self.r = {}


class Eng:
    def __init__(self, name, h):
        self.name = name; self.h = h; self.sem = None; self.cnt = 0; self.waited = {}; self.own = set()


def _r32(n):
    return 32 if n <= 32 else (64 if n <= 64 else 128)


class PEProxy:
    def __init__(self, ctx, e):
        self.ctx = ctx; self.e = e; self.last = None

    def _mode(self, key):
        e = self.e
        if key != self.last and e.cnt > 0:
            k = id(e.sem)
            if e.waited.get(k, 0) < e.cnt:
                e.h.wait_ge(e.sem, e.cnt)
                e.waited[k] = e.cnt
        self.last = key

    def matmul(self, out, lhsT, rhs, start=True, stop=True):
        self._mode(("mm", str(lhsT.dtype), _r32(lhsT.shape[0]), _r32(int(np.prod(lhsT.shape[1:]))), out.base_partition()))
        return self.e.h.matmul(out, lhsT=lhsT, rhs=rhs, start=start, stop=stop)

    def transpose(self, out, in_, identity):
        self._mode(("tr", str(in_.dtype), _r32(in_.shape[0]), _r32(int(np.prod(in_.shape[1:]))), out.base_partition()))
        return self.e.h.transpose(out=out, in_=in_, identity=identity)


class Ctx:
    EPOCH = 12000

    def __init__(self, nc, es):
        self.nc = nc; self.es = es
        self.E = {"pe": Eng("pe", nc.tensor), "act": Eng("act", nc.scalar), "dve": Eng("dve", nc.vector),
                  "pool": Eng("pool", nc.gpsimd), "sp": Eng("sp", nc.sync)}
        self.nsem = 0
        for e in self.E.values():
            e.sem = self._newsem(); e.own.add(id(e.sem))
        self.slots = {"sp": [[self._newsem(), 0] for _ in range(10)],
                      "pool": [[self._newsem(), 0] for _ in range(10)]}
        self.slot_i = {"sp": 0, "pool": 0}
        self.semkey = {}
        self.stopped = False
        self.pe_proxy = PEProxy(self, self.E["pe"])
        self.pe_fast = False

    @contextlib.contextmanager
    def fast(self):
        old = self.pe_fast
        self.pe_fast = True; self.pe_proxy.last = "edge"
        try:
            yield
        finally:
            self.pe_fast = old; self.pe_proxy.last = "edge"

    def _newsem(self):
        self.nsem += 1
        return self.es.enter_context(self.nc.semaphore("s%d" % self.nsem))

    def _wait(self, e, ev):
        sem, val = ev
        k = id(sem)
        if e.name == "pe" and k in e.own and self.pe_fast:
            return
        if e.waited.get(k, 0) >= val:
            return
        e.h.wait_ge(sem, val)
        e.waited[k] = val

    def _sync(self, e, reads, writes):
        for t in reads:
            if t.w is not None:
                self._wait(e, t.w)
        for t in writes:
            if t.w is not None:
                self._wait(e, t.w)
            for ev in t.r.values():
                self._wait(e, ev)

    def _commit(self, ev, reads, writes):
        for t in writes:
            t.w = ev; t.r = {}
        for t in reads:
            k = id(ev[0])
            if k not in t.r or t.r[k][1] < ev[1]:
                t.r[k] = ev

    def op(self, eng, reads, writes, fn):
        if self.stopped:
            return None
        e = self.E[eng]
        self._sync(e, reads, writes)
        if e.cnt >= self.EPOCH:
            e.sem = self._newsem(); e.cnt = 0; e.own.add(id(e.sem))
        inst = fn(self.pe_proxy if eng == "pe" else e.h)
        e.cnt += 1
        inst.then_inc(e.sem, 1)
        if DEBUG_MAP is not None:
            import traceback
            nm = None
            for a in ("name", "inst", "instruction", "ins"):
                v = getattr(inst, a, None)
                if v is not None:
                    nm = getattr(v, "name", v) if a != "name" else v
                    break
            fr = traceback.extract_stack(limit=4)[-2]; fr0 = traceback.extract_stack(limit=4)[-3]
            DEBUG_MAP[str(nm)] = "%s:%d < %s:%d" % (fr.name, fr.lineno, fr0.name, fr0.lineno)
        ev = (e.sem, e.cnt)
        e.waited[id(e.sem)] = max(e.waited.get(id(e.sem), 0), 0)
        self._commit(ev, reads, writes)
        return ev

    def dma(self, eng, reads, writes, out, in_, **kw):
        if self.stopped:
            return None
        e = self.E[eng]
        sl = self.slots[eng][self.slot_i[eng]]
        self.slot_i[eng] = (self.slot_i[eng] + 1) % len(self.slots[eng])
        if sl[1] > 0:
            self._wait(e, (sl[0], sl[1]))
        self._sync(e, reads, writes)
        inst = e.h.dma_start(out=out, in_=in_, **kw)
        sl[1] += 16
        inst.then_inc(sl[0], 16)
        ev = (sl[0], sl[1])
        self._commit(ev, reads, writes)
        return ev

    def barrier(self):
        if self.stopped:
            return
        evs = []
        for e in self.E.values():
            if e.cnt > 0:
                evs.append((e.sem, e.cnt))
        for q in self.slots.values():
            for sl in q:
                if sl[1] > 0:
                    evs.append((sl[0], sl[1]))
        for e in self.E.values():
            for ev in evs:
                if ev[0] is e.sem:
                    continue
                self._wait(e, ev)

    def finish(self):
        self.stopped = False
        self.barrier()


def build(SEQ, NSS, NTG):
    NS = NSS * 4
    NG = SEQ // NTG
    nc = bass.Bass("TRN2", target_bir_lowering=False)
    di = lambda n, s: nc.dram_tensor(n, s, F32, kind="ExternalInput").ap()
    do = lambda n, s: nc.dram_tensor(n, s, F32, kind="ExternalOutput").ap()
    xp = di("xp", [SEQ, D]); pp = di("pp", [DEPTH, SEQ, DPLE]); xs = di("xs", [NS, D]); psm = di("psm", [DEPTH, NS, DPLE])
    sssm = di("sssm", [DEPTH, NSS, 512, 128]); sconv = di("sconv", [DEPTH, NSS * 3, 1024])
    ck = di("ck", [DEPTH, NSS, 128, 128]); cv = di("cv", [DEPTH, NSS, 128, 128])
    g_ffn1 = di("g_ffn1", [DEPTH, D]); g_mix = di("g_mix", [DEPTH, D]); g_ffn2 = di("g_ffn2", [DEPTH, D]); g_ple = di("g_ple", [DEPTH, D])
    w1a = di("w1_a", [DEPTH, D, DFF]); w3a = di("w3_a", [DEPTH, D, DFF]); w2a = di("w2_a", [DEPTH, DFF, D])
    w1b = di("w1_b", [DEPTH, D, DFF]); w3b = di("w3_b", [DEPTH, D, DFF]); w2b = di("w2_b", [DEPTH, DFF, D])
    w_in = di("w_in", [DEPTH, D, DPROJ]); w_out = di("w_out", [DEPTH, D, D])
    conv_w = di("conv_w", [DEPTH, 4, 1024]); conv_b = di("conv_b", [DEPTH, 1024])
    dt_bias = di("dt_bias", [DEPTH, 8]); a_log = di("a_log", [DEPTH, 8]); d_skip = di("d_skip", [DEPTH, 8])
    ssm_norm = di("ssm_norm", [DEPTH, 512]); q_norm = di("q_norm", [DEPTH, 64]); k_norm = di("k_norm", [DEPTH, 64])
    sinks = di("sinks", [DEPTH, 8]); w_pg = di("w_ple_gate", [DEPTH, D, D]); w_pp = di("w_ple_proj", [DEPTH, DPLE, D])
    c_ident = di("c_ident", [128, 128]); c_U = di("c_U", [128, 128]); c_SL = di("c_SL", [128, 128])
    c_Ubd = di("c_Ubd", [64, 64]); c_SLbd = di("c_SLbd", [64, 64]); c_BMt = di("c_BMt", [64, 16]); c_BM = di("c_BM", [128, 16 * 64])
    c_bones = di("c_bones", [128, 128]); c_Eprev = di("c_Eprev", [128, 1024]); c_Ecur = di("c_Ecur", [128, 1024])
    c_Ecache = di("c_Ecache", [128, 32]); c_Enew = di("c_Enew", [64, 512])
    yp = do("yp", [SEQ, D]); ys = do("ys", [NS, D])
    ossm_p = do("ossm_p", [DEPTH, 512, 128]); oconv_p = do("oconv_p", [DEPTH, 3, 1024])
    ock_p = do("ock_p", [DEPTH, 128, 128]); ocv_p = do("ocv_p", [DEPTH, 128, 128])
    ossm_s = do("ossm_s", [DEPTH, NSS, 512, 128]); oconv_s = do("oconv_s", [DEPTH, NSS * 3, 1024])
    ock_s = do("ock_s", [DEPTH, NSS, 128, 128]); ocv_s = do("ocv_s", [DEPTH, NSS, 128, 128])
    hscr = nc.dram_tensor("hscr", [8, 128, SEQ], F32, kind="Internal").ap()
    wsc13 = nc.dram_tensor("wsc13", [2, 2, len(FCH), 128, 2048], BF16, kind="Internal").ap()
    wsc2 = nc.dram_tensor("wsc2", [2, 2, 128, NFT * 512], BF16, kind="Internal").ap()
    wscm = nc.dram_tensor("wscm", [3, 128, 18560], BF16, kind="Internal").ap()
    t_sc13 = [[[Tok() for _ in FCH] for _ in range(2)] for _ in range(2)]
    t_sc2 = [[[Tok() for _ in W2CH] for _ in range(2)] for _ in range(2)]
    t_scm = [Tok() for _ in range(3)]
    t_hscr = [Tok() for _ in range(NG)]

    with contextlib.ExitStack() as es:
        cx = Ctx(nc, es)
        op = cx.op; dma = cx.dma

        uniq = [0]

        def SB(scope, name, shape, dt=F32):
            uniq[0] += 1
            t = scope.enter_context(nc.sbuf_tensor("%s_%d" % (name, uniq[0]), shape, dt))
            try:
                SBUF_PEAK[0] = max(SBUF_PEAK[0], int(nc.sbuf_base))
                SBUF_PEAK[1] = int(nc.sbuf_top)
            except Exception:
                pass
            return t

        ident = SB(es, "ident", [128, 128]); t_c = Tok()
        identb = SB(es, "identb", [128, 128], BF16)
        Um = SB(es, "Um", [128, 128]); SLm = SB(es, "SLm", [128, 128])
        Ubd = SB(es, "Ubd", [64, 64]); SLbd = SB(es, "SLbd", [64, 64]); BMt = SB(es, "BMt", [64, 16]); BMtb = SB(es, "BMtb", [64, 16], BF16)
        BMb = SB(es, "BMb", [128, 16 * 64], BF16)
        bones = SB(es, "bones", [128, 128], BF16); onesb = SB(es, "onesb", [128, 128], BF16); onesf = SB(es, "onesf", [128, 128])
        Eprev = SB(es, "Eprev", [128, 1024]); Ecur = SB(es, "Ecur", [128, 1024]); Ecache = SB(es, "Ecache", [128, 32]); Enew = SB(es, "Enew", [64, 512])
        gcol = SB(es, "gcol", [128, 4 * DEPTH * 8])
        lay = {}
        for nm, w in [("cw", 32), ("cb", 8), ("dtb", 8), ("aneg", 8), ("dsk", 8), ("esk", 8), ("gq", 1), ("gk", 1)]:
            lay[nm] = SB(es, "l_" + nm, [128, w])
        gssm = SB(es, "gssm", [128, 512]); esinkrow = SB(es, "esinkrow", [64, 1024])
        t_lay = Tok()
        hs = SB(es, "hs", [128, 8, 64]); t_hs = Tok()
        w13 = [[SB(es, "w13_%d_%d" % (m, b), [128, 8, 256], BF16) for b in range(2)] for m in range(2)]
        t_w13 = [[Tok() for _ in range(2)] for _ in range(2)]
        wreg = SB(es, "wreg", [128, 18560], BF16); t_wreg = Tok()
        hT = SB(es, "hT", [128, 512]); hTb = SB(es, "hTb", [128, 512], BF16); t_hT = Tok(); t_hTb = Tok()
        xhalo = SB(es, "xhalo", [128, 8, 3]); t_xhalo = Tok()
        khalo = SB(es, "khalo", [128, 128], BF16); vhalo = SB(es, "vhalo", [128, 128], BF16); t_kvh = Tok()
        psb = [es.enter_context(nc.psum_tensor("ps%d" % i, [128, 512], F32)) for i in range(8)]
        t_ps = [Tok() for _ in range(8)]
        psi = [0]

        held = set()

        def PS(hold=False):
            for _try in range(9):
                i = psi[0]; psi[0] = (i + 1) % 8
                if i not in held:
                    break
            else:
                raise RuntimeError("all PSUM banks held")
            if hold:
                held.add(i)
            return psb[i], t_ps[i]

        def REL(tok):
            held.discard(t_ps.index(tok))

        def ld(eng, dst, src, toks, **kw):
            dma(eng, [], toks, dst, src, **kw)
        ld("sp", ident[:], c_ident[:, :], [t_c]); ld("pool", identb[:], c_ident[:, :], [t_c])
        ld("sp", Um[:], c_U[:, :], [t_c]); ld("sp", SLm[:], c_SL[:, :], [t_c])
        ld("sp", Ubd[:], c_Ubd[:, :], [t_c]); ld("sp", SLbd[:], c_SLbd[:, :], [t_c]); ld("sp", BMt[:], c_BMt[:, :], [t_c])
        ld("pool", BMtb[:], c_BMt[:, :], [t_c]); ld("pool", BMb[:], c_BM[:, :], [t_c]); ld("pool", bones[:], c_bones[:, :], [t_c])
        ld("sp", Eprev[:], c_Eprev[:, :], [t_c]); ld("sp", Ecur[:], c_Ecur[:, :], [t_c]); ld("sp", Ecache[:], c_Ecache[:, :], [t_c]); ld("sp", Enew[:], c_Enew[:, :], [t_c])
        op("dve", [], [t_c], lambda v: v.memset(onesb[:], 1.0))
        op("dve", [], [t_c], lambda v: v.memset(onesf[:], 1.0))
        for ni, g in enumerate([g_ffn1, g_mix, g_ffn2, g_ple]):
            for l in range(DEPTH):
                o = (ni * DEPTH + l) * 8
                ld("sp", gcol[:, o:o + 8], g[l, :].rearrange("(j p) -> p j", p=128), [t_c], allow_slow_non_contiguous=True)
        op("dve", [t_c], [t_c], lambda v: v.tensor_scalar(out=gcol[:], in0=gcol[:], scalar1=32.0, scalar2=None, op0=ALU.mult))

        def load_layer_consts(l):
            T = [t_lay]
            for j in range(4):
                ld("sp", lay["cw"][:, j * 8:(j + 1) * 8], conv_w[l, j, :].rearrange("(c p) -> p c", p=128), T, allow_slow_non_contiguous=True)
            ld("sp", lay["cb"][:], conv_b[l, :].rearrange("(c p) -> p c", p=128), T, allow_slow_non_contiguous=True)
            ld("sp", lay["dtb"][:], dt_bias[l, :].partition_broadcast(128), T)
            ld("sp", lay["aneg"][:], a_log[l, :].partition_broadcast(128), T)
            ld("sp", lay["dsk"][:], d_skip[l, :].partition_broadcast(128), T)
            ld("sp", lay["esk"][:], sinks[l, :].partition_broadcast(128), T)
            for hh in range(2):
                ld("sp", lay["gq"][hh * 64:(hh + 1) * 64, :], q_norm[l, :].rearrange("(p o) -> p o", o=1), T, allow_slow_non_contiguous=True)
                ld("sp", lay["gk"][hh * 64:(hh + 1) * 64, :], k_norm[l, :].rearrange("(p o) -> p o", o=1), T, allow_slow_non_contiguous=True)
            ld("sp", gssm[:], ssm_norm[l, :].partition_broadcast(128), T)
            op("act", T, T, lambda a: a.activation(out=lay["aneg"][:], in_=lay["aneg"][:], func=AF.Exp))
            op("dve", T, T, lambda v: v.tensor_scalar(out=lay["aneg"][:], in0=lay["aneg"][:], scalar1=-1.0, scalar2=None, op0=ALU.mult))
            op("act", T, T, lambda a: a.activation(out=lay["esk"][:], in_=lay["esk"][:], func=AF.Exp))
            op("dve", T, T, lambda v: v.tensor_scalar(out=lay["gq"][:], in0=lay["gq"][:], scalar1=8.0, scalar2=None, op0=ALU.mult))
            op("dve", T, T, lambda v: v.tensor_scalar(out=lay["gk"][:], in0=lay["gk"][:], scalar1=8.0, scalar2=None, op0=ALU.mult))
            op("dve", T, T, lambda v: v.tensor_scalar(out=gssm[:], in0=gssm[:], scalar1=16.0, scalar2=None, op0=ALU.mult))
            op("dve", T, T, lambda v: v.tensor_copy(out=esinkrow[:].rearrange("p (h q) -> p h q", h=8),
                                                     in_=lay["esk"][0:64, :].unsqueeze(2).broadcast_to([64, 8, 128])))

        def nblocks(NT):
            return [(n0, min(512, NT - n0)) for n0 in range(0, NT, 512)]

        def norm(sc, h, t_h, NT, ni, l, xn, t_xn):
            go = (ni * DEPTH + l) * 8
            sq = SB(sc, "sq", [128, 8, 512], BF16); t_sq = Tok()
            rstd = SB(sc, "rstd", [128, 512]); t_rstd = Tok()
            for (n0, nw) in nblocks(NT):
                for half in range(2):
                    op("act", [t_h], [t_sq], lambda a: a.activation(out=sq[:, half * 4:(half + 1) * 4, :nw], in_=h[:, half * 4:(half + 1) * 4, n0:n0 + nw], func=AF.Square))
                ps, tp = PS()
                for j in range(8):
                    op("pe", [t_sq, t_c], [tp], lambda p: p.matmul(ps[:, :nw], lhsT=onesb[:], rhs=sq[:, j, :nw], start=(j == 0), stop=(j == 7)))
                op("act", [tp], [t_rstd], lambda a: a.activation(out=rstd[:, :nw], in_=ps[:, :nw], func=AF.Ln, bias=1024.0 * EPS, scale=1.0))
                op("act", [t_rstd], [t_rstd], lambda a: a.activation(out=rstd[:, :nw], in_=rstd[:, :nw], func=AF.Exp, scale=-0.5))
                for j in range(8):
                    op("dve", [t_h, t_rstd, t_c], [t_xn[j]], lambda v: v.scalar_tensor_tensor(out=xn[:, j, n0:n0 + nw], in0=h[:, j, n0:n0 + nw], scalar=gcol[:, go + j:go + j + 1], in1=rstd[:, :nw], op0=ALU.mult, op1=ALU.mult))

        w13_next = [None]

        def issue_w13(W1, W3, l, ci, parity, ab, cached):
            cols = FCH[ci]; f0 = FT[cols[0]][0]; fw = sum(FT[c][1] for c in cols)
            for m, W in enumerate((W1, W3)):
                scr = wsc13[ab, m, ci, :, :].rearrange("p (k f) -> p k f", k=8)[:, :, :fw]
                if cached:
                    dma("pool", [t_sc13[ab][m][ci]], [t_w13[m][parity]], w13[m][parity][:, :, :fw], scr)
                else:
                    dma("pool", [], [t_w13[m][parity]], w13[m][parity][:, :, :fw], W[l, :, f0:f0 + fw].rearrange("(k p) f -> p k f", p=128))
                    dma("sp", [t_w13[m][parity]], [t_sc13[ab][m][ci]], scr, w13[m][parity][:, :, :fw])

        def ffn(sc, h, t_h, NT, xn, t_xn, W1, W3, W2, l, ab, cached):
            gT = SB(sc, "gT", [128, NFT, NT], BF16); t_g = [Tok() for _ in range(NFT)]
            s1 = [SB(sc, "s1_%d" % i, [128, 512]) for i in range(2)]; t_s1 = [Tok(), Tok()]
            w2 = SB(sc, "w2", [128, NFT, 512], BF16); t_w2 = [Tok() for _ in W2CH]
            si = 0
            issue_w13(W1, W3, l, 0, 0, ab, cached)
            for ci, cols in enumerate(FCH):
                par = ci % 2
                if ci + 1 < len(FCH):
                    issue_w13(W1, W3, l, ci + 1, (ci + 1) % 2, ab, cached)
                for fi, ft in enumerate(cols):
                    fw = FT[ft][1]; fo = fi * 128
                    for (n0, nw) in nblocks(NT):
                        p1, tp1 = PS(); p3, tp3 = PS()
                        for (pp_, tpp, m) in ((p1, tp1, 0), (p3, tp3, 1)):
                            for k in range(8):
                                op("pe", [t_w13[m][par], t_xn[k]], [tpp], lambda p: p.matmul(pp_[:fw, :nw], lhsT=w13[m][par][:, k, fo:fo + fw], rhs=xn[:, k, n0:n0 + nw], start=(k == 0), stop=(k == 7)))
                        sb_, ts_ = s1[si], t_s1[si]; si ^= 1
                        op("act", [tp1], [ts_], lambda a: a.activation(out=sb_[:fw, :nw], in_=p1[:fw, :nw], func=AF.Silu))
                        op("dve", [ts_, tp3], [t_g[ft]], lambda v: v.tensor_tensor(out=gT[:fw, ft, n0:n0 + nw], in0=sb_[:fw, :nw], in1=p3[:fw, :nw], op=ALU.mult))
            for half in range(2):
                for wi, rows in enumerate(W2CH):
                    r0 = FT[rows[0]][0]
                    nfull = [r for r in rows if FT[r][1] == 128]
                    scr2 = wsc2[ab, half, :, :].rearrange("p (f c) -> p f c", f=NFT)
                    if nfull:
                        sl = slice(nfull[0], nfull[-1] + 1)
                        if cached:
                            dma("pool", [t_sc2[ab][half][wi]], [t_w2[wi]], w2[:, sl, :], scr2[:, sl, :])
                        else:
                            dma("pool", [], [t_w2[wi]], w2[:, sl, :], W2[l, r0:r0 + 128 * len(nfull), half * 512:(half + 1) * 512].rearrange("(f p) c -> p f c", p=128))
                    for r in rows:
                        if FT[r][1] != 128:
                            if cached:
                                dma("pool", [t_sc2[ab][half][wi]], [t_w2[wi]], w2[:64, r, :], scr2[:64, r, :])
                            else:
                                dma("pool", [], [t_w2[wi]], w2[:64, r, :], W2[l, FT[r][0]:FT[r][0] + 64, half * 512:(half + 1) * 512])
                    if not cached:
                        if nfull:
                            dma("sp", [t_w2[wi]], [t_sc2[ab][half][wi]], scr2[:, sl, :], w2[:, sl, :])
                        for r in rows:
                            if FT[r][1] != 128:
                                dma("sp", [t_w2[wi]], [t_sc2[ab][half][wi]], scr2[:64, r, :], w2[:64, r, :])
                for (n0, nw) in nblocks(NT):
                    acc = [PS() for _ in range(4)]
                    for ft in range(NFT):
                        fw = FT[ft][1]
                        wi = [i for i, rows in enumerate(W2CH) if ft in rows][0]
                        for dj in range(4):
                            op("pe", [t_w2[wi], t_g[ft]], [acc[dj][1]], lambda p: p.matmul(acc[dj][0][:, :nw], lhsT=w2[:fw, ft, dj * 128:(dj + 1) * 128], rhs=gT[:fw, ft, n0:n0 + nw], start=(ft == 0), stop=(ft == NFT - 1)))
                    for dj in range(4):
                        d = half * 4 + dj
                        op("dve", [acc[dj][1], t_h], [t_h], lambda v: v.scalar_tensor_tensor(out=h[:, d, n0:n0 + nw], in0=acc[dj][0][:, :nw], scalar=0.5, in1=h[:, d, n0:n0 + nw], op0=ALU.mult, op1=ALU.add))

        def transpose_in(xtm, t_x, src, ntok, h, t_h, c0):
            dma("sp", [], [t_x], xtm[:ntok, :], src)
            for a in range(2):
                ps, tp = PS()
                for j in range(4):
                    op("pe", [t_x, t_c], [tp], lambda p: p.transpose(out=ps[:, j * 128:j * 128 + ntok], in_=xtm[:ntok, (a * 4 + j) * 128:(a * 4 + j + 1) * 128], identity=ident[:ntok, :ntok]))
                op("act", [tp], [t_h], lambda a_: a_.activation(out=h[:, a * 4:(a + 1) * 4, c0:c0 + ntok], in_=ps[:].rearrange("p (j t) -> p j t", j=4)[:, :, :ntok], func=AF.Copy))

        def transpose_out(ytm, t_y, h, t_h, c0, ntok, dst):
            for a in range(2):
                ps, tp = PS()
                for j in range(4):
                    op("pe", [t_h, t_c], [tp], lambda p: p.transpose(out=ps[:ntok, j * 128:(j + 1) * 128], in_=h[:, a * 4 + j, c0:c0 + ntok], identity=ident[:]))
                op("act", [tp], [t_y], lambda a_: a_.activation(out=ytm[:ntok, a * 512:(a + 1) * 512], in_=ps[:ntok, :], func=AF.Copy))
            dma("sp", [t_y], [], dst, ytm[:ntok, :])

        def mixer(sc, h, t_h, NT, l, sample, first_group, last_group, cached):
            xn = SB(sc, "xn", [128, 8, NT], BF16); t_xn = [Tok() for _ in range(8)]
            norm(sc, h, t_h, NT, 1, l, xn, t_xn)
            chk('m_norm')
            NTT = 64 if sample else 128
            ntile = NT // NTT
            o = 0
            def carve(n, shape_str, **kw):
                nonlocal o
                a = wreg[:, o:o + n]; o += n
                return a.rearrange(shape_str, **kw)
            wz = carve(8 * 512, "p (k c) -> p k c", k=8); wx = carve(8 * 1024, "p (k c) -> p k c", k=8)
            wq = carve(8 * 512, "p (k c) -> p k c", k=8); wk = carve(8 * 128, "p (k c) -> p k c", k=8)
            wv = carve(8 * 128, "p (k c) -> p k c", k=8); wdt = carve(8 * 8, "p (k c) -> p k c", k=8)
            wl = w_in[l, :, :].rearrange("(k p) c -> p k c", p=128)
            T = [t_wreg]
            if cached:
                dma("pool", [t_scm[0]], T, wreg[:, 0:18496], wscm[0, :, 0:18496])
            else:
                dma("pool", [], T, wz, wl[:, :, 0:512]); dma("pool", [], T, wx, wl[:, :, 512:1536]); dma("pool", [], T, wdt, wl[:, :, 1536:1544])
                for hh in range(4):
                    for g in range(2):
                        c = 1544 + g * 256 + hh * 64
                        dma("pool", [], T, wq[:, :, hh * 128 + g * 64:hh * 128 + g * 64 + 64], wl[:, :, c:c + 64])
                dma("pool", [], T, wk, wl[:, :, 2056:2184]); dma("pool", [], T, wv, wl[:, :, 2184:2312])
                dma("sp", T, [t_scm[0]], wscm[0, :, 0:18496], wreg[:, 0:18496])
            chk('m_wdma')
            HAL = 0 if sample else 3
            xin = SB(sc, "xin", [128, 8, NT + HAL]); t_xin = Tok()
            xc = SB(sc, "xc", [128, 8, NT], BF16); t_xc = Tok()
            qn = SB(sc, "qn", [128, 4, NT], BF16); t_qn = Tok()
            KH = 0 if sample else 128
            kn = SB(sc, "kn", [128, KH + NT], BF16); t_kn = Tok()
            knf = SB(sc, "knf", [128, NT]); t_knf = Tok()
            vb = SB(sc, "vb", [128, ntile + 1, 128], BF16); t_vb = Tok()
            vf = SB(sc, "vf", [128, 128]); t_vf = Tok()
            mixT = SB(sc, "mixT", [128, 4, NT], BF16); t_mixT = Tok()
            oT = SB(sc, "oT", [64, 8, NT], BF16); t_oT = Tok()
            tmpa = SB(sc, "tmpa", [128, 512]); t_tmpa = Tok()
            rq = SB(sc, "rq", [128, 512]); t_rq = Tok()
            sqb = SB(sc, "sqb", [128, 512], BF16); t_sqb = Tok()
            if not sample:
                op("dve", [t_xhalo], [t_xin], lambda v: v.tensor_copy(out=xin[:, :, 0:3], in_=xhalo[:]))
                op("dve", [t_kvh], [t_kn], lambda v: v.tensor_copy(out=kn[:, 0:128], in_=khalo[:]))
                op("dve", [t_kvh], [t_vb], lambda v: v.tensor_copy(out=vb[:, 0, :], in_=vhalo[:]))
            chk('m_halo')
            with cx.fast():
                for c in range(8):
                    for (n0, nw) in nblocks(NT):
                        ps, tp = PS()
                        for k in range(8):
                            op("pe", T + [t_xn[k]], [tp], lambda p: p.matmul(ps[:, :nw], lhsT=wx[:, k, c * 128:(c + 1) * 128], rhs=xn[:, k, n0:n0 + nw], start=(k == 0), stop=(k == 7)))
                        if not sample:
                            op("act", [tp], [t_xin], lambda a: a.activation(out=xin[:, c, HAL + n0:HAL + n0 + nw], in_=ps[:, :nw], func=AF.Copy))
                        else:
                            op("act", [tp], [t_xin], lambda a: a.activation(out=xin[:, c, n0:n0 + nw], in_=ps[:, :nw], func=AF.Copy))
                chk('m_xbc')
                for qi in range(5):
                    for (n0, nw) in nblocks(NT):
                        ps, tp = PS()
                        for k in range(8):
                            lw = wq[:, k, qi * 128:(qi + 1) * 128] if qi < 4 else wk[:, k, :]
                            op("pe", T + [t_xn[k]], [tp], lambda p: p.matmul(ps[:, :nw], lhsT=lw, rhs=xn[:, k, n0:n0 + nw], start=(k == 0), stop=(k == 7)))
                        op("act", [tp], [t_sqb], lambda a: a.activation(out=sqb[:, :nw], in_=ps[:, :nw], func=AF.Square))
                        ps2, tp2 = PS()
                        op("pe", [t_sqb, t_c], [tp2], lambda p: p.matmul(ps2[:, :nw], lhsT=bones[:], rhs=sqb[:, :nw], start=True, stop=True))
                        op("act", [tp2], [t_rq], lambda a: a.activation(out=rq[:, :nw], in_=ps2[:, :nw], func=AF.Ln, bias=64.0 * EPS, scale=1.0))
                        op("act", [t_rq], [t_rq], lambda a: a.activation(out=rq[:, :nw], in_=rq[:, :nw], func=AF.Exp, scale=-0.5))
                        if qi < 4:
                            op("dve", [tp, t_rq, t_lay], [t_qn], lambda v: v.scalar_tensor_tensor(out=qn[:, qi, n0:n0 + nw], in0=ps[:, :nw], scalar=lay["gq"][:, 0:1], in1=rq[:, :nw], op0=ALU.mult, op1=ALU.mult))
                        else:
                            op("dve", [tp, t_rq, t_lay], [t_knf], lambda v: v.scalar_tensor_tensor(out=knf[:, n0:n0 + nw], in0=ps[:, :nw], scalar=lay["gk"][:, 0:1], in1=rq[:, :nw], op0=ALU.mult, op1=ALU.mult))
                            op("act", [t_knf], [t_kn], lambda a: a.activation(out=kn[:, KH + n0:KH + n0 + nw], in_=knf[:, n0:n0 + nw], func=AF.Copy))
            chk('m_qk')
            cacc = SB(sc, "cacc", [128, 512]); t_cacc = Tok()
            if not sample:
                for c in range(8):
                    for (n0, nw) in nblocks(NT):
                        op("dve", [t_xin, t_lay], [t_cacc], lambda v: v.tensor_scalar(out=cacc[:, :nw], in0=xin[:, c, n0:n0 + nw], scalar1=lay["cw"][:, c:c + 1], scalar2=None, op0=ALU.mult))
                        for j in range(1, 4):
                            op("dve", [t_xin, t_lay, t_cacc], [t_cacc], lambda v: v.scalar_tensor_tensor(out=cacc[:, :nw], in0=xin[:, c, n0 + j:n0 + j + nw], scalar=lay["cw"][:, j * 8 + c:j * 8 + c + 1], in1=cacc[:, :nw], op0=ALU.mult, op1=ALU.add))
                        op("act", [t_cacc, t_lay], [t_xc], lambda a: a.activation(out=xc[:, c, n0:n0 + nw], in_=cacc[:, :nw], func=AF.Silu, bias=lay["cb"][:, c:c + 1], scale=1.0))
                op("dve", [t_xin], [t_xhalo], lambda v: v.tensor_copy(out=xhalo[:], in_=xin[:, :, NT:NT + 3]))
                if last_group:
                    cst = SB(sc, "cst", [128, 8, 4]); t_cst = Tok()
                    op("dve", [t_xin], [t_cst], lambda v: v.tensor_copy(out=cst[:, :, 0:3], in_=xin[:, :, NT:NT + 3]))
                    ps, tp = PS(); ps2, tp2 = PS()
                    for c in range(8):
                        pp_, tpp = (ps, tp) if c < 4 else (ps2, tp2)
                        op("pe", [t_cst, t_c], [tpp], lambda p: p.transpose(out=pp_[:3, (c % 4) * 128:(c % 4 + 1) * 128], in_=cst[:, c, 0:3], identity=ident[:]))
                    cso = SB(sc, "cso", [4, 1024]); t_cso = Tok()
                    op("act", [tp], [t_cso], lambda a: a.activation(out=cso[:3, 0:512], in_=ps[:3, :], func=AF.Copy))
                    op("act", [tp2], [t_cso], lambda a: a.activation(out=cso[:3, 512:1024], in_=ps2[:3, :], func=AF.Copy))
                    dma("sp", [t_cso], [], oconv_p[l, :, :], cso[:3, :])
            else:
                xfull = SB(sc, "xfull", [128, 8, NSS, 7]); t_xf = Tok()
                scm = SB(sc, "scm", [64, 1024]); t_scmb = Tok()
                dma("sp", [], [t_scmb], scm[:NSS * 3, :], sconv[l, :, :])
                for c in range(8):
                    ps, tp = PS()
                    op("pe", [t_scmb, t_c], [tp], lambda p: p.transpose(out=ps[:, :NSS * 3], in_=scm[:NSS * 3, c * 128:(c + 1) * 128], identity=ident[:NSS * 3, :NSS * 3]))
                    op("act", [tp], [t_xf], lambda a: a.activation(out=xfull[:, c, :, 0:3], in_=ps[:, :NSS * 3].rearrange("p (b j) -> p b j", j=3), func=AF.Copy))
                    op("dve", [t_xin], [t_xf], lambda v: v.tensor_copy(out=xfull[:, c, :, 3:7], in_=xin[:, c, :].rearrange("p (b t) -> p b t", t=4)))
                    ca = cacc[:, :NS].rearrange("p (b t) -> p b t", t=4)
                    op("dve", [t_xf, t_lay], [t_cacc], lambda v: v.tensor_scalar(out=ca, in0=xfull[:, c, :, 0:4], scalar1=lay["cw"][:, c:c + 1], scalar2=None, op0=ALU.mult))
                    for j in range(1, 4):
                        op("dve", [t_xf, t_lay, t_cacc], [t_cacc], lambda v: v.scalar_tensor_tensor(out=ca, in0=xfull[:, c, :, j:j + 4], scalar=lay["cw"][:, j * 8 + c:j * 8 + c + 1], in1=ca, op0=ALU.mult, op1=ALU.add))
                    op("act", [t_cacc, t_lay], [t_xc], lambda a: a.activation(out=xc[:, c, :], in_=cacc[:, :NS], func=AF.Silu, bias=lay["cb"][:, c:c + 1], scale=1.0))
                cso = SB(sc, "cso", [64, 1024]); t_cso = Tok()
                cst = SB(sc, "cst", [128, 8, NSS * 3]); t_cst = Tok()
                op("dve", [t_xf], [t_cst], lambda v: v.tensor_copy(out=cst[:].rearrange("p c (b j) -> p c b j", j=3), in_=xfull[:, :, :, 4:7]))
                for a_ in range(2):
                    ps, tp = PS()
                    for j in range(4):
                        op("pe", [t_cst, t_c], [tp], lambda p: p.transpose(out=ps[:NSS * 3, j * 128:(j + 1) * 128], in_=cst[:, a_ * 4 + j, :], identity=ident[:]))
                    op("act", [tp], [t_cso], lambda a: a.activation(out=cso[:NSS * 3, a_ * 512:(a_ + 1) * 512], in_=ps[:NSS * 3, :], func=AF.Copy))
                dma("sp", [t_cso], [], oconv_s[l, :, :], cso[:NSS * 3, :])

            chk('m_conv')
            Ut = Ubd if sample else Um; SLt = SLbd if sample else SLm
            P = NTT
            st = {}
            for nm, shp, dt in [("dt", [128, 8], F32), ("dta", [128, 8], F32), ("t8", [128, 8], F32), ("ecum", [128, 8], F32), ("etot", [128, 8], F32),
                                ("dend", [128, 8], F32), ("w2s", [128, 8], F32), ("DL", [128, 8, 128], F32), ("LT", [128, 8, 128], F32),
                                ("GM", [128, 2, 128], F32), ("MT", [128, 8, 128], BF16), ("xdt", [128, 512], BF16), ("xdd", [128, 512], BF16),
                                ("Btm", [128, 256], BF16), ("y1", [128, 512], F32), ("sz", [128, 512], F32), ("ysq", [128, 512], F32), ("xsk", [128, 512], F32), ("xtok", [128, 512], F32),
                                ("ss2", [128, 2], F32), ("ytm", [128, 512], BF16), ("pe0", [128, 512], F32), ("pe1", [128, 512], F32),
                                ("PT0", [128, 512], BF16), ("PT1", [128, 512], BF16), ("den", [64, 512], F32)]:
                st[nm] = (SB(sc, "st_" + nm, shp, dt), Tok())
            if sample:
                Snat = SB(sc, "Snat", [128, 8, 4, 128]); t_Sn = Tok()
                Sb1 = [SB(sc, "Sb1_%d" % i, [128, 4, 128], BF16) for i in range(2)]; t_Sb1 = [Tok(), Tok()]
                STb1 = [SB(sc, "STb1_%d" % i, [128, 512], BF16) for i in range(2)]; t_ST1 = [Tok(), Tok()]
                CTm = SB(sc, "CTm", [128, NSS, 2, 64], BF16); t_CT = Tok()
                xdm = [SB(sc, "xdm%d" % i, [64, 512], BF16) for i in range(2)]; t_xdm = [Tok(), Tok()]
                dtaE = SB(sc, "dtaE", [64, 512]); t_dE = Tok()
                cdT = SB(sc, "cdT", [128, 4, 16]); t_cd = Tok()
                Snew = [SB(sc, "Snew%d" % i, [128, 4, 128]) for i in range(2)]; t_Snew = [Tok(), Tok()]
                kcb = SB(sc, "kcb", [128, NSS, 128], BF16); vcb = SB(sc, "vcb", [128, NSS, 128], BF16); t_kcb = Tok(); t_vcb = Tok()
                KcT = SB(sc, "KcT", [128, NSS, 128], BF16); t_KcT = Tok()
                PTc = SB(sc, "PTc", [128, 512], BF16); t_PTc = Tok()
                for b4 in range(0, NSS, 4):
                    dma("pool", [], [t_kcb], kcb[:, b4:b4 + 4, :], ck[l, b4:b4 + 4, :, :].rearrange("b i c -> i b c"))
                    dma("pool", [], [t_vcb], vcb[:, b4:b4 + 4, :], cv[l, b4:b4 + 4, :, :].rearrange("b i c -> i b c"))
                for b4 in range(0, NSS, 4):
                    dma("sp", [], [], ock_s[l, b4:b4 + 4, 0:124, :], ck[l, b4:b4 + 4, 4:128, :])
                    dma("sp", [], [], ocv_s[l, b4:b4 + 4, 0:124, :], cv[l, b4:b4 + 4, 4:128, :])

            A = lambda nm: st[nm][0]
            K_ = lambda nm: st[nm][1]
            PSH = lambda: PS(hold=not sample)

            def RELH(tok):
                if not sample:
                    REL(tok)

            def tile_ssd(ti):
                c0 = ti * NTT
                cs = slice(c0, c0 + P)
                first_tile = first_group and ti == 0 and not sample
                pz, tpz = PSH(); pdv, tpdv = PSH()
                with cx.fast():
                    for k in range(8):
                        op("pe", T + [t_xn[k]], [tpz], lambda p: p.matmul(pz[:P, :], lhsT=xn[:, k, cs], rhs=wz[:, k, :], start=(k == 0), stop=(k == 7)))
                    for k in range(8):
                        op("pe", T + [t_xn[k]], [tpdv], lambda p: p.matmul(pdv[:P, 0:128], lhsT=xn[:, k, cs], rhs=wv[:, k, :], start=(k == 0), stop=(k == 7)))
                    for k in range(8):
                        op("pe", T + [t_xn[k]], [tpdv], lambda p: p.matmul(pdv[:P, 128:136], lhsT=xn[:, k, cs], rhs=wdt[:, k, :], start=(k == 0), stop=(k == 7)))
                op("act", [tpz], [st["sz"][1]], lambda a: a.activation(out=st["sz"][0][:P, :], in_=pz[:P, :], func=AF.Silu))
                RELH(tpz)
                op("act", [tpdv], [t_vb], lambda a: a.activation(out=vb[:P, ti + 1, :], in_=pdv[:P, 0:128], func=AF.Copy))
                need_vf = sample or (last_group and ti == ntile - 1)
                if need_vf:
                    op("act", [tpdv], [t_vf], lambda a: a.activation(out=vf[:P, :], in_=pdv[:P, 0:128], func=AF.Copy))
                chk('t_zdv')
                op("dve", [tpdv, t_lay], [K_("t8")], lambda v: v.tensor_tensor(out=A("t8")[:P, :], in0=pdv[:P, 128:136], in1=lay["dtb"][:P, :], op=ALU.add))
                RELH(tpdv)
                yield
                op("act", [K_("t8")], [K_("t8")], lambda a: a.activation(out=A("t8")[:P, :], in_=A("t8")[:P, :], func=AF.Exp))
                op("act", [K_("t8")], [K_("dt")], lambda a: a.activation(out=A("dt")[:P, :], in_=A("t8")[:P, :], func=AF.Ln, bias=1.0, scale=1.0))
                op("dve", [K_("dt"), t_lay], [K_("dta")], lambda v: v.tensor_tensor(out=A("dta")[:P, :], in0=A("dt")[:P, :], in1=lay["aneg"][:P, :], op=ALU.mult))
                op("dve", [K_("dta"), t_c], [K_("DL")], lambda v: v.tensor_tensor(out=A("DL")[:P, :, :P], in0=SLt[:P, :P].unsqueeze(1).broadcast_to([P, 8, P]), in1=A("dta")[:P, :].unsqueeze(2).broadcast_to([P, 8, P]), op=ALU.mult))
                yield
                pD0, tD0 = PSH(); pD1, tD1 = PSH(); pc, tpc = PSH()
                chk('t_dt')
                for hd in range(8):
                    pd_, td_ = (pD0, tD0) if hd < 4 else (pD1, tD1)
                    op("pe", [K_("DL"), t_c], [td_], lambda p: p.matmul(pd_[:P, (hd % 4) * 128:(hd % 4) * 128 + P], lhsT=A("DL")[:P, hd, :P], rhs=Ut[:P, :P], start=True, stop=True))
                op("pe", [K_("dta"), t_c], [tpc], lambda p: p.matmul(pc[:P, 0:8], lhsT=Ut[:P, :P], rhs=A("dta")[:P, :], start=True, stop=True))
                if not sample:
                    op("pe", [K_("dta"), t_c], [tpc], lambda p: p.matmul(pc[:, 8:16], lhsT=onesf[:, :], rhs=A("dta")[:, :], start=True, stop=True))
                else:
                    pass
                for hf, (pd_, td_) in enumerate(((pD0, tD0), (pD1, tD1))):
                    op("act", [td_], [K_("LT")], lambda a: a.activation(out=A("LT")[:P, hf * 4:(hf + 1) * 4, :P], in_=pd_[:P, :].rearrange("p (h t) -> p h t", h=4)[:, :, :P], func=AF.Exp))
                chk('u_LT')
                op("act", [tpc], [K_("ecum")], lambda a: a.activation(out=A("ecum")[:P, :], in_=pc[:P, 0:8], func=AF.Exp))
                RELH(tD0); RELH(tD1)
                chk('u_ecum')
                chk('t_D')
                pG, tpG = PSH()
                with cx.fast():
                    for g in range(2):
                        op("pe", [t_xc], [tpG], lambda p: p.matmul(pG[:P, g * 128:g * 128 + P], lhsT=xc[:, 4 + g, cs], rhs=xc[:, 6 + g, cs], start=True, stop=True))
                    chk('u_pG')
                op("dve", [tpG, t_c], [K_("GM")], lambda v: v.tensor_tensor(out=A("GM")[:P, :, :P], in0=pG[:P, 0:256].rearrange("p (g t) -> p g t", g=2)[:, :, :P], in1=Ut[:P, :P].unsqueeze(1).broadcast_to([P, 2, P]), op=ALU.mult))
                RELH(tpG)
                chk('u_GM')
                for g in range(2):
                    op("dve", [K_("GM"), K_("LT")], [K_("MT")], lambda v: v.tensor_tensor(out=A("MT")[:P, g * 4:(g + 1) * 4, :P], in0=A("LT")[:P, g * 4:(g + 1) * 4, :P], in1=A("GM")[:P, g, :P].unsqueeze(1).broadcast_to([P, 4, P]), op=ALU.mult))
                chk('t_G')
                px, tpx = PSH(); pxb = px[:].bitcast(BF16)
                with cx.fast():
                    for j in range(4):
                        op("pe", [t_xc, t_c], [tpx], lambda p: p.transpose(out=pxb[:P, j * 128:(j + 1) * 128], in_=xc[:, j, cs], identity=identb[:]))
                    for g in range(2):
                        op("pe", [t_xc, t_c], [tpx], lambda p: p.transpose(out=pxb[:P, 512 + g * 128:512 + (g + 1) * 128], in_=xc[:, 4 + g, cs], identity=identb[:]))
                    chk('v_tr')
                op("act", [tpx], [K_("Btm")], lambda a: a.activation(out=A("Btm")[:P, :], in_=pxb[:P, 512:768], func=AF.Copy))
                op("act", [tpx], [K_("xtok")], lambda a: a.activation(out=A("xtok")[:P, :], in_=pxb[:P, 0:512], func=AF.Copy))
                RELH(tpx)
                chk('v_Btm')
                op("dve", [K_("xtok"), K_("dt")], [K_("xdt")], lambda v: v.tensor_tensor(out=A("xdt")[:P, :].rearrange("p (h d) -> p h d", h=8), in0=A("xtok")[:P, :].rearrange("p (h d) -> p h d", h=8), in1=A("dt")[:P, :].unsqueeze(2).broadcast_to([P, 8, 64]), op=ALU.mult))
                chk('v_xdt')
                op("dve", [K_("xtok"), t_lay], [K_("xsk")], lambda v: v.tensor_tensor(out=A("xsk")[:P, :].rearrange("p (h d) -> p h d", h=8), in0=A("xtok")[:P, :].rearrange("p (h d) -> p h d", h=8), in1=lay["dsk"][:P, :].unsqueeze(2).broadcast_to([P, 8, 64]), op=ALU.mult))
                yield
                chk('t_tr')
                py, tpy = PS(hold=True)
                with cx.fast():
                    for hd in range(8):
                        op("pe", [K_("MT"), K_("xdt")], [tpy], lambda p: p.matmul(py[:P, hd * 64:(hd + 1) * 64], lhsT=A("MT")[:P, hd, :P], rhs=A("xdt")[:P, hd * 64:(hd + 1) * 64], start=True, stop=True))
                pyo, tpyo = PS(hold=True)
                if sample:
                    pyo2, tpyo2 = PS(hold=True)
                if not sample:
                    op("act", [tpc], [K_("w2s")], lambda a: a.activation(out=A("w2s")[:, :], in_=pc[:, 0:8], func=AF.Copy))
                    op("dve", [tpc, K_("w2s")], [K_("t8")], lambda v: v.tensor_tensor(out=A("t8")[:, :], in0=pc[:, 8:16], in1=A("w2s")[:, :], op=ALU.subtract))
                    op("act", [K_("t8")], [K_("dend")], lambda a: a.activation(out=A("dend")[:, :], in_=A("t8")[:, :], func=AF.Exp))
                    op("act", [tpc], [K_("etot")], lambda a: a.activation(out=A("etot")[:, :], in_=pc[:, 8:16], func=AF.Exp))
                    RELH(tpc)
                    op("dve", [K_("dend"), K_("dt")], [K_("w2s")], lambda v: v.tensor_tensor(out=A("w2s")[:, :], in0=A("dend")[:, :], in1=A("dt")[:, :], op=ALU.mult))
                    op("dve", [K_("xtok"), K_("w2s")], [K_("xdd")], lambda v: v.tensor_tensor(out=A("xdd")[:, :].rearrange("p (h d) -> p h d", h=8), in0=A("xtok")[:, :].rearrange("p (h d) -> p h d", h=8), in1=A("w2s")[:, :].unsqueeze(2).broadcast_to([128, 8, 64]), op=ALU.mult))
                    for g in range(2):
                        op("pe", [t_xc, t_hTb], [tpyo], lambda p: p.matmul(pyo[:, g * 256:(g + 1) * 256], lhsT=xc[:, 6 + g, cs], rhs=hTb[:, g * 256:(g + 1) * 256], start=True, stop=True))
                    pst, tpst = PSH()
                    for g in range(2):
                        op("pe", [K_("Btm"), K_("xdd")], [tpst], lambda p: p.matmul(pst[:, g * 256:(g + 1) * 256], lhsT=A("Btm")[:, g * 128:(g + 1) * 128], rhs=A("xdd")[:, g * 256:(g + 1) * 256], start=True, stop=True))
                    op("dve", [t_hT, K_("etot")], [t_hT], lambda v: v.tensor_tensor(out=hT[:].rearrange("p (h d) -> p h d", h=8), in0=hT[:].rearrange("p (h d) -> p h d", h=8), in1=A("etot")[:, :].unsqueeze(2).broadcast_to([128, 8, 64]), op=ALU.mult))
                    op("dve", [t_hT, tpst], [t_hT], lambda v: v.tensor_tensor(out=hT[:], in0=hT[:], in1=pst[:, :], op=ALU.add))
                    RELH(tpst)
                    op("act", [t_hT], [t_hTb], lambda a: a.activation(out=hTb[:], in_=hT[:], func=AF.Copy))
                else:
                    op("pe", [K_("dta"), t_c], [tpc], lambda p: p.matmul(pc[:P, 8:16], lhsT=Ubd[:P, :P], rhs=A("dta")[:P, :], start=True, stop=False))
                    op("pe", [K_("dta"), t_c], [tpc], lambda p: p.matmul(pc[:P, 8:16], lhsT=SLbd[:P, :P], rhs=A("dta")[:P, :], start=False, stop=True))
                    op("act", [tpc], [K_("w2s")], lambda a: a.activation(out=A("w2s")[:P, :], in_=pc[:P, 0:8], func=AF.Copy))
                    op("dve", [tpc, K_("w2s")], [K_("t8")], lambda v: v.tensor_tensor(out=A("t8")[:P, :], in0=pc[:P, 8:16], in1=A("w2s")[:P, :], op=ALU.subtract))
                    op("act", [K_("t8")], [K_("dend")], lambda a: a.activation(out=A("dend")[:P, :], in_=A("t8")[:P, :], func=AF.Exp))
                    op("dve", [K_("dend"), K_("dt")], [K_("w2s")], lambda v: v.tensor_tensor(out=A("w2s")[:P, :], in0=A("dend")[:P, :], in1=A("dt")[:P, :], op=ALU.mult))
                    op("dve", [K_("xtok"), K_("w2s")], [K_("xdd")], lambda v: v.tensor_tensor(out=A("xdd")[:P, :].rearrange("p (h d) -> p h d", h=8), in0=A("xtok")[:P, :].rearrange("p (h d) -> p h d", h=8), in1=A("w2s")[:P, :].unsqueeze(2).broadcast_to([P, 8, 64]), op=ALU.mult))
                    for g in range(2):
                        op("dve", [t_xc, t_c], [t_CT], lambda v: v.tensor_tensor(out=CTm[:, :, g, :], in0=xc[:, 6 + g, :].unsqueeze(1).broadcast_to([128, NSS, 64]), in1=BMb[:].rearrange("p (b t) -> p b t", b=16)[:, :NSS, :], op=ALU.mult))
                    op("dve", [K_("dta")], [t_dE], lambda v: v.tensor_copy(out=dtaE[:].rearrange("p (h d) -> p h d", h=8), in_=A("dta")[:P, :].unsqueeze(2).broadcast_to([P, 8, 64])))
                    pcd, tpcd = PS()
                    for j in range(4):
                        op("pe", [t_dE, t_c], [tpcd], lambda p: p.matmul(pcd[:, j * 16:(j + 1) * 16], lhsT=dtaE[:, j * 128:(j + 1) * 128], rhs=BMt[:, :], start=True, stop=True))
                    op("act", [tpcd], [t_cd], lambda a: a.activation(out=cdT[:].rearrange("p j b -> p (j b)"), in_=pcd[:, 0:64], func=AF.Exp))
                    for b in range(NSS):
                        bl = b % 8
                        if bl == 0:
                            for b8 in range(8):
                                dma("sp", [], [t_Sn], Snat[:, b8, :, :], sssm[l, b + b8, :, :].rearrange("(j p) n -> p j n", p=128))
                        sb1, tsb1 = Sb1[b % 2], t_Sb1[b % 2]
                        stb, tstb = STb1[b % 2], t_ST1[b % 2]
                        op("act", [t_Sn], [tsb1], lambda a: a.activation(out=sb1[:], in_=Snat[:, bl, :, :], func=AF.Copy))
                        pt_, tpt = PS(); ptb = pt_[:].bitcast(BF16)
                        for j in range(4):
                            op("pe", [tsb1, t_c], [tpt], lambda p: p.transpose(out=ptb[:, j * 128:(j + 1) * 128], in_=sb1[:, j, :], identity=identb[:]))
                        op("act", [tpt], [tstb], lambda a: a.activation(out=stb[:, :], in_=ptb[:, 0:512], func=AF.Copy))
                        for g in range(2):
                            pq_, tq_ = (pyo, tpyo) if g == 0 else (pyo2, tpyo2)
                            op("pe", [t_CT, tstb], [tq_], lambda p: p.matmul(pq_[:P, 0:256], lhsT=CTm[:, b, g, :], rhs=stb[:, g * 256:(g + 1) * 256], start=(b == 0), stop=(b == NSS - 1)))
                        xm, txm = xdm[b % 2], t_xdm[b % 2]
                        op("dve", [K_("xdd"), t_c], [txm], lambda v: v.tensor_scalar(out=xm[:, :], in0=A("xdd")[:P, :], scalar1=BMt[:, b:b + 1], scalar2=None, op0=ALU.mult))
                        pst, tpst = PS()
                        for j in range(4):
                            op("pe", [txm, K_("Btm")], [tpst], lambda p: p.matmul(pst[:, j * 128:(j + 1) * 128], lhsT=xm[:, j * 128:(j + 1) * 128], rhs=A("Btm")[:P, (j // 2) * 128:(j // 2 + 1) * 128], start=True, stop=True))
                        sn, tsn = Snew[b % 2], t_Snew[b % 2]
                        for j in range(4):
                            op("dve", [t_Sn, t_cd, tpst], [tsn], lambda v: v.scalar_tensor_tensor(out=sn[:, j, :], in0=Snat[:, bl, j, :], scalar=cdT[:, j, b:b + 1], in1=pst[:, j * 128:(j + 1) * 128], op0=ALU.mult, op1=ALU.add))
                        dma("sp", [tsn], [], ossm_s[l, b, :, :].rearrange("(j p) n -> p j n", p=128), sn[:])
                yield
                chk('t_ssd')
                if not sample:
                    op("dve", [tpyo, K_("ecum")], [K_("y1")], lambda v: v.tensor_tensor(out=A("y1")[:P, :].rearrange("p (h d) -> p h d", h=8), in0=pyo[:P, :].rearrange("p (h d) -> p h d", h=8), in1=A("ecum")[:P, :].unsqueeze(2).broadcast_to([P, 8, 64]), op=ALU.mult))
                else:
                    for g, (pq_, tq_) in enumerate(((pyo, tpyo), (pyo2, tpyo2))):
                        op("dve", [tq_, K_("ecum")], [K_("y1")], lambda v: v.tensor_tensor(out=A("y1")[:P, g * 256:(g + 1) * 256].rearrange("p (h d) -> p h d", h=4), in0=pq_[:P, 0:256].rearrange("p (h d) -> p h d", h=4), in1=A("ecum")[:P, g * 4:(g + 1) * 4].unsqueeze(2).broadcast_to([P, 4, 64]), op=ALU.mult))
                op("dve", [K_("y1"), tpy], [K_("y1")], lambda v: v.tensor_tensor(out=A("y1")[:P, :], in0=A("y1")[:P, :], in1=py[:P, :], op=ALU.add))
                op("dve", [K_("y1"), K_("xsk")], [K_("y1")], lambda v: v.tensor_tensor(out=A("y1")[:P, :], in0=A("y1")[:P, :], in1=A("xsk")[:P, :], op=ALU.add))
                REL(tpy); REL(tpyo)
                if sample:
                    REL(tpyo2)
                op("dve", [K_("y1"), K_("sz")], [K_("y1")], lambda v: v.tensor_tensor(out=A("y1")[:P, :], in0=A("y1")[:P, :], in1=A("sz")[:P, :], op=ALU.mult))
                op("dve", [K_("y1")], [K_("ysq")], lambda v: v.tensor_tensor(out=A("ysq")[:P, :], in0=A("y1")[:P, :], in1=A("y1")[:P, :], op=ALU.mult))
                op("dve", [K_("ysq")], [K_("ss2")], lambda v: v.reduce_sum(out=A("ss2")[:P, :], in_=A("ysq")[:P, :].rearrange("p (g d) -> p g d", g=2), axis=AX.X))
                op("act", [K_("ss2")], [K_("ss2")], lambda a: a.activation(out=A("ss2")[:P, :], in_=A("ss2")[:P, :], func=AF.Ln, bias=256.0 * EPS, scale=1.0))
                op("act", [K_("ss2")], [K_("ss2")], lambda a: a.activation(out=A("ss2")[:P, :], in_=A("ss2")[:P, :], func=AF.Exp, scale=-0.5))
                for g2 in range(2):
                    op("dve", [K_("y1"), K_("ss2"), t_lay], [K_("ytm")], lambda v: v.scalar_tensor_tensor(out=A("ytm")[:P, g2 * 256:(g2 + 1) * 256], in0=A("y1")[:P, g2 * 256:(g2 + 1) * 256], scalar=A("ss2")[:P, g2:g2 + 1], in1=gssm[:P, g2 * 256:(g2 + 1) * 256], op0=ALU.mult, op1=ALU.mult))
                yield
                pyt, tpyt = PSH(); pytb = pyt[:].bitcast(BF16)
                with cx.fast():
                    for j in range(4):
                        op("pe", [K_("ytm"), t_c], [tpyt], lambda p: p.transpose(out=pytb[:, j * 128:j * 128 + P], in_=A("ytm")[:P, j * 128:(j + 1) * 128], identity=identb[:P, :P]))
                op("act", [tpyt], [t_mixT], lambda a: a.activation(out=mixT[:, :, cs], in_=pytb[:, 0:512].rearrange("p (j t) -> p j t", j=4)[:, :, :P], func=AF.Copy))
                RELH(tpyt)

            def tile_attn(ti):
                c0 = ti * NTT
                cs = slice(c0, c0 + P)
                first_tile = first_group and ti == 0 and not sample
                chk('t_y')
                if not sample:
                    for g in range(2):
                        bs = slice(64 * g, 64 * g + 64)
                        kbs = [1] if first_tile else [0, 1]
                        for kb in kbs:
                            kcols = slice(c0 + kb * 128, c0 + kb * 128 + 128)
                            pS, tpS = PSH()
                            for hh in range(4):
                                op("pe", [t_kn, t_qn], [tpS], lambda p: p.matmul(pS[:, hh * 128:(hh + 1) * 128], lhsT=kn[bs, kcols], rhs=qn[bs, hh, cs], start=True, stop=True))
                            pe_, tpe = st["pe%d" % kb]; PT_, tPT = st["PT%d" % kb]
                            op("act", [tpS], [tpe], lambda a: a.activation(out=pe_[:], in_=pS[:], func=AF.Exp, scale=0.125))
                            RELH(tpS)
                            Et = Eprev if kb == 0 else Ecur
                            op("dve", [tpe, t_c], [tPT], lambda v: v.tensor_tensor(out=PT_[:], in0=pe_[:], in1=Et[:, g * 512:(g + 1) * 512], op=ALU.mult))
                        chk('a_S')
                        yield
                        po, tpo = PSH(); pdn, tpdn = PSH()
                        for ii, kb in enumerate(kbs):
                            PT_, tPT = st["PT%d" % kb]
                            op("pe", [t_vb, tPT], [tpo], lambda p: p.matmul(po[:64, :], lhsT=vb[:, ti + kb, g * 64:(g + 1) * 64], rhs=PT_[:], start=(ii == 0), stop=(ii == len(kbs) - 1)))
                        chk('a_po')
                        for ii, kb in enumerate(kbs):
                            PT_, tPT = st["PT%d" % kb]
                            op("pe", [t_c, tPT], [tpdn], lambda p: p.matmul(pdn[:64, :], lhsT=onesb[:, 0:64], rhs=PT_[:], start=(ii == 0), stop=(ii == len(kbs) - 1)))
                        chk('a_pdn')
                        op("dve", [tpdn, t_lay], [K_("den")], lambda v: v.tensor_tensor(out=A("den")[:, :], in0=pdn[:64, :], in1=esinkrow[:, g * 512:(g + 1) * 512], op=ALU.add))
                        RELH(tpdn)
                        op("act", [K_("den")], [K_("den")], lambda a: a.activation(out=A("den")[:, :], in_=A("den")[:, :], func=AF.Ln))
                        op("act", [K_("den")], [K_("den")], lambda a: a.activation(out=A("den")[:, :], in_=A("den")[:, :], func=AF.Exp, scale=-1.0))
                        op("dve", [tpo, K_("den")], [t_oT], lambda v: v.tensor_tensor(out=oT[:, g * 4:(g + 1) * 4, cs], in0=po[:64, :].rearrange("p (h q) -> p h q", h=4), in1=A("den")[:, :].rearrange("p (h q) -> p h q", h=4), op=ALU.mult))
                        RELH(tpo)
                        yield
                else:
                    for b4 in range(0, NSS, 4):
                        pt_, tpt = PS(); ptb = pt_[:].bitcast(BF16)
                        for bb in range(4):
                            op("pe", [t_kcb, t_c], [tpt], lambda p: p.transpose(out=ptb[:, bb * 128:(bb + 1) * 128], in_=kcb[:, b4 + bb, :], identity=identb[:]))
                        op("act", [tpt], [t_KcT], lambda a: a.activation(out=KcT[:, b4:b4 + 4, :], in_=ptb[:, 0:512].rearrange("p (b i) -> p b i", b=4), func=AF.Copy))
                    pSc, tpSc = PS()
                    for b in range(NSS):
                        for g in range(2):
                            bs = slice(64 * g, 64 * g + 64)
                            for hh in range(4):
                                idx = (b * 8 + g * 4 + hh) * 4
                                op("pe", [t_KcT, t_qn], [tpSc], lambda p: p.matmul(pSc[:, idx:idx + 4], lhsT=KcT[bs, b, :], rhs=qn[bs, hh, b * 4:b * 4 + 4], start=True, stop=True))
                    pe_, tpe = st["pe0"]
                    op("act", [tpSc], [tpe], lambda a: a.activation(out=pe_[:, :NSS * 32], in_=pSc[:, :NSS * 32], func=AF.Exp, scale=0.125))
                    op("dve", [tpe, t_c], [t_PTc], lambda v: v.tensor_tensor(out=PTc[:, :NSS * 32].rearrange("p (b x) -> p b x", b=NSS), in0=pe_[:, :NSS * 32].rearrange("p (b x) -> p b x", b=NSS), in1=Ecache[:, :].unsqueeze(1).broadcast_to([128, NSS, 32]), op=ALU.mult))
                    pSn, tpSn = PS()
                    for g in range(2):
                        bs = slice(64 * g, 64 * g + 64)
                        for hh in range(4):
                            op("pe", [t_kn, t_qn], [tpSn], lambda p: p.matmul(pSn[:P, (g * 4 + hh) * 64:(g * 4 + hh) * 64 + P], lhsT=kn[bs, 0:P], rhs=qn[bs, hh, 0:P], start=True, stop=True))
                    pe1_, tpe1 = st["pe1"]; PT1_, tPT1 = st["PT1"]
                    op("act", [tpSn], [tpe1], lambda a: a.activation(out=pe1_[:P, :], in_=pSn[:P, :], func=AF.Exp, scale=0.125))
                    for g in range(2):
                        ov = PT1_[:P, g * 256:(g + 1) * 256].rearrange("p (b hh t) -> p hh b t", b=16, hh=4)
                        i0 = pe1_[:P, g * 256:(g + 1) * 256].rearrange("p (hh b t) -> p hh b t", hh=4, b=16)
                        i1 = Enew[:P, g * 256:(g + 1) * 256].rearrange("p (hh b t) -> p hh b t", hh=4, b=16)
                        op("dve", [tpe1, t_c], [tPT1], lambda v: v.tensor_tensor(out=ov, in0=i0, in1=i1, op=ALU.mult))
                    po, tpo = PS(); pdn, tpdn = PS()
                    for (pp_, tpp, use_v) in ((po, tpo, True), (pdn, tpdn, False)):
                        for g in range(2):
                            lw = vb[:P, 1, g * 64:(g + 1) * 64] if use_v else onesb[:P, 0:64]
                            op("pe", [t_vb, tPT1, t_c], [tpp], lambda p: p.matmul(pp_[:64, g * 256:(g + 1) * 256], lhsT=lw, rhs=PT1_[:P, g * 256:(g + 1) * 256], start=True, stop=False))
                            for b in range(NSS):
                                lw2 = vcb[:, b, g * 64:(g + 1) * 64] if use_v else onesb[:, 0:64]
                                op("pe", [t_vcb, t_PTc, t_c], [tpp], lambda p: p.matmul(pp_[:64, g * 256 + b * 16:g * 256 + b * 16 + 16], lhsT=lw2, rhs=PTc[:, b * 32 + g * 16:b * 32 + g * 16 + 16], start=False, stop=(b == NSS - 1)))
                    for g in range(2):
                        dv = A("den")[:, g * 256:(g + 1) * 256].rearrange("p (b hh t) -> p hh b t", b=16, hh=4)
                        ek = esinkrow[:, :].rearrange("p (h q) -> p h q", h=8)[:, g * 4:(g + 1) * 4, 0:64].rearrange("p hh (b t) -> p hh b t", t=4)
                        op("dve", [tpdn, t_lay], [K_("den")], lambda v: v.tensor_tensor(out=dv, in0=pdn[:64, g * 256:(g + 1) * 256].rearrange("p (b hh t) -> p hh b t", b=16, hh=4), in1=ek, op=ALU.add))
                        op("dve", [K_("den")], [K_("den")], lambda v: v.reciprocal(out=A("den")[:, g * 256:(g + 1) * 256], in_=A("den")[:, g * 256:(g + 1) * 256]))
                        op("dve", [tpo, K_("den")], [t_oT], lambda v: v.tensor_tensor(out=oT[:, g * 4:(g + 1) * 4, 0:64].rearrange("p hh (b t) -> p hh b t", t=4), in0=po[:64, g * 256:(g + 1) * 256].rearrange("p (b hh t) -> p hh b t", b=16, hh=4), in1=dv, op=ALU.mult))

                yield

            for ti in range(ntile):
                if sample:
                    for _ in tile_ssd(ti):
                        pass
                    for _ in tile_attn(ti):
                        pass
                else:
                    gens = [tile_ssd(ti), tile_attn(ti)]
                    while gens:
                        for g_ in list(gens):
                            try:
                                next(g_)
                            except StopIteration:
                                gens.remove(g_)
            chk('m_tiles')
            if sample or last_group:
                ktm = SB(sc, "ktm", [128, 128]); t_ktm = Tok()
                pk, tpk = PS()
                lastc = slice(NT - P, NT)
                op("pe", [t_knf, t_c], [tpk], lambda p: p.transpose(out=pk[:P, 0:128], in_=knf[:, lastc], identity=ident[:]))
                op("act", [tpk], [t_ktm], lambda a: a.activation(out=ktm[:P, :], in_=pk[:P, 0:128], func=AF.Copy))
                if sample:
                    for b in range(NSS):
                        dma("sp", [t_ktm], [], ock_s[l, b, 124:128, :], ktm[b * 4:(b + 1) * 4, :])
                        dma("sp", [t_vf], [], ocv_s[l, b, 124:128, :], vf[b * 4:(b + 1) * 4, :])
                else:
                    dma("sp", [t_ktm], [], ock_p[l, :, :], ktm[:, :])
                    dma("sp", [t_vf], [], ocv_p[l, :, :], vf[:, :])
                    hTo = SB(sc, "hTo", [128, 4, 128]); t_hTo = Tok()
                    ph, tph = PS()
                    for j in range(4):
                        op("pe", [t_hT, t_c], [tph], lambda p: p.transpose(out=ph[:, j * 128:(j + 1) * 128], in_=hT[:, j * 128:(j + 1) * 128], identity=ident[:]))
                    op("act", [tph], [t_hTo], lambda a: a.activation(out=hTo[:].rearrange("p j n -> p (j n)"), in_=ph[:, :], func=AF.Copy))
                    dma("sp", [t_hTo], [], ossm_p[l, :, :].rearrange("(j p) n -> p j n", p=128), hTo[:])
            if not sample:
                op("dve", [t_kn], [t_kvh], lambda v: v.tensor_copy(out=khalo[:], in_=kn[:, NT:NT + 128]))
                op("dve", [t_vb], [t_kvh], lambda v: v.tensor_copy(out=vhalo[:], in_=vb[:, ntile, :]))

            chk('m_outs')
            woT = wreg[:, 0:4096].rearrange("p (k c) -> p k c", k=4)
            woA = wreg[:64, 4096:4096 + 8192].rearrange("p (k c) -> p k c", k=8)
            if cached:
                dma("pool", [t_scm[1]], T, wreg[:, 0:4096], wscm[1, :, 0:4096])
                dma("pool", [t_scm[1]], T, wreg[:64, 4096:12288], wscm[1, :64, 4096:12288])
            else:
                dma("pool", [], T, woT, w_out[l, 0:512, :].rearrange("(k p) c -> p k c", p=128))
                dma("pool", [], T, woA, w_out[l, 512:1024, :].rearrange("(hd p) c -> p hd c", p=64))
                dma("sp", T, [t_scm[1]], wscm[1, :, 0:4096], wreg[:, 0:4096])
                dma("sp", T, [t_scm[1]], wscm[1, :64, 4096:12288], wreg[:64, 4096:12288])
            with cx.fast():
                for d in range(8):
                    for (n0, nw) in nblocks(NT):
                        ps, tp = PS()
                        for k in range(4):
                            op("pe", T + [t_mixT], [tp], lambda p: p.matmul(ps[:, :nw], lhsT=woT[:, k, d * 128:(d + 1) * 128], rhs=mixT[:, k, n0:n0 + nw], start=(k == 0), stop=False))
                        for hd in range(8):
                            op("pe", T + [t_oT], [tp], lambda p: p.matmul(ps[:, :nw], lhsT=woA[:, hd, d * 128:(d + 1) * 128], rhs=oT[:, hd, n0:n0 + nw], start=False, stop=(hd == 7)))
                        op("dve", [tp, t_h], [t_h], lambda v: v.tensor_tensor(out=h[:, d, n0:n0 + nw], in0=h[:, d, n0:n0 + nw], in1=ps[:, :nw], op=ALU.add))

        def ple(sc, h, t_h, NT, l, psrc, ntok_tile, cached):
            xn = SB(sc, "xn", [128, 8, NT], BF16); t_xn = [Tok() for _ in range(8)]
            norm(sc, h, t_h, NT, 3, l, xn, t_xn)
            wg = wreg[:, 0:8192].rearrange("p (k c) -> p k c", k=8)
            wp = wreg[:, 8192:8192 + 2048].rearrange("p (k c) -> p k c", k=2)
            T = [t_wreg]
            if cached:
                dma("pool", [t_scm[2]], T, wreg[:, 0:10240], wscm[2, :, 0:10240])
            else:
                dma("pool", [], T, wg, w_pg[l, :, :].rearrange("(k p) c -> p k c", p=128))
                dma("pool", [], T, wp, w_pp[l, :, :].rearrange("(k p) c -> p k c", p=128))
                dma("sp", T, [t_scm[2]], wscm[2, :, 0:10240], wreg[:, 0:10240])
            peT = SB(sc, "peT", [128, 2, NT], BF16); t_peT = Tok()
            ptm = [SB(sc, "ptm%d" % i, [128, 256]) for i in range(2)]; t_ptm = [Tok(), Tok()]
            P = ntok_tile
            for ti in range(NT // P):
                pt_, tpt_ = ptm[ti % 2], t_ptm[ti % 2]
                dma("sp", [], [tpt_], pt_[:P, :], psrc[ti * P:(ti + 1) * P, :])
                ps, tp = PS()
                for j in range(2):
                    op("pe", [tpt_, t_c], [tp], lambda p: p.transpose(out=ps[:, j * 128:j * 128 + P], in_=pt_[:P, j * 128:(j + 1) * 128], identity=ident[:P, :P]))
                op("act", [tp], [t_peT], lambda a: a.activation(out=peT[:, :, ti * P:(ti + 1) * P], in_=ps[:, 0:256].rearrange("p (j t) -> p j t", j=2)[:, :, :P], func=AF.Copy))
            with cx.fast():
                sg = SB(sc, "sg", [128, 512]); t_sg = Tok()
                for d in range(8):
                    for (n0, nw) in nblocks(NT):
                        pg, tpg = PS(); pq, tpq = PS()
                        for k in range(8):
                            op("pe", T + [t_xn[k]], [tpg], lambda p: p.matmul(pg[:, :nw], lhsT=wg[:, k, d * 128:(d + 1) * 128], rhs=xn[:, k, n0:n0 + nw], start=(k == 0), stop=(k == 7)))
                        for k in range(2):
                            op("pe", T + [t_peT], [tpq], lambda p: p.matmul(pq[:, :nw], lhsT=wp[:, k, d * 128:(d + 1) * 128], rhs=peT[:, k, n0:n0 + nw], start=(k == 0), stop=(k == 1)))
                        op("act", [tpg], [t_sg], lambda a: a.activation(out=sg[:, :nw], in_=pg[:, :nw], func=AF.Sigmoid))
                        op("dve", [t_sg, tpq], [t_sg], lambda v: v.tensor_tensor(out=sg[:, :nw], in0=sg[:, :nw], in1=pq[:, :nw], op=ALU.mult))
                        op("dve", [t_sg, t_h], [t_h], lambda v: v.tensor_tensor(out=h[:, d, n0:n0 + nw], in0=h[:, d, n0:n0 + nw], in1=sg[:, :nw], op=ALU.add))

        stage = [0]

        def chk(name):
            stage[0] += 1
            if STAGE_LIMIT is not None and stage[0] >= STAGE_LIMIT:
                if not cx.stopped:
                    print("STOP at stage", stage[0], name)
                cx.stopped = True

        try:
          _main_body = True
          with contextlib.ExitStack() as sc:
              xtm0 = SB(sc, "xtm", [128, 1024])
              transpose_in(xtm0, Tok(), xs[:, :], NS, hs, t_hs, 0)
              cx.barrier()
          chk('sample_load')
          for l in range(DEPTH):
              load_layer_consts(l)
              chk('layer_consts')
              op("dve", [], [t_hT], lambda v: v.memset(hT[:], 0.0))
              op("dve", [], [t_hTb], lambda v: v.memset(hTb[:], 0.0))
              op("dve", [], [t_xhalo], lambda v: v.memset(xhalo[:], 0.0))
              op("dve", [], [t_kvh], lambda v: v.memset(khalo[:], 0.0))
              op("dve", [], [t_kvh], lambda v: v.memset(vhalo[:], 0.0))
              for gi in range(NG + 1):
                  sample = gi == NG
                  NT = NS if sample else NTG
                  with contextlib.ExitStack() as gsc:
                      if sample:
                          h, t_h = hs, t_hs
                      else:
                          h = SB(gsc, "hgrp", [128, 8, NTG]); t_h = Tok()
                          if l == 0:
                              with contextlib.ExitStack() as sc:
                                  xtms = [(SB(sc, "xtm", [128, 1024]), Tok()) for _ in range(2)]
                                  for ti in range(NTG // 128):
                                      transpose_in(xtms[ti % 2][0], xtms[ti % 2][1], xp[gi * NTG + ti * 128:gi * NTG + (ti + 1) * 128, :], 128, h, t_h, ti * 128)
                                  cx.barrier()
                          else:
                              dma("sp", [t_hscr[gi]], [t_h], h[:], hscr[:, :, gi * NTG:(gi + 1) * NTG].rearrange("j p t -> p j t"))
                      cx.barrier()
                      chk('group_load')
                      with contextlib.ExitStack() as sc:
                          xn = SB(sc, "xn", [128, 8, NT], BF16); t_xn = [Tok() for _ in range(8)]
                          with cx.fast():
                              norm(sc, h, t_h, NT, 0, l, xn, t_xn)
                              chk('norm')
                              ffn(sc, h, t_h, NT, xn, t_xn, w1a, w3a, w2a, l, 0, gi > 0)
                          cx.barrier()
                          chk('ffn_a')
                      with contextlib.ExitStack() as sc:
                          mixer(sc, h, t_h, NT, l, sample, gi == 0, gi == NG - 1, gi > 0)
                          cx.barrier()
                          chk('mixer')
                      with contextlib.ExitStack() as sc:
                          xn = SB(sc, "xn", [128, 8, NT], BF16); t_xn = [Tok() for _ in range(8)]
                          with cx.fast():
                              norm(sc, h, t_h, NT, 2, l, xn, t_xn)
                              ffn(sc, h, t_h, NT, xn, t_xn, w1b, w3b, w2b, l, 1, gi > 0)
                          cx.barrier()
                          chk('ffn_b')
                      with contextlib.ExitStack() as sc:
                          if sample:
                              ple(sc, h, t_h, NT, l, psm[l, :, :], 64, gi > 0)
                          else:
                              ple(sc, h, t_h, NT, l, pp[l, gi * NTG:(gi + 1) * NTG, :], 128, gi > 0)
                          cx.barrier()
                      if not sample:
                          if l == 0:
                              dma("sp", [t_h], [t_hscr[gi]], hscr[:, :, gi * NTG:(gi + 1) * NTG].rearrange("j p t -> p j t"), h[:])
                          else:
                              with contextlib.ExitStack() as sc:
                                  ytms = [(SB(sc, "ytm", [128, 1024]), Tok()) for _ in range(2)]
                                  for ti in range(NTG // 128):
                                      transpose_out(ytms[ti % 2][0], ytms[ti % 2][1], h, t_h, ti * 128, 128, yp[gi * NTG + ti * 128:gi * NTG + (ti + 1) * 128, :])
                                  cx.barrier()
                      elif l == DEPTH - 1:
                          with contextlib.ExitStack() as sc:
                              ytm0 = SB(sc, "ytm", [128, 1024])
                              transpose_out(ytm0, Tok(), h, t_h, 0, NS, ys[:, :])
                              cx.barrier()
                      cx.barrier()
        except _Stop:
            pass
        cx.finish()
    return nc


def make_consts():
    c = {}
    c["c_ident"] = np.eye(128, dtype=np.float32)
    i = np.arange(128)
    c["c_U"] = (i[:, None] <= i[None, :]).astype(np.float32)
    c["c_SL"] = (i[:, None] > i[None, :]).astype(np.float32)
    j = np.arange(64); same = (j[:, None] // 4) == (j[None, :] // 4)
    c["c_Ubd"] = (same & (j[:, None] <= j[None, :])).astype(np.float32)
    c["c_SLbd"] = (same & (j[:, None] > j[None, :])).astype(np.float32)
    c["c_BMt"] = ((j[:, None] // 4) == np.arange(16)[None, :]).astype(np.float32)
    bm = ((np.arange(16)[:, None]) == (j[None, :] // 4)).astype(np.float32)
    c["c_BM"] = np.broadcast_to(bm.reshape(1, 16 * 64), (128, 16 * 64)).copy()
    bo = np.zeros((128, 128), np.float32); bo[:64, :64] = 1; bo[64:, 64:] = 1
    c["c_bones"] = bo
    slopes = np.power(np.float32(2.0), -8.0 * np.arange(1, 9, dtype=np.float32) / 8).astype(np.float32)
    s = i[:, None, None]; q = i[None, None, :]; sl = slopes[None, :, None]
    ecur = np.where(q >= s, np.exp(-sl * (q - s).astype(np.float32)), 0.0)
    eprev = np.where(s > q, np.exp(-sl * (q - s + 128).astype(np.float32)), 0.0)
    c["c_Ecur"] = ecur.astype(np.float32).reshape(128, 1024)
    c["c_Eprev"] = eprev.astype(np.float32).reshape(128, 1024)
    t = np.arange(4)[None, None, :]
    ecache = np.where(s > t, np.exp(-sl * (128 + t - s).astype(np.float32)), 0.0)
    c["c_Ecache"] = ecache.astype(np.float32).reshape(128, 32)
    sj = j[:, None, None]; qj = j[None, None, :]
    enew = np.where(((sj // 4) == (qj // 4)) & (sj <= qj), np.exp(-sl * (qj - sj).astype(np.float32)), 0.0)
    c["c_Enew"] = enew.astype(np.float32).reshape(64, 512)
    return c


_WNAMES = ["g_ffn1", "w1_a", "w3_a", "w2_a", "g_mix", "w_in", "conv_w", "conv_b", "dt_bias", "a_log", "d_skip", "ssm_norm",
           "q_norm", "k_norm", "sinks", "w_out", "g_ffn2", "w1_b", "w3_b", "w2_b", "g_ple", "w_ple_gate", "w_ple_proj"]


def run(inputs, SEQ, NSS, NTG, n_prompt, ncores):
    f = lambda a: np.ascontiguousarray(np.asarray(a, dtype=np.float32))
    nc = build(SEQ, NSS, NTG)
    consts = make_consts()
    wts = {n: f(inputs[n]) for n in _WNAMES}
    xpr = f(inputs["x_prompt"]); ppr = f(inputs["p_prompt"]); xsm = f(inputs["x_sample"]); psm = f(inputs["p_sample"])
    sssm = f(inputs["state_ssm"]); sconv = f(inputs["state_conv"]); ck = f(inputs["cache_k_win"]); cv = f(inputs["cache_v_win"])
    in_maps = []
    for c in range(ncores):
        b = c % n_prompt
        bs = slice(c * NSS, (c + 1) * NSS)
        m = dict(wts); m.update(consts)
        m["xp"] = f(xpr[b]); m["pp"] = f(ppr[:, b])
        m["xs"] = f(xsm[bs].reshape(NSS * 4, D)); m["psm"] = f(psm[:, bs].reshape(DEPTH, NSS * 4, DPLE))
        m["sssm"] = f(sssm[:, bs].reshape(DEPTH, NSS, 512, 128)); m["sconv"] = f(sconv[:, bs].reshape(DEPTH, NSS * 3, 1024))
        m["ck"] = f(ck[:, bs].reshape(DEPTH, NSS, 128, 128)); m["cv"] = f(cv[:, bs].reshape(DEPTH, NSS, 128, 128))
        in_maps.append(m)
    res = run_bass_kernel_spmd(nc, in_maps, core_ids=list(range(ncores))).results
    P = n_prompt
    y_p = np.stack([res[b]["yp"] for b in range(P)])
    y_s = np.concatenate([res[c]["ys"].reshape(NSS, 4, D) for c in range(ncores)])
    ssm_p = np.stack([res[b]["ossm_p"].reshape(DEPTH, 8, 64, 128) for b in range(P)], axis=1)
    conv_p = np.stack([res[b]["oconv_p"] for b in range(P)], axis=1)
    k_p = np.stack([res[b]["ock_p"].reshape(DEPTH, 128, 2, 64) for b in range(P)], axis=1)
    v_p = np.stack([res[b]["ocv_p"].reshape(DEPTH, 128, 2, 64) for b in range(P)], axis=1)
    ssm_s = np.concatenate([res[c]["ossm_s"].reshape(DEPTH, NSS, 8, 64, 128) for c in range(ncores)], axis=1)
    conv_s = np.concatenate([res[c]["oconv_s"].reshape(DEPTH, NSS, 3, 1024) for c in range(ncores)], axis=1)
    k_s = np.concatenate([res[c]["ock_s"].reshape(DEPTH, NSS, 128, 2, 64) for c in range(ncores)], axis=1)
    v_s = np.concatenate([res[c]["ocv_s"].reshape(DEPTH, NSS, 128, 2, 64) for c in range(ncores)], axis=1)
    return tuple(np.ascontiguousarray(a, dtype=np.float32) for a in (y_p, y_s, ssm_p, conv_p, k_p, v_p, ssm_s, conv_s, k_s, v_s))


def kernel(**inputs):
    return run(inputs, SEQ=4096, NSS=16, NTG=512, n_prompt=4, ncores=NCORES)
```

```python
import contextlib
import numpy as np
import concourse.bass as bass
import concourse.mybir as mybir
from concourse.bass_utils import run_bass_kernel_spmd

F32 = mybir.dt.float32
BF16 = mybir.dt.bfloat16
AF = mybir.ActivationFunctionType
ALU = mybir.AluOpType
AX = mybir.AxisListType

D = 1024; DFF = 2752; DPROJ = 2312; DPLE = 256; DEPTH = 2
NCORES = 8
EPS = 1e-6
FT = [(i * 128, 128) for i in range(21)] + [(2688, 64)]
NFT = len(FT)
FCH = [list(range(i, min(i + 2, NFT))) for i in range(0, NFT, 2)]
W2CH = [list(range(i, min(i + 6, NFT))) for i in range(0, NFT, 6)]


DEBUG_MAP = None
SBUF_PEAK = [0, 0]
STAGE_LIMIT = None


class _Stop(Exception):
    pass


class Tok:
    __slots__ = ("w", "r")

    def __init__(self):
        self.w = None
        self.r = {}


class Eng:
    def __init__(self, name, h):
        self.name = name; self.h = h; self.sem = None; self.cnt = 0; self.waited = {}; self.own = set()


def _r32(n):
    return 32 if n <= 32 else (64 if n <= 64 else 128)


class PEProxy:
    def __init__(self, ctx, e):
        self.ctx = ctx; self.e = e; self.last = None

    def _mode(self, key):
        e = self.e
        if key != self.last and e.cnt > 0:
            k = id(e.sem)
            if e.waited.get(k, 0) < e.cnt:
                e.h.wait_ge(e.sem, e.cnt)
                e.waited[k] = e.cnt
        self.last = key

    def matmul(self, out, lhsT, rhs, start=True, stop=True):
        self._mode(("mm", str(lhsT.dtype), _r32(lhsT.shape[0]), _r32(int(np.prod(lhsT.shape[1:]))), out.base_partition()))
        return self.e.h.matmul(out, lhsT=lhsT, rhs=rhs, start=start, stop=stop)

    def transpose(self, out, in_, identity):
        self._mode(("tr", str(in_.dtype), _r32(in_.shape[0]), _r32(int(np.prod(in_.shape[1:]))), out.base_partition()))
        return self.e.h.transpose(out=out, in_=in_, identity=identity)


class Ctx:
    EPOCH = 12000

    def __init__(self, nc, es):
        self.nc = nc; self.es = es
        self.E = {"pe": Eng("pe", nc.tensor), "act": Eng("act", nc.scalar), "dve": Eng("dve", nc.vector),
                  "pool": Eng("pool", nc.gpsimd), "sp": Eng("sp", nc.sync)}
        self.nsem = 0
        for e in self.E.values():
            e.sem = self._newsem(); e.own.add(id(e.sem))
        self.slots = {"sp": [[self._newsem(), 0] for _ in range(10)],
                      "pool": [[self._newsem(), 0] for _ in range(10)]}
        self.slot_i = {"sp": 0, "pool": 0}
        self.semkey = {}
        self.stopped = False
        self.pe_proxy = PEProxy(self, self.E["pe"])
        self.pe_fast = False

    @contextlib.contextmanager
    def fast(self):
        old = self.pe_fast
        self.pe_fast = True; self.pe_proxy.last = "edge"
        try:
            yield
        finally:
            self.pe_fast = old; self.pe_proxy.last = "edge"

    def _newsem(self):
        self.nsem += 1
        return self.es.enter_context(self.nc.semaphore("s%d" % self.nsem))

    def _wait(self, e, ev):
        sem, val = ev
        k = id(sem)
        if e.name == "pe" and k in e.own and self.pe_fast:
            return
        if e.waited.get(k, 0) >= val:
            return
        e.h.wait_ge(sem, val)
        e.waited[k] = val

    def _sync(self, e, reads, writes):
        for t in reads:
            if t.w is not None:
                self._wait(e, t.w)
        for t in writes:
            if t.w is not None:
                self._wait(e, t.w)
            for ev in t.r.values():
                self._wait(e, ev)

    def _commit(self, ev, reads, writes):
        for t in writes:
            t.w = ev; t.r = {}
        for t in reads:
            k = id(ev[0])
            if k not in t.r or t.r[k][1] < ev[1]:
                t.r[k] = ev

    def op(self, eng, reads, writes, fn):
        if self.stopped:
            return None
        e = self.E[eng]
        self._sync(e, reads, writes)
        if e.cnt >= self.EPOCH:
            e.sem = self._newsem(); e.cnt = 0; e.own.add(id(e.sem))
        inst = fn(self.pe_proxy if eng == "pe" else e.h)
        e.cnt += 1
        inst.then_inc(e.sem, 1)
        if DEBUG_MAP is not None:
            import traceback
            nm = None
            for a in ("name", "inst", "instruction", "ins"):
                v = getattr(inst, a, None)
                if v is not None:
                    nm = getattr(v, "name", v) if a != "name" else v
                    break
            fr = traceback.extract_stack(limit=4)[-2]; fr0 = traceback.extract_stack(limit=4)[-3]
            DEBUG_MAP[str(nm)] = "%s:%d < %s:%d" % (fr.name, fr.lineno, fr0.name, fr0.lineno)
        ev = (e.sem, e.cnt)
        e.waited[id(e.sem)] = max(e.waited.get(id(e.sem), 0), 0)
        self._commit(ev, reads, writes)
        return ev

    def dma(self, eng, reads, writes, out, in_, **kw):
        if self.stopped:
            return None
        e = self.E[eng]
        sl = self.slots[eng][self.slot_i[eng]]
        self.slot_i[eng] = (self.slot_i[eng] + 1) % len(self.slots[eng])
        if sl[1] > 0:
            self._wait(e, (sl[0], sl[1]))
        self._sync(e, reads, writes)
        inst = e.h.dma_start(out=out, in_=in_, **kw)
        sl[1] += 16
        inst.then_inc(sl[0], 16)
        ev = (sl[0], sl[1])
        self._commit(ev, reads, writes)
        return ev

    def barrier(self):
        if self.stopped:
            return
        evs = []
        for e in self.E.values():
            if e.cnt > 0:
                evs.append((e.sem, e.cnt))
        for q in self.slots.values():
            for sl in q:
                if sl[1] > 0:
                    evs.append((sl[0], sl[1]))
        for e in self.E.values():
            for ev in evs:
                if ev[0] is e.sem:
                    continue
                self._wait(e, ev)

    def finish(self):
        self.stopped = False
        self.barrier()


def build(SEQ, NSS, NTG):
    NS = NSS * 4
    NG = SEQ // NTG
    nc = bass.Bass("TRN2", target_bir_lowering=False)
    di = lambda n, s: nc.dram_tensor(n, s, F32, kind="ExternalInput").ap()
    do = lambda n, s: nc.dram_tensor(n, s, F32, kind="ExternalOutput").ap()
    xp = di("xp", [SEQ, D]); pp = di("pp", [DEPTH, SEQ, DPLE]); xs = di("xs", [NS, D]); psm = di("psm", [DEPTH, NS, DPLE])
    sssm = di("sssm", [DEPTH, NSS, 512, 128]); sconv = di("sconv", [DEPTH, NSS * 3, 1024])
    ck = di("ck", [DEPTH, NSS, 128, 128]); cv = di("cv", [DEPTH, NSS, 128, 128])
    g_ffn1 = di("g_ffn1", [DEPTH, D]); g_mix = di("g_mix", [DEPTH, D]); g_ffn2 = di("g_ffn2", [DEPTH, D]); g_ple = di("g_ple", [DEPTH, D])
    w1a = di("w1_a", [DEPTH, D, DFF]); w3a = di("w3_a", [DEPTH, D, DFF]); w2a = di("w2_a", [DEPTH, DFF, D])
    w1b = di("w1_b", [DEPTH, D, DFF]); w3b = di("w3_b", [DEPTH, D, DFF]); w2b = di("w2_b", [DEPTH, DFF, D])
    w_in = di("w_in", [DEPTH, D, DPROJ]); w_out = di("w_out", [DEPTH, D, D])
    conv_w = di("conv_w", [DEPTH, 4, 1024]); conv_b = di("conv_b", [DEPTH, 1024])
    dt_bias = di("dt_bias", [DEPTH, 8]); a_log = di("a_log", [DEPTH, 8]); d_skip = di("d_skip", [DEPTH, 8])
    ssm_norm = di("ssm_norm", [DEPTH, 512]); q_norm = di("q_norm", [DEPTH, 64]); k_norm = di("k_norm", [DEPTH, 64])
    sinks = di("sinks", [DEPTH, 8]); w_pg = di("w_ple_gate", [DEPTH, D, D]); w_pp = di("w_ple_proj", [DEPTH, DPLE, D])
    c_ident = di("c_ident", [128, 128]); c_U = di("c_U", [128, 128]); c_SL = di("c_SL", [128, 128])
    c_Ubd = di("c_Ubd", [64, 64]); c_SLbd = di("c_SLbd", [64, 64]); c_BMt = di("c_BMt", [64, 16]); c_BM = di("c_BM", [128, 16 * 64])
    c_bones = di("c_bones", [128, 128]); c_Eprev = di("c_Eprev", [128, 1024]); c_Ecur = di("c_Ecur", [128, 1024])
    c_Ecache = di("c_Ecache", [128, 32]); c_Enew = di("c_Enew", [64, 512])
    yp = do("yp", [SEQ, D]); ys = do("ys", [NS, D])
    ossm_p = do("ossm_p", [DEPTH, 512, 128]); oconv_p = do("oconv_p", [DEPTH, 3, 1024])
    ock_p = do("ock_p", [DEPTH, 128, 128]); ocv_p = do("ocv_p", [DEPTH, 128, 128])
    ossm_s = do("ossm_s", [DEPTH, NSS, 512, 128]); oconv_s = do("oconv_s", [DEPTH, NSS * 3, 1024])
    ock_s = do("ock_s", [DEPTH, NSS, 128, 128]); ocv_s = do("ocv_s", [DEPTH, NSS, 128, 128])
    hscr = nc.dram_tensor("hscr", [8, 128, SEQ], F32, kind="Internal").ap()
    wsc13 = nc.dram_tensor("wsc13", [2, 2, len(FCH), 128, 2048], BF16, kind="Internal").ap()
    wsc2 = nc.dram_tensor("wsc2", [2, 2, 128, NFT * 512], BF16, kind="Internal").ap()
    wscm = nc.dram_tensor("wscm", [3, 128, 18560], BF16, kind="Internal").ap()
    t_sc13 = [[[Tok() for _ in FCH] for _ in range(2)] for _ in range(2)]
    t_sc2 = [[[Tok() for _ in W2CH] for _ in range(2)] for _ in range(2)]
    t_scm = [Tok() for _ in range(3)]
    t_hscr = [Tok() for _ in range(NG)]

    with contextlib.ExitStack() as es:
        cx = Ctx(nc, es)
        op = cx.op; dma = cx.dma

        uniq = [0]

        def SB(scope, name, shape, dt=F32):
            uniq[0] += 1
            t = scope.enter_context(nc.sbuf_tensor("%s_%d" % (name, uniq[0]), shape, dt))
            try:
                SBUF_PEAK[0] = max(SBUF_PEAK[0], int(nc.sbuf_base))
                SBUF_PEAK[1] = int(nc.sbuf_top)
            except Exception:
                pass
            return t

        ident = SB(es, "ident", [128, 128]); t_c = Tok()
        identb = SB(es, "identb", [128, 128], BF16)
        Um = SB(es, "Um", [128, 128]); SLm = SB(es, "SLm", [128, 128])
        Ubd = SB(es, "Ubd", [64, 64]); SLbd = SB(es, "SLbd", [64, 64]); BMt = SB(es, "BMt", [64, 16]); BMtb = SB(es, "BMtb", [64, 16], BF16)
        BMb = SB(es, "BMb", [128, 16 * 64], BF16)
        bones = SB(es, "bones", [128, 128], BF16); onesb = SB(es, "onesb", [128, 128], BF16); onesf = SB(es, "onesf", [128, 128])
        Eprev = SB(es, "Eprev", [128, 1024]); Ecur = SB(es, "Ecur", [128, 1024]); Ecache = SB(es, "Ecache", [128, 32]); Enew = SB(es, "Enew", [64, 512])
        gcol = SB(es, "gcol", [128, 4 * DEPTH * 8])
        lay = {}
        for nm, w in [("cw", 32), ("cb", 8), ("dtb", 8), ("aneg", 8), ("dsk", 8), ("esk", 8), ("gq", 1), ("gk", 1)]:
            lay[nm] = SB(es, "l_" + nm, [128, w])
        gssm = SB(es, "gssm", [128, 512]); esinkrow = SB(es, "esinkrow", [64, 1024])
        t_lay = Tok()
        hs = SB(es, "hs", [128, 8, 64]); t_hs = Tok()
        w13 = [[SB(es, "w13_%d_%d" % (m, b), [128, 8, 256], BF16) for b in range(2)] for m in range(2)]
        t_w13 = [[Tok() for _ in range(2)] for _ in range(2)]
        wreg = SB(es, "wreg", [128, 18560], BF16); t_wreg = Tok()
        hT = SB(es, "hT", [128, 512]); hTb = SB(es, "hTb", [128, 512], BF16); t_hT = Tok(); t_hTb = Tok()
        xhalo = SB(es, "xhalo", [128, 8, 3]); t_xhalo = Tok()
        khalo = SB(es, "khalo", [128, 128], BF16); vhalo = SB(es, "vhalo", [128, 128], BF16); t_kvh = Tok()
        psb = [es.enter_context(nc.psum_tensor("ps%d" % i, [128, 512], F32)) for i in range(8)]
        t_ps = [Tok() for _ in range(8)]
        psi = [0]

        held = set()

        def PS(hold=False):
            for _try in range(9):
                i = psi[0]; psi[0] = (i + 1) % 8
                if i not in held:
                    break
            else:
                raise RuntimeError("all PSUM banks held")
            if hold:
                held.add(i)
            return psb[i], t_ps[i]

        def REL(tok):
            held.discard(t_ps.index(tok))

        def ld(eng, dst, src, toks, **kw):
            dma(eng, [], toks, dst, src, **kw)
        ld("sp", ident[:], c_ident[:, :], [t_c]); ld("pool", identb[:], c_ident[:, :], [t_c])
        ld("sp", Um[:], c_U[:, :], [t_c]); ld("sp", SLm[:], c_SL[:, :], [t_c])
        ld("sp", Ubd[:], c_Ubd[:, :], [t_c]); ld("sp", SLbd[:], c_SLbd[:, :], [t_c]); ld("sp", BMt[:], c_BMt[:, :], [t_c])
        ld("pool", BMtb[:], c_BMt[:, :], [t_c]); ld("pool", BMb[:], c_BM[:, :], [t_c]); ld("pool", bones[:], c_bones[:, :], [t_c])
        ld("sp", Eprev[:], c_Eprev[:, :], [t_c]); ld("sp", Ecur[:], c_Ecur[:, :], [t_c]); ld("sp", Ecache[:], c_Ecache[:, :], [t_c]); ld("sp", Enew[:], c_Enew[:, :], [t_c])
        op("dve", [], [t_c], lambda v: v.memset(onesb[:], 1.0))
        op("dve", [], [t_c], lambda v: v.memset(onesf[:], 1.0))
        for ni, g in enumerate([g_ffn1, g_mix, g_ffn2, g_ple]):
            for l in range(DEPTH):
                o = (ni * DEPTH + l) * 8
                ld("sp", gcol[:, o:o + 8], g[l, :].rearrange("(j p) -> p j", p=128), [t_c], allow_slow_non_contiguous=True)
        op("dve", [t_c], [t_c], lambda v: v.tensor_scalar(out=gcol[:], in0=gcol[:], scalar1=32.0, scalar2=None, op0=ALU.mult))

        def load_layer_consts(l):
            T = [t_lay]
            for j in range(4):
                ld("sp", lay["cw"][:, j * 8:(j + 1) * 8], conv_w[l, j, :].rearrange("(c p) -> p c", p=128), T, allow_slow_non_contiguous=True)
            ld("sp", lay["cb"][:], conv_b[l, :].rearrange("(c p) -> p c", p=128), T, allow_slow_non_contiguous=True)
            ld("sp", lay["dtb"][:], dt_bias[l, :].partition_broadcast(128), T)
            ld("sp", lay["aneg"][:], a_log[l, :].partition_broadcast(128), T)
            ld("sp", lay["dsk"][:], d_skip[l, :].partition_broadcast(128), T)
            ld("sp", lay["esk"][:], sinks[l, :].partition_broadcast(128), T)
            for hh in range(2):
                ld("sp", lay["gq"][hh * 64:(hh + 1) * 64, :], q_norm[l, :].rearrange("(p o) -> p o", o=1), T, allow_slow_non_contiguous=True)
                ld("sp", lay["gk"][hh * 64:(hh + 1) * 64, :], k_norm[l, :].rearrange("(p o) -> p o", o=1), T, allow_slow_non_contiguous=True)
            ld("sp", gssm[:], ssm_norm[l, :].partition_broadcast(128), T)
            op("act", T, T, lambda a: a.activation(out=lay["aneg"][:], in_=lay["aneg"][:], func=AF.Exp))
            op("dve", T, T, lambda v: v.tensor_scalar(out=lay["aneg"][:], in0=lay["aneg"][:], scalar1=-1.0, scalar2=None, op0=ALU.mult))
            op("act", T, T, lambda a: a.activation(out=lay["esk"][:], in_=lay["esk"][:], func=AF.Exp))
            op("dve", T, T, lambda v: v.tensor_scalar(out=lay["gq"][:], in0=lay["gq"][:], scalar1=8.0, scalar2=None, op0=ALU.mult))
            op("dve", T, T, lambda v: v.tensor_scalar(out=lay["gk"][:], in0=lay["gk"][:], scalar1=8.0, scalar2=None, op0=ALU.mult))
            op("dve", T, T, lambda v: v.tensor_scalar(out=gssm[:], in0=gssm[:], scalar1=16.0, scalar2=None, op0=ALU.mult))
            op("dve", T, T, lambda v: v.tensor_copy(out=esinkrow[:].rearrange("p (h q) -> p h q", h=8),
                                                     in_=lay["esk"][0:64, :].unsqueeze(2).broadcast_to([64, 8, 128])))

        def nblocks(NT):
            return [(n0, min(512, NT - n0)) for n0 in range(0, NT, 512)]

        def norm(sc, h, t_h, NT, ni, l, xn, t_xn):
            go = (ni * DEPTH + l) * 8
            sq = SB(sc, "sq", [128, 8, 512], BF16); t_sq = [Tok(), Tok()]
            rstd = SB(sc, "rstd", [128, 512]); t_rstd = Tok()
            for (n0, nw) in nblocks(NT):
                for half in range(2):
                    op("act", [t_h], [t_sq[half]], lambda a: a.activation(out=sq[:, half * 4:(half + 1) * 4, :nw], in_=h[:, half * 4:(half + 1) * 4, n0:n0 + nw], func=AF.Square))
                ps, tp = PS()
                for j in range(8):
                    op("pe", [t_sq[j // 4], t_c], [tp], lambda p: p.matmul(ps[:, :nw], lhsT=onesb[:], rhs=sq[:, j, :nw], start=(j == 0), stop=(j == 7)))
                op("act", [tp], [t_rstd], lambda a: a.activation(out=rstd[:, :nw], in_=ps[:, :nw], func=AF.Ln, bias=1024.0 * EPS, scale=1.0))
                op("act", [t_rstd], [t_rstd], lambda a: a.activation(out=rstd[:, :nw], in_=rstd[:, :nw], func=AF.Exp, scale=-0.5))
                for j in range(8):
                    op("dve", [t_h, t_rstd, t_c], [t_xn[j]], lambda v: v.scalar_tensor_tensor(out=xn[:, j, n0:n0 + nw], in0=h[:, j, n0:n0 + nw], scalar=gcol[:, go + j:go + j + 1], in1=rstd[:, :nw], op0=ALU.mult, op1=ALU.mult))

        w13_next = [None]

        def issue_w13(W1, W3, l, ci, parity, ab, cached):
            cols = FCH[ci]; f0 = FT[cols[0]][0]; fw = sum(FT[c][1] for c in cols)
            for m, W in enumerate((W1, W3)):
                scr = wsc13[ab, m, ci, :, :].rearrange("p (k f) -> p k f", k=8)[:, :, :fw]
                if cached:
                    dma("pool", [t_sc13[ab][m][ci]], [t_w13[m][parity]], w13[m][parity][:, :, :fw], scr)
                else:
                    dma("pool", [], [t_w13[m][parity]], w13[m][parity][:, :, :fw], W[l, :, f0:f0 + fw].rearrange("(k p) f -> p k f", p=128))
                    dma("sp", [t_w13[m][parity]], [t_sc13[ab][m][ci]], scr, w13[m][parity][:, :, :fw])

        def ffn(sc, h, t_h, NT, xn, t_xn, W1, W3, W2, l, ab, cached):
            gT = SB(sc, "gT", [128, NFT, NT], BF16); t_g = [Tok() for _ in range(NFT)]
            s1 = [SB(sc, "s1_%d" % i, [128, 512]) for i in range(2)]; t_s1 = [Tok(), Tok()]
            w2 = SB(sc, "w2", [128, NFT, 512], BF16); t_w2 = [Tok() for _ in W2CH]
            si = 0
            issue_w13(W1, W3, l, 0, 0, ab, cached)
            for ci, cols in enumerate(FCH):
                par = ci % 2
                if ci + 1 < len(FCH):
                    issue_w13(W1, W3, l, ci + 1, (ci + 1) % 2, ab, cached)
                for fi, ft in enumerate(cols):
                    fw = FT[ft][1]; fo = fi * 128
                    for (n0, nw) in nblocks(NT):
                        p1, tp1 = PS(); p3, tp3 = PS()
                        for (pp_, tpp, m) in ((p1, tp1, 0), (p3, tp3, 1)):
                            for k in range(8):
                                op("pe", [t_w13[m][par], t_xn[k]], [tpp], lambda p: p.matmul(pp_[:fw, :nw], lhsT=w13[m][par][:, k, fo:fo + fw], rhs=xn[:, k, n0:n0 + nw], start=(k == 0), stop=(k == 7)))
                        sb_, ts_ = s1[si], t_s1[si]; si ^= 1
                        op("act", [tp1], [ts_], lambda a: a.activation(out=sb_[:fw, :nw], in_=p1[:fw, :nw], func=AF.Silu))
                        op("dve", [ts_, tp3], [t_g[ft]], lambda v: v.tensor_tensor(out=gT[:fw, ft, n0:n0 + nw], in0=sb_[:fw, :nw], in1=p3[:fw, :nw], op=ALU.mult))
            for half in range(2):
                for wi, rows in enumerate(W2CH):
                    r0 = FT[rows[0]][0]
                    nfull = [r for r in rows if FT[r][1] == 128]
                    scr2 = wsc2[ab, half, :, :].rearrange("p (f c) -> p f c", f=NFT)
                    if nfull:
                        sl = slice(nfull[0], nfull[-1] + 1)
                        if cached:
                            dma("pool", [t_sc2[ab][half][wi]], [t_w2[wi]], w2[:, sl, :], scr2[:, sl, :])
                        else:
                            dma("pool", [], [t_w2[wi]], w2[:, sl, :], W2[l, r0:r0 + 128 * len(nfull), half * 512:(half + 1) * 512].rearrange("(f p) c -> p f c", p=128))
                    for r in rows:
                        if FT[r][1] != 128:
                            if cached:
                                dma("pool", [t_sc2[ab][half][wi]], [t_w2[wi]], w2[:64, r, :], scr2[:64, r, :])
                            else:
                                dma("pool", [], [t_w2[wi]], w2[:64, r, :], W2[l, FT[r][0]:FT[r][0] + 64, half * 512:(half + 1) * 512])
                    if not cached:
                        if nfull:
                            dma("sp", [t_w2[wi]], [t_sc2[ab][half][wi]], scr2[:, sl, :], w2[:, sl, :])
                        for r in rows:
                            if FT[r][1] != 128:
                                dma("sp", [t_w2[wi]], [t_sc2[ab][half][wi]], scr2[:64, r, :], w2[:64, r, :])
                for (n0, nw) in nblocks(NT):
                    acc = [PS() for _ in range(4)]
                    for ft in range(NFT):
                        fw = FT[ft][1]
                        wi = [i for i, rows in enumerate(W2CH) if ft in rows][0]
                        for dj in range(4):
                            op("pe", [t_w2[wi], t_g[ft]], [acc[dj][1]], lambda p: p.matmul(acc[dj][0][:, :nw], lhsT=w2[:fw, ft, dj * 128:(dj + 1) * 128], rhs=gT[:fw, ft, n0:n0 + nw], start=(ft == 0), stop=(ft == NFT - 1)))
                    for dj in range(4):
                        d = half * 4 + dj
                        op("dve", [acc[dj][1], t_h], [t_h], lambda v: v.scalar_tensor_tensor(out=h[:, d, n0:n0 + nw], in0=acc[dj][0][:, :nw], scalar=0.5, in1=h[:, d, n0:n0 + nw], op0=ALU.mult, op1=ALU.add))

        def transpose_in(xtm, t_x, src, ntok, h, t_h, c0):
            dma("sp", [], [t_x], xtm[:ntok, :], src)
            for a in range(2):
                ps, tp = PS()
                for j in range(4):
                    op("pe", [t_x, t_c], [tp], lambda p: p.transpose(out=ps[:, j * 128:j * 128 + ntok], in_=xtm[:ntok, (a * 4 + j) * 128:(a * 4 + j + 1) * 128], identity=ident[:ntok, :ntok]))
                op("act", [tp], [t_h], lambda a_: a_.activation(out=h[:, a * 4:(a + 1) * 4, c0:c0 + ntok], in_=ps[:].rearrange("p (j t) -> p j t", j=4)[:, :, :ntok], func=AF.Copy))

        def transpose_out(ytm, t_y, h, t_h, c0, ntok, dst):
            for a in range(2):
                ps, tp = PS()
                for j in range(4):
                    op("pe", [t_h, t_c], [tp], lambda p: p.transpose(out=ps[:ntok, j * 128:(j + 1) * 128], in_=h[:, a * 4 + j, c0:c0 + ntok], identity=ident[:]))
                op("act", [tp], [t_y], lambda a_: a_.activation(out=ytm[:ntok, a * 512:(a + 1) * 512], in_=ps[:ntok, :], func=AF.Copy))
            dma("sp", [t_y], [], dst, ytm[:ntok, :])

        def mixer(sc, h, t_h, NT, l, sample, first_group, last_group, cached):
            xn = SB(sc, "xn", [128, 8, NT], BF16); t_xn = [Tok() for _ in range(8)]
            norm(sc, h, t_h, NT, 1, l, xn, t_xn)
            chk('m_norm')
            NTT = 64 if sample else 128
            ntile = NT // NTT
            o = 0
            def carve(n, shape_str, **kw):
                nonlocal o
                a = wreg[:, o:o + n]; o += n
                return a.rearrange(shape_str, **kw)
            wz = carve(8 * 512, "p (k c) -> p k c", k=8); wx = carve(8 * 1024, "p (k c) -> p k c", k=8)
            wq = carve(8 * 512, "p (k c) -> p k c", k=8); wk = carve(8 * 128, "p (k c) -> p k c", k=8)
            wv = carve(8 * 128, "p (k c) -> p k c", k=8); wdt = carve(8 * 8, "p (k c) -> p k c", k=8)
            wl = w_in[l, :, :].rearrange("(k p) c -> p k c", p=128)
            T = [t_wreg]
            if cached:
                dma("pool", [t_scm[0]], T, wreg[:, 0:18496], wscm[0, :, 0:18496])
            else:
                dma("pool", [], T, wz, wl[:, :, 0:512]); dma("pool", [], T, wx, wl[:, :, 512:1536]); dma("pool", [], T, wdt, wl[:, :, 1536:1544])
                for hh in range(4):
                    for g in range(2):
                        c = 1544 + g * 256 + hh * 64
                        dma("pool", [], T, wq[:, :, hh * 128 + g * 64:hh * 128 + g * 64 + 64], wl[:, :, c:c + 64])
                dma("pool", [], T, wk, wl[:, :, 2056:2184]); dma("pool", [], T, wv, wl[:, :, 2184:2312])
                dma("sp", T, [t_scm[0]], wscm[0, :, 0:18496], wreg[:, 0:18496])
            chk('m_wdma')
            HAL = 0 if sample else 3
            xin = SB(sc, "xin", [128, 8, NT + HAL]); t_xin = Tok()
            xc = SB(sc, "xc", [128, 8, NT], BF16); t_xc = Tok()
            qn = SB(sc, "qn", [128, 4, NT], BF16); t_qn = Tok()
            KH = 0 if sample else 128
            kn = SB(sc, "kn", [128, KH + NT], BF16); t_kn = Tok()
            knf = SB(sc, "knf", [128, NT]); t_knf = Tok()
            vb = SB(sc, "vb", [128, ntile + 1, 128], BF16); t_vb = Tok()
            vf = SB(sc, "vf", [128, 128]); t_vf = Tok()
            mixT = SB(sc, "mixT", [128, 4, NT], BF16); t_mixT = Tok()
            oT = SB(sc, "oT", [64, 8, NT], BF16); t_oT = Tok()
            tmpa = SB(sc, "tmpa", [128, 512]); t_tmpa = Tok()
            rq = SB(sc, "rq", [128, 512]); t_rq = Tok()
            sqb = SB(sc, "sqb", [128, 512], BF16); t_sqb = Tok()
            if not sample:
                op("dve", [t_xhalo], [t_xin], lambda v: v.tensor_copy(out=xin[:, :, 0:3], in_=xhalo[:]))
                op("dve", [t_kvh], [t_kn], lambda v: v.tensor_copy(out=kn[:, 0:128], in_=khalo[:]))
                op("dve", [t_kvh], [t_vb], lambda v: v.tensor_copy(out=vb[:, 0, :], in_=vhalo[:]))
            chk('m_halo')
            with cx.fast():
                for c in range(8):
                    for (n0, nw) in nblocks(NT):
                        ps, tp = PS()
                        for k in range(8):
                            op("pe", T + [t_xn[k]], [tp], lambda p: p.matmul(ps[:, :nw], lhsT=wx[:, k, c * 128:(c + 1) * 128], rhs=xn[:, k, n0:n0 + nw], start=(k == 0), stop=(k == 7)))
                        if not sample:
                            op("act", [tp], [t_xin], lambda a: a.activation(out=xin[:, c, HAL + n0:HAL + n0 + nw], in_=ps[:, :nw], func=AF.Copy))
                        else:
                            op("act", [tp], [t_xin], lambda a: a.activation(out=xin[:, c, n0:n0 + nw], in_=ps[:, :nw], func=AF.Copy))
                chk('m_xbc')
                for qi in range(5):
                    for (n0, nw) in nblocks(NT):
                        ps, tp = PS()
                        for k in range(8):
                            lw = wq[:, k, qi * 128:(qi + 1) * 128] if qi < 4 else wk[:, k, :]
                            op("pe", T + [t_xn[k]], [tp], lambda p: p.matmul(ps[:, :nw], lhsT=lw, rhs=xn[:, k, n0:n0 + nw], start=(k == 0), stop=(k == 7)))
                        op("act", [tp], [t_sqb], lambda a: a.activation(out=sqb[:, :nw], in_=ps[:, :nw], func=AF.Square))
                        ps2, tp2 = PS()
                        op("pe", [t_sqb, t_c], [tp2], lambda p: p.matmul(ps2[:, :nw], lhsT=bones[:], rhs=sqb[:, :nw], start=True, stop=True))
                        op("act", [tp2], [t_rq], lambda a: a.activation(out=rq[:, :nw], in_=ps2[:, :nw], func=AF.Ln, bias=64.0 * EPS, scale=1.0))
                        op("act", [t_rq], [t_rq], lambda a: a.activation(out=rq[:, :nw], in_=rq[:, :nw], func=AF.Exp, scale=-0.5))
                        if qi < 4:
                            op("dve", [tp, t_rq, t_lay], [t_qn], lambda v: v.scalar_tensor_tensor(out=qn[:, qi, n0:n0 + nw], in0=ps[:, :nw], scalar=lay["gq"][:, 0:1], in1=rq[:, :nw], op0=ALU.mult, op1=ALU.mult))
                        else:
                            op("dve", [tp, t_rq, t_lay], [t_knf], lambda v: v.scalar_tensor_tensor(out=knf[:, n0:n0 + nw], in0=ps[:, :nw], scalar=lay["gk"][:, 0:1], in1=rq[:, :nw], op0=ALU.mult, op1=ALU.mult))
                            op("act", [t_knf], [t_kn], lambda a: a.activation(out=kn[:, KH + n0:KH + n0 + nw], in_=knf[:, n0:n0 + nw], func=AF.Copy))
            chk('m_qk')
            cacc = SB(sc, "cacc", [128, 512]); t_cacc = Tok()
            if not sample:
                for c in range(8):
                    for (n0, nw) in nblocks(NT):
                        op("dve", [t_xin, t_lay], [t_cacc], lambda v: v.tensor_scalar(out=cacc[:, :nw], in0=xin[:, c, n0:n0 + nw], scalar1=lay["cw"][:, c:c + 1], scalar2=None, op0=ALU.mult))
                        for j in range(1, 4):
                            op("dve", [t_xin, t_lay, t_cacc], [t_cacc], lambda v: v.scalar_tensor_tensor(out=cacc[:, :nw], in0=xin[:, c, n0 + j:n0 + j + nw], scalar=lay["cw"][:, j * 8 + c:j * 8 + c + 1], in1=cacc[:, :nw], op0=ALU.mult, op1=ALU.add))
                        op("act", [t_cacc, t_lay], [t_xc], lambda a: a.activation(out=xc[:, c, n0:n0 + nw], in_=cacc[:, :nw], func=AF.Silu, bias=lay["cb"][:, c:c + 1], scale=1.0))
                op("dve", [t_xin], [t_xhalo], lambda v: v.tensor_copy(out=xhalo[:], in_=xin[:, :, NT:NT + 3]))
                if last_group:
                    cst = SB(sc, "cst", [128, 8, 4]); t_cst = Tok()
                    op("dve", [t_xin], [t_cst], lambda v: v.tensor_copy(out=cst[:, :, 0:3], in_=xin[:, :, NT:NT + 3]))
                    ps, tp = PS(); ps2, tp2 = PS()
                    for c in range(8):
                        pp_, tpp = (ps, tp) if c < 4 else (ps2, tp2)
                        op("pe", [t_cst, t_c], [tpp], lambda p: p.transpose(out=pp_[:3, (c % 4) * 128:(c % 4 + 1) * 128], in_=cst[:, c, 0:3], identity=ident[:]))
                    cso = SB(sc, "cso", [4, 1024]); t_cso = Tok()
                    op("act", [tp], [t_cso], lambda a: a.activation(out=cso[:3, 0:512], in_=ps[:3, :], func=AF.Copy))
                    op("act", [tp2], [t_cso], lambda a: a.activation(out=cso[:3, 512:1024], in_=ps2[:3, :], func=AF.Copy))
                    dma("sp", [t_cso], [], oconv_p[l, :, :], cso[:3, :])
            else:
                xfull = SB(sc, "xfull", [128, 8, NSS, 7]); t_xf = Tok()
                scm = SB(sc, "scm", [64, 1024]); t_scmb = Tok()
                dma("sp", [], [t_scmb], scm[:NSS * 3, :], sconv[l, :, :])
                for c in range(8):
                    ps, tp = PS()
                    op("pe", [t_scmb, t_c], [tp], lambda p: p.transpose(out=ps[:, :NSS * 3], in_=scm[:NSS * 3, c * 128:(c + 1) * 128], identity=ident[:NSS * 3, :NSS * 3]))
                    op("act", [tp], [t_xf], lambda a: a.activation(out=xfull[:, c, :, 0:3], in_=ps[:, :NSS * 3].rearrange("p (b j) -> p b j", j=3), func=AF.Copy))
                    op("dve", [t_xin], [t_xf], lambda v: v.tensor_copy(out=xfull[:, c, :, 3:7], in_=xin[:, c, :].rearrange("p (b t) -> p b t", t=4)))
                    ca = cacc[:, :NS].rearrange("p (b t) -> p b t", t=4)
                    op("dve", [t_xf, t_lay], [t_cacc], lambda v: v.tensor_scalar(out=ca, in0=xfull[:, c, :, 0:4], scalar1=lay["cw"][:, c:c + 1], scalar2=None, op0=ALU.mult))
                    for j in range(1, 4):
                        op("dve", [t_xf, t_lay, t_cacc], [t_cacc], lambda v: v.scalar_tensor_tensor(out=ca, in0=xfull[:, c, :, j:j + 4], scalar=lay["cw"][:, j * 8 + c:j * 8 + c + 1], in1=ca, op0=ALU.mult, op1=ALU.add))
                    op("act", [t_cacc, t_lay], [t_xc], lambda a: a.activation(out=xc[:, c, :], in_=cacc[:, :NS], func=AF.Silu, bias=lay["cb"][:, c:c + 1], scale=1.0))
                cso = SB(sc, "cso", [64, 1024]); t_cso = Tok()
                cst = SB(sc, "cst", [128, 8, NSS * 3]); t_cst = Tok()
                op("dve", [t_xf], [t_cst], lambda v: v.tensor_copy(out=cst[:].rearrange("p c (b j) -> p c b j", j=3), in_=xfull[:, :, :, 4:7]))
                for a_ in range(2):
                    ps, tp = PS()
                    for j in range(4):
                        op("pe", [t_cst, t_c], [tp], lambda p: p.transpose(out=ps[:NSS * 3, j * 128:(j + 1) * 128], in_=cst[:, a_ * 4 + j, :], identity=ident[:]))
                    op("act", [tp], [t_cso], lambda a: a.activation(out=cso[:NSS * 3, a_ * 512:(a_ + 1) * 512], in_=ps[:NSS * 3, :], func=AF.Copy))
                dma("sp", [t_cso], [], oconv_s[l, :, :], cso[:NSS * 3, :])

            chk('m_conv')
            Ut = Ubd if sample else Um; SLt = SLbd if sample else SLm
            P = NTT
            st = {}
            for nm, shp, dt in [("dt", [128, 8], F32), ("dta", [128, 8], F32), ("t8", [128, 8], F32), ("ecum", [128, 8], F32), ("etot", [128, 8], F32),
                                ("dend", [128, 8], F32), ("w2s", [128, 8], F32), ("DL", [128, 8, 128], F32), ("LT", [128, 8, 128], F32),
                                ("GM", [128, 2, 128], F32), ("MT", [128, 8, 128], BF16), ("xdt", [128, 512], BF16), ("xdd", [128, 512], BF16),
                                ("Btm", [128, 256], BF16), ("y1", [128, 512], F32), ("sz", [128, 512], F32), ("ysq", [128, 512], F32), ("xsk", [128, 512], F32), ("xtok", [128, 512], F32),
                                ("ss2", [128, 2], F32), ("ytm", [128, 512], BF16), ("pe0", [128, 512], F32), ("pe1", [128, 512], F32),
                                ("PT0", [128, 512], BF16), ("PT1", [128, 512], BF16), ("den", [64, 512], F32)]:
                st[nm] = (SB(sc, "st_" + nm, shp, dt), Tok())
            if sample:
                Snat = SB(sc, "Snat", [128, 8, 4, 128]); t_Sn = Tok()
                Sb1 = [SB(sc, "Sb1_%d" % i, [128, 4, 128], BF16) for i in range(2)]; t_Sb1 = [Tok(), Tok()]
                STb1 = [SB(sc, "STb1_%d" % i, [128, 512], BF16) for i in range(2)]; t_ST1 = [Tok(), Tok()]
                CTm = SB(sc, "CTm", [128, NSS, 2, 64], BF16); t_CT = Tok()
                xdm = [SB(sc, "xdm%d" % i, [64, 512], BF16) for i in range(2)]; t_xdm = [Tok(), Tok()]
                dtaE = SB(sc, "dtaE", [64, 512]); t_dE = Tok()
                cdT = SB(sc, "cdT", [128, 4, 16]); t_cd = Tok()
                Snew = [SB(sc, "Snew%d" % i, [128, 4, 128]) for i in range(2)]; t_Snew = [Tok(), Tok()]
                kcb = SB(sc, "kcb", [128, NSS, 128], BF16); vcb = SB(sc, "vcb", [128, NSS, 128], BF16); t_kcb = Tok(); t_vcb = Tok()
                KcT = SB(sc, "KcT", [128, NSS, 128], BF16); t_KcT = Tok()
                PTc = SB(sc, "PTc", [128, 512], BF16); t_PTc = Tok()
                for b4 in range(0, NSS, 4):
                    dma("pool", [], [t_kcb], kcb[:, b4:b4 + 4, :], ck[l, b4:b4 + 4, :, :].rearrange("b i c -> i b c"))
                    dma("pool", [], [t_vcb], vcb[:, b4:b4 + 4, :], cv[l, b4:b4 + 4, :, :].rearrange("b i c -> i b c"))
                for b4 in range(0, NSS, 4):
                    dma("sp", [], [], ock_s[l, b4:b4 + 4, 0:124, :], ck[l, b4:b4 + 4, 4:128, :])
                    dma("sp", [], [], ocv_s[l, b4:b4 + 4, 0:124, :], cv[l, b4:b4 + 4, 4:128, :])

            A = lambda nm: st[nm][0]
            K_ = lambda nm: st[nm][1]
            PSH = lambda: PS(hold=not sample)

            def RELH(tok):
                if not sample:
                    REL(tok)

            def tile_ssd(ti):
                c0 = ti * NTT
                cs = slice(c0, c0 + P)
                first_tile = first_group and ti == 0 and not sample
                pz, tpz = PSH(); pdv, tpdv = PSH()
                with cx.fast():
                    for k in range(8):
                        op("pe", T + [t_xn[k]], [tpz], lambda p: p.matmul(pz[:P, :], lhsT=xn[:, k, cs], rhs=wz[:, k, :], start=(k == 0), stop=(k == 7)))
                    for k in range(8):
                        op("pe", T + [t_xn[k]], [tpdv], lambda p: p.matmul(pdv[:P, 0:128], lhsT=xn[:, k, cs], rhs=wv[:, k, :], start=(k == 0), stop=(k == 7)))
                    for k in range(8):
                        op("pe", T + [t_xn[k]], [tpdv], lambda p: p.matmul(pdv[:P, 128:136], lhsT=xn[:, k, cs], rhs=wdt[:, k, :], start=(k == 0), stop=(k == 7)))
                op("act", [tpz], [st["sz"][1]], lambda a: a.activation(out=st["sz"][0][:P, :], in_=pz[:P, :], func=AF.Silu))
                RELH(tpz)
                op("act", [tpdv], [t_vb], lambda a: a.activation(out=vb[:P, ti + 1, :], in_=pdv[:P, 0:128], func=AF.Copy))
                need_vf = sample or (last_group and ti == ntile - 1)
                if need_vf:
                    op("act", [tpdv], [t_vf], lambda a: a.activation(out=vf[:P, :], in_=pdv[:P, 0:128], func=AF.Copy))
                chk('t_zdv')
                op("dve", [tpdv, t_lay], [K_("t8")], lambda v: v.tensor_tensor(out=A("t8")[:P, :], in0=pdv[:P, 128:136], in1=lay["dtb"][:P, :], op=ALU.add))
                RELH(tpdv)
                yield
                op("act", [K_("t8")], [K_("t8")], lambda a: a.activation(out=A("t8")[:P, :], in_=A("t8")[:P, :], func=AF.Exp))
                op("act", [K_("t8")], [K_("dt")], lambda a: a.activation(out=A("dt")[:P, :], in_=A("t8")[:P, :], func=AF.Ln, bias=1.0, scale=1.0))
                op("dve", [K_("dt"), t_lay], [K_("dta")], lambda v: v.tensor_tensor(out=A("dta")[:P, :], in0=A("dt")[:P, :], in1=lay["aneg"][:P, :], op=ALU.mult))
                op("dve", [K_("dta"), t_c], [K_("DL")], lambda v: v.tensor_tensor(out=A("DL")[:P, :, :P], in0=SLt[:P, :P].unsqueeze(1).broadcast_to([P, 8, P]), in1=A("dta")[:P, :].unsqueeze(2).broadcast_to([P, 8, P]), op=ALU.mult))
                yield
                pD0, tD0 = PSH(); pD1, tD1 = PSH(); pc, tpc = PSH()
                chk('t_dt')
                for hd in range(8):
                    pd_, td_ = (pD0, tD0) if hd < 4 else (pD1, tD1)
                    op("pe", [K_("DL"), t_c], [td_], lambda p: p.matmul(pd_[:P, (hd % 4) * 128:(hd % 4) * 128 + P], lhsT=A("DL")[:P, hd, :P], rhs=Ut[:P, :P], start=True, stop=True))
                op("pe", [K_("dta"), t_c], [tpc], lambda p: p.matmul(pc[:P, 0:8], lhsT=Ut[:P, :P], rhs=A("dta")[:P, :], start=True, stop=True))
                if not sample:
                    op("pe", [K_("dta"), t_c], [tpc], lambda p: p.matmul(pc[:, 8:16], lhsT=onesf[:, :], rhs=A("dta")[:, :], start=True, stop=True))
                else:
                    pass
                for hf, (pd_, td_) in enumerate(((pD0, tD0), (pD1, tD1))):
                    op("act", [td_], [K_("LT")], lambda a: a.activation(out=A("LT")[:P, hf * 4:(hf + 1) * 4, :P], in_=pd_[:P, :].rearrange("p (h t) -> p h t", h=4)[:, :, :P], func=AF.Exp))
                chk('u_LT')
                op("act", [tpc], [K_("ecum")], lambda a: a.activation(out=A("ecum")[:P, :], in_=pc[:P, 0:8], func=AF.Exp))
                RELH(tD0); RELH(tD1)
                chk('u_ecum')
                chk('t_D')
                pG, tpG = PSH()
                with cx.fast():
                    for g in range(2):
                        op("pe", [t_xc], [tpG], lambda p: p.matmul(pG[:P, g * 128:g * 128 + P], lhsT=xc[:, 4 + g, cs], rhs=xc[:, 6 + g, cs], start=True, stop=True))
                    chk('u_pG')
                op("dve", [tpG, t_c], [K_("GM")], lambda v: v.tensor_tensor(out=A("GM")[:P, :, :P], in0=pG[:P, 0:256].rearrange("p (g t) -> p g t", g=2)[:, :, :P], in1=Ut[:P, :P].unsqueeze(1).broadcast_to([P, 2, P]), op=ALU.mult))
                RELH(tpG)
                chk('u_GM')
                for g in range(2):
                    op("dve", [K_("GM"), K_("LT")], [K_("MT")], lambda v: v.tensor_tensor(out=A("MT")[:P, g * 4:(g + 1) * 4, :P], in0=A("LT")[:P, g * 4:(g + 1) * 4, :P], in1=A("GM")[:P, g, :P].unsqueeze(1).broadcast_to([P, 4, P]), op=ALU.mult))
                chk('t_G')
                px, tpx = PSH(); pxb = px[:].bitcast(BF16)
                with cx.fast():
                    for j in range(4):
                        op("pe", [t_xc, t_c], [tpx], lambda p: p.transpose(out=pxb[:P, j * 128:(j + 1) * 128], in_=xc[:, j, cs], identity=identb[:]))
                    for g in range(2):
                        op("pe", [t_xc, t_c], [tpx], lambda p: p.transpose(out=pxb[:P, 512 + g * 128:512 + (g + 1) * 128], in_=xc[:, 4 + g, cs], identity=identb[:]))
                    chk('v_tr')
                op("act", [tpx], [K_("Btm")], lambda a: a.activation(out=A("Btm")[:P, :], in_=pxb[:P, 512:768], func=AF.Copy))
                op("act", [tpx], [K_("xtok")], lambda a: a.activation(out=A("xtok")[:P, :], in_=pxb[:P, 0:512], func=AF.Copy))
                RELH(tpx)
                chk('v_Btm')
                op("dve", [K_("xtok"), K_("dt")], [K_("xdt")], lambda v: v.tensor_tensor(out=A("xdt")[:P, :].rearrange("p (h d) -> p h d", h=8), in0=A("xtok")[:P, :].rearrange("p (h d) -> p h d", h=8), in1=A("dt")[:P, :].unsqueeze(2).broadcast_to([P, 8, 64]), op=ALU.mult))
                chk('v_xdt')
                op("dve", [K_("xtok"), t_lay], [K_("xsk")], lambda v: v.tensor_tensor(out=A("xsk")[:P, :].rearrange("p (h d) -> p h d", h=8), in0=A("xtok")[:P, :].rearrange("p (h d) -> p h d", h=8), in1=lay["dsk"][:P, :].unsqueeze(2).broadcast_to([P, 8, 64]), op=ALU.mult))
                yield
                chk('t_tr')
                py, tpy = PS(hold=True)
                with cx.fast():
                    for hd in range(8):
                        op("pe", [K_("MT"), K_("xdt")], [tpy], lambda p: p.matmul(py[:P, hd * 64:(hd + 1) * 64], lhsT=A("MT")[:P, hd, :P], rhs=A("xdt")[:P, hd * 64:(hd + 1) * 64], start=True, stop=True))
                pyo, tpyo = PS(hold=True)
                if sample:
                    pyo2, tpyo2 = PS(hold=True)
                if not sample:
                    op("act", [tpc], [K_("w2s")], lambda a: a.activation(out=A("w2s")[:, :], in_=pc[:, 0:8], func=AF.Copy))
                    op("dve", [tpc, K_("w2s")], [K_("t8")], lambda v: v.tensor_tensor(out=A("t8")[:, :], in0=pc[:, 8:16], in1=A("w2s")[:, :], op=ALU.subtract))
                    op("act", [K_("t8")], [K_("dend")], lambda a: a.activation(out=A("dend")[:, :], in_=A("t8")[:, :], func=AF.Exp))
                    op("act", [tpc], [K_("etot")], lambda a: a.activation(out=A("etot")[:, :], in_=pc[:, 8:16], func=AF.Exp))
                    RELH(tpc)
                    op("dve", [K_("dend"), K_("dt")], [K_("w2s")], lambda v: v.tensor_tensor(out=A("w2s")[:, :], in0=A("dend")[:, :], in1=A("dt")[:, :], op=ALU.mult))
                    op("dve", [K_("xtok"), K_("w2s")], [K_("xdd")], lambda v: v.tensor_tensor(out=A("xdd")[:, :].rearrange("p (h d) -> p h d", h=8), in0=A("xtok")[:, :].rearrange("p (h d) -> p h d", h=8), in1=A("w2s")[:, :].unsqueeze(2).broadcast_to([128, 8, 64]), op=ALU.mult))
                    for g in range(2):
                        op("pe", [t_xc, t_hTb], [tpyo], lambda p: p.matmul(pyo[:, g * 256:(g + 1) * 256], lhsT=xc[:, 6 + g, cs], rhs=hTb[:, g * 256:(g + 1) * 256], start=True, stop=True))
                    pst, tpst = PSH()
                    for g in range(2):
                        op("pe", [K_("Btm"), K_("xdd")], [tpst], lambda p: p.matmul(pst[:, g * 256:(g + 1) * 256], lhsT=A("Btm")[:, g * 128:(g + 1) * 128], rhs=A("xdd")[:, g * 256:(g + 1) * 256], start=True, stop=True))
                    op("dve", [t_hT, K_("etot")], [t_hT], lambda v: v.tensor_tensor(out=hT[:].rearrange("p (h d) -> p h d", h=8), in0=hT[:].rearrange("p (h d) -> p h d", h=8), in1=A("etot")[:, :].unsqueeze(2).broadcast_to([128, 8, 64]), op=ALU.mult))
                    op("dve", [t_hT, tpst], [t_hT], lambda v: v.tensor_tensor(out=hT[:], in0=hT[:], in1=pst[:, :], op=ALU.add))
                    RELH(tpst)
                    op("act", [t_hT], [t_hTb], lambda a: a.activation(out=hTb[:], in_=hT[:], func=AF.Copy))
                else:
                    op("pe", [K_("dta"), t_c], [tpc], lambda p: p.matmul(pc[:P, 8:16], lhsT=Ubd[:P, :P], rhs=A("dta")[:P, :], start=True, stop=False))
                    op("pe", [K_("dta"), t_c], [tpc], lambda p: p.matmul(pc[:P, 8:16], lhsT=SLbd[:P, :P], rhs=A("dta")[:P, :], start=False, stop=True))
                    op("act", [tpc], [K_("w2s")], lambda a: a.activation(out=A("w2s")[:P, :], in_=pc[:P, 0:8], func=AF.Copy))
                    op("dve", [tpc, K_("w2s")], [K_("t8")], lambda v: v.tensor_tensor(out=A("t8")[:P, :], in0=pc[:P, 8:16], in1=A("w2s")[:P, :], op=ALU.subtract))
                    op("act", [K_("t8")], [K_("dend")], lambda a: a.activation(out=A("dend")[:P, :], in_=A("t8")[:P, :], func=AF.Exp))
                    op("dve", [K_("dend"), K_("dt")], [K_("w2s")], lambda v: v.tensor_tensor(out=A("w2s")[:P, :], in0=A("dend")[:P, :], in1=A("dt")[:P, :], op=ALU.mult))
                    op("dve", [K_("xtok"), K_("w2s")], [K_("xdd")], lambda v: v.tensor_tensor(out=A("xdd")[:P, :].rearrange("p (h d) -> p h d", h=8), in0=A("xtok")[:P, :].rearrange("p (h d) -> p h d", h=8), in1=A("w2s")[:P, :].unsqueeze(2).broadcast_to([P, 8, 64]), op=ALU.mult))
                    for g in range(2):
                        op("dve", [t_xc, t_c], [t_CT], lambda v: v.tensor_tensor(out=CTm[:, :, g, :], in0=xc[:, 6 + g, :].unsqueeze(1).broadcast_to([128, NSS, 64]), in1=BMb[:].rearrange("p (b t) -> p b t", b=16)[:, :NSS, :], op=ALU.mult))
                    op("dve", [K_("dta")], [t_dE], lambda v: v.tensor_copy(out=dtaE[:].rearrange("p (h d) -> p h d", h=8), in_=A("dta")[:P, :].unsqueeze(2).broadcast_to([P, 8, 64])))
                    pcd, tpcd = PS()
                    for j in range(4):
                        op("pe", [t_dE, t_c], [tpcd], lambda p: p.matmul(pcd[:, j * 16:(j + 1) * 16], lhsT=dtaE[:, j * 128:(j + 1) * 128], rhs=BMt[:, :], start=True, stop=True))
                    op("act", [tpcd], [t_cd], lambda a: a.activation(out=cdT[:].rearrange("p j b -> p (j b)"), in_=pcd[:, 0:64], func=AF.Exp))
                    for b in range(NSS):
                        bl = b % 8
                        if bl == 0:
                            for b8 in range(8):
                                dma("sp", [], [t_Sn], Snat[:, b8, :, :], sssm[l, b + b8, :, :].rearrange("(j p) n -> p j n", p=128))
                        sb1, tsb1 = Sb1[b % 2], t_Sb1[b % 2]
                        stb, tstb = STb1[b % 2], t_ST1[b % 2]
                        op("act", [t_Sn], [tsb1], lambda a: a.activation(out=sb1[:], in_=Snat[:, bl, :, :], func=AF.Copy))
                        pt_, tpt = PS(); ptb = pt_[:].bitcast(BF16)
                        for j in range(4):
                            op("pe", [tsb1, t_c], [tpt], lambda p: p.transpose(out=ptb[:, j * 128:(j + 1) * 128], in_=sb1[:, j, :], identity=identb[:]))
                        op("act", [tpt], [tstb], lambda a: a.activation(out=stb[:, :], in_=ptb[:, 0:512], func=AF.Copy))
                        for g in range(2):
                            pq_, tq_ = (pyo, tpyo) if g == 0 else (pyo2, tpyo2)
                            op("pe", [t_CT, tstb], [tq_], lambda p: p.matmul(pq_[:P, 0:256], lhsT=CTm[:, b, g, :], rhs=stb[:, g * 256:(g + 1) * 256], start=(b == 0), stop=(b == NSS - 1)))
                        xm, txm = xdm[b % 2], t_xdm[b % 2]
                        op("dve", [K_("xdd"), t_c], [txm], lambda v: v.tensor_scalar(out=xm[:, :], in0=A("xdd")[:P, :], scalar1=BMt[:, b:b + 1], scalar2=None, op0=ALU.mult))
                        pst, tpst = PS()
                        for j in range(4):
                            op("pe", [txm, K_("Btm")], [tpst], lambda p: p.matmul(pst[:, j * 128:(j + 1) * 128], lhsT=xm[:, j * 128:(j + 1) * 128], rhs=A("Btm")[:P, (j // 2) * 128:(j // 2 + 1) * 128], start=True, stop=True))
                        sn, tsn = Snew[b % 2], t_Snew[b % 2]
                        for j in range(4):
                            op("dve", [t_Sn, t_cd, tpst], [tsn], lambda v: v.scalar_tensor_tensor(out=sn[:, j, :], in0=Snat[:, bl, j, :], scalar=cdT[:, j, b:b + 1], in1=pst[:, j * 128:(j + 1) * 128], op0=ALU.mult, op1=ALU.add))
                        dma("sp", [tsn], [], ossm_s[l, b, :, :].rearrange("(j p) n -> p j n", p=128), sn[:])
                yield
                chk('t_ssd')
                if not sample:
                    op("dve", [tpyo, K_("ecum")], [K_("y1")], lambda v: v.tensor_tensor(out=A("y1")[:P, :].rearrange("p (h d) -> p h d", h=8), in0=pyo[:P, :].rearrange("p (h d) -> p h d", h=8), in1=A("ecum")[:P, :].unsqueeze(2).broadcast_to([P, 8, 64]), op=ALU.mult))
                else:
                    for g, (pq_, tq_) in enumerate(((pyo, tpyo), (pyo2, tpyo2))):
                        op("dve", [tq_, K_("ecum")], [K_("y1")], lambda v: v.tensor_tensor(out=A("y1")[:P, g * 256:(g + 1) * 256].rearrange("p (h d) -> p h d", h=4), in0=pq_[:P, 0:256].rearrange("p (h d) -> p h d", h=4), in1=A("ecum")[:P, g * 4:(g + 1) * 4].unsqueeze(2).broadcast_to([P, 4, 64]), op=ALU.mult))
                op("dve", [K_("y1"), tpy], [K_("y1")], lambda v: v.tensor_tensor(out=A("y1")[:P, :], in0=A("y1")[:P, :], in1=py[:P, :], op=ALU.add))
                op("dve", [K_("y1"), K_("xsk")], [K_("y1")], lambda v: v.tensor_tensor(out=A("y1")[:P, :], in0=A("y1")[:P, :], in1=A("xsk")[:P, :], op=ALU.add))
                REL(tpy); REL(tpyo)
                if sample:
                    REL(tpyo2)
                op("dve", [K_("y1"), K_("sz")], [K_("y1")], lambda v: v.tensor_tensor(out=A("y1")[:P, :], in0=A("y1")[:P, :], in1=A("sz")[:P, :], op=ALU.mult))
                op("dve", [K_("y1")], [K_("ysq")], lambda v: v.tensor_tensor(out=A("ysq")[:P, :], in0=A("y1")[:P, :], in1=A("y1")[:P, :], op=ALU.mult))
                op("dve", [K_("ysq")], [K_("ss2")], lambda v: v.reduce_sum(out=A("ss2")[:P, :], in_=A("ysq")[:P, :].rearrange("p (g d) -> p g d", g=2), axis=AX.X))
                op("act", [K_("ss2")], [K_("ss2")], lambda a: a.activation(out=A("ss2")[:P, :], in_=A("ss2")[:P, :], func=AF.Ln, bias=256.0 * EPS, scale=1.0))
                op("act", [K_("ss2")], [K_("ss2")], lambda a: a.activation(out=A("ss2")[:P, :], in_=A("ss2")[:P, :], func=AF.Exp, scale=-0.5))
                for g2 in range(2):
                    op("dve", [K_("y1"), K_("ss2"), t_lay], [K_("ytm")], lambda v: v.scalar_tensor_tensor(out=A("ytm")[:P, g2 * 256:(g2 + 1) * 256], in0=A("y1")[:P, g2 * 256:(g2 + 1) * 256], scalar=A("ss2")[:P, g2:g2 + 1], in1=gssm[:P, g2 * 256:(g2 + 1) * 256], op0=ALU.mult, op1=ALU.mult))
                yield
                pyt, tpyt = PSH(); pytb = pyt[:].bitcast(BF16)
                with cx.fast():
                    for j in range(4):
                        op("pe", [K_("ytm"), t_c], [tpyt], lambda p: p.transpose(out=pytb[:, j * 128:j * 128 + P], in_=A("ytm")[:P, j * 128:(j + 1) * 128], identity=identb[:P, :P]))
                op("act", [tpyt], [t_mixT], lambda a: a.activation(out=mixT[:, :, cs], in_=pytb[:, 0:512].rearrange("p (j t) -> p j t", j=4)[:, :, :P], func=AF.Copy))
                RELH(tpyt)

            def tile_attn(ti):
                c0 = ti * NTT
                cs = slice(c0, c0 + P)
                first_tile = first_group and ti == 0 and not sample
                chk('t_y')
                if not sample:
                    for g in range(2):
                        bs = slice(64 * g, 64 * g + 64)
                        kbs = [1] if first_tile else [0, 1]
                        for kb in kbs:
                            kcols = slice(c0 + kb * 128, c0 + kb * 128 + 128)
                            pS, tpS = PSH()
                            for hh in range(4):
                                op("pe", [t_kn, t_qn], [tpS], lambda p: p.matmul(pS[:, hh * 128:(hh + 1) * 128], lhsT=kn[bs, kcols], rhs=qn[bs, hh, cs], start=True, stop=True))
                            pe_, tpe = st["pe%d" % kb]; PT_, tPT = st["PT%d" % kb]
                            op("act", [tpS], [tpe], lambda a: a.activation(out=pe_[:], in_=pS[:], func=AF.Exp, scale=0.125))
                            RELH(tpS)
                            Et = Eprev if kb == 0 else Ecur
                            op("dve", [tpe, t_c], [tPT], lambda v: v.tensor_tensor(out=PT_[:], in0=pe_[:], in1=Et[:, g * 512:(g + 1) * 512], op=ALU.mult))
                        chk('a_S')
                        yield
                        po, tpo = PSH(); pdn, tpdn = PSH()
                        for ii, kb in enumerate(kbs):
                            PT_, tPT = st["PT%d" % kb]
                            op("pe", [t_vb, tPT], [tpo], lambda p: p.matmul(po[:64, :], lhsT=vb[:, ti + kb, g * 64:(g + 1) * 64], rhs=PT_[:], start=(ii == 0), stop=(ii == len(kbs) - 1)))
                        chk('a_po')
                        for ii, kb in enumerate(kbs):
                            PT_, tPT = st["PT%d" % kb]
                            op("pe", [t_c, tPT], [tpdn], lambda p: p.matmul(pdn[:64, :], lhsT=onesb[:, 0:64], rhs=PT_[:], start=(ii == 0), stop=(ii == len(kbs) - 1)))
                        chk('a_pdn')
                        op("dve", [tpdn, t_lay], [K_("den")], lambda v: v.tensor_tensor(out=A("den")[:, :], in0=pdn[:64, :], in1=esinkrow[:, g * 512:(g + 1) * 512], op=ALU.add))
                        RELH(tpdn)
                        op("act", [K_("den")], [K_("den")], lambda a: a.activation(out=A("den")[:, :], in_=A("den")[:, :], func=AF.Ln))
                        op("act", [K_("den")], [K_("den")], lambda a: a.activation(out=A("den")[:, :], in_=A("den")[:, :], func=AF.Exp, scale=-1.0))
                        op("dve", [tpo, K_("den")], [t_oT], lambda v: v.tensor_tensor(out=oT[:, g * 4:(g + 1) * 4, cs], in0=po[:64, :].rearrange("p (h q) -> p h q", h=4), in1=A("den")[:, :].rearrange("p (h q) -> p h q", h=4), op=ALU.mult))
                        RELH(tpo)
                        yield
                else:
                    for b4 in range(0, NSS, 4):
                        pt_, tpt = PS(); ptb = pt_[:].bitcast(BF16)
                        for bb in range(4):
                            op("pe", [t_kcb, t_c], [tpt], lambda p: p.transpose(out=ptb[:, bb * 128:(bb + 1) * 128], in_=kcb[:, b4 + bb, :], identity=identb[:]))
                        op("act", [tpt], [t_KcT], lambda a: a.activation(out=KcT[:, b4:b4 + 4, :], in_=ptb[:, 0:512].rearrange("p (b i) -> p b i", b=4), func=AF.Copy))
                    pSc, tpSc = PS()
                    for b in range(NSS):
                        for g in range(2):
                            bs = slice(64 * g, 64 * g + 64)
                            for hh in range(4):
                                idx = (b * 8 + g * 4 + hh) * 4
                                op("pe", [t_KcT, t_qn], [tpSc], lambda p: p.matmul(pSc[:, idx:idx + 4], lhsT=KcT[bs, b, :], rhs=qn[bs, hh, b * 4:b * 4 + 4], start=True, stop=True))
                    pe_, tpe = st["pe0"]
                    op("act", [tpSc], [tpe], lambda a: a.activation(out=pe_[:, :NSS * 32], in_=pSc[:, :NSS * 32], func=AF.Exp, scale=0.125))
                    op("dve", [tpe, t_c], [t_PTc], lambda v: v.tensor_tensor(out=PTc[:, :NSS * 32].rearrange("p (b x) -> p b x", b=NSS), in0=pe_[:, :NSS * 32].rearrange("p (b x) -> p b x", b=NSS), in1=Ecache[:, :].unsqueeze(1).broadcast_to([128, NSS, 32]), op=ALU.mult))
                    pSn, tpSn = PS()
                    for g in range(2):
                        bs = slice(64 * g, 64 * g + 64)
                        for hh in range(4):
                            op("pe", [t_kn, t_qn], [tpSn], lambda p: p.matmul(pSn[:P, (g * 4 + hh) * 64:(g * 4 + hh) * 64 + P], lhsT=kn[bs, 0:P], rhs=qn[bs, hh, 0:P], start=True, stop=True))
                    pe1_, tpe1 = st["pe1"]; PT1_, tPT1 = st["PT1"]
                    op("act", [tpSn], [tpe1], lambda a: a.activation(out=pe1_[:P, :], in_=pSn[:P, :], func=AF.Exp, scale=0.125))
                    for g in range(2):
                        ov = PT1_[:P, g * 256:(g + 1) * 256].rearrange("p (b hh t) -> p hh b t", b=16, hh=4)
                        i0 = pe1_[:P, g * 256:(g + 1) * 256].rearrange("p (hh b t) -> p hh b t", hh=4, b=16)
                        i1 = Enew[:P, g * 256:(g + 1) * 256].rearrange("p (hh b t) -> p hh b t", hh=4, b=16)
                        op("dve", [tpe1, t_c], [tPT1], lambda v: v.tensor_tensor(out=ov, in0=i0, in1=i1, op=ALU.mult))
                    po, tpo = PS(); pdn, tpdn = PS()
                    for (pp_, tpp, use_v) in ((po, tpo, True), (pdn, tpdn, False)):
                        for g in range(2):
                            lw = vb[:P, 1, g * 64:(g + 1) * 64] if use_v else onesb[:P, 0:64]
                            op("pe", [t_vb, tPT1, t_c], [tpp], lambda p: p.matmul(pp_[:64, g * 256:(g + 1) * 256], lhsT=lw, rhs=PT1_[:P, g * 256:(g + 1) * 256], start=True, stop=False))
                            for b in range(NSS):
                                lw2 = vcb[:, b, g * 64:(g + 1) * 64] if use_v else onesb[:, 0:64]
                                op("pe", [t_vcb, t_PTc, t_c], [tpp], lambda p: p.matmul(pp_[:64, g * 256 + b * 16:g * 256 + b * 16 + 16], lhsT=lw2, rhs=PTc[:, b * 32 + g * 16:b * 32 + g * 16 + 16], start=False, stop=(b == NSS - 1)))
                    for g in range(2):
                        dv = A("den")[:, g * 256:(g + 1) * 256].rearrange("p (b hh t) -> p hh b t", b=16, hh=4)
                        ek = esinkrow[:, :].rearrange("p (h q) -> p h q", h=8)[:, g * 4:(g + 1) * 4, 0:64].rearrange("p hh (b t) -> p hh b t", t=4)
                        op("dve", [tpdn, t_lay], [K_("den")], lambda v: v.tensor_tensor(out=dv, in0=pdn[:64, g * 256:(g + 1) * 256].rearrange("p (b hh t) -> p hh b t", b=16, hh=4), in1=ek, op=ALU.add))
                        op("dve", [K_("den")], [K_("den")], lambda v: v.reciprocal(out=A("den")[:, g * 256:(g + 1) * 256], in_=A("den")[:, g * 256:(g + 1) * 256]))
                        op("dve", [tpo, K_("den")], [t_oT], lambda v: v.tensor_tensor(out=oT[:, g * 4:(g + 1) * 4, 0:64].rearrange("p hh (b t) -> p hh b t", t=4), in0=po[:64, g * 256:(g + 1) * 256].rearrange("p (b hh t) -> p hh b t", b=16, hh=4), in1=dv, op=ALU.mult))

                yield

            for ti in range(ntile):
                if sample:
                    for _ in tile_ssd(ti):
                        pass
                    for _ in tile_attn(ti):
                        pass
                else:
                    gens = [tile_ssd(ti), tile_attn(ti)]
                    while gens:
                        for g_ in list(gens):
                            try:
                                next(g_)
                            except StopIteration:
                                gens.remove(g_)
            chk('m_tiles')
            if sample or last_group:
                ktm = SB(sc, "ktm", [128, 128]); t_ktm = Tok()
                pk, tpk = PS()
                lastc = slice(NT - P, NT)
                op("pe", [t_knf, t_c], [tpk], lambda p: p.transpose(out=pk[:P, 0:128], in_=knf[:, lastc], identity=ident[:]))
                op("act", [tpk], [t_ktm], lambda a: a.activation(out=ktm[:P, :], in_=pk[:P, 0:128], func=AF.Copy))
                if sample:
                    for b in range(NSS):
                        dma("sp", [t_ktm], [], ock_s[l, b, 124:128, :], ktm[b * 4:(b + 1) * 4, :])
                        dma("sp", [t_vf], [], ocv_s[l, b, 124:128, :], vf[b * 4:(b + 1) * 4, :])
                else:
                    dma("sp", [t_ktm], [], ock_p[l, :, :], ktm[:, :])
                    dma("sp", [t_vf], [], ocv_p[l, :, :], vf[:, :])
                    hTo = SB(sc, "hTo", [128, 4, 128]); t_hTo = Tok()
                    ph, tph = PS()
                    for j in range(4):
                        op("pe", [t_hT, t_c], [tph], lambda p: p.transpose(out=ph[:, j * 128:(j + 1) * 128], in_=hT[:, j * 128:(j + 1) * 128], identity=ident[:]))
                    op("act", [tph], [t_hTo], lambda a: a.activation(out=hTo[:].rearrange("p j n -> p (j n)"), in_=ph[:, :], func=AF.Copy))
                    dma("sp", [t_hTo], [], ossm_p[l, :, :].rearrange("(j p) n -> p j n", p=128), hTo[:])
            if not sample:
                op("dve", [t_kn], [t_kvh], lambda v: v.tensor_copy(out=khalo[:], in_=kn[:, NT:NT + 128]))
                op("dve", [t_vb], [t_kvh], lambda v: v.tensor_copy(out=vhalo[:], in_=vb[:, ntile, :]))

            chk('m_outs')
            woT = wreg[:, 0:4096].rearrange("p (k c) -> p k c", k=4)
            woA = wreg[:64, 4096:4096 + 8192].rearrange("p (k c) -> p k c", k=8)
            if cached:
                dma("pool", [t_scm[1]], T, wreg[:, 0:4096], wscm[1, :, 0:4096])
                dma("pool", [t_scm[1]], T, wreg[:64, 4096:12288], wscm[1, :64, 4096:12288])
            else:
                dma("pool", [], T, woT, w_out[l, 0:512, :].rearrange("(k p) c -> p k c", p=128))
                dma("pool", [], T, woA, w_out[l, 512:1024, :].rearrange("(hd p) c -> p hd c", p=64))
                dma("sp", T, [t_scm[1]], wscm[1, :, 0:4096], wreg[:, 0:4096])
                dma("sp", T, [t_scm[1]], wscm[1, :64, 4096:12288], wreg[:64, 4096:12288])
            with cx.fast():
                for d in range(8):
                    for (n0, nw) in nblocks(NT):
                        ps, tp = PS()
                        for k in range(4):
                            op("pe", T + [t_mixT], [tp], lambda p: p.matmul(ps[:, :nw], lhsT=woT[:, k, d * 128:(d + 1) * 128], rhs=mixT[:, k, n0:n0 + nw], start=(k == 0), stop=False))
                        for hd in range(8):
                            op("pe", T + [t_oT], [tp], lambda p: p.matmul(ps[:, :nw], lhsT=woA[:, hd, d * 128:(d + 1) * 128], rhs=oT[:, hd, n0:n0 + nw], start=False, stop=(hd == 7)))
                        op("dve", [tp, t_h], [t_h], lambda v: v.tensor_tensor(out=h[:, d, n0:n0 + nw], in0=h[:, d, n0:n0 + nw], in1=ps[:, :nw], op=ALU.add))

        def ple(sc, h, t_h, NT, l, psrc, ntok_tile, cached):
            xn = SB(sc, "xn", [128, 8, NT], BF16); t_xn = [Tok() for _ in range(8)]
            norm(sc, h, t_h, NT, 3, l, xn, t_xn)
            wg = wreg[:, 0:8192].rearrange("p (k c) -> p k c", k=8)
            wp = wreg[:, 8192:8192 + 2048].rearrange("p (k c) -> p k c", k=2)
            T = [t_wreg]
            if cached:
                dma("pool", [t_scm[2]], T, wreg[:, 0:10240], wscm[2, :, 0:10240])
            else:
                dma("pool", [], T, wg, w_pg[l, :, :].rearrange("(k p) c -> p k c", p=128))
                dma("pool", [], T, wp, w_pp[l, :, :].rearrange("(k p) c -> p k c", p=128))
                dma("sp", T, [t_scm[2]], wscm[2, :, 0:10240], wreg[:, 0:10240])
            peT = SB(sc, "peT", [128, 2, NT], BF16); t_peT = Tok()
            ptm = [SB(sc, "ptm%d" % i, [128, 256]) for i in range(2)]; t_ptm = [Tok(), Tok()]
            P = ntok_tile
            for ti in range(NT // P):
                pt_, tpt_ = ptm[ti % 2], t_ptm[ti % 2]
                dma("sp", [], [tpt_], pt_[:P, :], psrc[ti * P:(ti + 1) * P, :])
                ps, tp = PS()
                for j in range(2):
                    op("pe", [tpt_, t_c], [tp], lambda p: p.transpose(out=ps[:, j * 128:j * 128 + P], in_=pt_[:P, j * 128:(j + 1) * 128], identity=ident[:P, :P]))
                op("act", [tp], [t_peT], lambda a: a.activation(out=peT[:, :, ti * P:(ti + 1) * P], in_=ps[:, 0:256].rearrange("p (j t) -> p j t", j=2)[:, :, :P], func=AF.Copy))
            with cx.fast():
                sg = SB(sc, "sg", [128, 512]); t_sg = Tok()
                for d in range(8):
                    for (n0, nw) in nblocks(NT):
                        pg, tpg = PS(); pq, tpq = PS()
                        for k in range(8):
                            op("pe", T + [t_xn[k]], [tpg], lambda p: p.matmul(pg[:, :nw], lhsT=wg[:, k, d * 128:(d + 1) * 128], rhs=xn[:, k, n0:n0 + nw], start=(k == 0), stop=(k == 7)))
                        for k in range(2):
                            op("pe", T + [t_peT], [tpq], lambda p: p.matmul(pq[:, :nw], lhsT=wp[:, k, d * 128:(d + 1) * 128], rhs=peT[:, k, n0:n0 + nw], start=(k == 0), stop=(k == 1)))
                        op("act", [tpg], [t_sg], lambda a: a.activation(out=sg[:, :nw], in_=pg[:, :nw], func=AF.Sigmoid))
                        op("dve", [t_sg, tpq], [t_sg], lambda v: v.tensor_tensor(out=sg[:, :nw], in0=sg[:, :nw], in1=pq[:, :nw], op=ALU.mult))
                        op("dve", [t_sg, t_h], [t_h], lambda v: v.tensor_tensor(out=h[:, d, n0:n0 + nw], in0=h[:, d, n0:n0 + nw], in1=sg[:, :nw], op=ALU.add))

        stage = [0]

        def chk(name):
            stage[0] += 1
            if STAGE_LIMIT is not None and stage[0] >= STAGE_LIMIT:
                if not cx.stopped:
                    print("STOP at stage", stage[0], name)
                cx.stopped = True

        try:
          _main_body = True
          with contextlib.ExitStack() as sc:
              xtm0 = SB(sc, "xtm", [128, 1024])
              transpose_in(xtm0, Tok(), xs[:, :], NS, hs, t_hs, 0)
              cx.barrier()
          chk('sample_load')
          for l in range(DEPTH):
              load_layer_consts(l)
              chk('layer_consts')
              op("dve", [], [t_hT], lambda v: v.memset(hT[:], 0.0))
              op("dve", [], [t_hTb], lambda v: v.memset(hTb[:], 0.0))
              op("dve", [], [t_xhalo], lambda v: v.memset(xhalo[:], 0.0))
              op("dve", [], [t_kvh], lambda v: v.memset(khalo[:], 0.0))
              op("dve", [], [t_kvh], lambda v: v.memset(vhalo[:], 0.0))
              for gi in range(NG + 1):
                  sample = gi == NG
                  NT = NS if sample else NTG
                  with contextlib.ExitStack() as gsc:
                      if sample:
                          h, t_h = hs, t_hs
                      else:
                          h = SB(gsc, "hgrp", [128, 8, NTG]); t_h = Tok()
                          if l == 0:
                              with contextlib.ExitStack() as sc:
                                  xtms = [(SB(sc, "xtm", [128, 1024]), Tok()) for _ in range(2)]
                                  for ti in range(NTG // 128):
                                      transpose_in(xtms[ti % 2][0], xtms[ti % 2][1], xp[gi * NTG + ti * 128:gi * NTG + (ti + 1) * 128, :], 128, h, t_h, ti * 128)
                                  cx.barrier()
                          else:
                              dma("sp", [t_hscr[gi]], [t_h], h[:], hscr[:, :, gi * NTG:(gi + 1) * NTG].rearrange("j p t -> p j t"))
                      cx.barrier()
                      chk('group_load')
                      with contextlib.ExitStack() as sc:
                          xn = SB(sc, "xn", [128, 8, NT], BF16); t_xn = [Tok() for _ in range(8)]
                          with cx.fast():
                              norm(sc, h, t_h, NT, 0, l, xn, t_xn)
                              chk('norm')
                              ffn(sc, h, t_h, NT, xn, t_xn, w1a, w3a, w2a, l, 0, gi > 0)
                          cx.barrier()
                          chk('ffn_a')
                      with contextlib.ExitStack() as sc:
                          mixer(sc, h, t_h, NT, l, sample, gi == 0, gi == NG - 1, gi > 0)
                          cx.barrier()
                          chk('mixer')
                      with contextlib.ExitStack() as sc:
                          xn = SB(sc, "xn", [128, 8, NT], BF16); t_xn = [Tok() for _ in range(8)]
                          with cx.fast():
                              norm(sc, h, t_h, NT, 2, l, xn, t_xn)
                              ffn(sc, h, t_h, NT, xn, t_xn, w1b, w3b, w2b, l, 1, gi > 0)
                          cx.barrier()
                          chk('ffn_b')
                      with contextlib.ExitStack() as sc:
                          if sample:
                              ple(sc, h, t_h, NT, l, psm[l, :, :], 64, gi > 0)
                          else:
                              ple(sc, h, t_h, NT, l, pp[l, gi * NTG:(gi + 1) * NTG, :], 128, gi > 0)
                          cx.barrier()
                      if not sample:
                          if l == 0:
                              dma("sp", [t_h], [t_hscr[gi]], hscr[:, :, gi * NTG:(gi + 1) * NTG].rearrange("j p t -> p j t"), h[:])
                          else:
                              with contextlib.ExitStack() as sc:
                                  ytms = [(SB(sc, "ytm", [128, 1024]), Tok()) for _ in range(2)]
                                  for ti in range(NTG // 128):
                                      transpose_out(ytms[ti % 2][0], ytms[ti % 2][1], h, t_h, ti * 128, 128, yp[gi * NTG + ti * 128:gi * NTG + (ti + 1) * 128, :])
                                  cx.barrier()
                      elif l == DEPTH - 1:
                          with contextlib.ExitStack() as sc:
                              ytm0 = SB(sc, "ytm", [128, 1024])
                              transpose_out(ytm0, Tok(), h, t_h, 0, NS, ys[:, :])
                              cx.barrier()
                      cx.barrier()
        except _Stop:
            pass
        cx.finish()
    return nc


def make_consts():
    c = {}
    c["c_ident"] = np.eye(128, dtype=np.float32)
    i = np.arange(128)
    c["c_U"] = (i[:, None] <= i[None, :]).astype(np.float32)
    c["c_SL"] = (i[:, None] > i[None, :]).astype(np.float32)
    j = np.arange(64); same = (j[:, None] // 4) == (j[None, :] // 4)
    c["c_Ubd"] = (same & (j[:, None] <= j[None, :])).astype(np.float32)
    c["c_SLbd"] = (same & (j[:, None] > j[None, :])).astype(np.float32)
    c["c_BMt"] = ((j[:, None] // 4) == np.arange(16)[None, :]).astype(np.float32)
    bm = ((np.arange(16)[:, None]) == (j[None, :] // 4)).astype(np.float32)
    c["c_BM"] = np.broadcast_to(bm.reshape(1, 16 * 64), (128, 16 * 64)).copy()
    bo = np.zeros((128, 128), np.float32); bo[:64, :64] = 1; bo[64:, 64:] = 1
    c["c_bones"] = bo
    slopes = np.power(np.float32(2.0), -8.0 * np.arange(1, 9, dtype=np.float32) / 8).astype(np.float32)
    s = i[:, None, None]; q = i[None, None, :]; sl = slopes[None, :, None]
    ecur = np.where(q >= s, np.exp(-sl * (q - s).astype(np.float32)), 0.0)
    eprev = np.where(s > q, np.exp(-sl * (q - s + 128).astype(np.float32)), 0.0)
    c["c_Ecur"] = ecur.astype(np.float32).reshape(128, 1024)
    c["c_Eprev"] = eprev.astype(np.float32).reshape(128, 1024)
    t = np.arange(4)[None, None, :]
    ecache = np.where(s > t, np.exp(-sl * (128 + t - s).astype(np.float32)), 0.0)
    c["c_Ecache"] = ecache.astype(np.float32).reshape(128, 32)
    sj = j[:, None, None]; qj = j[None, None, :]
    enew = np.where(((sj // 4) == (qj // 4)) & (sj <= qj), np.exp(-sl * (qj - sj).astype(np.float32)), 0.0)
    c["c_Enew"] = enew.astype(np.float32).reshape(64, 512)
    return c


_WNAMES = ["g_ffn1", "w1_a", "w3_a", "w2_a", "g_mix", "w_in", "conv_w", "conv_b", "dt_bias", "a_log", "d_skip", "ssm_norm",
           "q_norm", "k_norm", "sinks", "w_out", "g_ffn2", "w1_b", "w3_b", "w2_b", "g_ple", "w_ple_gate", "w_ple_proj"]


def run(inputs, SEQ, NSS, NTG, n_prompt, ncores):
    f = lambda a: np.ascontiguousarray(np.asarray(a, dtype=np.float32))
    nc = build(SEQ, NSS, NTG)
    consts = make_consts()
    wts = {n: f(inputs[n]) for n in _WNAMES}
    xpr = f(inputs["x_prompt"]); ppr = f(inputs["p_prompt"]); xsm = f(inputs["x_sample"]); psm = f(inputs["p_sample"])
    sssm = f(inputs["state_ssm"]); sconv = f(inputs["state_conv"]); ck = f(inputs["cache_k_win"]); cv = f(inputs["cache_v_win"])
    in_maps = []
    for c in range(ncores):
        b = c % n_prompt
        bs = slice(c * NSS, (c + 1) * NSS)
        m = dict(wts); m.update(consts)
        m["xp"] = f(xpr[b]); m["pp"] = f(ppr[:, b])
        m["xs"] = f(xsm[bs].reshape(NSS * 4, D)); m["psm"] = f(psm[:, bs].reshape(DEPTH, NSS * 4, DPLE))
        m["sssm"] = f(sssm[:, bs].reshape(DEPTH, NSS, 512, 128)); m["sconv"] = f(sconv[:, bs].reshape(DEPTH, NSS * 3, 1024))
        m["ck"] = f(ck[:, bs].reshape(DEPTH, NSS, 128, 128)); m["cv"] = f(cv[:, bs].reshape(DEPTH, NSS, 128, 128))
        in_maps.append(m)
    res = run_bass_kernel_spmd(nc, in_maps, core_ids=list(range(ncores))).results
    P = n_prompt
    y_p = np.stack([res[b]["yp"] for b in range(P)])
    y_s = np.concatenate([res[c]["ys"].reshape(NSS, 4, D) for c in range(ncores)])
    ssm_p = np.stack([res[b]["ossm_p"].reshape(DEPTH, 8, 64, 128) for b in range(P)], axis=1)
    conv_p = np.stack([res[b]["oconv_p"] for b in range(P)], axis=1)
    k_p = np.stack([res[b]["ock_p"].reshape(DEPTH, 128, 2, 64) for b in range(P)], axis=1)
    v_p = np.stack([res[b]["ocv_p"].reshape(DEPTH, 128, 2, 64) for b in range(P)], axis=1)
    ssm_s = np.concatenate([res[c]["ossm_s"].reshape(DEPTH, NSS, 8, 64, 128) for c in range(ncores)], axis=1)
    conv_s = np.concatenate([res[c]["oconv_s"].reshape(DEPTH, NSS, 3, 1024) for c in range(ncores)], axis=1)
    k_s = np.concatenate([res[c]["ock_s"].reshape(DEPTH, NSS, 128, 2, 64) for c in range(ncores)], axis=1)
    v_s = np.concatenate([res[c]["ocv_s"].reshape(DEPTH, NSS, 128, 2, 64) for c in range(ncores)], axis=1)
    return tuple(np.ascontiguousarray(a, dtype=np.float32) for a in (y_p, y_s, ssm_p, conv_p, k_p, v_p, ssm_s, conv_s, k_s, v_s))


def kernel(**inputs):
    return run(inputs, SEQ=4096, NSS=16, NTG=512, n_prompt=4, ncores=NCORES)
```
